# Optimizing a Trainium2 kernel written in Bass

```python
import math
import jax, jax.numpy as jnp
from jax import lax
import numpy as np

D_MODEL = 1024
BATCH = 4
SEQ = 4096
DEPTH = 4

EPS = 1e-6
N_BRANCH = 4
Q_BLOCK = 128

A_PATTERNS = ((128, 1), (512, 4), (2048, 16))
A_GROUPS = len(A_PATTERNS)
A_SLOTS = 6
A_HEAD_DIM = 64
A_HEADS = A_GROUPS * A_SLOTS
A_QKV = A_HEADS * A_HEAD_DIM
A_WIDTH = A_SLOTS * A_HEAD_DIM

B_WIDTH = 384
B_BLOCKS = 6
B_BLOCK_DIM = B_WIDTH // B_BLOCKS
B_CONV = 4
B_CONV_LEFT = 1
B_C = 8.0

C_HEADS = 4
C_HALF = 64
C_VDIM = 2 * C_HALF
C_QK = C_HEADS * 2 * C_HALF
C_WIDTH = C_HEADS * C_VDIM

D_HEADS = 6
D_NOPE = 64
D_ROPE = 32
D_VDIM = 64
D_QLR = 256
D_KVLR = 128
D_WIDTH = D_HEADS * D_VDIM
ROPE_BASE = 10000.0

IN_SPLITS = (
    ("a_q", A_QKV), ("a_k", A_QKV), ("a_v", A_QKV), ("a_g", A_WIDTH),
    ("b_x", B_WIDTH), ("b_g", B_WIDTH),
    ("c_q", C_QK), ("c_k", C_QK), ("c_v", C_WIDTH), ("c_g", C_WIDTH),
    ("d_cq", D_QLR), ("d_ckv", D_KVLR), ("d_kr", D_ROPE), ("d_g", D_WIDTH),
    ("gate", N_BRANCH * D_MODEL),
)
IN_WIDTH = sum(size for _, size in IN_SPLITS)

kernel_name = "hybrid_gated_parallel_mixer_encoder"


def rmsnorm(x, g):
    xf = x.astype(jnp.float32)
    y = xf * lax.rsqrt(jnp.mean(xf * xf, axis=-1, keepdims=True) + EPS)
    return (y * g.astype(jnp.float32)).astype(x.dtype)


def alibi_slopes(n):
    return jnp.asarray([2.0 ** (-8.0 * (i + 1) / n) for i in range(n)], dtype=jnp.float32)


def rope_tables(s):
    inv = ROPE_BASE ** (-jnp.arange(0, D_ROPE, 2, dtype=jnp.float32) / D_ROPE)
    ang = jnp.arange(s, dtype=jnp.float32)[:, None] * inv[None, :]
    return jnp.cos(ang), jnp.sin(ang)


def apply_rope(x, cos, sin):
    x1, x2 = jnp.split(x, 2, axis=-1)
    return jnp.concatenate([x1 * cos - x2 * sin, x1 * sin + x2 * cos], axis=-1)


def split_in(p):
    out, off = {}, 0
    for name, size in IN_SPLITS:
        out[name] = p[..., off:off + size]
        off += size
    return out


def dilated_window_attention(q, k, v, window, dilation, slopes):
    b, s, h, dh = q.shape
    n = window // (2 * dilation)
    L = s // dilation
    nb = -(-L // n)
    Lp = nb * n

    def to_sub(t):
        t = t.reshape(b, L, dilation, h, dh).transpose(0, 2, 3, 1, 4)
        return jnp.pad(t, ((0, 0), (0, 0), (0, 0), (0, Lp - L), (0, 0)))

    def neighbourhood(t):
        t = jnp.pad(to_sub(t), ((0, 0), (0, 0), (0, 0), (n, n), (0, 0)))
        t = t.reshape(b, dilation, h, nb + 2, n, dh)
        return jnp.concatenate([t[:, :, :, :-2], t[:, :, :, 1:-1], t[:, :, :, 2:]], axis=4)

    qs = to_sub(q).reshape(b, dilation, h, nb, n, dh)
    ks, vs = neighbourhood(k), neighbourhood(v)
    qi = jnp.arange(nb)[:, None] * n + jnp.arange(n)[None, :]
    ki = jnp.arange(nb)[:, None] * n - n + jnp.arange(3 * n)[None, :]
    rel = jnp.abs(ki[:, None, :] - qi[:, :, None])
    valid = (rel <= n) & (ki[:, None, :] >= 0) & (ki[:, None, :] < L)
    dist = (rel * dilation).astype(jnp.float32)
    sc = jnp.einsum("brhnqd,brhnkd->brhnqk", qs, ks) * (dh ** -0.5)
    sc = sc - slopes[:, None, None, None] * dist
    sc = jnp.where(valid, sc, -jnp.inf)
    lse = jax.nn.logsumexp(sc, axis=-1)
    o = jnp.einsum("brhnqk,brhnkd->brhnqd", jnp.exp(sc - lse[..., None]), vs)
    o = o.reshape(b, dilation, h, Lp, dh)[:, :, :, :L].transpose(0, 3, 1, 2, 4).reshape(b, s, h, dh)
    lse = lse.reshape(b, dilation, h, Lp)[..., :L].transpose(0, 3, 1, 2).reshape(b, s, h)
    return o, lse


def mixer_a(q, k, v):
    b, s, _ = q.shape
    shp = (b, s, A_GROUPS, A_SLOTS, A_HEAD_DIM)
    q, k, v = q.reshape(shp), k.reshape(shp), v.reshape(shp)
    slopes = alibi_slopes(A_SLOTS)
    outs, lses = [], []
    for g, (window, dilation) in enumerate(A_PATTERNS):
        o, l = dilated_window_attention(q[:, :, g], k[:, :, g], v[:, :, g], window, dilation, slopes)
        outs.append(o)
        lses.append(l)
    wts = jax.nn.softmax(jnp.stack(lses, axis=2), axis=2)
    o = jnp.einsum("bsgh,bsghd->bshd", wts, jnp.stack(outs, axis=2))
    return o.reshape(b, s, A_WIDTH)


def rglru_scan(xc, w_r, b_r, w_i, b_i, lam, reverse):
    b, s, w = xc.shape
    xb = xc.reshape(b, s, B_BLOCKS, B_BLOCK_DIM)
    r = jax.nn.sigmoid(jnp.einsum("bsnc,ncd->bsnd", xb, w_r).reshape(b, s, w) + b_r)
    i = jax.nn.sigmoid(jnp.einsum("bsnc,ncd->bsnd", xb, w_i).reshape(b, s, w) + b_i)
    log_a = -B_C * r * jax.nn.softplus(-lam.astype(jnp.float32))
    a = jnp.exp(log_a)
    u = jnp.sqrt(-jnp.expm1(2.0 * log_a)) * (i * xc)

    def combine(e1, e2):
        a1, u1 = e1
        a2, u2 = e2
        return a1 * a2, a2 * u1 + u2

    _, hseq = lax.associative_scan(combine, (a, u), reverse=reverse, axis=1)
    return hseq


def mixer_b(xb, conv_w, conv_b, w_r, b_r, w_i, b_i, lam):
    s = xb.shape[1]
    xp = jnp.pad(xb, ((0, 0), (B_CONV_LEFT, B_CONV - 1 - B_CONV_LEFT), (0, 0)))
    xc = conv_b + sum(xp[:, j:j + s] * conv_w[j] for j in range(B_CONV))
    h_fwd = rglru_scan(xc, w_r[0], b_r[0], w_i[0], b_i[0], lam[0], reverse=False)
    h_bwd = rglru_scan(xc, w_r[1], b_r[1], w_i[1], b_i[1], lam[1], reverse=True)
    return h_fwd + h_bwd


def mixer_c(q, k, v, lam_q1, lam_k1, lam_q2, lam_k2, subln, layer):
    b, s, _ = q.shape
    nq = s // Q_BLOCK
    q = q.reshape(b, nq, Q_BLOCK, C_HEADS, 2, C_HALF).transpose(1, 0, 2, 3, 4, 5)
    k = k.reshape(b, s, C_HEADS, 2, C_HALF)
    v = v.reshape(b, s, C_HEADS, C_VDIM)
    lam_init = 0.8 - 0.6 * math.exp(-0.3 * layer)
    lam = jnp.exp(jnp.sum(lam_q1 * lam_k1)) - jnp.exp(jnp.sum(lam_q2 * lam_k2)) + lam_init
    slopes = alibi_slopes(C_HEADS)
    pos = jnp.arange(s, dtype=jnp.float32)
    qpos = pos.reshape(nq, Q_BLOCK)

    def block(args):
        qb, qp = args
        sc = jnp.einsum("bqhcd,bkhcd->bhcqk", qb, k) * (C_HALF ** -0.5)
        sc = sc - slopes[None, :, None, None, None] * jnp.abs(qp[:, None] - pos[None, :])
        p = jax.nn.softmax(sc, axis=-1)
        return jnp.einsum("bhqk,bkhd->bqhd", p[:, :, 0] - lam * p[:, :, 1], v)

    o = lax.map(block, (q, qpos))
    o = o.transpose(1, 0, 2, 3, 4).reshape(b, s, C_HEADS, C_VDIM)
    o = rmsnorm(o, subln) * (1.0 - lam_init)
    return o.reshape(b, s, C_WIDTH)


def mixer_d(c_q, c_kv, k_rope, q_norm, kv_norm, w_uq, w_ukv, cos, sin):
    b, s, _ = c_q.shape
    nq = s // Q_BLOCK
    q = jnp.einsum("bsr,re->bse", rmsnorm(c_q, q_norm), w_uq).reshape(b, s, D_HEADS, D_NOPE + D_ROPE)
    kv = jnp.einsum("bsr,re->bse", rmsnorm(c_kv, kv_norm), w_ukv).reshape(b, s, D_HEADS, D_NOPE + D_VDIM)
    q_rope = apply_rope(q[..., D_NOPE:], cos[None, :, None], sin[None, :, None])
    k_nope, v = kv[..., :D_NOPE], kv[..., D_NOPE:]
    k_rope = apply_rope(k_rope, cos[None], sin[None])
    scale = (D_NOPE + D_ROPE) ** -0.5
    qn = q[..., :D_NOPE].reshape(b, nq, Q_BLOCK, D_HEADS, D_NOPE).transpose(1, 0, 2, 3, 4)
    qr = q_rope.reshape(b, nq, Q_BLOCK, D_HEADS, D_ROPE).transpose(1, 0, 2, 3, 4)

    def block(args):
        qnb, qrb = args
        sc = (jnp.einsum("bqhd,bkhd->bhqk", qnb, k_nope)
              + jnp.einsum("bqhr,bkr->bhqk", qrb, k_rope)) * scale
        p = jax.nn.softmax(sc, axis=-1)
        return jnp.einsum("bhqk,bkhd->bqhd", p, v)

    o = lax.map(block, (qn, qr))
    return o.transpose(1, 0, 2, 3, 4).reshape(b, s, D_WIDTH)


def setup_inputs(seed: int = 0) -> dict:
    key = jax.random.key(seed)
    ks = jax.random.split(key, 28)
    f32 = jnp.float32

    def nrm(k, shape, scale):
        return jax.random.normal(k, shape, f32) * scale

    def gain(k, shape):
        return 1.0 + 0.02 * jax.random.normal(k, shape, f32)

    a0 = jax.random.uniform(ks[11], (DEPTH, 2, B_WIDTH), f32, 0.9, 0.999)
    return {
        "x": nrm(ks[0], (BATCH, SEQ, D_MODEL), 1.0),
        "norm_pre": gain(ks[1], (DEPTH, D_MODEL)),
        "norm_post": gain(ks[2], (DEPTH, D_MODEL)),
        "w_in": nrm(ks[3], (DEPTH, D_MODEL, IN_WIDTH), D_MODEL ** -0.5),
        "conv_w": nrm(ks[4], (DEPTH, B_CONV, B_WIDTH), B_CONV ** -0.5),
        "conv_b": nrm(ks[5], (DEPTH, B_WIDTH), 0.01),
        "lru_wr": nrm(ks[6], (DEPTH, 2, B_BLOCKS, B_BLOCK_DIM, B_BLOCK_DIM), B_BLOCK_DIM ** -0.5),
        "lru_br": nrm(ks[7], (DEPTH, 2, B_WIDTH), 0.01),
        "lru_wi": nrm(ks[8], (DEPTH, 2, B_BLOCKS, B_BLOCK_DIM, B_BLOCK_DIM), B_BLOCK_DIM ** -0.5),
        "lru_bi": nrm(ks[9], (DEPTH, 2, B_WIDTH), 0.01),
        "lru_lambda": jnp.log(a0) - jnp.log1p(-a0),
        "diff_lam_q1": nrm(ks[12], (DEPTH, C_HALF), 0.1),
        "diff_lam_k1": nrm(ks[13], (DEPTH, C_HALF), 0.1),
        "diff_lam_q2": nrm(ks[14], (DEPTH, C_HALF), 0.1),
        "diff_lam_k2": nrm(ks[15], (DEPTH, C_HALF), 0.1),
        "diff_subln": gain(ks[16], (DEPTH, C_VDIM)),
        "mla_q_norm": gain(ks[17], (DEPTH, D_QLR)),
        "mla_kv_norm": gain(ks[18], (DEPTH, D_KVLR)),
        "mla_w_uq": nrm(ks[19], (DEPTH, D_QLR, D_HEADS * (D_NOPE + D_ROPE)), D_QLR ** -0.5),
        "mla_w_ukv": nrm(ks[20], (DEPTH, D_KVLR, D_HEADS * (D_NOPE + D_VDIM)), D_KVLR ** -0.5),
        "w_br_a": nrm(ks[21], (DEPTH, A_WIDTH, D_MODEL), A_WIDTH ** -0.5),
        "w_br_b": nrm(ks[22], (DEPTH, B_WIDTH, D_MODEL), B_WIDTH ** -0.5),
        "w_br_c": nrm(ks[23], (DEPTH, C_WIDTH, D_MODEL), C_WIDTH ** -0.5),
        "w_br_d": nrm(ks[24], (DEPTH, D_WIDTH, D_MODEL), D_WIDTH ** -0.5),
        "b_gate": nrm(ks[25], (DEPTH, N_BRANCH, D_MODEL), 0.01),
        "w_out": nrm(ks[26], (DEPTH, D_MODEL, D_MODEL), D_MODEL ** -0.5),
    }


def reference(x, norm_pre, norm_post, w_in, conv_w, conv_b, lru_wr, lru_br, lru_wi, lru_bi,
              lru_lambda, diff_lam_q1, diff_lam_k1, diff_lam_q2, diff_lam_k2, diff_subln,
              mla_q_norm, mla_kv_norm, mla_w_uq, mla_w_ukv, w_br_a, w_br_b, w_br_c, w_br_d,
              b_gate, w_out):
    b, s, _ = x.shape
    cos, sin = rope_tables(s)
    for l in range(DEPTH):
        h = rmsnorm(x, norm_pre[l])
        p = split_in(jnp.einsum("bsd,de->bse", h, w_in[l], preferred_element_type=jnp.float32))
        y_a = mixer_a(p["a_q"], p["a_k"], p["a_v"]) * jax.nn.silu(p["a_g"])
        y_b = mixer_b(p["b_x"], conv_w[l], conv_b[l], lru_wr[l], lru_br[l], lru_wi[l], lru_bi[l],
                      lru_lambda[l]) * jax.nn.silu(p["b_g"])
        y_c = mixer_c(p["c_q"], p["c_k"], p["c_v"], diff_lam_q1[l], diff_lam_k1[l], diff_lam_q2[l],
                      diff_lam_k2[l], diff_subln[l], l) * jax.nn.silu(p["c_g"])
        y_d = mixer_d(p["d_cq"], p["d_ckv"], p["d_kr"], mla_q_norm[l], mla_kv_norm[l], mla_w_uq[l],
                      mla_w_ukv[l], cos, sin) * jax.nn.silu(p["d_g"])
        g = jax.nn.sigmoid(p["gate"].reshape(b, s, N_BRANCH, D_MODEL) + b_gate[l])
        merged = (g[:, :, 0] * (y_a @ w_br_a[l]) + g[:, :, 1] * (y_b @ w_br_b[l])
                  + g[:, :, 2] * (y_c @ w_br_c[l]) + g[:, :, 3] * (y_d @ w_br_d[l]))
        x = x + rmsnorm(merged @ w_out[l], norm_post[l]).astype(x.dtype)
    return x
```

```python
import math
from contextlib import ExitStack

import numpy as np
import ml_dtypes

import concourse.bass as bass
import concourse.mybir as mybir
from concourse.bass_utils import run_bass_kernel_spmd

F32 = mybir.dt.float32
BF16 = mybir.dt.bfloat16
AF = mybir.ActivationFunctionType
ALU = mybir.AluOpType
AX = mybir.AxisListType

S = 4096
DM = 1024
DEPTH = 4
NT = 8
TC = 512
KC = 8
EPS = 1e-6
A_DIL = (1, 4, 16)
A_SLOTS = 6
C_HEADS = 4
D_HEADS = 6
IN_W = 11552
OFF = dict(a_q=0, a_k=1152, a_v=2304, a_g=3456, b_x=3840, b_g=4224, c_q=4608, c_k=5120,
           c_v=5632, c_g=6144, d_cq=6656, d_ckv=6912, d_kr=7040, d_g=7072, gate=7456)
SKIP_T = 100.0

SM_GPRE, SM_GPOST, SM_BGATE, SM_CONVW, SM_CONVB = 0, 8, 16, 48, 60
SM_BR, SM_BI, SM_LAM, SM_SUBLN, SM_QN, SM_KVN, SM_LAMINIT, SM_OML = 63, 69, 75, 81, 82, 84, 85, 86
NSMALL = 88


def a_slopes():
    return [2.0 ** (-8.0 * (i + 1) / A_SLOTS) for i in range(A_SLOTS)]


def c_slopes():
    return [2.0 ** (-8.0 * (i + 1) / C_HEADS) for i in range(C_HEADS)]


class Sched:
    ENGS = ("pe", "act", "dve", "pool", "sp")
    NDMA = {"sp": 24, "act": 12, "pool": 4}

    def __init__(self):
        self.ops = []

    def add(self, eng, fn, reads=(), writes=(), dma=False):
        self.ops.append(dict(eng=eng, fn=fn, reads=tuple(reads), writes=tuple(writes), dma=dma,
                             needs_inc=False, deps=[]))

    def barrier(self):
        self.ops.append(dict(barrier=True))

    def analyze(self):
        ops = self.ops
        last_w, readers = {}, {}
        last_on = {}
        for i, op in enumerate(ops):
            if op.get("barrier"):
                for e, j in last_on.items():
                    ops[j]["needs_inc"] = True
                last_w, readers = {}, {}
                continue
            deps = set()
            for r in op["reads"]:
                if r in last_w:
                    deps.add((last_w[r], "raw"))
            for w in op["writes"]:
                if w in last_w:
                    deps.add((last_w[w], "waw"))
                for rd in readers.get(w, ()):
                    deps.add((rd, "war"))
            keep = set()
            for d, kind in deps:
                if d == i:
                    continue
                p = ops[d]
                if p["dma"]:
                    keep.add(d)
                elif p["eng"] == op["eng"]:
                    if op["dma"]:
                        keep.add(d)
                    elif kind == "raw" and op["eng"] in ("act", "dve", "pool"):
                        keep.add(d)
                else:
                    keep.add(d)
            op["deps"] = sorted(keep)
            for d in keep:
                ops[d]["needs_inc"] = True
            for w in op["writes"]:
                last_w[w] = i
                readers[w] = []
            for r in op["reads"]:
                if r not in op["writes"]:
                    readers.setdefault(r, []).append(i)
            if not op["dma"]:
                last_on[op["eng"]] = i
        tick = {e: 0 for e in self.ENGS}
        dma_cnt = {q: [0] * n for q, n in self.NDMA.items()}
        dma_rr = {q: 0 for q in self.NDMA}
        seen = {e: {} for e in self.ENGS}
        pending = {e: [] for e in self.ENGS}
        for op in ops:
            if op.get("barrier"):
                snap = [(("c", e), tick[e]) for e in self.ENGS if tick[e] > 0]
                for q, cnts in dma_cnt.items():
                    for k, c in enumerate(cnts):
                        if c > 0:
                            snap.append((("d", q, k), 16 * c))
                for e in self.ENGS:
                    pending[e] = [(s_, v) for (s_, v) in snap if s_ != ("c", e)]
                continue
            e = op["eng"]
            waits = list(pending[e])
            pending[e] = []
            for d in op["deps"]:
                p = ops[d]
                waits.append((p["sem"], p["tick"]))
            if op["dma"]:
                k = dma_rr[e]
                dma_rr[e] = (k + 1) % self.NDMA[e]
                if dma_cnt[e][k] > 0:
                    waits.append((("d", e, k), 16 * dma_cnt[e][k]))
                dma_cnt[e][k] += 1
                op["sem"] = ("d", e, k)
                op["tick"] = 16 * dma_cnt[e][k]
            elif op["needs_inc"]:
                tick[e] += 1
                op["sem"] = ("c", e)
                op["tick"] = tick[e]
            fw = []
            for s_, v in waits:
                if seen[e].get(s_, 0) >= v:
                    continue
                seen[e][s_] = v
                fw.append((s_, v))
            mx = {}
            for s_, v in fw:
                mx[s_] = max(mx.get(s_, 0), v)
            op["waits"] = sorted(mx.items(), key=lambda kv: str(kv[0]))
        self.final_dma = {(q, k): 16 * c for q, cnts in dma_cnt.items() for k, c in enumerate(cnts) if c > 0}

    def emit(self, nc, es):
        sems = {}
        for e in self.ENGS:
            sems[("c", e)] = es.enter_context(nc.semaphore("c_" + e))
        for q, n in self.NDMA.items():
            for k in range(n):
                sems[("d", q, k)] = es.enter_context(nc.semaphore("d_%s_%d" % (q, k)))
        blk = es.enter_context(nc.Block())
        ops = self.ops

        def run(engname):
            def body(e):
                for op in ops:
                    if op.get("barrier") or op["eng"] != engname:
                        continue
                    for s_, v in op["waits"]:
                        e.wait_ge(sems[s_], v)
                    ins = op["fn"](e)
                    if op["dma"]:
                        ins.then_inc(sems[op["sem"]], 16)
                    elif op["needs_inc"]:
                        ins.then_inc(sems[op["sem"]], 1)
                if engname == "sp":
                    for (q, k), v in sorted(self.final_dma.items()):
                        e.wait_ge(sems[("d", q, k)], v)
            return body

        blk.tensor(run("pe"))
        blk.scalar(run("act"))
        blk.vector(run("dve"))
        blk.gpsimd(run("pool"))
        blk.sync(run("sp"))


class Arena:
    def __init__(self, ap, nbytes, name):
        self.ap = ap
        self.nbytes = nbytes
        self.off = 0
        self.name = name
        self.cnt = 0

    def reset(self):
        self.off = 0

    def alloc(self, free_shape, dt, parts=128):
        n = 1
        for d in free_shape:
            n *= d
        nb = n * (4 if dt == F32 else 2)
        nb = (nb + 63) // 64 * 64
        assert self.off + nb <= self.nbytes, "arena %s overflow: %d + %d > %d" % (self.name, self.off, nb, self.nbytes)
        a = self.ap[:, self.off // 4:(self.off + nb) // 4]
        self.off += nb
        if dt == BF16:
            a = a.bitcast(BF16)
        a = a[:, 0:n]
        if len(free_shape) == 2:
            a = a.rearrange("p (a b) -> p a b", a=free_shape[0])
        elif len(free_shape) == 3:
            a = a.rearrange("p (a b c) -> p a b c", a=free_shape[0], b=free_shape[1])
        self.cnt += 1
        return a


class Rot:
    def __init__(self, arena, n, free_shape, dt, name):
        self.tiles = [arena.alloc(free_shape, dt) for _ in range(n)]
        self.name = name
        self.i = 0

    def next(self):
        k = self.i % len(self.tiles)
        self.i += 1
        return self.tiles[k], (self.name, k)


def build_program(nlayers, debug=False):
    nc = bass.Bass("TRN2", target_bir_lowering=False)
    L = nlayers

    def din(name, shape, dt=F32):
        return nc.dram_tensor(name, list(shape), dt, kind="ExternalInput").ap()

    def dscr(name, shape, dt=BF16):
        ext = bool(debug) and name in debug
        return nc.dram_tensor(name, list(shape), dt, kind="ExternalOutput" if ext else "Internal").ap()

    xT_in = din("xT", [DM, S])
    w_in = din("w_in", [L, DM, IN_W])
    small_d = din("small", [L, 128, NSMALL])
    lamv_d = din("lamv", [L, 1, 256])
    lruw_d = din("lruw", [L, 128, 12 * 128])
    wuqa_d = din("wuqa", [L, 128, 2 * 576])
    wuqb_d = din("wuqb", [L, 128, 2 * 576])
    wukvk_d = din("wukvk", [L, 128, 384])
    wukvv_d = din("wukvv", [L, 128, 384])
    wbr_d = din("wbr", [L, 128, 13 * 1024])
    wout_d = din("wout", [L, 128, 8 * 1024])
    caugq_d = din("caugq", [C_HEADS, 2, 4, TC], BF16)
    caugk_d = din("caugk", [C_HEADS, 4, S], BF16)
    mdiag_d = din("mdiag", [128, C_HEADS * 128])
    ma_d = din("ma", [128, 18 * 384])
    ropec_d = din("ropec", [32, S])
    ropes_d = din("ropes", [32, S])
    outT = nc.dram_tensor("outT", [DM, S], F32, kind="ExternalOutput").ap()

    AQ = [dscr("AQ%d" % g, [384, S]) for g in range(3)]
    AK = [dscr("AK%d" % g, [384, S]) for g in range(3)]
    AV = [dscr("AV%d" % g, [128, 6 * 32 * 64]) for g in range(3)]
    AG = dscr("AG", [384, S])
    BX = dscr("BX", [384, S], F32)
    BG = dscr("BG", [384, S])
    CQ = dscr("CQ", [512, S])
    CK = dscr("CK", [512, S])
    CV = dscr("CV", [128, 4 * 32 * 128])
    CG = dscr("CG", [512, S])
    DLAT = dscr("DLAT", [384, S], F32)
    DKRAW = dscr("DKRAW", [64, S], F32)
    DG = dscr("DG", [384, S])
    GT = dscr("GT", [4096, S])
    DQ = dscr("DQ", [D_HEADS * 96, S])
    DK = dscr("DK", [D_HEADS * 64, S])
    DKR = dscr("DKR", [32, S])
    DV = dscr("DV", [128, 6 * 32 * 64])
    Y = dscr("Y", [1664, S])
    XS = [nc.dram_tensor("XS%d" % i, [DM, S], F32, kind="Internal").ap() for i in range(2)]

    sc = Sched()
    es = ExitStack()
    PERS_BYTES = 56 * 1024
    ARENA_BYTES = 136 * 1024
    pers_t = es.enter_context(nc.sbuf_tensor("pers", [128, PERS_BYTES // 4], F32))
    arena_t = es.enter_context(nc.sbuf_tensor("arena", [128, ARENA_BYTES // 4], F32))
    pers = Arena(pers_t[:], PERS_BYTES, "pers")
    ar = Arena(arena_t[:], ARENA_BYTES, "arena")
    banks = [es.enter_context(nc.psum_tensor("bank%d" % i, [128, 512], F32)) for i in range(8)]

    def bank(i):
        return banks[i][:], ("bank", i)

    small = pers.alloc([NSMALL], F32)
    lamv = pers.alloc([256], F32)
    lamt = pers.alloc([16], F32)
    spc = pers.alloc([8], F32)
    ones_bf = pers.alloc([128], BF16)
    ones_f = pers.alloc([128], F32)
    mdiag = pers.alloc([C_HEADS, 128], F32)
    wst = [pers.alloc([4096], F32) for _ in range(2)]
    wbf = [pers.alloc([4096], BF16) for _ in range(2)]
    wst_i = [0]

    def col(c, n=1):
        return small[:, c:c + n]

    sc.add("dve", lambda e: e.memset(ones_bf, 1.0), writes=["ones_bf"])
    sc.add("dve", lambda e: e.memset(ones_f, 1.0), writes=["ones_f"])
    sc.add("sp", lambda e: e.dma_start(out=mdiag.rearrange("p a b -> p (a b)"), in_=mdiag_d), writes=["mdiag"], dma=True)

    def load_cast(src_ap, ncols_f32, dst_bf, dst_res):
        k = wst_i[0] % 2
        wst_i[0] += 1
        st = wst[k]
        sc.add("sp", lambda e: e.dma_start(out=st[:, 0:ncols_f32], in_=src_ap), writes=[("wst", k)], dma=True)
        sc.add("pool", lambda e: e.tensor_copy(out=dst_bf, in_=st[:, 0:ncols_f32]), reads=[("wst", k)], writes=[dst_res])

    def layer(l):
        x_src = xT_in if l == 0 else XS[(l - 1) % 2]
        x_dst = outT if l == L - 1 else XS[l % 2]
        w_l = w_in[l]
        w_l_v = w_l.rearrange("(kc p) n -> p kc n", p=128)

        sc.add("sp", lambda e, l=l: e.dma_start(out=small, in_=small_d[l]), writes=["small"], dma=True)
        sc.add("sp", lambda e, l=l: e.dma_start(out=lamv, in_=lamv_d[l].partition_broadcast(128)), writes=["lamv"], dma=True)
        ar.reset()
        tmpl = ar.alloc([128], F32)

        sc.add("dve", lambda e: e.tensor_tensor(out=tmpl[:, 0:64], in0=lamv[:, 0:64], in1=lamv[:, 64:128], op=ALU.mult), reads=["lamv"], writes=["tmpl0"])
        sc.add("dve", lambda e: e.tensor_tensor(out=tmpl[:, 64:128], in0=lamv[:, 128:192], in1=lamv[:, 192:256], op=ALU.mult), reads=["lamv"], writes=["tmpl1"])
        sc.add("dve", lambda e: e.reduce_sum(out=lamt[:, 0:1], in_=tmpl[:, 0:64], axis=AX.X), reads=["tmpl0"], writes=["lamt0"])
        sc.add("dve", lambda e: e.reduce_sum(out=lamt[:, 1:2], in_=tmpl[:, 64:128], axis=AX.X), reads=["tmpl1"], writes=["lamt1"])
        sc.add("act", lambda e: e.activation(out=lamt[:, 2:4], in_=lamt[:, 0:2], func=AF.Exp), reads=["lamt0", "lamt1"], writes=["lamt23"])
        sc.add("dve", lambda e: e.tensor_tensor(out=lamt[:, 6:7], in0=lamt[:, 3:4], in1=lamt[:, 2:3], op=ALU.subtract), reads=["lamt23"], writes=["lamt6"])
        sc.add("dve", lambda e: e.tensor_tensor(out=lamt[:, 4:5], in0=lamt[:, 6:7], in1=col(SM_LAMINIT), op=ALU.subtract), reads=["lamt6", "small"], writes=["lamt4"])
        sc.add("dve", lambda e: e.tensor_tensor(out=lamt[:, 5:6], in0=col(SM_SUBLN), in1=col(SM_OML), op=ALU.mult), reads=["small"], writes=["lamt5"])
        sc.add("act", lambda e: e.activation(out=spc[:, 0:6], in_=col(SM_LAM, 6), func=AF.Exp, scale=-1.0), reads=["small"], writes=["spc_a"])
        sc.add("act", lambda e: e.activation(out=lamt[:, 8:14], in_=spc[:, 0:6], func=AF.Ln, bias=1.0), reads=["spc_a"], writes=["spc_b"])
        sc.add("dve", lambda e: e.tensor_scalar(out=spc[:, 0:6], in0=lamt[:, 8:14], scalar1=-8.0, scalar2=None, op0=ALU.mult),
               reads=["spc_b"], writes=["spc"])
        sc.barrier()

        ar.reset()
        hT = ar.alloc([KC, S], BF16)
        p0mark = ar.off
        xt_rot = Rot(ar, 2, [KC, TC], F32, "xt")
        sq_t = ar.alloc([KC, TC], F32)
        rstd_rot = Rot(ar, 2, [TC], F32, "rstd")
        x_src_v = x_src.rearrange("(kc p) s -> p kc s", p=128)
        for t in range(NT):
            xt, xr = xt_rot.next()
            sc.add("sp", lambda e, xt=xt, t=t: e.dma_start(out=xt, in_=x_src_v[:, :, t * TC:(t + 1) * TC]), writes=[xr], dma=True)
            sc.add("act", lambda e, xt=xt: e.activation(out=sq_t, in_=xt, func=AF.Square), reads=[xr], writes=["sq_t"])
            bk, br = bank(t % 2)

            def ssmm(e, bk=bk):
                for kc in range(KC):
                    r = e.matmul(bk, lhsT=ones_f, rhs=sq_t[:, kc, :], start=(kc == 0), stop=(kc == KC - 1))
                return r
            sc.add("pe", ssmm, reads=["sq_t", "ones_f"], writes=[br])
            rs, rr = rstd_rot.next()
            sc.add("act", lambda e, bk=bk, rs=rs: e.activation(out=rs, in_=bk, func=AF.Sqrt, scale=1.0 / DM, bias=EPS), reads=[br], writes=[rr])
            sc.add("dve", lambda e, rs=rs: e.reciprocal(out=rs, in_=rs), reads=[rr], writes=[rr])

            def hmk(e, xt=xt, rs=rs, t=t):
                for kc in range(KC):
                    r = e.scalar_tensor_tensor(out=hT[:, kc, t * TC:(t + 1) * TC], in0=xt[:, kc, :], scalar=col(SM_GPRE + kc),
                                               in1=rs, op0=ALU.mult, op1=ALU.mult)
                return r
            sc.add("dve", hmk, reads=[xr, rr, "small"], writes=[("hT", t)])
        sc.barrier()

        ar.off = p0mark
        ob_rot_a = Rot(ar, 3, [TC], BF16, "ob_a")
        ob_rot_d = Rot(ar, 3, [TC], BF16, "ob_d")
        of_rot = Rot(ar, 2, [TC], F32, "of")
        vstage = ar.alloc([4 * 32 * 128], BF16)
        hT_res = [("hT", t) for t in range(NT)]
        bank_i = [0]

        def nbank():
            b = bank_i[0] % 8
            bank_i[0] += 1
            return bank(b)

        fm_jobs = []
        for g in range(3):
            fm_jobs.append((OFF["a_q"] + g * 384, 384, A_DIL[g], "scale", AQ[g], 0.125))
        for g in range(3):
            fm_jobs.append((OFF["a_k"] + g * 384, 384, A_DIL[g], "copy", AK[g], None))
        fm_jobs.append((OFF["a_g"], 384, 1, "silu", AG, None))
        fm_jobs.append((OFF["b_x"], 384, 1, "copyf", BX, None))
        fm_jobs.append((OFF["b_g"], 384, 1, "silu", BG, None))
        fm_jobs.append((OFF["c_q"], 512, 1, "scale", CQ, 0.125))
        fm_jobs.append((OFF["c_k"], 512, 1, "copy", CK, None))
        fm_jobs.append((OFF["c_g"], 512, 1, "silu", CG, None))
        fm_jobs.append((OFF["d_cq"], 384, 1, "copyf", DLAT, None))
        fm_jobs.append((OFF["d_kr"], 64, 1, "copyf_kr", DKRAW, None))
        fm_jobs.append((OFF["d_g"], 384, 1, "silu", DG, None))
        for i in range(8):
            fm_jobs.append((OFF["gate"] + i * 512, 512, 1, "gate", GT[i * 512:(i + 1) * 512, :], i))
        tm_jobs = []
        for g in range(3):
            tm_jobs.append((OFF["a_v"] + g * 384, 384, A_DIL[g], AV[g], 6, 64))
        tm_jobs.append((OFF["c_v"], 512, 1, CV, 4, 128))
        jobs = [("fm", j) for j in fm_jobs] + [("tm", j) for j in tm_jobs]

        def load_job(ji):
            kind, j = jobs[ji]
            c0, n = j[0], j[1]
            k = ji % 2
            st = wst[k].rearrange("p (kc n) -> p kc n", kc=KC)
            wb = wbf[k].rearrange("p (kc n) -> p kc n", kc=KC)
            if kind == "fm" and j[3] == "copyf_kr":
                sc.add("sp", lambda e: e.dma_start(out=st[:, :, 0:32], in_=w_l_v[:, :, c0:c0 + 32]), writes=[("wst", k)], dma=True)
                sc.add("sp", lambda e: e.dma_start(out=st[:, :, 32:48], in_=w_l_v[:, :, c0 + 16:c0 + 32]), writes=[("wst", k, 1)], dma=True)
                sc.add("sp", lambda e: e.dma_start(out=st[:, :, 48:64], in_=w_l_v[:, :, c0:c0 + 16]), writes=[("wst", k, 2)], dma=True)
                rd = [("wst", k), ("wst", k, 1), ("wst", k, 2)]
            else:
                sc.add("sp", lambda e: e.dma_start(out=st[:, :, 0:n], in_=w_l_v[:, :, c0:c0 + n]), writes=[("wst", k)], dma=True)
                rd = [("wst", k)]
            sc.add("pool", lambda e: e.tensor_copy(out=wb[:, :, 0:n], in_=st[:, :, 0:n]), reads=rd, writes=[("wbf", k)])

        def rhs_view(d, kc, t):
            if d == 1:
                return [(hT[:, kc, t * TC:(t + 1) * TC], None)]
            hv = hT[:, kc, :].rearrange("p (l r) -> p r l", r=d)
            if d == 4:
                return [(hv[:, t // 2, (t % 2) * 512:(t % 2) * 512 + 512], None)]
            return [(hv[:, 2 * t:2 * t + 2, :], 2)]

        def lhs_view(d, kc, b):
            if d == 1:
                return hT[:, kc, b * 128:(b + 1) * 128]
            hv = hT[:, kc, :].rearrange("p (l r) -> p r l", r=d)
            nbs = 32 // d
            return hv[:, b // nbs, (b % nbs) * 128:(b % nbs) * 128 + 128]

        def compute_fm(ji):
            _, (c0, n, d, kind, dst, extra) = jobs[ji]
            k = ji % 2
            wb = wbf[k].rearrange("p (kc n) -> p kc n", kc=KC)
            ncb = (n + 127) // 128
            for cb in range(ncb):
                m = min(128, n - cb * 128)
                for t in range(NT):
                    bk, bres = nbank()

                    def mm(e, bk=bk, cb=cb, m=m, t=t):
                        r = None
                        for kc in range(KC):
                            (rv, a), = rhs_view(d, kc, t)
                            o = bk[0:m, :]
                            if a is not None:
                                o = o.rearrange("p (a b) -> p a b", a=a)
                            r = e.matmul(o, lhsT=wb[:, kc, cb * 128:cb * 128 + m], rhs=rv, start=(kc == 0), stop=(kc == KC - 1))
                        return r
                    sc.add("pe", mm, reads=[("wbf", k)] + hT_res, writes=[bres])
                    drows = dst[cb * 128:cb * 128 + m, t * TC:(t + 1) * TC]
                    if kind in ("silu", "gate"):
                        ot, ores = ob_rot_a.next()
                        if kind == "silu":
                            sc.add("act", lambda e, ot=ot, bk=bk, m=m: e.activation(out=ot[0:m, :], in_=bk[0:m, :], func=AF.Silu),
                                   reads=[bres], writes=[ores])
                        else:
                            bcol = SM_BGATE + (extra // 2) * 8 + (extra % 2) * 4 + cb
                            sc.add("act", lambda e, ot=ot, bk=bk, bcol=bcol: e.activation(out=ot, in_=bk, func=AF.Sigmoid, bias=col(bcol)),
                                   reads=[bres, "small"], writes=[ores])
                        sc.add("act", lambda e, ot=ot, drows=drows, m=m: e.dma_start(out=drows, in_=ot[0:m, :]), reads=[ores], dma=True)
                    elif kind in ("copy", "scale"):
                        ot, ores = ob_rot_d.next()
                        if kind == "copy":
                            sc.add("dve", lambda e, ot=ot, bk=bk, m=m: e.tensor_copy(out=ot[0:m, :], in_=bk[0:m, :]), reads=[bres], writes=[ores])
                        else:
                            sc.add("dve", lambda e, ot=ot, bk=bk, m=m: e.tensor_scalar(out=ot[0:m, :], in0=bk[0:m, :], scalar1=extra, scalar2=None, op0=ALU.mult),
                                   reads=[bres], writes=[ores])
                        sc.add("sp", lambda e, ot=ot, drows=drows, m=m: e.dma_start(out=drows, in_=ot[0:m, :]), reads=[ores], dma=True)
                    else:
                        ot, ores = of_rot.next()
                        sc.add("dve", lambda e, ot=ot, bk=bk, m=m: e.tensor_copy(out=ot[0:m, :], in_=bk[0:m, :]), reads=[bres], writes=[ores])
                        sc.add("sp", lambda e, ot=ot, drows=drows, m=m: e.dma_start(out=drows, in_=ot[0:m, :]), reads=[ores], dma=True)

        def compute_tm(ji):
            _, (c0, n, d, dst, nh, hd) = jobs[ji]
            k = ji % 2
            wb = wbf[k].rearrange("p (kc n) -> p kc n", kc=KC)
            vs = vstage[:, 0:nh * 32 * hd].rearrange("p (h b d) -> p h b d", h=nh, b=32)
            for b in range(32):
                bk, bres = nbank()

                def mm(e, bk=bk, b=b):
                    r = None
                    for kc in range(KC):
                        r = e.matmul(bk[:, 0:n], lhsT=lhs_view(d, kc, b), rhs=wb[:, kc, 0:n], start=(kc == 0), stop=(kc == KC - 1))
                    return r
                sc.add("pe", mm, reads=[("wbf", k)] + hT_res, writes=[bres])
                sc.add("dve", lambda e, bk=bk, b=b: e.tensor_copy(out=vs[:, :, b, :], in_=bk[:, 0:n].rearrange("p (h d) -> p h d", h=nh)),
                       reads=[bres], writes=[("vstage", b)])
            sc.add("sp", lambda e: e.dma_start(out=dst, in_=vstage[:, 0:nh * 32 * hd]), reads=[("vstage", b) for b in range(32)], dma=True)

        load_job(0)
        for ji in range(len(jobs)):
            if ji + 1 < len(jobs):
                load_job(ji + 1)
            if jobs[ji][0] == "fm":
                compute_fm(ji)
            else:
                compute_tm(ji)
        sc.barrier()

        ar.reset()
        wuqa = ar.alloc([2, 576], BF16)
        wuqb = ar.alloc([2, 576], BF16)
        wukvk = ar.alloc([384], BF16)
        wukvv = ar.alloc([384], BF16)
        load_cast(wuqa_d[l], 1152, wuqa.rearrange("p a b -> p (a b)"), "wuqa")
        load_cast(wuqb_d[l], 1152, wuqb.rearrange("p a b -> p (a b)"), "wuqb")
        load_cast(wukvk_d[l], 384, wukvk, "wukvk")
        load_cast(wukvv_d[l], 384, wukvv, "wukvv")
        lat_rot = Rot(ar, 2, [3, TC], F32, "lat")
        kra_rot = Rot(ar, 2, [TC], F32, "kra")
        krb_rot = Rot(ar, 2, [TC], F32, "krb")
        cc_rot = Rot(ar, 2, [TC], F32, "cc")
        ss_rot = Rot(ar, 2, [TC], F32, "ss")
        sq2 = ar.alloc([3, TC], F32)
        rq_rot = Rot(ar, 2, [TC], F32, "rq")
        rkv_rot = Rot(ar, 2, [TC], F32, "rkv")
        cqn_rot = Rot(ar, 2, [3, TC], BF16, "cqn")
        qd_rot = Rot(ar, 3, [TC], BF16, "qd")
        kd_rot = Rot(ar, 3, [TC], BF16, "kd")
        t1_rot = Rot(ar, 2, [TC], F32, "t1")
        t2_rot = Rot(ar, 2, [TC], F32, "t2")
        krr_rot = Rot(ar, 2, [TC], BF16, "krr")
        dvst = ar.alloc([6, 32, 64], BF16)
        DLv = DLAT.rearrange("(c p) s -> p c s", p=128)
        for t in range(NT):
            tsl = slice(t * TC, (t + 1) * TC)
            lat, latr = lat_rot.next()
            kra, krar = kra_rot.next()
            krb, krbr = krb_rot.next()
            cct, ccr = cc_rot.next()
            sst, ssr = ss_rot.next()
            sc.add("sp", lambda e, lat=lat, tsl=tsl: e.dma_start(out=lat, in_=DLv[:, :, tsl]), writes=[latr], dma=True)
            sc.add("sp", lambda e, kra=kra, tsl=tsl: e.dma_start(out=kra[64:96, :], in_=DKRAW[0:32, tsl]), writes=[krar], dma=True)
            sc.add("sp", lambda e, krb=krb, tsl=tsl: e.dma_start(out=krb[64:96, :], in_=DKRAW[32:64, tsl]), writes=[krbr], dma=True)
            sc.add("sp", lambda e, cct=cct, tsl=tsl: e.dma_start(out=cct[64:96, :], in_=ropec_d[:, tsl]), writes=[ccr], dma=True)
            sc.add("sp", lambda e, sst=sst, tsl=tsl: e.dma_start(out=sst[64:96, :], in_=ropes_d[:, tsl]), writes=[ssr], dma=True)
            sc.add("act", lambda e, lat=lat: e.activation(out=sq2, in_=lat, func=AF.Square), reads=[latr], writes=["sq2"])
            b0, b0r = bank(0)
            b1, b1r = bank(1)

            def ssq(e, b0=b0, b1=b1):
                e.matmul(b0, lhsT=ones_f, rhs=sq2[:, 0, :], start=True, stop=False)
                e.matmul(b0, lhsT=ones_f, rhs=sq2[:, 1, :], start=False, stop=True)
                return e.matmul(b1, lhsT=ones_f, rhs=sq2[:, 2, :], start=True, stop=True)
            sc.add("pe", ssq, reads=["sq2", "ones_f"], writes=[b0r, b1r])
            rq, rqr = rq_rot.next()
            rkv, rkvr = rkv_rot.next()

            def rsq(e, rq=rq, rkv=rkv, b0=b0, b1=b1):
                e.activation(out=rq, in_=b0, func=AF.Sqrt, scale=1.0 / 256, bias=EPS)
                return e.activation(out=rkv, in_=b1, func=AF.Sqrt, scale=1.0 / 128, bias=EPS)
            sc.add("act", rsq, reads=[b0r, b1r], writes=[rqr, rkvr])
            cqn, cqnr = cqn_rot.next()

            def nrm(e, rq=rq, rkv=rkv, lat=lat, cqn=cqn):
                e.reciprocal(out=rq, in_=rq)
                e.reciprocal(out=rkv, in_=rkv)
                e.scalar_tensor_tensor(out=cqn[:, 0, :], in0=lat[:, 0, :], scalar=col(SM_QN), in1=rq, op0=ALU.mult, op1=ALU.mult)
                e.scalar_tensor_tensor(out=cqn[:, 1, :], in0=lat[:, 1, :], scalar=col(SM_QN + 1), in1=rq, op0=ALU.mult, op1=ALU.mult)
                return e.scalar_tensor_tensor(out=cqn[:, 2, :], in0=lat[:, 2, :], scalar=col(SM_KVN), in1=rkv, op0=ALU.mult, op1=ALU.mult)
            sc.add("dve", nrm, reads=[rqr, rkvr, latr, "small"], writes=[cqnr, rqr, rkvr])
            t1, t1r = t1_rot.next()
            t2, t2r = t2_rot.next()
            krr, krrr = krr_rot.next()

            def krope(e, kra=kra, krb=krb, cct=cct, sst=sst, t1=t1, t2=t2, krr=krr):
                e.tensor_tensor(out=t1[64:96, :], in0=krb[64:96, :], in1=sst[64:96, :], op=ALU.mult)
                e.tensor_tensor(out=t2[64:96, :], in0=kra[64:96, :], in1=cct[64:96, :], op=ALU.mult)
                return e.tensor_tensor(out=krr[64:96, :], in0=t1[64:96, :], in1=t2[64:96, :], op=ALU.add)
            sc.add("dve", krope, reads=[krar, krbr, ccr, ssr], writes=[t1r, t2r, krrr])
            sc.add("sp", lambda e, krr=krr, tsl=tsl: e.dma_start(out=DKR[:, tsl], in_=krr[64:96, :]), reads=[krrr], dma=True)
            for h in range(D_HEADS):
                ba, bar_ = bank(2 + (h % 2) * 3)
                bb, bbr = bank(3 + (h % 2) * 3)
                bkk, bkr = bank(4 + (h % 2) * 3)

                def upq(e, ba=ba, bb=bb, bkk=bkk, h=h, cqn=cqn):
                    for c in range(2):
                        e.matmul(ba[0:96, :], lhsT=wuqa[:, c, h * 96:(h + 1) * 96], rhs=cqn[:, c, :], start=(c == 0), stop=(c == 1))
                    for c in range(2):
                        e.matmul(bb[0:96, :], lhsT=wuqb[:, c, h * 96:(h + 1) * 96], rhs=cqn[:, c, :], start=(c == 0), stop=(c == 1))
                    return e.matmul(bkk[0:64, :], lhsT=wukvk[:, h * 64:(h + 1) * 64], rhs=cqn[:, 2, :], start=True, stop=True)
                sc.add("pe", upq, reads=[cqnr, "wuqa", "wuqb", "wukvk"], writes=[bar_, bbr, bkr])
                qd, qdr = qd_rot.next()
                kd, kdr = kd_rot.next()
                t1, t1r = t1_rot.next()
                t2, t2r = t2_rot.next()

                def qrope(e, ba=ba, bb=bb, qd=qd, t1=t1, t2=t2, cct=cct, sst=sst):
                    e.tensor_copy(out=qd[0:64, :], in_=ba[0:64, :])
                    e.tensor_tensor(out=t1[64:96, :], in0=bb[64:96, :], in1=sst[64:96, :], op=ALU.mult)
                    e.tensor_tensor(out=t2[64:96, :], in0=ba[64:96, :], in1=cct[64:96, :], op=ALU.mult)
                    return e.tensor_tensor(out=qd[64:96, :], in0=t1[64:96, :], in1=t2[64:96, :], op=ALU.add)
                sc.add("dve", qrope, reads=[bar_, bbr, ccr, ssr], writes=[qdr, t1r, t2r])
                sc.add("sp", lambda e, qd=qd, h=h, tsl=tsl: e.dma_start(out=DQ[h * 96:(h + 1) * 96, tsl], in_=qd[0:96, :]), reads=[qdr], dma=True)
                sc.add("act", lambda e, kd=kd, bkk=bkk: e.activation(out=kd[0:64, :], in_=bkk[0:64, :], func=AF.Copy), reads=[bkr], writes=[kdr])
                sc.add("act", lambda e, kd=kd, h=h, tsl=tsl: e.dma_start(out=DK[h * 64:(h + 1) * 64, tsl], in_=kd[0:64, :]), reads=[kdr], dma=True)
            for tb in range(4):
                bv, bvr = bank(tb % 2)
                b = t * 4 + tb
                sc.add("pe", lambda e, bv=bv, tb=tb, cqn=cqn: e.matmul(bv[:, 0:384], lhsT=cqn[:, 2, tb * 128:(tb + 1) * 128], rhs=wukvv, start=True, stop=True),
                       reads=[cqnr, "wukvv"], writes=[bvr])
                sc.add("act", lambda e, bv=bv, b=b: e.activation(out=dvst[:, :, b, :], in_=bv[:, 0:384].rearrange("p (h d) -> p h d", h=6), func=AF.Copy),
                       reads=[bvr], writes=[("dvst", b)])
        sc.add("sp", lambda e: e.dma_start(out=DV, in_=dvst.rearrange("p h b d -> p (h b d)")), reads=[("dvst", b) for b in range(32)], dma=True)
        sc.barrier()

        ar.reset()
        ma = ar.alloc([18, 384], F32)
        sc.add("sp", lambda e: e.dma_start(out=ma.rearrange("p a b -> p (a b)"), in_=ma_d), writes=["ma"], dma=True)
        acc = ar.alloc([S], F32)
        kt_rot = Rot(ar, 2, [S], BF16, "kt")
        qt_rot = Rot(ar, 2, [S], BF16, "qt")
        vraw_rot = Rot(ar, 2, [32 * 64], BF16, "vraw")
        vt_tiles = [ar.alloc([32, 128], BF16) for _ in range(2)]
        for k in range(2):
            sc.add("pool", lambda e, k=k: e.memset(vt_tiles[k][:, :, 64:128], 1.0), writes=[("vt1", k)])
        pf_rot = Rot(ar, 2, [384], F32, "pf")
        pb_rot = Rot(ar, 2, [384], BF16, "pb")
        ag_rot = Rot(ar, 2, [TC], BF16, "ag")
        rz_rot = Rot(ar, 2, [TC], F32, "rz")
        yf_rot = Rot(ar, 2, [TC], F32, "yf")
        yb_rot = Rot(ar, 2, [TC], BF16, "yb")
        vt_cnt = [0]

        def a_group(s_, g):
            if True:
                d = A_DIL[g]
                nbs = 32 // d
                kt, ktr = kt_rot.next()
                qt, qtr = qt_rot.next()
                vraw, vrr = vraw_rot.next()
                vk = vt_cnt[0] % 2
                vt_cnt[0] += 1
                vt, vtr = vt_tiles[vk], ("vt", vk)
                sc.add("sp", lambda e, kt=kt, g=g, s_=s_: e.dma_start(out=kt[0:64, :], in_=AK[g][s_ * 64:(s_ + 1) * 64, :]), writes=[ktr], dma=True)
                sc.add("sp", lambda e, qt=qt, g=g, s_=s_: e.dma_start(out=qt[0:64, :], in_=AQ[g][s_ * 64:(s_ + 1) * 64, :]), writes=[qtr], dma=True)
                sc.add("sp", lambda e, vraw=vraw, g=g, s_=s_: e.dma_start(out=vraw, in_=AV[g][:, s_ * 2048:(s_ + 1) * 2048]), writes=[vrr], dma=True)
                sc.add("pool", lambda e, vt=vt, vraw=vraw: e.tensor_copy(out=vt[:, :, 0:64], in_=vraw.rearrange("p (b d) -> p b d", b=32)),
                       reads=[vrr, ("vt1", vk)], writes=[vtr])
                accv = acc if d == 1 else acc.rearrange("p (l r) -> p r l", r=d)

                def s_op(qb, psS, psSr):
                    lb = qb % nbs
                    js = [j for j in range(3) if 0 <= lb - 1 + j < nbs]

                    def f(e):
                        r = None
                        for j in js:
                            kb = qb - 1 + j
                            r = e.matmul(psS[:, j * 128:(j + 1) * 128], lhsT=kt[0:64, kb * 128:(kb + 1) * 128], rhs=qt[0:64, qb * 128:(qb + 1) * 128],
                                         start=True, stop=True)
                        return r
                    sc.add("pe", f, reads=[ktr, qtr], writes=[psSr])
                    return js

                pend = None
                for qb in range(33):
                    cur = None
                    if qb < 32:
                        psS, psSr = bank(qb % 2)
                        js = s_op(qb, psS, psSr)
                        cur = (qb, psS, psSr, js)
                    if pend is not None:
                        pq, ps_, psr_, pjs = pend
                        c0, c1 = pjs[0] * 128, (pjs[-1] + 1) * 128
                        pf, pfr = pf_rot.next()
                        pb, pbr = pb_rot.next()
                        sc.add("act", lambda e, pf=pf, ps_=ps_, c0=c0, c1=c1: e.activation(out=pf[:, c0:c1], in_=ps_[:, c0:c1], func=AF.Exp),
                               reads=[psr_], writes=[pfr])
                        sc.add("dve", lambda e, pf=pf, pb=pb, c0=c0, c1=c1, mi=g * 6 + s_: e.tensor_tensor(out=pb[:, c0:c1], in0=pf[:, c0:c1], in1=ma[:, mi, c0:c1], op=ALU.mult),
                               reads=[pfr, "ma"], writes=[pbr])
                        psO, psOr = bank(2 + pq % 2)

                        def pv(e, pq=pq, pjs=pjs, pb=pb, psO=psO):
                            r = None
                            for j in pjs:
                                kb = pq - 1 + j
                                r = e.matmul(psO[:, 0:128], lhsT=vt[:, kb, :], rhs=pb[:, j * 128:(j + 1) * 128], start=(j == pjs[0]), stop=(j == pjs[-1]))
                            return r
                        sc.add("pe", pv, reads=[pbr, vtr], writes=[psOr])
                        rr_, lb_ = pq // nbs, pq % nbs
                        av = acc[:, pq * 128:(pq + 1) * 128] if d == 1 else accv[:, rr_, lb_ * 128:(lb_ + 1) * 128]
                        if g == 0:
                            sc.add("dve", lambda e, av=av, psO=psO: e.tensor_copy(out=av, in_=psO[:, 0:128]), reads=[psOr], writes=["acc"])
                        else:
                            sc.add("dve", lambda e, av=av, psO=psO: e.tensor_tensor(out=av, in0=psO[:, 0:128], in1=av, op=ALU.add), reads=[psOr, "acc"], writes=["acc"])
                    pend = cur
        for s_ in range(A_SLOTS):
            for g in range(3):
                a_group(s_, g)
            for t in range(NT):
                tsl = slice(t * TC, (t + 1) * TC)
                agt, agr = ag_rot.next()
                rz, rzr = rz_rot.next()
                yf, yfr = yf_rot.next()
                yb, ybr = yb_rot.next()
                sc.add("sp", lambda e, agt=agt, tsl=tsl, s_=s_: e.dma_start(out=agt[0:64, :], in_=AG[s_ * 64:(s_ + 1) * 64, tsl]), writes=[agr], dma=True)

                def epi(e, rz=rz, yf=yf, yb=yb, agt=agt, tsl=tsl):
                    e.reciprocal(out=rz[0:64, :], in_=acc[64:128, tsl])
                    e.tensor_tensor(out=yf[0:64, :], in0=acc[0:64, tsl], in1=rz[0:64, :], op=ALU.mult)
                    return e.tensor_tensor(out=yb[0:64, :], in0=yf[0:64, :], in1=agt[0:64, :], op=ALU.mult)
                sc.add("dve", epi, reads=["acc", agr], writes=[rzr, yfr, ybr])
                sc.add("sp", lambda e, yb=yb, tsl=tsl, s_=s_: e.dma_start(out=Y[s_ * 64:(s_ + 1) * 64, tsl], in_=yb[0:64, :]), reads=[ybr], dma=True)
        sc.barrier()

        ar.reset()
        lw = ar.alloc([12, 128], BF16)
        load_cast(lruw_d[l], 1536, lw.rearrange("p a b -> p (a b)"), "lw")
        xp = ar.alloc([S + 4], F32)
        xc = ar.alloc([S], F32)
        Rb = ar.alloc([S], F32)
        Ib = ar.alloc([S], F32)
        Ab_ = ar.alloc([S], F32)
        Hf = ar.alloc([S], F32)
        Hb = ar.alloc([S], F32)
        xcb = ar.alloc([S], BF16)
        bgt = ar.alloc([S], BF16)
        sc.add("dve", lambda e: e.memset(xp[:, 0:1], 0.0), writes=["xp_pad0"])
        sc.add("dve", lambda e: e.memset(xp[:, S + 1:S + 4], 0.0), writes=["xp_pad1"])
        for c in range(3):
            sc.add("sp", lambda e, c=c: e.dma_start(out=xp[:, 1:S + 1], in_=BX[c * 128:(c + 1) * 128, :]), writes=["xp"], dma=True)
            sc.add("sp", lambda e, c=c: e.dma_start(out=bgt, in_=BG[c * 128:(c + 1) * 128, :]), writes=["bgt"], dma=True)

            def conv(e, c=c):
                e.tensor_scalar(out=xc, in0=xp[:, 0:S], scalar1=col(SM_CONVW + 0 * 3 + c), scalar2=col(SM_CONVB + c), op0=ALU.mult, op1=ALU.add)
                for j in range(1, 4):
                    r = e.scalar_tensor_tensor(out=xc, in0=xp[:, j:j + S], scalar=col(SM_CONVW + j * 3 + c), in1=xc, op0=ALU.mult, op1=ALU.add)
                return r
            sc.add("dve", conv, reads=["xp", "xp_pad0", "xp_pad1", "small"], writes=["xc"])
            sc.add("pool", lambda e: e.tensor_copy(out=xcb, in_=xc), reads=["xc"], writes=["xcb"])
            for dr in range(2):
                for t in range(NT):
                    tsl = slice(t * TC, (t + 1) * TC)
                    bR, bRr = bank((2 * t) % 8)
                    bI, bIr = bank((2 * t + 1) % 8)

                    def gmm(e, bR=bR, bI=bI, tsl=tsl, c=c, dr=dr):
                        e.matmul(bR, lhsT=lw[:, (0 * 2 + dr) * 3 + c, :], rhs=xcb[:, tsl], start=True, stop=True)
                        return e.matmul(bI, lhsT=lw[:, (1 * 2 + dr) * 3 + c, :], rhs=xcb[:, tsl], start=True, stop=True)
                    sc.add("pe", gmm, reads=["xcb", "lw"], writes=[bRr, bIr])
                    sc.add("dve", lambda e, bR=bR, tsl=tsl, c=c, dr=dr: e.tensor_scalar(out=Rb[:, tsl], in0=bR, scalar1=col(SM_BR + dr * 3 + c), scalar2=None, op0=ALU.add),
                           reads=[bRr, "small"], writes=[("Rb", t)])
                    sc.add("dve", lambda e, bI=bI, tsl=tsl, c=c, dr=dr: e.tensor_scalar(out=Ib[:, tsl], in0=bI, scalar1=col(SM_BI + dr * 3 + c), scalar2=None, op0=ALU.add),
                           reads=[bIr, "small"], writes=[("Ib", t)])
                Rres = [("Rb", t) for t in range(NT)]
                Ires = [("Ib", t) for t in range(NT)]

                def gates(e, c=c, dr=dr):
                    e.activation(out=Rb, in_=Rb, func=AF.Sigmoid)
                    e.activation(out=Ib, in_=Ib, func=AF.Sigmoid)
                    e.activation(out=Ab_, in_=Rb, func=AF.Exp, scale=spc[:, dr * 3 + c:dr * 3 + c + 1])
                    e.activation(out=Rb, in_=Ab_, func=AF.Square)
                    return e.activation(out=Rb, in_=Rb, func=AF.Sqrt, scale=-1.0, bias=1.0)
                sc.add("act", gates, reads=Rres + Ires + ["spc", "Hscan%d" % dr], writes=["Rw", "Ig", "Ab"])
                Hd = Hf if dr == 0 else Hb

                def premul(e):
                    e.tensor_tensor(out=Ib, in0=Ib, in1=xc, op=ALU.mult)
                    return e.tensor_tensor(out=Ib, in0=Ib, in1=Rb, op=ALU.mult)
                sc.add("dve", premul, reads=["Rw", "Ig", "Ab", "xc"], writes=["U", "Hscan%d" % (1 - dr)] + Rres + Ires)
                order = list(range(NT)) if dr == 0 else list(range(NT - 1, -1, -1))
                for oi, t in enumerate(order):
                    tsl = slice(t * TC, (t + 1) * TC)
                    if oi == 0:
                        init = 0.0
                    elif dr == 0:
                        init = Hd[:, t * TC - 1:t * TC]
                    else:
                        init = Hd[:, (t + 1) * TC:(t + 1) * TC + 1]
                    if dr == 0:
                        sc.add("dve", lambda e, Hd=Hd, tsl=tsl, init=init: e.tensor_tensor_scan(out=Hd[:, tsl], data0=Ab_[:, tsl], data1=Ib[:, tsl], initial=init, op0=ALU.mult, op1=ALU.add),
                               reads=["U", "Ab"] + ([("Hc", dr, order[oi - 1])] if oi else []), writes=[("Hc", dr, t)])
                    else:
                        sc.add("dve", lambda e, Hd=Hd, tsl=tsl, init=init: e.tensor_tensor_scan(out=Hd[:, tsl][:, ::-1], data0=Ab_[:, tsl][:, ::-1], data1=Ib[:, tsl][:, ::-1], initial=init, op0=ALU.mult, op1=ALU.add),
                               reads=["U", "Ab"] + ([("Hc", dr, order[oi - 1])] if oi else []), writes=[("Hc", dr, t)])

            def fin(e):
                e.tensor_tensor(out=Hf, in0=Hf, in1=Hb, op=ALU.add)
                return e.tensor_tensor(out=bgt, in0=Hf, in1=bgt, op=ALU.mult)
            sc.add("dve", fin, reads=[("Hc", 0, NT - 1), ("Hc", 1, 0), "bgt"], writes=["bgt"])
            sc.add("sp", lambda e, c=c: e.dma_start(out=Y[384 + c * 128:384 + (c + 1) * 128, :], in_=bgt), reads=["bgt"], dma=True)
        sc.barrier()

        ar.reset()
        cs = c_slopes()
        ka_rot = Rot(ar, 4, [S], BF16, "ka")
        vc_rot = Rot(ar, 2, [32 * 128], BF16, "vc")
        qb_rot = Rot(ar, 4, [TC], BF16, "qbf")
        qa_rot = Rot(ar, 4, [TC], BF16, "qaf")
        cg_rot = Rot(ar, 2, [TC], BF16, "cg")
        pbc_rot = Rot(ar, 3, [TC], BF16, "pbc")
        pfc_rot = Rot(ar, 2, [128], F32, "pfc")
        e_rz = ar.alloc([TC], F32)
        e_o0 = ar.alloc([TC], F32)
        e_o1 = ar.alloc([TC], F32)
        e_sq = ar.alloc([TC], F32)
        e_sd = ar.alloc([TC], F32)
        ybc_rot = Rot(ar, 2, [TC], BF16, "ybc")
        def c_head(h):
            m = cs[h]
            kas = []
            for c in range(2):
                ka, kar = ka_rot.next()
                kas.append((ka, kar))
                sc.add("sp", lambda e, ka=ka, h=h, c=c: e.dma_start(out=ka[0:64, :], in_=CK[(h * 2 + c) * 64:(h * 2 + c + 1) * 64, :]), writes=[kar], dma=True)
                sc.add("sp", lambda e, ka=ka, h=h: e.dma_start(out=ka[64:68, :], in_=caugk_d[h]), writes=[(kar, "aug")], dma=True)
            vc, vcr = vc_rot.next()
            vcv = vc.rearrange("p (b d) -> p b d", b=32)
            sc.add("sp", lambda e, vc=vc, h=h: e.dma_start(out=vc, in_=CV[:, h * 4096:(h + 1) * 4096]), writes=[vcr], dma=True)
            def c_chunk(qc):
                tsl = slice(qc * TC, (qc + 1) * TC)
                i0 = qc * TC
                qs = []
                for c in range(2):
                    qbf, qbr = qb_rot.next()
                    qaf, qar = qa_rot.next()
                    src = CQ[(h * 2 + c) * 64:(h * 2 + c + 1) * 64, tsl]
                    sc.add("sp", lambda e, qbf=qbf, src=src: e.dma_start(out=qbf[0:64, :], in_=src), writes=[qbr], dma=True)
                    sc.add("sp", lambda e, qbf=qbf, h=h: e.dma_start(out=qbf[64:68, :], in_=caugq_d[h, 0]), writes=[(qbr, "aug")], dma=True)
                    sc.add("sp", lambda e, qaf=qaf, src=src: e.dma_start(out=qaf[0:64, :], in_=src), writes=[qar], dma=True)
                    sc.add("sp", lambda e, qaf=qaf, h=h: e.dma_start(out=qaf[64:68, :], in_=caugq_d[h, 1]), writes=[(qar, "aug")], dma=True)
                    qs.append((qbf, qbr, qaf, qar))
                cgt, cgr = cg_rot.next()
                sc.add("sp", lambda e, cgt=cgt, h=h, tsl=tsl: e.dma_start(out=cgt, in_=CG[h * 128:(h + 1) * 128, tsl]), writes=[cgr], dma=True)
                units = []
                for c in range(2):
                    kbs = []
                    for kb in range(32):
                        j0 = kb * 128
                        if j0 + 128 <= i0:
                            if m * (i0 - (j0 + 127)) > SKIP_T:
                                continue
                        elif j0 >= i0 + TC:
                            if m * (j0 - (i0 + TC - 1)) > SKIP_T:
                                continue
                        kbs.append(kb)
                    for ii, kb in enumerate(kbs):
                        units.append((c, kb, ii == 0, ii == len(kbs) - 1))
                psO = [bank(2), bank(4)]
                psZ = [bank(3), bank(5)]

                def emit_s(u, ui):
                    c, kb, first, last = u
                    ka, kar = kas[c]
                    qbf, qbr, qaf, qar = qs[c]
                    psS, psSr = bank(ui % 2)
                    j0 = kb * 128
                    rds = [kar, (kar, "aug"), qbr, (qbr, "aug"), qar, (qar, "aug")]
                    pbt, pbr = pbc_rot.next()
                    if j0 + 128 <= i0 or j0 >= i0 + TC:
                        before = j0 + 128 <= i0
                        qq = qbf if before else qaf
                        bias = (-m * (i0 - j0)) if before else (m * (i0 - j0))
                        sc.add("pe", lambda e: e.matmul(psS, lhsT=ka[0:68, j0:j0 + 128], rhs=qq[0:68, :], start=True, stop=True), reads=rds, writes=[psSr])
                        sc.add("act", lambda e: e.activation(out=pbt, in_=psS, func=AF.Exp, bias=float(bias)), reads=[psSr], writes=[pbr])
                    else:
                        sb = (j0 - i0) // 128
                        ca, cb_, cc_ = sb * 128, (sb + 1) * 128, TC

                        def mmd(e):
                            r = None
                            if sb > 0:
                                r = e.matmul(psS[:, 0:ca], lhsT=ka[0:68, j0:j0 + 128], rhs=qaf[0:68, 0:ca], start=True, stop=True)
                            r = e.matmul(psS[:, ca:cb_], lhsT=ka[0:64, j0:j0 + 128], rhs=qbf[0:64, ca:cb_], start=True, stop=True)
                            if sb < 3:
                                r = e.matmul(psS[:, cb_:cc_], lhsT=ka[0:68, j0:j0 + 128], rhs=qbf[0:68, cb_:cc_], start=True, stop=True)
                            return r
                        sc.add("pe", mmd, reads=rds, writes=[psSr])
                        pfc, pfr = pfc_rot.next()

                        def actd(e):
                            if sb > 0:
                                e.activation(out=pbt[:, 0:ca], in_=psS[:, 0:ca], func=AF.Exp, bias=float(m * (i0 - j0)))
                            if sb < 3:
                                e.activation(out=pbt[:, cb_:cc_], in_=psS[:, cb_:cc_], func=AF.Exp, bias=float(-m * (i0 - j0)))
                            return e.activation(out=pfc, in_=psS[:, ca:cb_], func=AF.Exp)
                        sc.add("act", actd, reads=[psSr], writes=[pbr, (pbr, "a"), pfr])
                        sc.add("dve", lambda e: e.tensor_tensor(out=pbt[:, ca:cb_], in0=pfc, in1=mdiag[:, h, :], op=ALU.mult), reads=[pfr, "mdiag", (pbr, "a")], writes=[pbr])
                    return pbt, pbr

                def emit_pv(u, pbt, pbr):
                    c, kb, first, last = u
                    o_, or_ = psO[c]
                    z_, zr_ = psZ[c]

                    def f(e):
                        e.matmul(o_, lhsT=vcv[:, kb, :], rhs=pbt, start=first, stop=last)
                        return e.matmul(z_, lhsT=ones_bf, rhs=pbt, start=first, stop=last)
                    sc.add("pe", f, reads=[pbr, vcr, "ones_bf"], writes=[or_, zr_])

                pend = None
                for ui in range(len(units) + 1):
                    cur = None
                    if ui < len(units):
                        pbt, pbr = emit_s(units[ui], ui)
                        cur = (units[ui], pbt, pbr)
                    if pend is not None:
                        emit_pv(*pend)
                    pend = cur
                (o0, o0r), (o1, o1r) = psO
                (z0, z0r), (z1, z1r) = psZ

                def ep1(e):
                    e.reciprocal(out=e_rz, in_=z0)
                    e.tensor_tensor(out=e_o0, in0=o0, in1=e_rz, op=ALU.mult)
                    e.reciprocal(out=e_rz, in_=z1)
                    e.tensor_tensor(out=e_o1, in0=o1, in1=e_rz, op=ALU.mult)
                    return e.scalar_tensor_tensor(out=e_o0, in0=e_o1, scalar=lamt[:, 4:5], in1=e_o0, op0=ALU.mult, op1=ALU.add)
                sc.add("dve", ep1, reads=[o0r, o1r, z0r, z1r, "lamt45", "e_y"], writes=["e_o"])
                sc.add("act", lambda e: e.activation(out=e_sq, in_=e_o0, func=AF.Square), reads=["e_o"], writes=["e_sq"])
                bn, bnr = bank(6)
                sc.add("pe", lambda e: e.matmul(bn, lhsT=ones_f, rhs=e_sq, start=True, stop=True), reads=["e_sq", "ones_f"], writes=[bnr])
                sc.add("act", lambda e: e.activation(out=e_sd, in_=bn, func=AF.Sqrt, scale=1.0 / 128, bias=EPS), reads=[bnr], writes=["e_sd"])
                ybc, ybcr = ybc_rot.next()

                def ep2(e, ybc=ybc, cgt=cgt):
                    e.reciprocal(out=e_sd, in_=e_sd)
                    e.scalar_tensor_tensor(out=e_o1, in0=e_o0, scalar=lamt[:, 5:6], in1=e_sd, op0=ALU.mult, op1=ALU.mult)
                    return e.tensor_tensor(out=ybc, in0=e_o1, in1=cgt, op=ALU.mult)
                sc.add("dve", ep2, reads=["e_sd", "e_o", cgr, "lamt45"], writes=[ybcr, "e_y"])
                sc.add("sp", lambda e, ybc=ybc, h=h, tsl=tsl: e.dma_start(out=Y[768 + h * 128:768 + (h + 1) * 128, tsl], in_=ybc), reads=[ybcr], dma=True)
            for qc in range(NT):
                c_chunk(qc)
        for h in range(C_HEADS):
            c_head(h)
        sc.barrier()

        ar.reset()
        scale_d = 96.0 ** -0.5
        kd2_rot = Rot(ar, 2, [S], BF16, "kd2")
        vraw2_rot = Rot(ar, 2, [32 * 64], BF16, "vraw2")
        vd_tiles = [ar.alloc([32, 128], BF16) for _ in range(2)]
        for k in range(2):
            sc.add("pool", lambda e, k=k: e.memset(vd_tiles[k][:, :, 64:128], 1.0), writes=[("vd1", k)])
        qd2_rot = Rot(ar, 2, [TC], BF16, "qd2")
        dg_rot = Rot(ar, 2, [TC], BF16, "dg")
        pbd_rot = Rot(ar, 3, [TC], BF16, "pbd")
        rzd_rot = Rot(ar, 2, [TC], F32, "rzd")
        yfd_rot = Rot(ar, 2, [TC], F32, "yfd")
        ybd_rot = Rot(ar, 2, [TC], BF16, "ybd")
        for h in range(D_HEADS):
            kd, kdr = kd2_rot.next()
            sc.add("sp", lambda e, kd=kd, h=h: e.dma_start(out=kd[0:64, :], in_=DK[h * 64:(h + 1) * 64, :]), writes=[kdr], dma=True)
            sc.add("sp", lambda e, kd=kd: e.dma_start(out=kd[64:96, :], in_=DKR), writes=[(kdr, "r")], dma=True)
            vraw, vrr = vraw2_rot.next()
            vk = h % 2
            vt, vtr = vd_tiles[vk], ("vd", vk)
            sc.add("sp", lambda e, vraw=vraw, h=h: e.dma_start(out=vraw, in_=DV[:, h * 2048:(h + 1) * 2048]), writes=[vrr], dma=True)
            sc.add("pool", lambda e, vt=vt, vraw=vraw: e.tensor_copy(out=vt[:, :, 0:64], in_=vraw.rearrange("p (b d) -> p b d", b=32)),
                   reads=[vrr, ("vd1", vk)], writes=[vtr])
            for qc in range(NT):
                tsl = slice(qc * TC, (qc + 1) * TC)
                qd, qdr = qd2_rot.next()
                sc.add("sp", lambda e, qd=qd, h=h, tsl=tsl: e.dma_start(out=qd[0:96, :], in_=DQ[h * 96:(h + 1) * 96, tsl]), writes=[qdr], dma=True)
                dgt, dgr = dg_rot.next()
                sc.add("sp", lambda e, dgt=dgt, h=h, tsl=tsl: e.dma_start(out=dgt[0:64, :], in_=DG[h * 64:(h + 1) * 64, tsl]), writes=[dgr], dma=True)
                psO, psOr = bank(2 + qc % 2)
                pend = None
                for kb in range(33):
                    cur = None
                    if kb < 32:
                        psS, psSr = bank(kb % 2)
                        pbt, pbr = pbd_rot.next()
                        sc.add("pe", lambda e, psS=psS, kb=kb, kd=kd, qd=qd: e.matmul(psS, lhsT=kd[0:96, kb * 128:(kb + 1) * 128], rhs=qd[0:96, :], start=True, stop=True),
                               reads=[kdr, (kdr, "r"), qdr], writes=[psSr])
                        sc.add("act", lambda e, psS=psS, pbt=pbt: e.activation(out=pbt, in_=psS, func=AF.Exp, scale=scale_d), reads=[psSr], writes=[pbr])
                        cur = (kb, pbt, pbr)
                    if pend is not None:
                        pk, ppb, ppbr = pend
                        sc.add("pe", lambda e, pk=pk, ppb=ppb, psO=psO, vt=vt: e.matmul(psO, lhsT=vt[:, pk, :], rhs=ppb, start=(pk == 0), stop=(pk == 31)),
                               reads=[ppbr, vtr], writes=[psOr])
                    pend = cur
                rz, rzr = rzd_rot.next()
                yf, yfr = yfd_rot.next()
                yb, ybr = ybd_rot.next()

                def epd(e, rz=rz, yf=yf, yb=yb, psO=psO, dgt=dgt):
                    e.reciprocal(out=rz[0:64, :], in_=psO[64:128, :])
                    e.tensor_tensor(out=yf[0:64, :], in0=psO[0:64, :], in1=rz[0:64, :], op=ALU.mult)
                    return e.tensor_tensor(out=yb[0:64, :], in0=yf[0:64, :], in1=dgt[0:64, :], op=ALU.mult)
                sc.add("dve", epd, reads=[psOr, dgr], writes=[rzr, yfr, ybr])
                sc.add("sp", lambda e, yb=yb, h=h, tsl=tsl: e.dma_start(out=Y[1280 + h * 64:1280 + (h + 1) * 64, tsl], in_=yb[0:64, :]), reads=[ybr], dma=True)
        sc.barrier()

        ar.reset()
        wbr = ar.alloc([13, 1024], BF16)
        wout = ar.alloc([8, 1024], BF16)
        wbr_f = wbr.rearrange("p a b -> p (a b)")
        wout_f = wout.rearrange("p a b -> p (a b)")
        for i in range(4):
            n = 4096 if i < 3 else 1024
            load_cast(wbr_d[l][:, i * 4096:i * 4096 + n], n, wbr_f[:, i * 4096:i * 4096 + n], ("wbr", i))
        for i in range(2):
            load_cast(wout_d[l][:, i * 4096:(i + 1) * 4096], 4096, wout_f[:, i * 4096:(i + 1) * 4096], ("wout", i))
        wbr_res = [("wbr", i) for i in range(4)]
        wout_res = [("wout", i) for i in range(2)]
        yt_rot = Rot(ar, 2, [13, TC], BF16, "yt")
        xt2 = ar.alloc([KC, TC], F32)
        out2 = ar.alloc([KC, TC], F32)
        merged = ar.alloc([KC, TC], BF16)
        g_rot = Rot(ar, 4, [TC], BF16, "g")
        tm_rot = Rot(ar, 2, [TC], F32, "tm")
        macc = ar.alloc([TC], F32)
        sqf_rot = Rot(ar, 2, [TC], F32, "sqf")
        rr2 = ar.alloc([TC], F32)
        br_chunks = [(0, 3), (3, 3), (6, 4), (10, 3)]
        Yv = Y.rearrange("(c p) s -> p c s", p=128)
        x_src_v2 = x_src.rearrange("(kc p) s -> p kc s", p=128)
        x_dst_v = x_dst.rearrange("(kc p) s -> p kc s", p=128)
        pbk = [0]
        for t in range(NT):
            tsl = slice(t * TC, (t + 1) * TC)
            yt, ytr = yt_rot.next()
            sc.add("sp", lambda e, yt=yt, tsl=tsl: e.dma_start(out=yt, in_=Yv[:, :, tsl]), writes=[ytr], dma=True)
            sc.add("sp", lambda e, tsl=tsl: e.dma_start(out=xt2, in_=x_src_v2[:, :, tsl]), writes=["xt2"], dma=True)
            for oc in range(8):
                for br in range(4):
                    gt, gr = g_rot.next()
                    sc.add("sp", lambda e, gt=gt, br=br, oc=oc, tsl=tsl: e.dma_start(out=gt, in_=GT[br * 1024 + oc * 128:br * 1024 + (oc + 1) * 128, tsl]), writes=[gr], dma=True)
                    bk, bkr = bank(pbk[0] % 4)
                    pbk[0] += 1
                    c0, ncn = br_chunks[br]

                    def bmm(e, bk=bk, c0=c0, ncn=ncn, oc=oc, yt=yt):
                        r = None
                        for ci in range(ncn):
                            r = e.matmul(bk, lhsT=wbr[:, c0 + ci, oc * 128:(oc + 1) * 128], rhs=yt[:, c0 + ci, :], start=(ci == 0), stop=(ci == ncn - 1))
                        return r
                    sc.add("pe", bmm, reads=[ytr] + wbr_res, writes=[bkr])
                    if br == 0:
                        sc.add("dve", lambda e, bk=bk, gt=gt: e.tensor_tensor(out=macc, in0=bk, in1=gt, op=ALU.mult), reads=[bkr, gr], writes=["macc"])
                    else:
                        tm, tmr = tm_rot.next()
                        sc.add("dve", lambda e, bk=bk, gt=gt, tm=tm: e.tensor_tensor(out=tm, in0=bk, in1=gt, op=ALU.mult), reads=[bkr, gr], writes=[tmr])
                        if br < 3:
                            sc.add("pool", lambda e, tm=tm: e.tensor_tensor(out=macc, in0=macc, in1=tm, op=ALU.add), reads=[tmr, "macc"], writes=["macc"])
                        else:
                            sc.add("pool", lambda e, tm=tm, oc=oc: e.tensor_tensor(out=merged[:, oc, :], in0=macc, in1=tm, op=ALU.add), reads=[tmr, "macc"], writes=[("merged", oc)])
            bn, bnr = bank(6)
            for oc2 in range(8):
                bo, bor = bank(4 + oc2 % 2)

                def omm(e, bo=bo, oc2=oc2):
                    r = None
                    for oc in range(8):
                        r = e.matmul(bo, lhsT=wout[:, oc, oc2 * 128:(oc2 + 1) * 128], rhs=merged[:, oc, :], start=(oc == 0), stop=(oc == 7))
                    return r
                sc.add("pe", omm, reads=[("merged", oc) for oc in range(8)] + wout_res, writes=[bor])
                sqf, sqfr = sqf_rot.next()

                def oev(e, bo=bo, oc2=oc2, sqf=sqf):
                    e.activation(out=out2[:, oc2, :], in_=bo, func=AF.Copy)
                    return e.activation(out=sqf, in_=bo, func=AF.Square)
                sc.add("act", oev, reads=[bor], writes=[("out2", oc2), sqfr])
                sc.add("pe", lambda e, sqf=sqf, oc2=oc2, bn=bn: e.matmul(bn, lhsT=ones_f, rhs=sqf, start=(oc2 == 0), stop=(oc2 == 7)), reads=[sqfr, "ones_f"], writes=[bnr])
            sc.add("act", lambda e, bn=bn: e.activation(out=rr2, in_=bn, func=AF.Sqrt, scale=1.0 / DM, bias=EPS), reads=[bnr], writes=["rr2"])

            def resid(e):
                e.reciprocal(out=rr2, in_=rr2)
                r = None
                for oc2 in range(8):
                    e.scalar_tensor_tensor(out=out2[:, oc2, :], in0=out2[:, oc2, :], scalar=col(SM_GPOST + oc2), in1=rr2, op0=ALU.mult, op1=ALU.mult)
                    r = e.tensor_tensor(out=xt2[:, oc2, :], in0=xt2[:, oc2, :], in1=out2[:, oc2, :], op=ALU.add)
                return r
            sc.add("dve", resid, reads=["rr2", "xt2", "small"] + [("out2", i) for i in range(8)], writes=["xt2", "rr2"] + [("out2", i) for i in range(8)])
            sc.add("sp", lambda e, tsl=tsl: e.dma_start(out=x_dst_v[:, :, tsl], in_=xt2), reads=["xt2"], dma=True)
        sc.barrier()

    for l in range(L):
        layer(l)
    sc.analyze()
    sc.emit(nc, es)
    es.close()
    return nc


def _bf(x):
    return np.asarray(x, dtype=np.float32).astype(ml_dtypes.bfloat16)


def build_consts():
    c = {}
    cs = c_slopes()
    caugq = np.zeros((C_HEADS, 2, 4, TC), np.float32)
    caugk = np.zeros((C_HEADS, 4, S), np.float32)
    ii = np.arange(TC, dtype=np.float64)
    jj = (np.arange(S) % 128).astype(np.float64)
    for h, m in enumerate(cs):
        qb = (-m * ii).astype(np.float32)
        qb_hi = _bf(qb).astype(np.float32)
        qb_lo = _bf(qb - qb_hi).astype(np.float32)
        caugq[h, 0] = np.stack([qb_hi, qb_lo, np.ones(TC), np.ones(TC)])
        caugq[h, 1] = -caugq[h, 0]
        kb = (m * jj).astype(np.float32)
        kb_hi = _bf(kb).astype(np.float32)
        kb_lo = _bf(kb - kb_hi).astype(np.float32)
        caugk[h] = np.stack([np.ones(S), np.ones(S), kb_hi, kb_lo])
    c["caugq"] = _bf(caugq)
    c["caugk"] = _bf(caugk)
    p = np.arange(128)[:, None].astype(np.float64)
    f = np.arange(128)[None, :].astype(np.float64)
    md = np.stack([np.exp(-m * np.abs(p - f)) for m in cs], axis=1)
    c["mdiag"] = md.reshape(128, C_HEADS * 128).astype(np.float32)
    sl = a_slopes()
    ma = np.zeros((128, 18, 384), np.float64)
    k = np.arange(128)[:, None]
    for g, d in enumerate(A_DIL):
        for s_ in range(A_SLOTS):
            for j in range(3):
                q = np.arange(128)[None, :]
                rel = np.abs((j - 1) * 128 + k - q)
                ma[:, g * 6 + s_, j * 128:(j + 1) * 128] = np.where(rel <= 64, np.exp(-sl[s_] * d * rel), 0.0)
    c["ma"] = ma.reshape(128, 18 * 384).astype(np.float32)
    inv = (10000.0 ** (-np.arange(0, 32, 2, dtype=np.float32) / 32)).astype(np.float32)
    ang = np.arange(S, dtype=np.float32)[:, None] * inv[None, :]
    cos, sin = np.cos(ang).astype(np.float32).T, np.sin(ang).astype(np.float32).T
    c["ropec"] = np.ascontiguousarray(np.concatenate([cos, cos], 0))
    c["ropes"] = np.ascontiguousarray(np.concatenate([-sin, sin], 0))
    return c


def pack_layers(inp, layers):
    f32 = np.float32
    L = len(layers)
    small = np.zeros((L, 128, NSMALL), f32)
    lamv = np.zeros((L, 1, 256), f32)
    lruw = np.zeros((L, 128, 12, 128), f32)
    wuqa = np.zeros((L, 128, 2, 576), f32)
    wuqb = np.zeros((L, 128, 2, 576), f32)
    wukvk = np.zeros((L, 128, 384), f32)
    wukvv = np.zeros((L, 128, 384), f32)
    wbr = np.zeros((L, 128, 13, 1024), f32)
    wout = np.zeros((L, 128, 8, 1024), f32)
    for li, l in enumerate(layers):
        sm = small[li]
        sm[:, SM_GPRE:SM_GPRE + 8] = inp["norm_pre"][l].reshape(8, 128).T
        sm[:, SM_GPOST:SM_GPOST + 8] = inp["norm_post"][l].reshape(8, 128).T
        sm[:, SM_BGATE:SM_BGATE + 32] = inp["b_gate"][l].reshape(32, 128).T
        cw = inp["conv_w"][l].reshape(4, 3, 128)
        for j in range(4):
            for c in range(3):
                sm[:, SM_CONVW + j * 3 + c] = cw[j, c]
        sm[:, SM_CONVB:SM_CONVB + 3] = inp["conv_b"][l].reshape(3, 128).T
        sm[:, SM_BR:SM_BR + 6] = inp["lru_br"][l].reshape(6, 128).T
        sm[:, SM_BI:SM_BI + 6] = inp["lru_bi"][l].reshape(6, 128).T
        sm[:, SM_LAM:SM_LAM + 6] = inp["lru_lambda"][l].reshape(6, 128).T
        sm[:, SM_SUBLN] = inp["diff_subln"][l]
        sm[:, SM_QN:SM_QN + 2] = inp["mla_q_norm"][l].reshape(2, 128).T
        sm[:, SM_KVN] = inp["mla_kv_norm"][l]
        lam_init = 0.8 - 0.6 * math.exp(-0.3 * l)
        sm[:, SM_LAMINIT] = lam_init
        sm[:, SM_OML] = (1.0 - lam_init)
        lamv[li, 0, 0:64] = inp["diff_lam_q1"][l]
        lamv[li, 0, 64:128] = inp["diff_lam_k1"][l]
        lamv[li, 0, 128:192] = inp["diff_lam_q2"][l]
        lamv[li, 0, 192:256] = inp["diff_lam_k2"][l]
        for gi, w in enumerate((inp["lru_wr"][l], inp["lru_wi"][l])):
            for dr in range(2):
                for c in range(3):
                    for b in range(2):
                        lruw[li, b * 64:(b + 1) * 64, (gi * 2 + dr) * 3 + c, b * 64:(b + 1) * 64] = w[dr, 2 * c + b]
        uq = inp["mla_w_uq"][l].reshape(2, 128, 6, 96)
        wuqa[li] = uq.transpose(1, 0, 2, 3).reshape(128, 2, 576)
        uqb = uq.copy()
        uqb[..., 64:80] = uq[..., 80:96]
        uqb[..., 80:96] = uq[..., 64:80]
        wuqb[li] = uqb.transpose(1, 0, 2, 3).reshape(128, 2, 576)
        ukv = inp["mla_w_ukv"][l].reshape(128, 6, 128)
        wukvk[li] = ukv[:, :, 0:64].reshape(128, 384)
        wukvv[li] = ukv[:, :, 64:128].reshape(128, 384)
        wcat = np.concatenate([inp["w_br_a"][l], inp["w_br_b"][l], inp["w_br_c"][l], inp["w_br_d"][l]], 0)
        wbr[li] = wcat.reshape(13, 128, 1024).transpose(1, 0, 2)
        wout[li] = inp["w_out"][l].reshape(8, 128, 1024).transpose(1, 0, 2)
    return dict(small=small, lamv=lamv, lruw=lruw.reshape(L, 128, 1536), wuqa=wuqa.reshape(L, 128, 1152),
                wuqb=wuqb.reshape(L, 128, 1152), wukvk=wukvk, wukvv=wukvv, wbr=wbr.reshape(L, 128, 13 * 1024),
                wout=wout.reshape(L, 128, 8 * 1024))


_PROG = {}


def get_prog(nl, debug=False):
    key = (nl, tuple(debug) if debug else None)
    if key not in _PROG:
        _PROG[key] = build_program(nl, debug)
    return _PROG[key]


FUSED = False


def kernel(**inputs):
    inp = {k: np.asarray(v) for k, v in inputs.items()}
    x = inp["x"].astype(np.float32)
    consts = build_consts()
    ncores = 8
    xT = [np.ascontiguousarray(x[c % 4].T) for c in range(ncores)]
    if FUSED:
        groups = [list(range(DEPTH))]
    else:
        groups = [[l] for l in range(DEPTH)]
    for layers in groups:
        nc = get_prog(len(layers))
        pk = pack_layers(inp, layers)
        w_in = np.ascontiguousarray(inp["w_in"][layers[0]:layers[-1] + 1]).astype(np.float32)
        in_maps = []
        for c in range(ncores):
            m = dict(xT=xT[c], w_in=w_in)
            m.update(pk)
            m.update(consts)
            in_maps.append(m)
        res = run_bass_kernel_spmd(nc, in_maps, core_ids=list(range(ncores)))
        xT = [np.asarray(res.results[c]["outT"]) for c in range(ncores)]
    out = np.stack([xT[b].T for b in range(4)], 0).astype(np.float32)
    return np.ascontiguousarray(out)
```

```python
import math
from contextlib import ExitStack

import numpy as np
import ml_dtypes

import concourse.bass as bass
import concourse.mybir as mybir
from concourse.bass_utils import run_bass_kernel_spmd

F32 = mybir.dt.float32
BF16 = mybir.dt.bfloat16
AF = mybir.ActivationFunctionType
ALU = mybir.AluOpType
AX = mybir.AxisListType

S = 4096
DM = 1024
DEPTH = 4
NT = 8
TC = 512
KC = 8
EPS = 1e-6
A_DIL = (1, 4, 16)
A_SLOTS = 6
C_HEADS = 4
D_HEADS = 6
IN_W = 11552
OFF = dict(a_q=0, a_k=1152, a_v=2304, a_g=3456, b_x=3840, b_g=4224, c_q=4608, c_k=5120,
           c_v=5632, c_g=6144, d_cq=6656, d_ckv=6912, d_kr=7040, d_g=7072, gate=7456)
SKIP_T = 100.0

SM_GPRE, SM_GPOST, SM_BGATE, SM_CONVW, SM_CONVB = 0, 8, 16, 48, 60
SM_BR, SM_BI, SM_LAM, SM_SUBLN, SM_QN, SM_KVN, SM_LAMINIT, SM_OML = 63, 69, 75, 81, 82, 84, 85, 86
NSMALL = 88


def a_slopes():
    return [2.0 ** (-8.0 * (i + 1) / A_SLOTS) for i in range(A_SLOTS)]


def c_slopes():
    return [2.0 ** (-8.0 * (i + 1) / C_HEADS) for i in range(C_HEADS)]


class Sched:
    ENGS = ("pe", "act", "dve", "pool", "sp")
    NDMA = {"sp": 24, "act": 12, "pool": 4}

    def __init__(self):
        self.ops = []

    def add(self, eng, fn, reads=(), writes=(), dma=False):
        self.ops.append(dict(eng=eng, fn=fn, reads=tuple(reads), writes=tuple(writes), dma=dma,
                             needs_inc=False, deps=[]))

    def barrier(self):
        self.ops.append(dict(barrier=True))

    def analyze(self):
        ops = self.ops
        last_w, readers = {}, {}
        last_on = {}
        for i, op in enumerate(ops):
            if op.get("barrier"):
                for e, j in last_on.items():
                    ops[j]["needs_inc"] = True
                last_w, readers = {}, {}
                continue
            deps = set()
            for r in op["reads"]:
                if r in last_w:
                    deps.add((last_w[r], "raw"))
            for w in op["writes"]:
                if w in last_w:
                    deps.add((last_w[w], "waw"))
                for rd in readers.get(w, ()):
                    deps.add((rd, "war"))
            keep = set()
            for d, kind in deps:
                if d == i:
                    continue
                p = ops[d]
                if p["dma"]:
                    keep.add(d)
                elif p["eng"] == op["eng"]:
                    if op["dma"]:
                        keep.add(d)
                    elif kind == "raw" and op["eng"] in ("act", "dve", "pool"):
                        keep.add(d)
                else:
                    keep.add(d)
            op["deps"] = sorted(keep)
            for d in keep:
                ops[d]["needs_inc"] = True
            for w in op["writes"]:
                last_w[w] = i
                readers[w] = []
            for r in op["reads"]:
                if r not in op["writes"]:
                    readers.setdefault(r, []).append(i)
            if not op["dma"]:
                last_on[op["eng"]] = i
        tick = {e: 0 for e in self.ENGS}
        dma_cnt = {q: [0] * n for q, n in self.NDMA.items()}
        dma_rr = {q: 0 for q in self.NDMA}
        seen = {e: {} for e in self.ENGS}
        pending = {e: [] for e in self.ENGS}
        for op in ops:
            if op.get("barrier"):
                snap = [(("c", e), tick[e]) for e in self.ENGS if tick[e] > 0]
                for q, cnts in dma_cnt.items():
                    for k, c in enumerate(cnts):
                        if c > 0:
                            snap.append((("d", q, k), 16 * c))
                for e in self.ENGS:
                    pending[e] = [(s_, v) for (s_, v) in snap if s_ != ("c", e)]
                continue
            e = op["eng"]
            waits = list(pending[e])
            pending[e] = []
            for d in op["deps"]:
                p = ops[d]
                waits.append((p["sem"], p["tick"]))
            if op["dma"]:
                k = dma_rr[e]
                dma_rr[e] = (k + 1) % self.NDMA[e]
                if dma_cnt[e][k] > 0:
                    waits.append((("d", e, k), 16 * dma_cnt[e][k]))
                dma_cnt[e][k] += 1
                op["sem"] = ("d", e, k)
                op["tick"] = 16 * dma_cnt[e][k]
            elif op["needs_inc"]:
                tick[e] += 1
                op["sem"] = ("c", e)
                op["tick"] = tick[e]
            fw = []
            for s_, v in waits:
                if seen[e].get(s_, 0) >= v:
                    continue
                seen[e][s_] = v
                fw.append((s_, v))
            mx = {}
            for s_, v in fw:
                mx[s_] = max(mx.get(s_, 0), v)
            op["waits"] = sorted(mx.items(), key=lambda kv: str(kv[0]))
        self.final_dma = {(q, k): 16 * c for q, cnts in dma_cnt.items() for k, c in enumerate(cnts) if c > 0}

    def emit(self, nc, es):
        sems = {}
        for e in self.ENGS:
            sems[("c", e)] = es.enter_context(nc.semaphore("c_" + e))
        for q, n in self.NDMA.items():
            for k in range(n):
                sems[("d", q, k)] = es.enter_context(nc.semaphore("d_%s_%d" % (q, k)))
        blk = es.enter_context(nc.Block())
        ops = self.ops

        def run(engname):
            def body(e):
                for op in ops:
                    if op.get("barrier") or op["eng"] != engname:
                        continue
                    for s_, v in op["waits"]:
                        e.wait_ge(sems[s_], v)
                    ins = op["fn"](e)
                    if op["dma"]:
                        ins.then_inc(sems[op["sem"]], 16)
                    elif op["needs_inc"]:
                        ins.then_inc(sems[op["sem"]], 1)
                if engname == "sp":
                    for (q, k), v in sorted(self.final_dma.items()):
                        e.wait_ge(sems[("d", q, k)], v)
            return body

        blk.tensor(run("pe"))
        blk.scalar(run("act"))
        blk.vector(run("dve"))
        blk.gpsimd(run("pool"))
        blk.sync(run("sp"))


class Arena:
    def __init__(self, ap, nbytes, name):
        self.ap = ap
        self.nbytes = nbytes
        self.off = 0
        self.name = name
        self.cnt = 0

    def reset(self):
        self.off = 0

    def alloc(self, free_shape, dt, parts=128):
        n = 1
        for d in free_shape:
            n *= d
        nb = n * (4 if dt == F32 else 2)
        nb = (nb + 63) // 64 * 64
        assert self.off + nb <= self.nbytes, "arena %s overflow: %d + %d > %d" % (self.name, self.off, nb, self.nbytes)
        a = self.ap[:, self.off // 4:(self.off + nb) // 4]
        self.off += nb
        if dt == BF16:
            a = a.bitcast(BF16)
        a = a[:, 0:n]
        if len(free_shape) == 2:
            a = a.rearrange("p (a b) -> p a b", a=free_shape[0])
        elif len(free_shape) == 3:
            a = a.rearrange("p (a b c) -> p a b c", a=free_shape[0], b=free_shape[1])
        self.cnt += 1
        return a


class Rot:
    def __init__(self, arena, n, free_shape, dt, name):
        self.tiles = [arena.alloc(free_shape, dt) for _ in range(n)]
        self.name = name
        self.i = 0

    def next(self):
        k = self.i % len(self.tiles)
        self.i += 1
        return self.tiles[k], (self.name, k)


def build_program(nlayers, debug=False):
    nc = bass.Bass("TRN2", target_bir_lowering=False)
    L = nlayers

    def din(name, shape, dt=F32):
        return nc.dram_tensor(name, list(shape), dt, kind="ExternalInput").ap()

    def dscr(name, shape, dt=BF16):
        ext = bool(debug) and name in debug
        return nc.dram_tensor(name, list(shape), dt, kind="ExternalOutput" if ext else "Internal").ap()

    xT_in = din("xT", [DM, S])
    w_in = din("w_in", [L, DM, IN_W])
    small_d = din("small", [L, 128, NSMALL])
    lamv_d = din("lamv", [L, 1, 256])
    lruw_d = din("lruw", [L, 128, 12 * 128])
    wuqa_d = din("wuqa", [L, 128, 2 * 576])
    wuqb_d = din("wuqb", [L, 128, 2 * 576])
    wukvk_d = din("wukvk", [L, 128, 384])
    wukvv_d = din("wukvv", [L, 128, 384])
    wbr_d = din("wbr", [L, 128, 13 * 1024])
    wout_d = din("wout", [L, 128, 8 * 1024])
    caugq_d = din("caugq", [C_HEADS, 2, 4, TC], BF16)
    caugk_d = din("caugk", [C_HEADS, 4, S], BF16)
    mdiag_d = din("mdiag", [128, C_HEADS * 128])
    ma_d = din("ma", [128, 18 * 384])
    ropec_d = din("ropec", [32, S])
    ropes_d = din("ropes", [32, S])
    outT = nc.dram_tensor("outT", [DM, S], F32, kind="ExternalOutput").ap()

    AQ = [dscr("AQ%d" % g, [384, S]) for g in range(3)]
    AK = [dscr("AK%d" % g, [384, S]) for g in range(3)]
    AV = [dscr("AV%d" % g, [128, 6 * 32 * 64]) for g in range(3)]
    AG = dscr("AG", [384, S])
    BX = dscr("BX", [384, S], F32)
    BG = dscr("BG", [384, S])
    CQ = dscr("CQ", [512, S])
    CK = dscr("CK", [512, S])
    CV = dscr("CV", [128, 4 * 32 * 128])
    CG = dscr("CG", [512, S])
    DLAT = dscr("DLAT", [384, S], F32)
    DKRAW = dscr("DKRAW", [64, S], F32)
    DG = dscr("DG", [384, S])
    GT = dscr("GT", [4096, S])
    DQ = dscr("DQ", [D_HEADS * 96, S])
    DK = dscr("DK", [D_HEADS * 64, S])
    DKR = dscr("DKR", [32, S])
    DV = dscr("DV", [128, 6 * 32 * 64])
    Y = dscr("Y", [1664, S])
    XS = [nc.dram_tensor("XS%d" % i, [DM, S], F32, kind="Internal").ap() for i in range(2)]

    sc = Sched()
    es = ExitStack()
    PERS_BYTES = 56 * 1024
    ARENA_BYTES = 136 * 1024
    pers_t = es.enter_context(nc.sbuf_tensor("pers", [128, PERS_BYTES // 4], F32))
    arena_t = es.enter_context(nc.sbuf_tensor("arena", [128, ARENA_BYTES // 4], F32))
    pers = Arena(pers_t[:], PERS_BYTES, "pers")
    ar = Arena(arena_t[:], ARENA_BYTES, "arena")
    banks = [es.enter_context(nc.psum_tensor("bank%d" % i, [128, 512], F32)) for i in range(8)]

    def bank(i):
        return banks[i][:], ("bank", i)

    small = pers.alloc([NSMALL], F32)
    lamv = pers.alloc([256], F32)
    lamt = pers.alloc([16], F32)
    spc = pers.alloc([8], F32)
    ones_bf = pers.alloc([128], BF16)
    ones_f = pers.alloc([128], F32)
    mdiag = pers.alloc([C_HEADS, 128], F32)
    wst = [pers.alloc([4096], F32) for _ in range(2)]
    wbf = [pers.alloc([4096], BF16) for _ in range(2)]
    wst_i = [0]

    def col(c, n=1):
        return small[:, c:c + n]

    sc.add("dve", lambda e: e.memset(ones_bf, 1.0), writes=["ones_bf"])
    sc.add("dve", lambda e: e.memset(ones_f, 1.0), writes=["ones_f"])
    sc.add("sp", lambda e: e.dma_start(out=mdiag.rearrange("p a b -> p (a b)"), in_=mdiag_d), writes=["mdiag"], dma=True)

    def load_cast(src_ap, ncols_f32, dst_bf, dst_res):
        k = wst_i[0] % 2
        wst_i[0] += 1
        st = wst[k]
        sc.add("sp", lambda e: e.dma_start(out=st[:, 0:ncols_f32], in_=src_ap), writes=[("wst", k)], dma=True)
        sc.add("pool", lambda e: e.tensor_copy(out=dst_bf, in_=st[:, 0:ncols_f32]), reads=[("wst", k)], writes=[dst_res])

    def layer(l):
        x_src = xT_in if l == 0 else XS[(l - 1) % 2]
        x_dst = outT if l == L - 1 else XS[l % 2]
        w_l = w_in[l]
        w_l_v = w_l.rearrange("(kc p) n -> p kc n", p=128)

        sc.add("sp", lambda e, l=l: e.dma_start(out=small, in_=small_d[l]), writes=["small"], dma=True)
        sc.add("sp", lambda e, l=l: e.dma_start(out=lamv, in_=lamv_d[l].partition_broadcast(128)), writes=["lamv"], dma=True)
        ar.reset()
        tmpl = ar.alloc([128], F32)

        sc.add("dve", lambda e: e.tensor_tensor(out=tmpl[:, 0:64], in0=lamv[:, 0:64], in1=lamv[:, 64:128], op=ALU.mult), reads=["lamv"], writes=["tmpl0"])
        sc.add("dve", lambda e: e.tensor_tensor(out=tmpl[:, 64:128], in0=lamv[:, 128:192], in1=lamv[:, 192:256], op=ALU.mult), reads=["lamv"], writes=["tmpl1"])
        sc.add("dve", lambda e: e.reduce_sum(out=lamt[:, 0:1], in_=tmpl[:, 0:64], axis=AX.X), reads=["tmpl0"], writes=["lamt0"])
        sc.add("dve", lambda e: e.reduce_sum(out=lamt[:, 1:2], in_=tmpl[:, 64:128], axis=AX.X), reads=["tmpl1"], writes=["lamt1"])
        sc.add("act", lambda e: e.activation(out=lamt[:, 2:4], in_=lamt[:, 0:2], func=AF.Exp), reads=["lamt0", "lamt1"], writes=["lamt23"])
        sc.add("dve", lambda e: e.tensor_tensor(out=lamt[:, 6:7], in0=lamt[:, 3:4], in1=lamt[:, 2:3], op=ALU.subtract), reads=["lamt23"], writes=["lamt6"])
        sc.add("dve", lambda e: e.tensor_tensor(out=lamt[:, 4:5], in0=lamt[:, 6:7], in1=col(SM_LAMINIT), op=ALU.subtract), reads=["lamt6", "small"], writes=["lamt4"])
        sc.add("dve", lambda e: e.tensor_tensor(out=lamt[:, 5:6], in0=col(SM_SUBLN), in1=col(SM_OML), op=ALU.mult), reads=["small"], writes=["lamt5"])
        sc.add("act", lambda e: e.activation(out=spc[:, 0:6], in_=col(SM_LAM, 6), func=AF.Exp, scale=-1.0), reads=["small"], writes=["spc_a"])
        sc.add("act", lambda e: e.activation(out=lamt[:, 8:14], in_=spc[:, 0:6], func=AF.Ln, bias=1.0), reads=["spc_a"], writes=["spc_b"])
        sc.add("dve", lambda e: e.tensor_scalar(out=spc[:, 0:6], in0=lamt[:, 8:14], scalar1=-8.0, scalar2=None, op0=ALU.mult),
               reads=["spc_b"], writes=["spc"])
        sc.barrier()

        ar.reset()
        hT = ar.alloc([KC, S], BF16)
        p0mark = ar.off
        xt_rot = Rot(ar, 2, [KC, TC], F32, "xt")
        sq_t = ar.alloc([KC, TC], F32)
        rstd_rot = Rot(ar, 2, [TC], F32, "rstd")
        x_src_v = x_src.rearrange("(kc p) s -> p kc s", p=128)
        for t in range(NT):
            xt, xr = xt_rot.next()
            sc.add("sp", lambda e, xt=xt, t=t: e.dma_start(out=xt, in_=x_src_v[:, :, t * TC:(t + 1) * TC]), writes=[xr], dma=True)
            sc.add("act", lambda e, xt=xt: e.activation(out=sq_t, in_=xt, func=AF.Square), reads=[xr], writes=["sq_t"])
            bk, br = bank(t % 2)

            def ssmm(e, bk=bk):
                for kc in range(KC):
                    r = e.matmul(bk, lhsT=ones_f, rhs=sq_t[:, kc, :], start=(kc == 0), stop=(kc == KC - 1))
                return r
            sc.add("pe", ssmm, reads=["sq_t", "ones_f"], writes=[br])
            rs, rr = rstd_rot.next()
            sc.add("act", lambda e, bk=bk, rs=rs: e.activation(out=rs, in_=bk, func=AF.Sqrt, scale=1.0 / DM, bias=EPS), reads=[br], writes=[rr])
            sc.add("dve", lambda e, rs=rs: e.reciprocal(out=rs, in_=rs), reads=[rr], writes=[rr])

            def hmk(e, xt=xt, rs=rs, t=t):
                for kc in range(KC):
                    r = e.scalar_tensor_tensor(out=hT[:, kc, t * TC:(t + 1) * TC], in0=xt[:, kc, :], scalar=col(SM_GPRE + kc),
                                               in1=rs, op0=ALU.mult, op1=ALU.mult)
                return r
            sc.add("dve", hmk, reads=[xr, rr, "small"], writes=[("hT", t)])
        sc.barrier()

        ar.off = p0mark
        ob_rot_a = Rot(ar, 3, [TC], BF16, "ob_a")
        ob_rot_d = Rot(ar, 3, [TC], BF16, "ob_d")
        of_rot = Rot(ar, 2, [TC], F32, "of")
        vstage = ar.alloc([4 * 32 * 128], BF16)
        hT_res = [("hT", t) for t in range(NT)]
        bank_i = [0]

        def nbank():
            b = bank_i[0] % 8
            bank_i[0] += 1
            return bank(b)

        fm_jobs = []
        for g in range(3):
            fm_jobs.append((OFF["a_q"] + g * 384, 384, A_DIL[g], "scale", AQ[g], 0.125))
        for g in range(3):
            fm_jobs.append((OFF["a_k"] + g * 384, 384, A_DIL[g], "copy", AK[g], None))
        fm_jobs.append((OFF["a_g"], 384, 1, "silu", AG, None))
        fm_jobs.append((OFF["b_x"], 384, 1, "copyf", BX, None))
        fm_jobs.append((OFF["b_g"], 384, 1, "silu", BG, None))
        fm_jobs.append((OFF["c_q"], 512, 1, "scale", CQ, 0.125))
        fm_jobs.append((OFF["c_k"], 512, 1, "copy", CK, None))
        fm_jobs.append((OFF["c_g"], 512, 1, "silu", CG, None))
        fm_jobs.append((OFF["d_cq"], 384, 1, "copyf", DLAT, None))
        fm_jobs.append((OFF["d_kr"], 64, 1, "copyf_kr", DKRAW, None))
        fm_jobs.append((OFF["d_g"], 384, 1, "silu", DG, None))
        for i in range(8):
            fm_jobs.append((OFF["gate"] + i * 512, 512, 1, "gate", GT[i * 512:(i + 1) * 512, :], i))
        tm_jobs = []
        for g in range(3):
            tm_jobs.append((OFF["a_v"] + g * 384, 384, A_DIL[g], AV[g], 6, 64))
        tm_jobs.append((OFF["c_v"], 512, 1, CV, 4, 128))
        jobs = [("fm", j) for j in fm_jobs] + [("tm", j) for j in tm_jobs]

        def load_job(ji):
            kind, j = jobs[ji]
            c0, n = j[0], j[1]
            k = ji % 2
            st = wst[k].rearrange("p (kc n) -> p kc n", kc=KC)
            wb = wbf[k].rearrange("p (kc n) -> p kc n", kc=KC)
            if kind == "fm" and j[3] == "copyf_kr":
                sc.add("sp", lambda e: e.dma_start(out=st[:, :, 0:32], in_=w_l_v[:, :, c0:c0 + 32]), writes=[("wst", k)], dma=True)
                sc.add("sp", lambda e: e.dma_start(out=st[:, :, 32:48], in_=w_l_v[:, :, c0 + 16:c0 + 32]), writes=[("wst", k, 1)], dma=True)
                sc.add("sp", lambda e: e.dma_start(out=st[:, :, 48:64], in_=w_l_v[:, :, c0:c0 + 16]), writes=[("wst", k, 2)], dma=True)
                rd = [("wst", k), ("wst", k, 1), ("wst", k, 2)]
            else:
                sc.add("sp", lambda e: e.dma_start(out=st[:, :, 0:n], in_=w_l_v[:, :, c0:c0 + n]), writes=[("wst", k)], dma=True)
                rd = [("wst", k)]
            sc.add("pool", lambda e: e.tensor_copy(out=wb[:, :, 0:n], in_=st[:, :, 0:n]), reads=rd, writes=[("wbf", k)])

        def rhs_view(d, kc, t):
            if d == 1:
                return [(hT[:, kc, t * TC:(t + 1) * TC], None)]
            hv = hT[:, kc, :].rearrange("p (l r) -> p r l", r=d)
            if d == 4:
                return [(hv[:, t // 2, (t % 2) * 512:(t % 2) * 512 + 512], None)]
            return [(hv[:, 2 * t:2 * t + 2, :], 2)]

        def lhs_view(d, kc, b):
            if d == 1:
                return hT[:, kc, b * 128:(b + 1) * 128]
            hv = hT[:, kc, :].rearrange("p (l r) -> p r l", r=d)
            nbs = 32 // d
            return hv[:, b // nbs, (b % nbs) * 128:(b % nbs) * 128 + 128]

        def compute_fm(ji):
            _, (c0, n, d, kind, dst, extra) = jobs[ji]
            k = ji % 2
            wb = wbf[k].rearrange("p (kc n) -> p kc n", kc=KC)
            ncb = (n + 127) // 128
            for cb in range(ncb):
                m = min(128, n - cb * 128)
                for t in range(NT):
                    bk, bres = nbank()

                    def mm(e, bk=bk, cb=cb, m=m, t=t):
                        r = None
                        for kc in range(KC):
                            (rv, a), = rhs_view(d, kc, t)
                            o = bk[0:m, :]
                            if a is not None:
                                o = o.rearrange("p (a b) -> p a b", a=a)
                            r = e.matmul(o, lhsT=wb[:, kc, cb * 128:cb * 128 + m], rhs=rv, start=(kc == 0), stop=(kc == KC - 1))
                        return r
                    sc.add("pe", mm, reads=[("wbf", k)] + hT_res, writes=[bres])
                    drows = dst[cb * 128:cb * 128 + m, t * TC:(t + 1) * TC]
                    if kind in ("silu", "gate"):
                        ot, ores = ob_rot_a.next()
                        if kind == "silu":
                            sc.add("act", lambda e, ot=ot, bk=bk, m=m: e.activation(out=ot[0:m, :], in_=bk[0:m, :], func=AF.Silu),
                                   reads=[bres], writes=[ores])
                        else:
                            bcol = SM_BGATE + (extra // 2) * 8 + (extra % 2) * 4 + cb
                            sc.add("act", lambda e, ot=ot, bk=bk, bcol=bcol: e.activation(out=ot, in_=bk, func=AF.Sigmoid, bias=col(bcol)),
                                   reads=[bres, "small"], writes=[ores])
                        sc.add("act", lambda e, ot=ot, drows=drows, m=m: e.dma_start(out=drows, in_=ot[0:m, :]), reads=[ores], dma=True)
                    elif kind in ("copy", "scale"):
                        ot, ores = ob_rot_d.next()
                        if kind == "copy":
                            sc.add("dve", lambda e, ot=ot, bk=bk, m=m: e.tensor_copy(out=ot[0:m, :], in_=bk[0:m, :]), reads=[bres], writes=[ores])
                        else:
                            sc.add("dve", lambda e, ot=ot, bk=bk, m=m: e.tensor_scalar(out=ot[0:m, :], in0=bk[0:m, :], scalar1=extra, scalar2=None, op0=ALU.mult),
                                   reads=[bres], writes=[ores])
                        sc.add("sp", lambda e, ot=ot, drows=drows, m=m: e.dma_start(out=drows, in_=ot[0:m, :]), reads=[ores], dma=True)
                    else:
                        ot, ores = of_rot.next()
                        sc.add("dve", lambda e, ot=ot, bk=bk, m=m: e.tensor_copy(out=ot[0:m, :], in_=bk[0:m, :]), reads=[bres], writes=[ores])
                        sc.add("sp", lambda e, ot=ot, drows=drows, m=m: e.dma_start(out=drows, in_=ot[0:m, :]), reads=[ores], dma=True)

        def compute_tm(ji):
            _, (c0, n, d, dst, nh, hd) = jobs[ji]
            k = ji % 2
            wb = wbf[k].rearrange("p (kc n) -> p kc n", kc=KC)
            vs = vstage[:, 0:nh * 32 * hd].rearrange("p (h b d) -> p h b d", h=nh, b=32)
            for b in range(32):
                bk, bres = nbank()

                def mm(e, bk=bk, b=b):
                    r = None
                    for kc in range(KC):
                        r = e.matmul(bk[:, 0:n], lhsT=lhs_view(d, kc, b), rhs=wb[:, kc, 0:n], start=(kc == 0), stop=(kc == KC - 1))
                    return r
                sc.add("pe", mm, reads=[("wbf", k)] + hT_res, writes=[bres])
                sc.add("dve", lambda e, bk=bk, b=b: e.tensor_copy(out=vs[:, :, b, :], in_=bk[:, 0:n].rearrange("p (h d) -> p h d", h=nh)),
                       reads=[bres], writes=[("vstage", b)])
            sc.add("sp", lambda e: e.dma_start(out=dst, in_=vstage[:, 0:nh * 32 * hd]), reads=[("vstage", b) for b in range(32)], dma=True)

        load_job(0)
        for ji in range(len(jobs)):
            if ji + 1 < len(jobs):
                load_job(ji + 1)
            if jobs[ji][0] == "fm":
                compute_fm(ji)
            else:
                compute_tm(ji)
        sc.barrier()

        ar.reset()
        wuqa = ar.alloc([2, 576], BF16)
        wuqb = ar.alloc([2, 576], BF16)
        wukvk = ar.alloc([384], BF16)
        wukvv = ar.alloc([384], BF16)
        load_cast(wuqa_d[l], 1152, wuqa.rearrange("p a b -> p (a b)"), "wuqa")
        load_cast(wuqb_d[l], 1152, wuqb.rearrange("p a b -> p (a b)"), "wuqb")
        load_cast(wukvk_d[l], 384, wukvk, "wukvk")
        load_cast(wukvv_d[l], 384, wukvv, "wukvv")
        lat_rot = Rot(ar, 2, [3, TC], F32, "lat")
        kra_rot = Rot(ar, 2, [TC], F32, "kra")
        krb_rot = Rot(ar, 2, [TC], F32, "krb")
        cc_rot = Rot(ar, 2, [TC], F32, "cc")
        ss_rot = Rot(ar, 2, [TC], F32, "ss")
        sq2 = ar.alloc([3, TC], F32)
        rq_rot = Rot(ar, 2, [TC], F32, "rq")
        rkv_rot = Rot(ar, 2, [TC], F32, "rkv")
        cqn_rot = Rot(ar, 2, [3, TC], BF16, "cqn")
        qd_rot = Rot(ar, 3, [TC], BF16, "qd")
        kd_rot = Rot(ar, 3, [TC], BF16, "kd")
        t1_rot = Rot(ar, 2, [TC], F32, "t1")
        t2_rot = Rot(ar, 2, [TC], F32, "t2")
        krr_rot = Rot(ar, 2, [TC], BF16, "krr")
        dvst = ar.alloc([6, 32, 64], BF16)
        DLv = DLAT.rearrange("(c p) s -> p c s", p=128)
        for t in range(NT):
            tsl = slice(t * TC, (t + 1) * TC)
            lat, latr = lat_rot.next()
            kra, krar = kra_rot.next()
            krb, krbr = krb_rot.next()
            cct, ccr = cc_rot.next()
            sst, ssr = ss_rot.next()
            sc.add("sp", lambda e, lat=lat, tsl=tsl: e.dma_start(out=lat, in_=DLv[:, :, tsl]), writes=[latr], dma=True)
            sc.add("sp", lambda e, kra=kra, tsl=tsl: e.dma_start(out=kra[64:96, :], in_=DKRAW[0:32, tsl]), writes=[krar], dma=True)
            sc.add("sp", lambda e, krb=krb, tsl=tsl: e.dma_start(out=krb[64:96, :], in_=DKRAW[32:64, tsl]), writes=[krbr], dma=True)
            sc.add("sp", lambda e, cct=cct, tsl=tsl: e.dma_start(out=cct[64:96, :], in_=ropec_d[:, tsl]), writes=[ccr], dma=True)
            sc.add("sp", lambda e, sst=sst, tsl=tsl: e.dma_start(out=sst[64:96, :], in_=ropes_d[:, tsl]), writes=[ssr], dma=True)
            sc.add("act", lambda e, lat=lat: e.activation(out=sq2, in_=lat, func=AF.Square), reads=[latr], writes=["sq2"])
            b0, b0r = bank(0)
            b1, b1r = bank(1)

            def ssq(e, b0=b0, b1=b1):
                e.matmul(b0, lhsT=ones_f, rhs=sq2[:, 0, :], start=True, stop=False)
                e.matmul(b0, lhsT=ones_f, rhs=sq2[:, 1, :], start=False, stop=True)
                return e.matmul(b1, lhsT=ones_f, rhs=sq2[:, 2, :], start=True, stop=True)
            sc.add("pe", ssq, reads=["sq2", "ones_f"], writes=[b0r, b1r])
            rq, rqr = rq_rot.next()
            rkv, rkvr = rkv_rot.next()

            def rsq(e, rq=rq, rkv=rkv, b0=b0, b1=b1):
                e.activation(out=rq, in_=b0, func=AF.Sqrt, scale=1.0 / 256, bias=EPS)
                return e.activation(out=rkv, in_=b1, func=AF.Sqrt, scale=1.0 / 128, bias=EPS)
            sc.add("act", rsq, reads=[b0r, b1r], writes=[rqr, rkvr])
            cqn, cqnr = cqn_rot.next()

            def nrm(e, rq=rq, rkv=rkv, lat=lat, cqn=cqn):
                e.reciprocal(out=rq, in_=rq)
                e.reciprocal(out=rkv, in_=rkv)
                e.scalar_tensor_tensor(out=cqn[:, 0, :], in0=lat[:, 0, :], scalar=col(SM_QN), in1=rq, op0=ALU.mult, op1=ALU.mult)
                e.scalar_tensor_tensor(out=cqn[:, 1, :], in0=lat[:, 1, :], scalar=col(SM_QN + 1), in1=rq, op0=ALU.mult, op1=ALU.mult)
                return e.scalar_tensor_tensor(out=cqn[:, 2, :], in0=lat[:, 2, :], scalar=col(SM_KVN), in1=rkv, op0=ALU.mult, op1=ALU.mult)
            sc.add("dve", nrm, reads=[rqr, rkvr, latr, "small"], writes=[cqnr, rqr, rkvr])
            t1, t1r = t1_rot.next()
            t2, t2r = t2_rot.next()
            krr, krrr = krr_rot.next()

            def krope(e, kra=kra, krb=krb, cct=cct, sst=sst, t1=t1, t2=t2, krr=krr):
                e.tensor_tensor(out=t1[64:96, :], in0=krb[64:96, :], in1=sst[64:96, :], op=ALU.mult)
                e.tensor_tensor(out=t2[64:96, :], in0=kra[64:96, :], in1=cct[64:96, :], op=ALU.mult)
                return e.tensor_tensor(out=krr[64:96, :], in0=t1[64:96, :], in1=t2[64:96, :], op=ALU.add)
            sc.add("dve", krope, reads=[krar, krbr, ccr, ssr], writes=[t1r, t2r, krrr])
            sc.add("sp", lambda e, krr=krr, tsl=tsl: e.dma_start(out=DKR[:, tsl], in_=krr[64:96, :]), reads=[krrr], dma=True)
            for h in range(D_HEADS):
                ba, bar_ = bank(2 + (h % 2) * 3)
                bb, bbr = bank(3 + (h % 2) * 3)
                bkk, bkr = bank(4 + (h % 2) * 3)

                def upq(e, ba=ba, bb=bb, bkk=bkk, h=h, cqn=cqn):
                    for c in range(2):
                        e.matmul(ba[0:96, :], lhsT=wuqa[:, c, h * 96:(h + 1) * 96], rhs=cqn[:, c, :], start=(c == 0), stop=(c == 1))
                    for c in range(2):
                        e.matmul(bb[0:96, :], lhsT=wuqb[:, c, h * 96:(h + 1) * 96], rhs=cqn[:, c, :], start=(c == 0), stop=(c == 1))
                    return e.matmul(bkk[0:64, :], lhsT=wukvk[:, h * 64:(h + 1) * 64], rhs=cqn[:, 2, :], start=True, stop=True)
                sc.add("pe", upq, reads=[cqnr, "wuqa", "wuqb", "wukvk"], writes=[bar_, bbr, bkr])
                qd, qdr = qd_rot.next()
                kd, kdr = kd_rot.next()
                t1, t1r = t1_rot.next()
                t2, t2r = t2_rot.next()

                def qrope(e, ba=ba, bb=bb, qd=qd, t1=t1, t2=t2, cct=cct, sst=sst):
                    e.tensor_copy(out=qd[0:64, :], in_=ba[0:64, :])
                    e.tensor_tensor(out=t1[64:96, :], in0=bb[64:96, :], in1=sst[64:96, :], op=ALU.mult)
                    e.tensor_tensor(out=t2[64:96, :], in0=ba[64:96, :], in1=cct[64:96, :], op=ALU.mult)
                    return e.tensor_tensor(out=qd[64:96, :], in0=t1[64:96, :], in1=t2[64:96, :], op=ALU.add)
                sc.add("dve", qrope, reads=[bar_, bbr, ccr, ssr], writes=[qdr, t1r, t2r])
                sc.add("sp", lambda e, qd=qd, h=h, tsl=tsl: e.dma_start(out=DQ[h * 96:(h + 1) * 96, tsl], in_=qd[0:96, :]), reads=[qdr], dma=True)
                sc.add("act", lambda e, kd=kd, bkk=bkk: e.activation(out=kd[0:64, :], in_=bkk[0:64, :], func=AF.Copy), reads=[bkr], writes=[kdr])
                sc.add("act", lambda e, kd=kd, h=h, tsl=tsl: e.dma_start(out=DK[h * 64:(h + 1) * 64, tsl], in_=kd[0:64, :]), reads=[kdr], dma=True)
            for tb in range(4):
                bv, bvr = bank(tb % 2)
                b = t * 4 + tb
                sc.add("pe", lambda e, bv=bv, tb=tb, cqn=cqn: e.matmul(bv[:, 0:384], lhsT=cqn[:, 2, tb * 128:(tb + 1) * 128], rhs=wukvv, start=True, stop=True),
                       reads=[cqnr, "wukvv"], writes=[bvr])
                sc.add("act", lambda e, bv=bv, b=b: e.activation(out=dvst[:, :, b, :], in_=bv[:, 0:384].rearrange("p (h d) -> p h d", h=6), func=AF.Copy),
                       reads=[bvr], writes=[("dvst", b)])
        sc.add("sp", lambda e: e.dma_start(out=DV, in_=dvst.rearrange("p h b d -> p (h b d)")), reads=[("dvst", b) for b in range(32)], dma=True)
        sc.barrier()

        ar.reset()
        ma = ar.alloc([18, 384], F32)
        sc.add("sp", lambda e: e.dma_start(out=ma.rearrange("p a b -> p (a b)"), in_=ma_d), writes=["ma"], dma=True)
        acc = ar.alloc([S], F32)
        kt_rot = Rot(ar, 2, [S], BF16, "kt")
        qt_rot = Rot(ar, 2, [S], BF16, "qt")
        vraw_rot = Rot(ar, 2, [32 * 64], BF16, "vraw")
        vt_tiles = [ar.alloc([32, 128], BF16) for _ in range(2)]
        for k in range(2):
            sc.add("pool", lambda e, k=k: e.memset(vt_tiles[k][:, :, 64:128], 1.0), writes=[("vt1", k)])
        pf_rot = Rot(ar, 2, [384], F32, "pf")
        pb_rot = Rot(ar, 2, [384], BF16, "pb")
        ag_rot = Rot(ar, 2, [TC], BF16, "ag")
        rz_rot = Rot(ar, 2, [TC], F32, "rz")
        yf_rot = Rot(ar, 2, [TC], F32, "yf")
        yb_rot = Rot(ar, 2, [TC], BF16, "yb")
        vt_cnt = [0]

        def a_group(s_, g):
            if True:
                d = A_DIL[g]
                nbs = 32 // d
                kt, ktr = kt_rot.next()
                qt, qtr = qt_rot.next()
                vraw, vrr = vraw_rot.next()
                vk = vt_cnt[0] % 2
                vt_cnt[0] += 1
                vt, vtr = vt_tiles[vk], ("vt", vk)
                sc.add("sp", lambda e, kt=kt, g=g, s_=s_: e.dma_start(out=kt[0:64, :], in_=AK[g][s_ * 64:(s_ + 1) * 64, :]), writes=[ktr], dma=True)
                sc.add("sp", lambda e, qt=qt, g=g, s_=s_: e.dma_start(out=qt[0:64, :], in_=AQ[g][s_ * 64:(s_ + 1) * 64, :]), writes=[qtr], dma=True)
                sc.add("sp", lambda e, vraw=vraw, g=g, s_=s_: e.dma_start(out=vraw, in_=AV[g][:, s_ * 2048:(s_ + 1) * 2048]), writes=[vrr], dma=True)
                sc.add("pool", lambda e, vt=vt, vraw=vraw: e.tensor_copy(out=vt[:, :, 0:64], in_=vraw.rearrange("p (b d) -> p b d", b=32)),
                       reads=[vrr, ("vt1", vk)], writes=[vtr])
                accv = acc if d == 1 else acc.rearrange("p (l r) -> p r l", r=d)

                def s_op(qb, psS, psSr):
                    lb = qb % nbs
                    js = [j for j in range(3) if 0 <= lb - 1 + j < nbs]

                    def f(e):
                        r = None
                        for j in js:
                            kb = qb - 1 + j
                            r = e.matmul(psS[:, j * 128:(j + 1) * 128], lhsT=kt[0:64, kb * 128:(kb + 1) * 128], rhs=qt[0:64, qb * 128:(qb + 1) * 128],
                                         start=True, stop=True)
                        return r
                    sc.add("pe", f, reads=[ktr, qtr], writes=[psSr])
                    return js

                pend = None
                for qb in range(33):
                    cur = None
                    if qb < 32:
                        psS, psSr = bank(qb % 2)
                        js = s_op(qb, psS, psSr)
                        cur = (qb, psS, psSr, js)
                    if pend is not None:
                        pq, ps_, psr_, pjs = pend
                        c0, c1 = pjs[0] * 128, (pjs[-1] + 1) * 128
                        pf, pfr = pf_rot.next()
                        pb, pbr = pb_rot.next()
                        sc.add("act", lambda e, pf=pf, ps_=ps_, c0=c0, c1=c1: e.activation(out=pf[:, c0:c1], in_=ps_[:, c0:c1], func=AF.Exp),
                               reads=[psr_], writes=[pfr])
                        sc.add("dve", lambda e, pf=pf, pb=pb, c0=c0, c1=c1, mi=g * 6 + s_: e.tensor_tensor(out=pb[:, c0:c1], in0=pf[:, c0:c1], in1=ma[:, mi, c0:c1], op=ALU.mult),
                               reads=[pfr, "ma"], writes=[pbr])
                        psO, psOr = bank(2 + pq % 2)

                        def pv(e, pq=pq, pjs=pjs, pb=pb, psO=psO):
                            r = None
                            for j in pjs:
                                kb = pq - 1 + j
                                r = e.matmul(psO[:, 0:128], lhsT=vt[:, kb, :], rhs=pb[:, j * 128:(j + 1) * 128], start=(j == pjs[0]), stop=(j == pjs[-1]))
                            return r
                        sc.add("pe", pv, reads=[pbr, vtr], writes=[psOr])
                        rr_, lb_ = pq // nbs, pq % nbs
                        av = acc[:, pq * 128:(pq + 1) * 128] if d == 1 else accv[:, rr_, lb_ * 128:(lb_ + 1) * 128]
                        if g == 0:
                            sc.add("dve", lambda e, av=av, psO=psO: e.tensor_copy(out=av, in_=psO[:, 0:128]), reads=[psOr], writes=["acc"])
                        else:
                            sc.add("dve", lambda e, av=av, psO=psO: e.tensor_tensor(out=av, in0=psO[:, 0:128], in1=av, op=ALU.add), reads=[psOr, "acc"], writes=["acc"])
                    pend = cur
        for s_ in range(A_SLOTS):
            for g in range(3):
                a_group(s_, g)
            for t in range(NT):
                tsl = slice(t * TC, (t + 1) * TC)
                agt, agr = ag_rot.next()
                rz, rzr = rz_rot.next()
                yf, yfr = yf_rot.next()
                yb, ybr = yb_rot.next()
                sc.add("sp", lambda e, agt=agt, tsl=tsl, s_=s_: e.dma_start(out=agt[0:64, :], in_=AG[s_ * 64:(s_ + 1) * 64, tsl]), writes=[agr], dma=True)

                def epi(e, rz=rz, yf=yf, yb=yb, agt=agt, tsl=tsl):
                    e.reciprocal(out=rz[0:64, :], in_=acc[64:128, tsl])
                    e.tensor_tensor(out=yf[0:64, :], in0=acc[0:64, tsl], in1=rz[0:64, :], op=ALU.mult)
                    return e.tensor_tensor(out=yb[0:64, :], in0=yf[0:64, :], in1=agt[0:64, :], op=ALU.mult)
                sc.add("dve", epi, reads=["acc", agr], writes=[rzr, yfr, ybr])
                sc.add("sp", lambda e, yb=yb, tsl=tsl, s_=s_: e.dma_start(out=Y[s_ * 64:(s_ + 1) * 64, tsl], in_=yb[0:64, :]), reads=[ybr], dma=True)
        sc.barrier()

        ar.reset()
        lw = ar.alloc([12, 128], BF16)
        load_cast(lruw_d[l], 1536, lw.rearrange("p a b -> p (a b)"), "lw")
        xp = ar.alloc([S + 4], F32)
        xc = ar.alloc([S], F32)
        Rb = ar.alloc([S], F32)
        Ib = ar.alloc([S], F32)
        Ab_ = ar.alloc([S], F32)
        Hf = ar.alloc([S], F32)
        Hb = ar.alloc([S], F32)
        xcb = ar.alloc([S], BF16)
        bgt = ar.alloc([S], BF16)
        sc.add("dve", lambda e: e.memset(xp[:, 0:1], 0.0), writes=["xp_pad0"])
        sc.add("dve", lambda e: e.memset(xp[:, S + 1:S + 4], 0.0), writes=["xp_pad1"])
        for c in range(3):
            sc.add("sp", lambda e, c=c: e.dma_start(out=xp[:, 1:S + 1], in_=BX[c * 128:(c + 1) * 128, :]), writes=["xp"], dma=True)
            sc.add("sp", lambda e, c=c: e.dma_start(out=bgt, in_=BG[c * 128:(c + 1) * 128, :]), writes=["bgt"], dma=True)

            def conv(e, c=c):
                e.tensor_scalar(out=xc, in0=xp[:, 0:S], scalar1=col(SM_CONVW + 0 * 3 + c), scalar2=col(SM_CONVB + c), op0=ALU.mult, op1=ALU.add)
                for j in range(1, 4):
                    r = e.scalar_tensor_tensor(out=xc, in0=xp[:, j:j + S], scalar=col(SM_CONVW + j * 3 + c), in1=xc, op0=ALU.mult, op1=ALU.add)
                return r
            sc.add("dve", conv, reads=["xp", "xp_pad0", "xp_pad1", "small"], writes=["xc"])
            sc.add("pool", lambda e: e.tensor_copy(out=xcb, in_=xc), reads=["xc"], writes=["xcb"])
            for dr in range(2):
                for t in range(NT):
                    tsl = slice(t * TC, (t + 1) * TC)
                    bR, bRr = bank((2 * t) % 8)
                    bI, bIr = bank((2 * t + 1) % 8)

                    def gmm(e, bR=bR, bI=bI, tsl=tsl, c=c, dr=dr):
                        e.matmul(bR, lhsT=lw[:, (0 * 2 + dr) * 3 + c, :], rhs=xcb[:, tsl], start=True, stop=True)
                        return e.matmul(bI, lhsT=lw[:, (1 * 2 + dr) * 3 + c, :], rhs=xcb[:, tsl], start=True, stop=True)
                    sc.add("pe", gmm, reads=["xcb", "lw"], writes=[bRr, bIr])
                    sc.add("dve", lambda e, bR=bR, tsl=tsl, c=c, dr=dr: e.tensor_scalar(out=Rb[:, tsl], in0=bR, scalar1=col(SM_BR + dr * 3 + c), scalar2=None, op0=ALU.add),
                           reads=[bRr, "small"], writes=[("Rb", t)])
                    sc.add("dve", lambda e, bI=bI, tsl=tsl, c=c, dr=dr: e.tensor_scalar(out=Ib[:, tsl], in0=bI, scalar1=col(SM_BI + dr * 3 + c), scalar2=None, op0=ALU.add),
                           reads=[bIr, "small"], writes=[("Ib", t)])
                Rres = [("Rb", t) for t in range(NT)]
                Ires = [("Ib", t) for t in range(NT)]

                def gates(e, c=c, dr=dr):
                    e.activation(out=Rb, in_=Rb, func=AF.Sigmoid)
                    e.activation(out=Ib, in_=Ib, func=AF.Sigmoid)
                    e.activation(out=Ab_, in_=Rb, func=AF.Exp, scale=spc[:, dr * 3 + c:dr * 3 + c + 1])
                    e.activation(out=Rb, in_=Ab_, func=AF.Square)
                    return e.activation(out=Rb, in_=Rb, func=AF.Sqrt, scale=-1.0, bias=1.0)
                sc.add("act", gates, reads=Rres + Ires + ["spc", "Hscan%d" % dr], writes=["Rw", "Ig", "Ab"])
                Hd = Hf if dr == 0 else Hb

                def premul(e):
                    e.tensor_tensor(out=Ib, in0=Ib, in1=xc, op=ALU.mult)
                    return e.tensor_tensor(out=Ib, in0=Ib, in1=Rb, op=ALU.mult)
                sc.add("dve", premul, reads=["Rw", "Ig", "Ab", "xc"], writes=["U", "Hscan%d" % (1 - dr)] + Rres + Ires)
                order = list(range(NT)) if dr == 0 else list(range(NT - 1, -1, -1))
                for oi, t in enumerate(order):
                    tsl = slice(t * TC, (t + 1) * TC)
                    if oi == 0:
                        init = 0.0
                    elif dr == 0:
                        init = Hd[:, t * TC - 1:t * TC]
                    else:
                        init = Hd[:, (t + 1) * TC:(t + 1) * TC + 1]
                    if dr == 0:
                        sc.add("dve", lambda e, Hd=Hd, tsl=tsl, init=init: e.tensor_tensor_scan(out=Hd[:, tsl], data0=Ab_[:, tsl], data1=Ib[:, tsl], initial=init, op0=ALU.mult, op1=ALU.add),
                               reads=["U", "Ab"] + ([("Hc", dr, order[oi - 1])] if oi else []), writes=[("Hc", dr, t)])
                    else:
                        sc.add("dve", lambda e, Hd=Hd, tsl=tsl, init=init: e.tensor_tensor_scan(out=Hd[:, tsl][:, ::-1], data0=Ab_[:, tsl][:, ::-1], data1=Ib[:, tsl][:, ::-1], initial=init, op0=ALU.mult, op1=ALU.add),
                               reads=["U", "Ab"] + ([("Hc", dr, order[oi - 1])] if oi else []), writes=[("Hc", dr, t)])

            def fin(e):
                e.tensor_tensor(out=Hf, in0=Hf, in1=Hb, op=ALU.add)
                return e.tensor_tensor(out=bgt, in0=Hf, in1=bgt, op=ALU.mult)
            sc.add("dve", fin, reads=[("Hc", 0, NT - 1), ("Hc", 1, 0), "bgt"], writes=["bgt"])
            sc.add("sp", lambda e, c=c: e.dma_start(out=Y[384 + c * 128:384 + (c + 1) * 128, :], in_=bgt), reads=["bgt"], dma=True)
        sc.barrier()

        ar.reset()
        cs = c_slopes()
        ka_rot = Rot(ar, 4, [S], BF16, "ka")
        vc_rot = Rot(ar, 2, [32 * 128], BF16, "vc")
        qb_rot = Rot(ar, 4, [TC], BF16, "qbf")
        qa_rot = Rot(ar, 4, [TC], BF16, "qaf")
        cg_rot = Rot(ar, 2, [TC], BF16, "cg")
        pbc_rot = Rot(ar, 3, [TC], BF16, "pbc")
        pfc_rot = Rot(ar, 2, [128], F32, "pfc")
        e_rz = ar.alloc([TC], F32)
        e_o0 = ar.alloc([TC], F32)
        e_o1 = ar.alloc([TC], F32)
        e_sq = ar.alloc([TC], F32)
        e_sd = ar.alloc([TC], F32)
        ybc_rot = Rot(ar, 2, [TC], BF16, "ybc")
        def c_head(h):
            m = cs[h]
            kas = []
            for c in range(2):
                ka, kar = ka_rot.next()
                kas.append((ka, kar))
                sc.add("sp", lambda e, ka=ka, h=h, c=c: e.dma_start(out=ka[0:64, :], in_=CK[(h * 2 + c) * 64:(h * 2 + c + 1) * 64, :]), writes=[kar], dma=True)
                sc.add("sp", lambda e, ka=ka, h=h: e.dma_start(out=ka[64:68, :], in_=caugk_d[h]), writes=[(kar, "aug")], dma=True)
            vc, vcr = vc_rot.next()
            vcv = vc.rearrange("p (b d) -> p b d", b=32)
            sc.add("sp", lambda e, vc=vc, h=h: e.dma_start(out=vc, in_=CV[:, h * 4096:(h + 1) * 4096]), writes=[vcr], dma=True)
            def c_chunk(qc):
                tsl = slice(qc * TC, (qc + 1) * TC)
                i0 = qc * TC
                qs = []
                for c in range(2):
                    qbf, qbr = qb_rot.next()
                    qaf, qar = qa_rot.next()
                    src = CQ[(h * 2 + c) * 64:(h * 2 + c + 1) * 64, tsl]
                    sc.add("sp", lambda e, qbf=qbf, src=src: e.dma_start(out=qbf[0:64, :], in_=src), writes=[qbr], dma=True)
                    sc.add("sp", lambda e, qbf=qbf, h=h: e.dma_start(out=qbf[64:68, :], in_=caugq_d[h, 0]), writes=[(qbr, "aug")], dma=True)
                    sc.add("sp", lambda e, qaf=qaf, src=src: e.dma_start(out=qaf[0:64, :], in_=src), writes=[qar], dma=True)
                    sc.add("sp", lambda e, qaf=qaf, h=h: e.dma_start(out=qaf[64:68, :], in_=caugq_d[h, 1]), writes=[(qar, "aug")], dma=True)
                    qs.append((qbf, qbr, qaf, qar))
                cgt, cgr = cg_rot.next()
                sc.add("sp", lambda e, cgt=cgt, h=h, tsl=tsl: e.dma_start(out=cgt, in_=CG[h * 128:(h + 1) * 128, tsl]), writes=[cgr], dma=True)
                units = []
                for c in range(2):
                    kbs = []
                    for kb in range(32):
                        j0 = kb * 128
                        if j0 + 128 <= i0:
                            if m * (i0 - (j0 + 127)) > SKIP_T:
                                continue
                        elif j0 >= i0 + TC:
                            if m * (j0 - (i0 + TC - 1)) > SKIP_T:
                                continue
                        kbs.append(kb)
                    for ii, kb in enumerate(kbs):
                        units.append((c, kb, ii == 0, ii == len(kbs) - 1))
                psO = [bank(2), bank(4)]
                psZ = [bank(3), bank(5)]

                def emit_s(u, ui):
                    c, kb, first, last = u
                    ka, kar = kas[c]
                    qbf, qbr, qaf, qar = qs[c]
                    psS, psSr = bank(ui % 2)
                    j0 = kb * 128
                    rds = [kar, (kar, "aug"), qbr, (qbr, "aug"), qar, (qar, "aug")]
                    pbt, pbr = pbc_rot.next()
                    if j0 + 128 <= i0 or j0 >= i0 + TC:
                        before = j0 + 128 <= i0
                        qq = qbf if before else qaf
                        bias = (-m * (i0 - j0)) if before else (m * (i0 - j0))
                        sc.add("pe", lambda e: e.matmul(psS, lhsT=ka[0:68, j0:j0 + 128], rhs=qq[0:68, :], start=True, stop=True), reads=rds, writes=[psSr])
                        sc.add("act", lambda e: e.activation(out=pbt, in_=psS, func=AF.Exp, bias=float(bias)), reads=[psSr], writes=[pbr])
                    else:
                        sb = (j0 - i0) // 128
                        ca, cb_, cc_ = sb * 128, (sb + 1) * 128, TC

                        def mmd(e):
                            r = None
                            if sb > 0:
                                r = e.matmul(psS[:, 0:ca], lhsT=ka[0:68, j0:j0 + 128], rhs=qaf[0:68, 0:ca], start=True, stop=True)
                            r = e.matmul(psS[:, ca:cb_], lhsT=ka[0:64, j0:j0 + 128], rhs=qbf[0:64, ca:cb_], start=True, stop=True)
                            if sb < 3:
                                r = e.matmul(psS[:, cb_:cc_], lhsT=ka[0:68, j0:j0 + 128], rhs=qbf[0:68, cb_:cc_], start=True, stop=True)
                            return r
                        sc.add("pe", mmd, reads=rds, writes=[psSr])
                        pfc, pfr = pfc_rot.next()

                        def actd(e):
                            if sb > 0:
                                e.activation(out=pbt[:, 0:ca], in_=psS[:, 0:ca], func=AF.Exp, bias=float(m * (i0 - j0)))
                            if sb < 3:
                                e.activation(out=pbt[:, cb_:cc_], in_=psS[:, cb_:cc_], func=AF.Exp, bias=float(-m * (i0 - j0)))
                            return e.activation(out=pfc, in_=psS[:, ca:cb_], func=AF.Exp)
                        sc.add("act", actd, reads=[psSr], writes=[pbr, (pbr, "a"), pfr])
                        sc.add("dve", lambda e: e.tensor_tensor(out=pbt[:, ca:cb_], in0=pfc, in1=mdiag[:, h, :], op=ALU.mult), reads=[pfr, "mdiag", (pbr, "a")], writes=[pbr])
                    return pbt, pbr

                def emit_pv(u, pbt, pbr):
                    c, kb, first, last = u
                    o_, or_ = psO[c]
                    z_, zr_ = psZ[c]

                    def f(e):
                        e.matmul(o_, lhsT=vcv[:, kb, :], rhs=pbt, start=first, stop=last)
                        return e.matmul(z_, lhsT=ones_bf, rhs=pbt, start=first, stop=last)
                    sc.add("pe", f, reads=[pbr, vcr, "ones_bf"], writes=[or_, zr_])

                pend = None
                for ui in range(len(units) + 1):
                    cur = None
                    if ui < len(units):
                        pbt, pbr = emit_s(units[ui], ui)
                        cur = (units[ui], pbt, pbr)
                    if pend is not None:
                        emit_pv(*pend)
                    pend = cur
                (o0, o0r), (o1, o1r) = psO
                (z0, z0r), (z1, z1r) = psZ

                def ep1(e):
                    e.reciprocal(out=e_rz, in_=z0)
                    e.tensor_tensor(out=e_o0, in0=o0, in1=e_rz, op=ALU.mult)
                    e.reciprocal(out=e_rz, in_=z1)
                    e.tensor_tensor(out=e_o1, in0=o1, in1=e_rz, op=ALU.mult)
                    return e.scalar_tensor_tensor(out=e_o0, in0=e_o1, scalar=lamt[:, 4:5], in1=e_o0, op0=ALU.mult, op1=ALU.add)
                sc.add("dve", ep1, reads=[o0r, o1r, z0r, z1r, "lamt45", "e_y"], writes=["e_o"])
                sc.add("act", lambda e: e.activation(out=e_sq, in_=e_o0, func=AF.Square), reads=["e_o"], writes=["e_sq"])
                bn, bnr = bank(6)
                sc.add("pe", lambda e: e.matmul(bn, lhsT=ones_f, rhs=e_sq, start=True, stop=True), reads=["e_sq", "ones_f"], writes=[bnr])
                sc.add("act", lambda e: e.activation(out=e_sd, in_=bn, func=AF.Sqrt, scale=1.0 / 128, bias=EPS), reads=[bnr], writes=["e_sd"])
                ybc, ybcr = ybc_rot.next()

                def ep2(e, ybc=ybc, cgt=cgt):
                    e.reciprocal(out=e_sd, in_=e_sd)
                    e.scalar_tensor_tensor(out=e_o1, in0=e_o0, scalar=lamt[:, 5:6], in1=e_sd, op0=ALU.mult, op1=ALU.mult)
                    return e.tensor_tensor(out=ybc, in0=e_o1, in1=cgt, op=ALU.mult)
                sc.add("dve", ep2, reads=["e_sd", "e_o", cgr, "lamt45"], writes=[ybcr, "e_y"])
                sc.add("sp", lambda e, ybc=ybc, h=h, tsl=tsl: e.dma_start(out=Y[768 + h * 128:768 + (h + 1) * 128, tsl], in_=ybc), reads=[ybcr], dma=True)
            for qc in range(NT):
                c_chunk(qc)
        for h in range(C_HEADS):
            c_head(h)
        sc.barrier()

        ar.reset()
        scale_d = 96.0 ** -0.5
        kd2_rot = Rot(ar, 2, [S], BF16, "kd2")
        vraw2_rot = Rot(ar, 2, [32 * 64], BF16, "vraw2")
        vd_tiles = [ar.alloc([32, 128], BF16) for _ in range(2)]
        for k in range(2):
            sc.add("pool", lambda e, k=k: e.memset(vd_tiles[k][:, :, 64:128], 1.0), writes=[("vd1", k)])
        qd2_rot = Rot(ar, 2, [TC], BF16, "qd2")
        dg_rot = Rot(ar, 2, [TC], BF16, "dg")
        pbd_rot = Rot(ar, 3, [TC], BF16, "pbd")
        rzd_rot = Rot(ar, 2, [TC], F32, "rzd")
        yfd_rot = Rot(ar, 2, [TC], F32, "yfd")
        ybd_rot = Rot(ar, 2, [TC], BF16, "ybd")
        for h in range(D_HEADS):
            kd, kdr = kd2_rot.next()
            sc.add("sp", lambda e, kd=kd, h=h: e.dma_start(out=kd[0:64, :], in_=DK[h * 64:(h + 1) * 64, :]), writes=[kdr], dma=True)
            sc.add("sp", lambda e, kd=kd: e.dma_start(out=kd[64:96, :], in_=DKR), writes=[(kdr, "r")], dma=True)
            vraw, vrr = vraw2_rot.next()
            vk = h % 2
            vt, vtr = vd_tiles[vk], ("vd", vk)
            sc.add("sp", lambda e, vraw=vraw, h=h: e.dma_start(out=vraw, in_=DV[:, h * 2048:(h + 1) * 2048]), writes=[vrr], dma=True)
            sc.add("pool", lambda e, vt=vt, vraw=vraw: e.tensor_copy(out=vt[:, :, 0:64], in_=vraw.rearrange("p (b d) -> p b d", b=32)),
                   reads=[vrr, ("vd1", vk)], writes=[vtr])
            for qc in range(NT):
                tsl = slice(qc * TC, (qc + 1) * TC)
                qd, qdr = qd2_rot.next()
                sc.add("sp", lambda e, qd=qd, h=h, tsl=tsl: e.dma_start(out=qd[0:96, :], in_=DQ[h * 96:(h + 1) * 96, tsl]), writes=[qdr], dma=True)
                dgt, dgr = dg_rot.next()
                sc.add("sp", lambda e, dgt=dgt, h=h, tsl=tsl: e.dma_start(out=dgt[0:64, :], in_=DG[h * 64:(h + 1) * 64, tsl]), writes=[dgr], dma=True)
                psO, psOr = bank(2 + qc % 2)
                pend = None
                for kb in range(33):
                    cur = None
                    if kb < 32:
                        psS, psSr = bank(kb % 2)
                        pbt, pbr = pbd_rot.next()
                        sc.add("pe", lambda e, psS=psS, kb=kb, kd=kd, qd=qd: e.matmul(psS, lhsT=kd[0:96, kb * 128:(kb + 1) * 128], rhs=qd[0:96, :], start=True, stop=True),
                               reads=[kdr, (kdr, "r"), qdr], writes=[psSr])
                        sc.add("act", lambda e, psS=psS, pbt=pbt: e.activation(out=pbt, in_=psS, func=AF.Exp, scale=scale_d), reads=[psSr], writes=[pbr])
                        cur = (kb, pbt, pbr)
                    if pend is not None:
                        pk, ppb, ppbr = pend
                        sc.add("pe", lambda e, pk=pk, ppb=ppb, psO=psO, vt=vt: e.matmul(psO, lhsT=vt[:, pk, :], rhs=ppb, start=(pk == 0), stop=(pk == 31)),
                               reads=[ppbr, vtr], writes=[psOr])
                    pend = cur
                rz, rzr = rzd_rot.next()
                yf, yfr = yfd_rot.next()
                yb, ybr = ybd_rot.next()

                def epd(e, rz=rz, yf=yf, yb=yb, psO=psO, dgt=dgt):
                    e.reciprocal(out=rz[0:64, :], in_=psO[64:128, :])
                    e.tensor_tensor(out=yf[0:64, :], in0=psO[0:64, :], in1=rz[0:64, :], op=ALU.mult)
                    return e.tensor_tensor(out=yb[0:64, :], in0=yf[0:64, :], in1=dgt[0:64, :], op=ALU.mult)
                sc.add("dve", epd, reads=[psOr, dgr], writes=[rzr, yfr, ybr])
                sc.add("sp", lambda e, yb=yb, h=h, tsl=tsl: e.dma_start(out=Y[1280 + h * 64:1280 + (h + 1) * 64, tsl], in_=yb[0:64, :]), reads=[ybr], dma=True)
        sc.barrier()

        ar.reset()
        wbr = ar.alloc([13, 1024], BF16)
        wout = ar.alloc([8, 1024], BF16)
        wbr_f = wbr.rearrange("p a b -> p (a b)")
        wout_f = wout.rearrange("p a b -> p (a b)")
        for i in range(4):
            n = 4096 if i < 3 else 1024
            load_cast(wbr_d[l][:, i * 4096:i * 4096 + n], n, wbr_f[:, i * 4096:i * 4096 + n], ("wbr", i))
        for i in range(2):
            load_cast(wout_d[l][:, i * 4096:(i + 1) * 4096], 4096, wout_f[:, i * 4096:(i + 1) * 4096], ("wout", i))
        wbr_res = [("wbr", i) for i in range(4)]
        wout_res = [("wout", i) for i in range(2)]
        yt_rot = Rot(ar, 2, [13, TC], BF16, "yt")
        xt2 = ar.alloc([KC, TC], F32)
        out2 = ar.alloc([KC, TC], F32)
        merged = ar.alloc([KC, TC], BF16)
        g_rot = Rot(ar, 4, [TC], BF16, "g")
        tm_rot = Rot(ar, 2, [TC], F32, "tm")
        macc = ar.alloc([TC], F32)
        sqf_rot = Rot(ar, 2, [TC], F32, "sqf")
        rr2 = ar.alloc([TC], F32)
        br_chunks = [(0, 3), (3, 3), (6, 4), (10, 3)]
        Yv = Y.rearrange("(c p) s -> p c s", p=128)
        x_src_v2 = x_src.rearrange("(kc p) s -> p kc s", p=128)
        x_dst_v = x_dst.rearrange("(kc p) s -> p kc s", p=128)
        pbk = [0]
        for t in range(NT):
            tsl = slice(t * TC, (t + 1) * TC)
            yt, ytr = yt_rot.next()
            sc.add("sp", lambda e, yt=yt, tsl=tsl: e.dma_start(out=yt, in_=Yv[:, :, tsl]), writes=[ytr], dma=True)
            sc.add("sp", lambda e, tsl=tsl: e.dma_start(out=xt2, in_=x_src_v2[:, :, tsl]), writes=["xt2"], dma=True)
            for oc in range(8):
                for br in range(4):
                    gt, gr = g_rot.next()
                    sc.add("sp", lambda e, gt=gt, br=br, oc=oc, tsl=tsl: e.dma_start(out=gt, in_=GT[br * 1024 + oc * 128:br * 1024 + (oc + 1) * 128, tsl]), writes=[gr], dma=True)
                    bk, bkr = bank(pbk[0] % 4)
                    pbk[0] += 1
                    c0, ncn = br_chunks[br]

                    def bmm(e, bk=bk, c0=c0, ncn=ncn, oc=oc, yt=yt):
                        r = None
                        for ci in range(ncn):
                            r = e.matmul(bk, lhsT=wbr[:, c0 + ci, oc * 128:(oc + 1) * 128], rhs=yt[:, c0 + ci, :], start=(ci == 0), stop=(ci == ncn - 1))
                        return r
                    sc.add("pe", bmm, reads=[ytr] + wbr_res, writes=[bkr])
                    if br == 0:
                        sc.add("dve", lambda e, bk=bk, gt=gt: e.tensor_tensor(out=macc, in0=bk, in1=gt, op=ALU.mult), reads=[bkr, gr], writes=["macc"])
                    else:
                        tm, tmr = tm_rot.next()
                        sc.add("dve", lambda e, bk=bk, gt=gt, tm=tm: e.tensor_tensor(out=tm, in0=bk, in1=gt, op=ALU.mult), reads=[bkr, gr], writes=[tmr])
                        if br < 3:
                            sc.add("pool", lambda e, tm=tm: e.tensor_tensor(out=macc, in0=macc, in1=tm, op=ALU.add), reads=[tmr, "macc"], writes=["macc"])
                        else:
                            sc.add("pool", lambda e, tm=tm, oc=oc: e.tensor_tensor(out=merged[:, oc, :], in0=macc, in1=tm, op=ALU.add), reads=[tmr, "macc"], writes=[("merged", oc)])
            bn, bnr = bank(6)
            for oc2 in range(8):
                bo, bor = bank(4 + oc2 % 2)

                def omm(e, bo=bo, oc2=oc2):
                    r = None
                    for oc in range(8):
                        r = e.matmul(bo, lhsT=wout[:, oc, oc2 * 128:(oc2 + 1) * 128], rhs=merged[:, oc, :], start=(oc == 0), stop=(oc == 7))
                    return r
                sc.add("pe", omm, reads=[("merged", oc) for oc in range(8)] + wout_res, writes=[bor])
                sqf, sqfr = sqf_rot.next()

                def oev(e, bo=bo, oc2=oc2, sqf=sqf):
                    e.activation(out=out2[:, oc2, :], in_=bo, func=AF.Copy)
                    return e.activation(out=sqf, in_=bo, func=AF.Square)
                sc.add("act", oev, reads=[bor], writes=[("out2", oc2), sqfr])
                sc.add("pe", lambda e, sqf=sqf, oc2=oc2, bn=bn: e.matmul(bn, lhsT=ones_f, rhs=sqf, start=(oc2 == 0), stop=(oc2 == 7)), reads=[sqfr, "ones_f"], writes=[bnr])
            sc.add("act", lambda e, bn=bn: e.activation(out=rr2, in_=bn, func=AF.Sqrt, scale=1.0 / DM, bias=EPS), reads=[bnr], writes=["rr2"])

            def resid(e):
                e.reciprocal(out=rr2, in_=rr2)
                r = None
                for oc2 in range(8):
                    e.scalar_tensor_tensor(out=out2[:, oc2, :], in0=out2[:, oc2, :], scalar=col(SM_GPOST + oc2), in1=rr2, op0=ALU.mult, op1=ALU.mult)
                    r = e.tensor_tensor(out=xt2[:, oc2, :], in0=xt2[:, oc2, :], in1=out2[:, oc2, :], op=ALU.add)
                return r
            sc.add("dve", resid, reads=["rr2", "xt2", "small"] + [("out2", i) for i in range(8)], writes=["xt2", "rr2"] + [("out2", i) for i in range(8)])
            sc.add("sp", lambda e, tsl=tsl: e.dma_start(out=x_dst_v[:, :, tsl], in_=xt2), reads=["xt2"], dma=True)
        sc.barrier()

    for l in range(L):
        layer(l)
    sc.analyze()
    sc.emit(nc, es)
    es.close()
    return nc


def _bf(x):
    return np.asarray(x, dtype=np.float32).astype(ml_dtypes.bfloat16)


def build_consts():
    c = {}
    cs = c_slopes()
    caugq = np.zeros((C_HEADS, 2, 4, TC), np.float32)
    caugk = np.zeros((C_HEADS, 4, S), np.float32)
    ii = np.arange(TC, dtype=np.float64)
    jj = (np.arange(S) % 128).astype(np.float64)
    for h, m in enumerate(cs):
        qb = (-m * ii).astype(np.float32)
        qb_hi = _bf(qb).astype(np.float32)
        qb_lo = _bf(qb - qb_hi).astype(np.float32)
        caugq[h, 0] = np.stack([qb_hi, qb_lo, np.ones(TC), np.ones(TC)])
        caugq[h, 1] = -caugq[h, 0]
        kb = (m * jj).astype(np.float32)
        kb_hi = _bf(kb).astype(np.float32)
        kb_lo = _bf(kb - kb_hi).astype(np.float32)
        caugk[h] = np.stack([np.ones(S), np.ones(S), kb_hi, kb_lo])
    c["caugq"] = _bf(caugq)
    c["caugk"] = _bf(caugk)
    p = np.arange(128)[:, None].astype(np.float64)
    f = np.arange(128)[None, :].astype(np.float64)
    md = np.stack([np.exp(-m * np.abs(p - f)) for m in cs], axis=1)
    c["mdiag"] = md.reshape(128, C_HEADS * 128).astype(np.float32)
    sl = a_slopes()
    ma = np.zeros((128, 18, 384), np.float64)
    k = np.arange(128)[:, None]
    for g, d in enumerate(A_DIL):
        for s_ in range(A_SLOTS):
            for j in range(3):
                q = np.arange(128)[None, :]
                rel = np.abs((j - 1) * 128 + k - q)
                ma[:, g * 6 + s_, j * 128:(j + 1) * 128] = np.where(rel <= 64, np.exp(-sl[s_] * d * rel), 0.0)
    c["ma"] = ma.reshape(128, 18 * 384).astype(np.float32)
    inv = (10000.0 ** (-np.arange(0, 32, 2, dtype=np.float32) / 32)).astype(np.float32)
    ang = np.arange(S, dtype=np.float32)[:, None] * inv[None, :]
    cos, sin = np.cos(ang).astype(np.float32).T, np.sin(ang).astype(np.float32).T
    c["ropec"] = np.ascontiguousarray(np.concatenate([cos, cos], 0))
    c["ropes"] = np.ascontiguousarray(np.concatenate([-sin, sin], 0))
    return c


def pack_layers(inp, layers):
    f32 = np.float32
    L = len(layers)
    small = np.zeros((L, 128, NSMALL), f32)
    lamv = np.zeros((L, 1, 256), f32)
    lruw = np.zeros((L, 128, 12, 128), f32)
    wuqa = np.zeros((L, 128, 2, 576), f32)
    wuqb = np.zeros((L, 128, 2, 576), f32)
    wukvk = np.zeros((L, 128, 384), f32)
    wukvv = np.zeros((L, 128, 384), f32)
    wbr = np.zeros((L, 128, 13, 1024), f32)
    wout = np.zeros((L, 128, 8, 1024), f32)
    for li, l in enumerate(layers):
        sm = small[li]
        sm[:, SM_GPRE:SM_GPRE + 8] = inp["norm_pre"][l].reshape(8, 128).T
        sm[:, SM_GPOST:SM_GPOST + 8] = inp["norm_post"][l].reshape(8, 128).T
        sm[:, SM_BGATE:SM_BGATE + 32] = inp["b_gate"][l].reshape(32, 128).T
        cw = inp["conv_w"][l].reshape(4, 3, 128)
        for j in range(4):
            for c in range(3):
                sm[:, SM_CONVW + j * 3 + c] = cw[j, c]
        sm[:, SM_CONVB:SM_CONVB + 3] = inp["conv_b"][l].reshape(3, 128).T
        sm[:, SM_BR:SM_BR + 6] = inp["lru_br"][l].reshape(6, 128).T
        sm[:, SM_BI:SM_BI + 6] = inp["lru_bi"][l].reshape(6, 128).T
        sm[:, SM_LAM:SM_LAM + 6] = inp["lru_lambda"][l].reshape(6, 128).T
        sm[:, SM_SUBLN] = inp["diff_subln"][l]
        sm[:, SM_QN:SM_QN + 2] = inp["mla_q_norm"][l].reshape(2, 128).T
        sm[:, SM_KVN] = inp["mla_kv_norm"][l]
        lam_init = 0.8 - 0.6 * math.exp(-0.3 * l)
        sm[:, SM_LAMINIT] = lam_init
        sm[:, SM_OML] = (1.0 - lam_init)
        lamv[li, 0, 0:64] = inp["diff_lam_q1"][l]
        lamv[li, 0, 64:128] = inp["diff_lam_k1"][l]
        lamv[li, 0, 128:192] = inp["diff_lam_q2"][l]
        lamv[li, 0, 192:256] = inp["diff_lam_k2"][l]
        for gi, w in enumerate((inp["lru_wr"][l], inp["lru_wi"][l])):
            for dr in range(2):
                for c in range(3):
                    for b in range(2):
                        lruw[li, b * 64:(b + 1) * 64, (gi * 2 + dr) * 3 + c, b * 64:(b + 1) * 64] = w[dr, 2 * c + b]
        uq = inp["mla_w_uq"][l].reshape(2, 128, 6, 96)
        wuqa[li] = uq.transpose(1, 0, 2, 3).reshape(128, 2, 576)
        uqb = uq.copy()
        uqb[..., 64:80] = uq[..., 80:96]
        uqb[..., 80:96] = uq[..., 64:80]
        wuqb[li] = uqb.transpose(1, 0, 2, 3).reshape(128, 2, 576)
        ukv = inp["mla_w_ukv"][l].reshape(128, 6, 128)
        wukvk[li] = ukv[:, :, 0:64].reshape(128, 384)
        wukvv[li] = ukv[:, :, 64:128].reshape(128, 384)
        wcat = np.concatenate([inp["w_br_a"][l], inp["w_br_b"][l], inp["w_br_c"][l], inp["w_br_d"][l]], 0)
        wbr[li] = wcat.reshape(13, 128, 1024).transpose(1, 0, 2)
        wout[li] = inp["w_out"][l].reshape(8, 128, 1024).transpose(1, 0, 2)
    return dict(small=small, lamv=lamv, lruw=lruw.reshape(L, 128, 1536), wuqa=wuqa.reshape(L, 128, 1152),
                wuqb=wuqb.reshape(L, 128, 1152), wukvk=wukvk, wukvv=wukvv, wbr=wbr.reshape(L, 128, 13 * 1024),
                wout=wout.reshape(L, 128, 8 * 1024))


_PROG = {}


def get_prog(nl, debug=False):
    key = (nl, tuple(debug) if debug else None)
    if key not in _PROG:
        _PROG[key] = build_program(nl, debug)
    return _PROG[key]


FUSED = True


def kernel(**inputs):
    inp = {k: np.asarray(v) for k, v in inputs.items()}
    x = inp["x"].astype(np.float32)
    consts = build_consts()
    ncores = 8
    xT = [np.ascontiguousarray(x[c % 4].T) for c in range(ncores)]
    if FUSED:
        groups = [list(range(DEPTH))]
    else:
        groups = [[l] for l in range(DEPTH)]
    for layers in groups:
        nc = get_prog(len(layers))
        pk = pack_layers(inp, layers)
        w_in = np.ascontiguousarray(inp["w_in"][layers[0]:layers[-1] + 1]).astype(np.float32)
        in_maps = []
        for c in range(ncores):
            m = dict(xT=xT[c], w_in=w_in)
            m.update(pk)
            m.update(consts)
            in_maps.append(m)
        res = run_bass_kernel_spmd(nc, in_maps, core_ids=list(range(ncores)))
        xT = [np.asarray(res.results[c]["outT"]) for c in range(ncores)]
    out = np.stack([xT[b].T for b in range(4)], 0).astype(np.float32)
    return np.ascontiguousarray(out)
```

```python
import math
from contextlib import ExitStack

import numpy as np
import ml_dtypes

import concourse.bass as bass
import concourse.mybir as mybir
from concourse.bass_utils import run_bass_kernel_spmd

F32 = mybir.dt.float32
BF16 = mybir.dt.bfloat16
AF = mybir.ActivationFunctionType
ALU = mybir.AluOpType
AX = mybir.AxisListType

S = 4096
DM = 1024
DEPTH = 4
NT = 8
TC = 512
KC = 8
EPS = 1e-6
A_DIL = (1, 4, 16)
SPLIT = True
A_SLOTS_ALL, C_HEADS_ALL, D_HEADS_ALL = 6, 4, 6
A_SLOTS = 3 if SPLIT else 6
C_HEADS = 2 if SPLIT else 4
D_HEADS = 3 if SPLIT else 6
NBC = 2 if SPLIT else 3
AW, BW, CW, DW = A_SLOTS * 64, NBC * 128, C_HEADS * 128, D_HEADS * 64
_fam = [("a_q", 3 * AW), ("a_k", 3 * AW), ("a_v", 3 * AW), ("a_g", AW), ("b_x", BW), ("b_g", BW), ("c_q", CW), ("c_k", CW),
        ("c_v", CW), ("c_g", CW), ("d_cq", 256), ("d_ckv", 128), ("d_kr", 32), ("d_g", DW), ("gate", 4096)]
OFF = {}
_o = 0
for _n, _w in _fam:
    OFF[_n] = _o
    _o += _w
IN_W = _o
OFF_ALL = dict(a_q=0, a_k=1152, a_v=2304, a_g=3456, b_x=3840, b_g=4224, c_q=4608, c_k=5120,
               c_v=5632, c_g=6144, d_cq=6656, d_ckv=6912, d_kr=7040, d_g=7072, gate=7456)
SKIP_T = 1e30 if SPLIT else 100.0
if SPLIT:
    YA0, YB0, YC0, YD0, NYC = 0, 256, 512, 768, 8
    BR_CHUNKS = [[(0, 128), (1, 64)], [(2, 128), (3, 64)], [(4, 128), (5, 128)], [(6, 128), (7, 64)]]
else:
    YA0, YB0, YC0, YD0, NYC = 0, 384, 768, 1280, 13
    BR_CHUNKS = [[(0, 128), (1, 128), (2, 128)], [(3, 128), (4, 128), (5, 128)], [(6, 128), (7, 128), (8, 128), (9, 128)],
                 [(10, 128), (11, 128), (12, 128)]]
RG = [[0, 1], [2, 3], [4, 5], [6, 7]]
CB_W = 36

SM_GPRE, SM_GPOST, SM_BGATE, SM_CONVW = 0, 8, 16, 48
SM_CONVB = SM_CONVW + 4 * NBC
SM_BR = SM_CONVB + NBC
SM_BI = SM_BR + 2 * NBC
SM_LAM = SM_BI + 2 * NBC
SM_SUBLN = SM_LAM + 2 * NBC
SM_QN, SM_KVN, SM_LAMINIT, SM_OML = SM_SUBLN + 1, SM_SUBLN + 3, SM_SUBLN + 4, SM_SUBLN + 5
NSMALL = SM_SUBLN + 8


def a_slopes():
    return [2.0 ** (-8.0 * (i + 1) / A_SLOTS_ALL) for i in range(A_SLOTS_ALL)]


def c_slopes():
    return [2.0 ** (-8.0 * (i + 1) / C_HEADS_ALL) for i in range(C_HEADS_ALL)]


class Sched:
    ENGS = ("pe", "act", "dve", "pool", "sp")
    NDMA = {"sp": 24, "act": 12, "pool": 4, "cc": 4}
    UNIT = {"sp": 16, "act": 16, "pool": 16, "cc": 1}

    def __init__(self):
        self.ops = []

    def add(self, eng, fn, reads=(), writes=(), dma=False, cc=False):
        self.ops.append(dict(eng=eng, fn=fn, reads=tuple(reads), writes=tuple(writes), dma=(dma or cc),
                             q=("cc" if cc else eng), needs_inc=False, deps=[]))

    def barrier(self):
        self.ops.append(dict(barrier=True))

    def analyze(self):
        ops = self.ops
        last_w, readers = {}, {}
        last_on = {}
        for i, op in enumerate(ops):
            if op.get("barrier"):
                for e, j in last_on.items():
                    ops[j]["needs_inc"] = True
                last_w, readers = {}, {}
                continue
            deps = set()
            for r in op["reads"]:
                if r in last_w:
                    deps.add((last_w[r], "raw"))
            for w in op["writes"]:
                if w in last_w:
                    deps.add((last_w[w], "waw"))
                for rd in readers.get(w, ()):
                    deps.add((rd, "war"))
            keep = set()
            for d, kind in deps:
                if d == i:
                    continue
                p = ops[d]
                if p["dma"]:
                    keep.add(d)
                elif p["eng"] == op["eng"]:
                    if op["dma"]:
                        keep.add(d)
                    elif kind == "raw" and op["eng"] in ("act", "dve", "pool"):
                        keep.add(d)
                else:
                    keep.add(d)
            op["deps"] = sorted(keep)
            for d in keep:
                ops[d]["needs_inc"] = True
            for w in op["writes"]:
                last_w[w] = i
                readers[w] = []
            for r in op["reads"]:
                if r not in op["writes"]:
                    readers.setdefault(r, []).append(i)
            if not op["dma"]:
                last_on[op["eng"]] = i
        tick = {e: 0 for e in self.ENGS}
        dma_cnt = {q: [0] * n for q, n in self.NDMA.items()}
        dma_rr = {q: 0 for q in self.NDMA}
        seen = {e: {} for e in self.ENGS}
        pending = {e: [] for e in self.ENGS}
        for op in ops:
            if op.get("barrier"):
                snap = [(("c", e), tick[e]) for e in self.ENGS if tick[e] > 0]
                for q, cnts in dma_cnt.items():
                    for k, c in enumerate(cnts):
                        if c > 0:
                            snap.append((("d", q, k), self.UNIT[q] * c))
                for e in self.ENGS:
                    pending[e] = [(s_, v) for (s_, v) in snap if s_ != ("c", e)]
                continue
            e = op["eng"]
            waits = list(pending[e])
            pending[e] = []
            for d in op["deps"]:
                p = ops[d]
                waits.append((p["sem"], p["tick"]))
            if op["dma"]:
                q = op["q"]
                k = dma_rr[q]
                dma_rr[q] = (k + 1) % self.NDMA[q]
                if dma_cnt[q][k] > 0:
                    waits.append((("d", q, k), self.UNIT[q] * dma_cnt[q][k]))
                dma_cnt[q][k] += 1
                op["sem"] = ("d", q, k)
                op["tick"] = self.UNIT[q] * dma_cnt[q][k]
            elif op["needs_inc"]:
                tick[e] += 1
                op["sem"] = ("c", e)
                op["tick"] = tick[e]
            fw = []
            for s_, v in waits:
                if seen[e].get(s_, 0) >= v:
                    continue
                seen[e][s_] = v
                fw.append((s_, v))
            mx = {}
            for s_, v in fw:
                mx[s_] = max(mx.get(s_, 0), v)
            op["waits"] = sorted(mx.items(), key=lambda kv: str(kv[0]))
        self.final_dma = {(q, k): self.UNIT[q] * c for q, cnts in dma_cnt.items() for k, c in enumerate(cnts) if c > 0}

    def emit(self, nc, es):
        sems = {}
        for e in self.ENGS:
            sems[("c", e)] = es.enter_context(nc.semaphore("c_" + e))
        for q, n in self.NDMA.items():
            for k in range(n):
                sems[("d", q, k)] = es.enter_context(nc.semaphore("d_%s_%d" % (q, k)))
        blk = es.enter_context(nc.Block())
        ops = self.ops

        def run(engname):
            def body(e):
                for op in ops:
                    if op.get("barrier") or op["eng"] != engname:
                        continue
                    for s_, v in op["waits"]:
                        e.wait_ge(sems[s_], v)
                    ins = op["fn"](e)
                    if op["dma"]:
                        ins.then_inc(sems[op["sem"]], self.UNIT[op["q"]])
                    elif op["needs_inc"]:
                        ins.then_inc(sems[op["sem"]], 1)
                if engname == "sp":
                    for (q, k), v in sorted(self.final_dma.items()):
                        e.wait_ge(sems[("d", q, k)], v)
            return body

        blk.tensor(run("pe"))
        blk.scalar(run("act"))
        blk.vector(run("dve"))
        blk.gpsimd(run("pool"))
        blk.sync(run("sp"))


class Arena:
    def __init__(self, ap, nbytes, name):
        self.ap = ap
        self.nbytes = nbytes
        self.off = 0
        self.name = name
        self.cnt = 0

    def reset(self):
        self.off = 0

    def alloc(self, free_shape, dt, parts=128):
        n = 1
        for d in free_shape:
            n *= d
        nb = n * (4 if dt == F32 else 2)
        nb = (nb + 63) // 64 * 64
        assert self.off + nb <= self.nbytes, "arena %s overflow: %d + %d > %d" % (self.name, self.off, nb, self.nbytes)
        a = self.ap[:, self.off // 4:(self.off + nb) // 4]
        self.off += nb
        if dt == BF16:
            a = a.bitcast(BF16)
        a = a[:, 0:n]
        if len(free_shape) == 2:
            a = a.rearrange("p (a b) -> p a b", a=free_shape[0])
        elif len(free_shape) == 3:
            a = a.rearrange("p (a b c) -> p a b c", a=free_shape[0], b=free_shape[1])
        self.cnt += 1
        return a


class Rot:
    def __init__(self, arena, n, free_shape, dt, name):
        self.tiles = [arena.alloc(free_shape, dt) for _ in range(n)]
        self.name = name
        self.i = 0

    def next(self):
        k = self.i % len(self.tiles)
        self.i += 1
        return self.tiles[k], (self.name, k)


def run_pipeline(items, stages, lags):
    n = len(items)
    mx = max(lags)
    for step in range(n + mx):
        for f, lg in zip(stages, lags):
            i = step - lg
            if 0 <= i < n:
                f(items[i])


def build_program(nlayers, debug=False):
    nc = bass.Bass("TRN2", target_bir_lowering=False)
    L = nlayers

    def din(name, shape, dt=F32):
        return nc.dram_tensor(name, list(shape), dt, kind="ExternalInput").ap()

    def dscr(name, shape, dt=BF16):
        ext = bool(debug) and name in debug
        return nc.dram_tensor(name, list(shape), dt, kind="ExternalOutput" if ext else "Internal").ap()

    xT_in = din("xT", [DM, S])
    w_in = din("w_in", [L, DM, IN_W])
    small_d = din("small", [L, 128, NSMALL])
    lamv_d = din("lamv", [L, 1, 256])
    lruw_d = din("lruw", [L, 128, 4 * NBC * 128])
    wuqa_d = din("wuqa", [L, 128, 2 * D_HEADS * 96])
    wuqb_d = din("wuqb", [L, 128, 2 * D_HEADS * 96])
    wukvk_d = din("wukvk", [L, 128, DW])
    wukvv_d = din("wukvv", [L, 128, DW])
    wbr_d = din("wbr", [L, 128, NYC * 1024])
    cbias_d = din("cbias", [128, C_HEADS * CB_W])
    wout_d = din("wout", [L, 128, 8 * 1024])
    caugq_d = din("caugq", [C_HEADS, 2, 4, TC], BF16)
    caugk_d = din("caugk", [C_HEADS, 4, S], BF16)
    mdiag_d = din("mdiag", [128, C_HEADS * 128])
    ma_d = din("ma", [128, 3 * A_SLOTS * 384])
    ropec_d = din("ropec", [32, S])
    ropes_d = din("ropes", [32, S])
    outT = nc.dram_tensor("outT", [DM, S], F32, kind="ExternalOutput").ap()

    AQ = [dscr("AQ%d" % g, [AW, S]) for g in range(3)]
    AK = [dscr("AK%d" % g, [AW, S]) for g in range(3)]
    AV = [dscr("AV%d" % g, [128, A_SLOTS * 32 * 64]) for g in range(3)]
    AG = dscr("AG", [AW, S])
    BX = dscr("BX", [BW, S], F32)
    BG = dscr("BG", [BW, S])
    CQ = dscr("CQ", [CW, S])
    CK = dscr("CK", [CW, S])
    CV = dscr("CV", [128, C_HEADS * 32 * 128])
    CG = dscr("CG", [CW, S])
    DLAT = dscr("DLAT", [384, S], F32)
    DKRAW = dscr("DKRAW", [64, S], F32)
    DG = dscr("DG", [DW, S])
    GT = dscr("GT", [4096, S])
    DQ = dscr("DQ", [D_HEADS * 96, S])
    DK = dscr("DK", [D_HEADS * 64, S])
    DKR = dscr("DKR", [32, S])
    DV = dscr("DV", [128, D_HEADS * 32 * 64])
    Y = dscr("Y", [NYC * 128, S])
    ARI = [nc.dram_tensor("ARI%d" % i, [DM, 2 * TC], BF16, kind="Internal").ap() for i in range(2)]
    ARO = [nc.dram_tensor("ARO%d" % i, [DM, 2 * TC], BF16, kind="Internal").ap() for i in range(2)]
    XS = [nc.dram_tensor("XS%d" % i, [DM, S], F32, kind="Internal").ap() for i in range(2)]

    sc = Sched()
    es = ExitStack()
    PERS_BYTES = 56 * 1024
    ARENA_BYTES = 136 * 1024
    pers_t = es.enter_context(nc.sbuf_tensor("pers", [128, PERS_BYTES // 4], F32))
    arena_t = es.enter_context(nc.sbuf_tensor("arena", [128, ARENA_BYTES // 4], F32))
    pers = Arena(pers_t[:], PERS_BYTES, "pers")
    ar = Arena(arena_t[:], ARENA_BYTES, "arena")
    banks = [es.enter_context(nc.psum_tensor("bank%d" % i, [128, 512], F32)) for i in range(8)]

    def bank(i):
        return banks[i][:], ("bank", i)

    small = pers.alloc([NSMALL], F32)
    lamv = pers.alloc([256], F32)
    lamt = pers.alloc([16], F32)
    spc = pers.alloc([8], F32)
    ones_bf = pers.alloc([128], BF16)
    ones_f = pers.alloc([128], F32)
    mdiag = pers.alloc([C_HEADS, 128], F32)
    cbias = pers.alloc([C_HEADS * CB_W], F32)
    wst = [pers.alloc([4096], F32) for _ in range(2)]
    wbf = [pers.alloc([4096], BF16) for _ in range(2)]
    wst_i = [0]

    def col(c, n=1):
        return small[:, c:c + n]

    sc.add("dve", lambda e: e.memset(ones_bf, 1.0), writes=["ones_bf"])
    sc.add("dve", lambda e: e.memset(ones_f, 1.0), writes=["ones_f"])
    sc.add("sp", lambda e: e.dma_start(out=mdiag.rearrange("p a b -> p (a b)"), in_=mdiag_d), writes=["mdiag"], dma=True)
    sc.add("sp", lambda e: e.dma_start(out=cbias, in_=cbias_d), writes=["cbias"], dma=True)

    def load_cast(src_ap, ncols_f32, dst_bf, dst_res):
        k = wst_i[0] % 2
        wst_i[0] += 1
        st = wst[k]
        sc.add("sp", lambda e: e.dma_start(out=st[:, 0:ncols_f32], in_=src_ap), writes=[("wst", k)], dma=True)
        sc.add("pool", lambda e: e.tensor_copy(out=dst_bf, in_=st[:, 0:ncols_f32]), reads=[("wst", k)], writes=[dst_res])

    def layer(l):
        x_src = xT_in if l == 0 else XS[(l - 1) % 2]
        x_dst = outT if l == L - 1 else XS[l % 2]
        w_l = w_in[l]
        w_l_v = w_l.rearrange("(kc p) n -> p kc n", p=128)

        sc.add("sp", lambda e, l=l: e.dma_start(out=small, in_=small_d[l]), writes=["small"], dma=True)
        sc.add("sp", lambda e, l=l: e.dma_start(out=lamv, in_=lamv_d[l].partition_broadcast(128)), writes=["lamv"], dma=True)
        ar.reset()
        tmpl = ar.alloc([128], F32)

        sc.add("dve", lambda e: e.tensor_tensor(out=tmpl[:, 0:64], in0=lamv[:, 0:64], in1=lamv[:, 64:128], op=ALU.mult), reads=["lamv"], writes=["tmpl0"])
        sc.add("dve", lambda e: e.tensor_tensor(out=tmpl[:, 64:128], in0=lamv[:, 128:192], in1=lamv[:, 192:256], op=ALU.mult), reads=["lamv"], writes=["tmpl1"])
        sc.add("dve", lambda e: e.reduce_sum(out=lamt[:, 0:1], in_=tmpl[:, 0:64], axis=AX.X), reads=["tmpl0"], writes=["lamt0"])
        sc.add("dve", lambda e: e.reduce_sum(out=lamt[:, 1:2], in_=tmpl[:, 64:128], axis=AX.X), reads=["tmpl1"], writes=["lamt1"])
        sc.add("act", lambda e: e.activation(out=lamt[:, 2:4], in_=lamt[:, 0:2], func=AF.Exp), reads=["lamt0", "lamt1"], writes=["lamt23"])
        sc.add("dve", lambda e: e.tensor_tensor(out=lamt[:, 6:7], in0=lamt[:, 3:4], in1=lamt[:, 2:3], op=ALU.subtract), reads=["lamt23"], writes=["lamt6"])
        sc.add("dve", lambda e: e.tensor_tensor(out=lamt[:, 4:5], in0=lamt[:, 6:7], in1=col(SM_LAMINIT), op=ALU.subtract), reads=["lamt6", "small"], writes=["lamt4"])
        sc.add("dve", lambda e: e.tensor_tensor(out=lamt[:, 5:6], in0=col(SM_SUBLN), in1=col(SM_OML), op=ALU.mult), reads=["small"], writes=["lamt5"])
        sc.add("act", lambda e: e.activation(out=spc[:, 0:6], in_=col(SM_LAM, 6), func=AF.Exp, scale=-1.0), reads=["small"], writes=["spc_a"])
        sc.add("act", lambda e: e.activation(out=lamt[:, 8:14], in_=spc[:, 0:6], func=AF.Ln, bias=1.0), reads=["spc_a"], writes=["spc_b"])
        sc.add("dve", lambda e: e.tensor_scalar(out=spc[:, 0:6], in0=lamt[:, 8:14], scalar1=-8.0, scalar2=None, op0=ALU.mult),
               reads=["spc_b"], writes=["spc"])
        sc.barrier()

        ar.reset()
        hT = ar.alloc([KC, S], BF16)
        p0mark = ar.off
        xt_rot = Rot(ar, 2, [KC, TC], F32, "xt")
        sq_t = ar.alloc([KC, TC], F32)
        rstd_rot = Rot(ar, 2, [TC], F32, "rstd")
        x_src_v = x_src.rearrange("(kc p) s -> p kc s", p=128)
        for t in range(NT):
            xt, xr = xt_rot.next()
            sc.add("sp", lambda e, xt=xt, t=t: e.dma_start(out=xt, in_=x_src_v[:, :, t * TC:(t + 1) * TC]), writes=[xr], dma=True)
            sc.add("act", lambda e, xt=xt: e.activation(out=sq_t, in_=xt, func=AF.Square), reads=[xr], writes=["sq_t"])
            bk, br = bank(t % 2)

            def ssmm(e, bk=bk):
                for kc in range(KC):
                    r = e.matmul(bk, lhsT=ones_f, rhs=sq_t[:, kc, :], start=(kc == 0), stop=(kc == KC - 1))
                return r
            sc.add("pe", ssmm, reads=["sq_t", "ones_f"], writes=[br])
            rs, rr = rstd_rot.next()
            sc.add("act", lambda e, bk=bk, rs=rs: e.activation(out=rs, in_=bk, func=AF.Sqrt, scale=1.0 / DM, bias=EPS), reads=[br], writes=[rr])
            sc.add("dve", lambda e, rs=rs: e.reciprocal(out=rs, in_=rs), reads=[rr], writes=[rr])

            def hmk(e, xt=xt, rs=rs, t=t):
                for kc in range(KC):
                    r = e.scalar_tensor_tensor(out=hT[:, kc, t * TC:(t + 1) * TC], in0=xt[:, kc, :], scalar=col(SM_GPRE + kc),
                                               in1=rs, op0=ALU.mult, op1=ALU.mult)
                return r
            sc.add("dve", hmk, reads=[xr, rr, "small"], writes=[("hT", t)])
        sc.barrier()

        ar.off = p0mark
        ob_rot_a = Rot(ar, 3, [TC], BF16, "ob_a")
        ob_rot_d = Rot(ar, 3, [TC], BF16, "ob_d")
        of_rot = Rot(ar, 2, [TC], F32, "of")
        vstage = ar.alloc([4 * 32 * 128], BF16)
        hT_res = [("hT", t) for t in range(NT)]
        bank_i = [0]

        def nbank():
            b = bank_i[0] % 8
            bank_i[0] += 1
            return bank(b)

        fm_jobs = []
        for g in range(3):
            fm_jobs.append((OFF["a_q"] + g * AW, AW, A_DIL[g], "scale", AQ[g], 0.125))
        for g in range(3):
            fm_jobs.append((OFF["a_k"] + g * AW, AW, A_DIL[g], "copy", AK[g], None))
        fm_jobs.append((OFF["a_g"], AW, 1, "silu", AG, None))
        fm_jobs.append((OFF["b_x"], BW, 1, "copyf", BX, None))
        fm_jobs.append((OFF["b_g"], BW, 1, "silu", BG, None))
        fm_jobs.append((OFF["c_q"], CW, 1, "scale", CQ, 0.125))
        fm_jobs.append((OFF["c_k"], CW, 1, "copy", CK, None))
        fm_jobs.append((OFF["c_g"], CW, 1, "silu", CG, None))
        fm_jobs.append((OFF["d_cq"], 384, 1, "copyf", DLAT, None))
        fm_jobs.append((OFF["d_kr"], 64, 1, "copyf_kr", DKRAW, None))
        fm_jobs.append((OFF["d_g"], DW, 1, "silu", DG, None))
        for i in range(8):
            fm_jobs.append((OFF["gate"] + i * 512, 512, 1, "gate", GT[i * 512:(i + 1) * 512, :], i))
        tm_jobs = []
        for g in range(3):
            tm_jobs.append((OFF["a_v"] + g * AW, AW, A_DIL[g], AV[g], A_SLOTS, 64))
        tm_jobs.append((OFF["c_v"], CW, 1, CV, C_HEADS, 128))
        jobs = [("fm", j) for j in fm_jobs] + [("tm", j) for j in tm_jobs]

        def load_job(ji):
            kind, j = jobs[ji]
            c0, n = j[0], j[1]
            k = ji % 2
            st = wst[k].rearrange("p (kc n) -> p kc n", kc=KC)
            wb = wbf[k].rearrange("p (kc n) -> p kc n", kc=KC)
            if kind == "fm" and j[3] == "copyf_kr":
                sc.add("sp", lambda e: e.dma_start(out=st[:, :, 0:32], in_=w_l_v[:, :, c0:c0 + 32]), writes=[("wst", k)], dma=True)
                sc.add("sp", lambda e: e.dma_start(out=st[:, :, 32:48], in_=w_l_v[:, :, c0 + 16:c0 + 32]), writes=[("wst", k, 1)], dma=True)
                sc.add("sp", lambda e: e.dma_start(out=st[:, :, 48:64], in_=w_l_v[:, :, c0:c0 + 16]), writes=[("wst", k, 2)], dma=True)
                rd = [("wst", k), ("wst", k, 1), ("wst", k, 2)]
            else:
                sc.add("sp", lambda e: e.dma_start(out=st[:, :, 0:n], in_=w_l_v[:, :, c0:c0 + n]), writes=[("wst", k)], dma=True)
                rd = [("wst", k)]
            sc.add("pool", lambda e: e.tensor_copy(out=wb[:, :, 0:n], in_=st[:, :, 0:n]), reads=rd, writes=[("wbf", k)])

        def rhs_view(d, kc, t):
            if d == 1:
                return [(hT[:, kc, t * TC:(t + 1) * TC], None)]
            hv = hT[:, kc, :].rearrange("p (l r) -> p r l", r=d)
            if d == 4:
                return [(hv[:, t // 2, (t % 2) * 512:(t % 2) * 512 + 512], None)]
            return [(hv[:, 2 * t:2 * t + 2, :], 2)]

        def lhs_view(d, kc, b):
            if d == 1:
                return hT[:, kc, b * 128:(b + 1) * 128]
            hv = hT[:, kc, :].rearrange("p (l r) -> p r l", r=d)
            nbs = 32 // d
            return hv[:, b // nbs, (b % nbs) * 128:(b % nbs) * 128 + 128]

        def compute_fm(ji):
            _, (c0, n, d, kind, dst, extra) = jobs[ji]
            k = ji % 2
            wb = wbf[k].rearrange("p (kc n) -> p kc n", kc=KC)
            ncb = (n + 127) // 128
            for cb in range(ncb):
                m = min(128, n - cb * 128)
                for t in range(NT):
                    bk, bres = nbank()

                    def mm(e, bk=bk, cb=cb, m=m, t=t):
                        r = None
                        for kc in range(KC):
                            (rv, a), = rhs_view(d, kc, t)
                            o = bk[0:m, :]
                            if a is not None:
                                o = o.rearrange("p (a b) -> p a b", a=a)
                            r = e.matmul(o, lhsT=wb[:, kc, cb * 128:cb * 128 + m], rhs=rv, start=(kc == 0), stop=(kc == KC - 1))
                        return r
                    sc.add("pe", mm, reads=[("wbf", k)] + hT_res, writes=[bres])
                    drows = dst[cb * 128:cb * 128 + m, t * TC:(t + 1) * TC]
                    if kind in ("silu", "gate"):
                        ot, ores = ob_rot_a.next()
                        if kind == "silu":
                            sc.add("act", lambda e, ot=ot, bk=bk, m=m: e.activation(out=ot[0:m, :], in_=bk[0:m, :], func=AF.Silu),
                                   reads=[bres], writes=[ores])
                        else:
                            bcol = SM_BGATE + (extra // 2) * 8 + (extra % 2) * 4 + cb
                            sc.add("act", lambda e, ot=ot, bk=bk, bcol=bcol: e.activation(out=ot, in_=bk, func=AF.Sigmoid, bias=col(bcol)),
                                   reads=[bres, "small"], writes=[ores])
                        sc.add("act", lambda e, ot=ot, drows=drows, m=m: e.dma_start(out=drows, in_=ot[0:m, :]), reads=[ores], dma=True)
                    elif kind in ("copy", "scale"):
                        ot, ores = ob_rot_d.next()
                        if kind == "copy":
                            sc.add("dve", lambda e, ot=ot, bk=bk, m=m: e.tensor_copy(out=ot[0:m, :], in_=bk[0:m, :]), reads=[bres], writes=[ores])
                        else:
                            sc.add("dve", lambda e, ot=ot, bk=bk, m=m: e.tensor_scalar(out=ot[0:m, :], in0=bk[0:m, :], scalar1=extra, scalar2=None, op0=ALU.mult),
                                   reads=[bres], writes=[ores])
                        sc.add("sp", lambda e, ot=ot, drows=drows, m=m: e.dma_start(out=drows, in_=ot[0:m, :]), reads=[ores], dma=True)
                    else:
                        ot, ores = of_rot.next()
                        sc.add("dve", lambda e, ot=ot, bk=bk, m=m: e.tensor_copy(out=ot[0:m, :], in_=bk[0:m, :]), reads=[bres], writes=[ores])
                        sc.add("sp", lambda e, ot=ot, drows=drows, m=m: e.dma_start(out=drows, in_=ot[0:m, :]), reads=[ores], dma=True)

        def compute_tm(ji):
            _, (c0, n, d, dst, nh, hd) = jobs[ji]
            k = ji % 2
            wb = wbf[k].rearrange("p (kc n) -> p kc n", kc=KC)
            vs = vstage[:, 0:nh * 32 * hd].rearrange("p (h b d) -> p h b d", h=nh, b=32)
            for b in range(32):
                bk, bres = nbank()

                def mm(e, bk=bk, b=b):
                    r = None
                    for kc in range(KC):
                        r = e.matmul(bk[:, 0:n], lhsT=lhs_view(d, kc, b), rhs=wb[:, kc, 0:n], start=(kc == 0), stop=(kc == KC - 1))
                    return r
                sc.add("pe", mm, reads=[("wbf", k)] + hT_res, writes=[bres])
                sc.add("dve", lambda e, bk=bk, b=b: e.tensor_copy(out=vs[:, :, b, :], in_=bk[:, 0:n].rearrange("p (h d) -> p h d", h=nh)),
                       reads=[bres], writes=[("vstage", b)])
            sc.add("sp", lambda e: e.dma_start(out=dst, in_=vstage[:, 0:nh * 32 * hd]), reads=[("vstage", b) for b in range(32)], dma=True)

        load_job(0)
        for ji in range(len(jobs)):
            if ji + 1 < len(jobs):
                load_job(ji + 1)
            if jobs[ji][0] == "fm":
                compute_fm(ji)
            else:
                compute_tm(ji)
        sc.barrier()

        ar.reset()
        wuqa = ar.alloc([2, D_HEADS * 96], BF16)
        wuqb = ar.alloc([2, D_HEADS * 96], BF16)
        wukvk = ar.alloc([DW], BF16)
        wukvv = ar.alloc([DW], BF16)
        load_cast(wuqa_d[l], 2 * D_HEADS * 96, wuqa.rearrange("p a b -> p (a b)"), "wuqa")
        load_cast(wuqb_d[l], 2 * D_HEADS * 96, wuqb.rearrange("p a b -> p (a b)"), "wuqb")
        load_cast(wukvk_d[l], DW, wukvk, "wukvk")
        load_cast(wukvv_d[l], DW, wukvv, "wukvv")
        lat_rot = Rot(ar, 2, [3, TC], F32, "lat")
        kra_rot = Rot(ar, 2, [TC], F32, "kra")
        krb_rot = Rot(ar, 2, [TC], F32, "krb")
        cc_rot = Rot(ar, 2, [TC], F32, "cc")
        ss_rot = Rot(ar, 2, [TC], F32, "ss")
        sq2 = ar.alloc([3, TC], F32)
        rq_rot = Rot(ar, 2, [TC], F32, "rq")
        rkv_rot = Rot(ar, 2, [TC], F32, "rkv")
        cqn_rot = Rot(ar, 2, [3, TC], BF16, "cqn")
        qd_rot = Rot(ar, 3, [TC], BF16, "qd")
        kd_rot = Rot(ar, 3, [TC], BF16, "kd")
        t1_rot = Rot(ar, 2, [TC], F32, "t1")
        t2_rot = Rot(ar, 2, [TC], F32, "t2")
        krr_rot = Rot(ar, 2, [TC], BF16, "krr")
        dvst = ar.alloc([D_HEADS, 32, 64], BF16)
        DLv = DLAT.rearrange("(c p) s -> p c s", p=128)
        for t in range(NT):
            tsl = slice(t * TC, (t + 1) * TC)
            lat, latr = lat_rot.next()
            kra, krar = kra_rot.next()
            krb, krbr = krb_rot.next()
            cct, ccr = cc_rot.next()
            sst, ssr = ss_rot.next()
            sc.add("sp", lambda e, lat=lat, tsl=tsl: e.dma_start(out=lat, in_=DLv[:, :, tsl]), writes=[latr], dma=True)
            sc.add("sp", lambda e, kra=kra, tsl=tsl: e.dma_start(out=kra[64:96, :], in_=DKRAW[0:32, tsl]), writes=[krar], dma=True)
            sc.add("sp", lambda e, krb=krb, tsl=tsl: e.dma_start(out=krb[64:96, :], in_=DKRAW[32:64, tsl]), writes=[krbr], dma=True)
            sc.add("sp", lambda e, cct=cct, tsl=tsl: e.dma_start(out=cct[64:96, :], in_=ropec_d[:, tsl]), writes=[ccr], dma=True)
            sc.add("sp", lambda e, sst=sst, tsl=tsl: e.dma_start(out=sst[64:96, :], in_=ropes_d[:, tsl]), writes=[ssr], dma=True)
            sc.add("act", lambda e, lat=lat: e.activation(out=sq2, in_=lat, func=AF.Square), reads=[latr], writes=["sq2"])
            b0, b0r = bank(0)
            b1, b1r = bank(1)

            def ssq(e, b0=b0, b1=b1):
                e.matmul(b0, lhsT=ones_f, rhs=sq2[:, 0, :], start=True, stop=False)
                e.matmul(b0, lhsT=ones_f, rhs=sq2[:, 1, :], start=False, stop=True)
                return e.matmul(b1, lhsT=ones_f, rhs=sq2[:, 2, :], start=True, stop=True)
            sc.add("pe", ssq, reads=["sq2", "ones_f"], writes=[b0r, b1r])
            rq, rqr = rq_rot.next()
            rkv, rkvr = rkv_rot.next()

            def rsq(e, rq=rq, rkv=rkv, b0=b0, b1=b1):
                e.activation(out=rq, in_=b0, func=AF.Sqrt, scale=1.0 / 256, bias=EPS)
                return e.activation(out=rkv, in_=b1, func=AF.Sqrt, scale=1.0 / 128, bias=EPS)
            sc.add("act", rsq, reads=[b0r, b1r], writes=[rqr, rkvr])
            cqn, cqnr = cqn_rot.next()

            def nrm(e, rq=rq, rkv=rkv, lat=lat, cqn=cqn):
                e.reciprocal(out=rq, in_=rq)
                e.reciprocal(out=rkv, in_=rkv)
                e.scalar_tensor_tensor(out=cqn[:, 0, :], in0=lat[:, 0, :], scalar=col(SM_QN), in1=rq, op0=ALU.mult, op1=ALU.mult)
                e.scalar_tensor_tensor(out=cqn[:, 1, :], in0=lat[:, 1, :], scalar=col(SM_QN + 1), in1=rq, op0=ALU.mult, op1=ALU.mult)
                return e.scalar_tensor_tensor(out=cqn[:, 2, :], in0=lat[:, 2, :], scalar=col(SM_KVN), in1=rkv, op0=ALU.mult, op1=ALU.mult)
            sc.add("dve", nrm, reads=[rqr, rkvr, latr, "small"], writes=[cqnr, rqr, rkvr])
            t1, t1r = t1_rot.next()
            t2, t2r = t2_rot.next()
            krr, krrr = krr_rot.next()

            def krope(e, kra=kra, krb=krb, cct=cct, sst=sst, t1=t1, t2=t2, krr=krr):
                e.tensor_tensor(out=t1[64:96, :], in0=krb[64:96, :], in1=sst[64:96, :], op=ALU.mult)
                e.tensor_tensor(out=t2[64:96, :], in0=kra[64:96, :], in1=cct[64:96, :], op=ALU.mult)
                return e.tensor_tensor(out=krr[64:96, :], in0=t1[64:96, :], in1=t2[64:96, :], op=ALU.add)
            sc.add("dve", krope, reads=[krar, krbr, ccr, ssr], writes=[t1r, t2r, krrr])
            sc.add("sp", lambda e, krr=krr, tsl=tsl: e.dma_start(out=DKR[:, tsl], in_=krr[64:96, :]), reads=[krrr], dma=True)
            for h in range(D_HEADS):
                ba, bar_ = bank(2 + (h % 2) * 3)
                bb, bbr = bank(3 + (h % 2) * 3)
                bkk, bkr = bank(4 + (h % 2) * 3)

                def upq(e, ba=ba, bb=bb, bkk=bkk, h=h, cqn=cqn):
                    for c in range(2):
                        e.matmul(ba[0:96, :], lhsT=wuqa[:, c, h * 96:(h + 1) * 96], rhs=cqn[:, c, :], start=(c == 0), stop=(c == 1))
                    for c in range(2):
                        e.matmul(bb[0:96, :], lhsT=wuqb[:, c, h * 96:(h + 1) * 96], rhs=cqn[:, c, :], start=(c == 0), stop=(c == 1))
                    return e.matmul(bkk[0:64, :], lhsT=wukvk[:, h * 64:(h + 1) * 64], rhs=cqn[:, 2, :], start=True, stop=True)
                sc.add("pe", upq, reads=[cqnr, "wuqa", "wuqb", "wukvk"], writes=[bar_, bbr, bkr])
                qd, qdr = qd_rot.next()
                kd, kdr = kd_rot.next()
                t1, t1r = t1_rot.next()
                t2, t2r = t2_rot.next()

                def qrope(e, ba=ba, bb=bb, qd=qd, t1=t1, t2=t2, cct=cct, sst=sst):
                    e.tensor_copy(out=qd[0:64, :], in_=ba[0:64, :])
                    e.tensor_tensor(out=t1[64:96, :], in0=bb[64:96, :], in1=sst[64:96, :], op=ALU.mult)
                    e.tensor_tensor(out=t2[64:96, :], in0=ba[64:96, :], in1=cct[64:96, :], op=ALU.mult)
                    return e.tensor_tensor(out=qd[64:96, :], in0=t1[64:96, :], in1=t2[64:96, :], op=ALU.add)
                sc.add("dve", qrope, reads=[bar_, bbr, ccr, ssr], writes=[qdr, t1r, t2r])
                sc.add("sp", lambda e, qd=qd, h=h, tsl=tsl: e.dma_start(out=DQ[h * 96:(h + 1) * 96, tsl], in_=qd[0:96, :]), reads=[qdr], dma=True)
                sc.add("act", lambda e, kd=kd, bkk=bkk: e.activation(out=kd[0:64, :], in_=bkk[0:64, :], func=AF.Copy), reads=[bkr], writes=[kdr])
                sc.add("act", lambda e, kd=kd, h=h, tsl=tsl: e.dma_start(out=DK[h * 64:(h + 1) * 64, tsl], in_=kd[0:64, :]), reads=[kdr], dma=True)
            for tb in range(4):
                bv, bvr = bank(tb % 2)
                b = t * 4 + tb
                sc.add("pe", lambda e, bv=bv, tb=tb, cqn=cqn: e.matmul(bv[:, 0:DW], lhsT=cqn[:, 2, tb * 128:(tb + 1) * 128], rhs=wukvv, start=True, stop=True),
                       reads=[cqnr, "wukvv"], writes=[bvr])
                sc.add("act", lambda e, bv=bv, b=b: e.activation(out=dvst[:, :, b, :], in_=bv[:, 0:DW].rearrange("p (h d) -> p h d", h=D_HEADS), func=AF.Copy),
                       reads=[bvr], writes=[("dvst", b)])
        sc.add("sp", lambda e: e.dma_start(out=DV, in_=dvst.rearrange("p h b d -> p (h b d)")), reads=[("dvst", b) for b in range(32)], dma=True)
        sc.barrier()

        ar.reset()
        ma = ar.alloc([3 * A_SLOTS, 384], F32)
        sc.add("sp", lambda e: e.dma_start(out=ma.rearrange("p a b -> p (a b)"), in_=ma_d), writes=["ma"], dma=True)
        acc = ar.alloc([S], F32)
        kt_rot = Rot(ar, 2, [S], BF16, "kt")
        qt_rot = Rot(ar, 2, [S], BF16, "qt")
        vraw_rot = Rot(ar, 2, [32 * 64], BF16, "vraw")
        vt_tiles = [ar.alloc([32, 128], BF16) for _ in range(2)]
        for k in range(2):
            sc.add("pool", lambda e, k=k: e.memset(vt_tiles[k][:, :, 64:128], 1.0), writes=[("vt1", k)])
        pf_rot = Rot(ar, 3, [384], F32, "pf")
        pb_rot = Rot(ar, 4, [384], BF16, "pb")
        ag_rot = Rot(ar, 2, [TC], BF16, "ag")
        rz_rot = Rot(ar, 2, [TC], F32, "rz")
        yf_rot = Rot(ar, 2, [TC], F32, "yf")
        yb_rot = Rot(ar, 2, [TC], BF16, "yb")
        groups = [(s_, g) for s_ in range(A_SLOTS) for g in range(3)]
        gctx = {}

        def a_load(gi):
            s_, g = groups[gi]
            kt, ktr = kt_rot.next()
            qt, qtr = qt_rot.next()
            vraw, vrr = vraw_rot.next()
            vk = gi % 2
            vt, vtr = vt_tiles[vk], ("vt", vk)
            sc.add("sp", lambda e: e.dma_start(out=kt[0:64, :], in_=AK[g][s_ * 64:(s_ + 1) * 64, :]), writes=[ktr], dma=True)
            sc.add("sp", lambda e: e.dma_start(out=qt[0:64, :], in_=AQ[g][s_ * 64:(s_ + 1) * 64, :]), writes=[qtr], dma=True)
            sc.add("sp", lambda e: e.dma_start(out=vraw, in_=AV[g][:, s_ * 2048:(s_ + 1) * 2048]), writes=[vrr], dma=True)
            sc.add("pool", lambda e: e.tensor_copy(out=vt[:, :, 0:64], in_=vraw.rearrange("p (b d) -> p b d", b=32)),
                   reads=[vrr, ("vt1", vk)], writes=[vtr])
            gctx[gi] = dict(kt=kt, ktr=ktr, qt=qt, qtr=qtr, vt=vt, vtr=vtr)

        items = []
        for gi, (s_, g) in enumerate(groups):
            for qb in range(32):
                items.append(dict(gi=gi, s_=s_, g=g, qb=qb, idx=len(items)))

        def a_S(it):
            gi, qb, g = it["gi"], it["qb"], it["g"]
            if qb == 0 and gi == 0:
                a_load(0)
            if qb == 16 and gi + 1 < len(groups):
                a_load(gi + 1)
            c = gctx[gi]
            d = A_DIL[g]
            nbs = 32 // d
            lb = qb % nbs
            js = [j for j in range(3) if 0 <= lb - 1 + j < nbs]
            psS, psSr = bank(it["idx"] % 3)
            kt, qt = c["kt"], c["qt"]

            def f(e):
                r = None
                for j in js:
                    kb = qb - 1 + j
                    r = e.matmul(psS[:, j * 128:(j + 1) * 128], lhsT=kt[0:64, kb * 128:(kb + 1) * 128], rhs=qt[0:64, qb * 128:(qb + 1) * 128],
                                 start=True, stop=True)
                return r
            sc.add("pe", f, reads=[c["ktr"], c["qtr"]], writes=[psSr])
            it.update(js=js, psS=psS, psSr=psSr, d=d, nbs=nbs, lb=lb)

        def a_E(it):
            js, psS = it["js"], it["psS"]
            c0, c1 = js[0] * 128, (js[-1] + 1) * 128
            pf, pfr = pf_rot.next()
            sc.add("act", lambda e: e.activation(out=pf[:, c0:c1], in_=psS[:, c0:c1], func=AF.Exp), reads=[it["psSr"]], writes=[pfr])
            it.update(pf=pf, pfr=pfr, c0=c0, c1=c1)

        def a_M(it):
            pf, c0, c1 = it["pf"], it["c0"], it["c1"]
            mi = it["g"] * A_SLOTS + it["s_"]
            pb, pbr = pb_rot.next()
            sc.add("dve", lambda e: e.tensor_tensor(out=pb[:, c0:c1], in0=pf[:, c0:c1], in1=ma[:, mi, c0:c1], op=ALU.mult),
                   reads=[it["pfr"], "ma"], writes=[pbr])
            it.update(pb=pb, pbr=pbr)

        def a_P(it):
            c = gctx[it["gi"]]
            vt, pb, js, qb = c["vt"], it["pb"], it["js"], it["qb"]
            psO, psOr = bank(3 + it["idx"] % 3)

            def pv(e):
                r = None
                for j in js:
                    kb = qb - 1 + j
                    r = e.matmul(psO[:, 0:128], lhsT=vt[:, kb, :], rhs=pb[:, j * 128:(j + 1) * 128], start=(j == js[0]), stop=(j == js[-1]))
                return r
            sc.add("pe", pv, reads=[it["pbr"], c["vtr"]], writes=[psOr])
            it.update(psO=psO, psOr=psOr)

        def a_A(it):
            d, nbs, qb, g, s_ = it["d"], it["nbs"], it["qb"], it["g"], it["s_"]
            psO = it["psO"]
            rr_, lb_ = qb // nbs, qb % nbs
            if d == 1:
                av = acc[:, qb * 128:(qb + 1) * 128]
            else:
                av = acc.rearrange("p (l r) -> p r l", r=d)[:, rr_, lb_ * 128:(lb_ + 1) * 128]
            if g == 0:
                sc.add("dve", lambda e: e.tensor_copy(out=av, in_=psO[:, 0:128]), reads=[it["psOr"]], writes=["acc"])
            else:
                sc.add("dve", lambda e: e.tensor_tensor(out=av, in0=psO[:, 0:128], in1=av, op=ALU.add), reads=[it["psOr"], "acc"], writes=["acc"])
            if g == 2 and qb == 31:
                for t in range(NT):
                    a_epi(s_, t)

        def a_epi(s_, t):
            tsl = slice(t * TC, (t + 1) * TC)
            agt, agr = ag_rot.next()
            rz, rzr = rz_rot.next()
            yf, yfr = yf_rot.next()
            yb, ybr = yb_rot.next()
            sc.add("sp", lambda e: e.dma_start(out=agt[0:64, :], in_=AG[s_ * 64:(s_ + 1) * 64, tsl]), writes=[agr], dma=True)

            def epi(e):
                e.reciprocal(out=rz[0:64, :], in_=acc[64:128, tsl])
                e.tensor_tensor(out=yf[0:64, :], in0=acc[0:64, tsl], in1=rz[0:64, :], op=ALU.mult)
                return e.tensor_tensor(out=yb[0:64, :], in0=yf[0:64, :], in1=agt[0:64, :], op=ALU.mult)
            sc.add("dve", epi, reads=["acc", agr], writes=[rzr, yfr, ybr])
            sc.add("sp", lambda e: e.dma_start(out=Y[YA0 + s_ * 64:YA0 + (s_ + 1) * 64, tsl], in_=yb[0:64, :]), reads=[ybr], dma=True)

        run_pipeline(items, [a_S, a_E, a_M, a_P, a_A], [0, 1, 2, 3, 4])
        sc.barrier()

        ar.reset()
        lw = ar.alloc([4 * NBC, 128], BF16)
        load_cast(lruw_d[l], 4 * NBC * 128, lw.rearrange("p a b -> p (a b)"), "lw")
        xp = ar.alloc([S + 4], F32)
        xc = ar.alloc([S], F32)
        Rb = ar.alloc([S], F32)
        Ib = ar.alloc([S], F32)
        Ab_ = ar.alloc([S], F32)
        Hf = ar.alloc([S], F32)
        Hb = ar.alloc([S], F32)
        xcb = ar.alloc([S], BF16)
        bgt = ar.alloc([S], BF16)
        sc.add("dve", lambda e: e.memset(xp[:, 0:1], 0.0), writes=["xp_pad0"])
        sc.add("dve", lambda e: e.memset(xp[:, S + 1:S + 4], 0.0), writes=["xp_pad1"])
        for c in range(NBC):
            sc.add("sp", lambda e, c=c: e.dma_start(out=xp[:, 1:S + 1], in_=BX[c * 128:(c + 1) * 128, :]), writes=["xp"], dma=True)
            sc.add("sp", lambda e, c=c: e.dma_start(out=bgt, in_=BG[c * 128:(c + 1) * 128, :]), writes=["bgt"], dma=True)

            def conv(e, c=c):
                e.tensor_scalar(out=xc, in0=xp[:, 0:S], scalar1=col(SM_CONVW + 0 * NBC + c), scalar2=col(SM_CONVB + c), op0=ALU.mult, op1=ALU.add)
                for j in range(1, 4):
                    r = e.scalar_tensor_tensor(out=xc, in0=xp[:, j:j + S], scalar=col(SM_CONVW + j * NBC + c), in1=xc, op0=ALU.mult, op1=ALU.add)
                return r
            sc.add("dve", conv, reads=["xp", "xp_pad0", "xp_pad1", "small"], writes=["xc"])
            sc.add("pool", lambda e: e.tensor_copy(out=xcb, in_=xc), reads=["xc"], writes=["xcb"])
            for dr in range(2):
                for t in range(NT):
                    tsl = slice(t * TC, (t + 1) * TC)
                    bR, bRr = bank((2 * t) % 8)
                    bI, bIr = bank((2 * t + 1) % 8)

                    def gmm(e, bR=bR, bI=bI, tsl=tsl, c=c, dr=dr):
                        e.matmul(bR, lhsT=lw[:, (0 * 2 + dr) * NBC + c, :], rhs=xcb[:, tsl], start=True, stop=True)
                        return e.matmul(bI, lhsT=lw[:, (1 * 2 + dr) * NBC + c, :], rhs=xcb[:, tsl], start=True, stop=True)
                    sc.add("pe", gmm, reads=["xcb", "lw"], writes=[bRr, bIr])
                    sc.add("dve", lambda e, bR=bR, tsl=tsl, c=c, dr=dr: e.tensor_scalar(out=Rb[:, tsl], in0=bR, scalar1=col(SM_BR + dr * NBC + c), scalar2=None, op0=ALU.add),
                           reads=[bRr, "small"], writes=[("Rb", t)])
                    sc.add("dve", lambda e, bI=bI, tsl=tsl, c=c, dr=dr: e.tensor_scalar(out=Ib[:, tsl], in0=bI, scalar1=col(SM_BI + dr * NBC + c), scalar2=None, op0=ALU.add),
                           reads=[bIr, "small"], writes=[("Ib", t)])
                Rres = [("Rb", t) for t in range(NT)]
                Ires = [("Ib", t) for t in range(NT)]

                def gates(e, c=c, dr=dr):
                    e.activation(out=Rb, in_=Rb, func=AF.Sigmoid)
                    e.activation(out=Ib, in_=Ib, func=AF.Sigmoid)
                    e.activation(out=Ab_, in_=Rb, func=AF.Exp, scale=spc[:, dr * NBC + c:dr * NBC + c + 1])
                    e.activation(out=Rb, in_=Ab_, func=AF.Square)
                    return e.activation(out=Rb, in_=Rb, func=AF.Sqrt, scale=-1.0, bias=1.0)
                sc.add("act", gates, reads=Rres + Ires + ["spc", "Hscan%d" % dr], writes=["Rw", "Ig", "Ab"])
                Hd = Hf if dr == 0 else Hb

                def premul(e):
                    e.tensor_tensor(out=Ib, in0=Ib, in1=xc, op=ALU.mult)
                    return e.tensor_tensor(out=Ib, in0=Ib, in1=Rb, op=ALU.mult)
                sc.add("dve", premul, reads=["Rw", "Ig", "Ab", "xc"], writes=["U", "Hscan%d" % (1 - dr)] + Rres + Ires)
                order = list(range(NT)) if dr == 0 else list(range(NT - 1, -1, -1))
                for oi, t in enumerate(order):
                    tsl = slice(t * TC, (t + 1) * TC)
                    if oi == 0:
                        init = 0.0
                    elif dr == 0:
                        init = Hd[:, t * TC - 1:t * TC]
                    else:
                        init = Hd[:, (t + 1) * TC:(t + 1) * TC + 1]
                    if dr == 0:
                        sc.add("dve", lambda e, Hd=Hd, tsl=tsl, init=init: e.tensor_tensor_scan(out=Hd[:, tsl], data0=Ab_[:, tsl], data1=Ib[:, tsl], initial=init, op0=ALU.mult, op1=ALU.add),
                               reads=["U", "Ab"] + ([("Hc", dr, order[oi - 1])] if oi else []), writes=[("Hc", dr, t)])
                    else:
                        sc.add("dve", lambda e, Hd=Hd, tsl=tsl, init=init: e.tensor_tensor_scan(out=Hd[:, tsl][:, ::-1], data0=Ab_[:, tsl][:, ::-1], data1=Ib[:, tsl][:, ::-1], initial=init, op0=ALU.mult, op1=ALU.add),
                               reads=["U", "Ab"] + ([("Hc", dr, order[oi - 1])] if oi else []), writes=[("Hc", dr, t)])

            def fin(e):
                e.tensor_tensor(out=Hf, in0=Hf, in1=Hb, op=ALU.add)
                return e.tensor_tensor(out=bgt, in0=Hf, in1=bgt, op=ALU.mult)
            sc.add("dve", fin, reads=[("Hc", 0, NT - 1), ("Hc", 1, 0), "bgt"], writes=["bgt"])
            sc.add("sp", lambda e, c=c: e.dma_start(out=Y[YB0 + c * 128:YB0 + (c + 1) * 128, :], in_=bgt), reads=["bgt"], dma=True)
        sc.barrier()

        ar.reset()
        cs = c_slopes() if not SPLIT else [min(c_slopes()[h], c_slopes()[h + 2]) for h in range(2)]
        ka_rot = Rot(ar, 4, [S], BF16, "ka")
        vc_rot = Rot(ar, 2, [32 * 128], BF16, "vc")
        qb_rot = Rot(ar, 4, [TC], BF16, "qbf")
        qa_rot = Rot(ar, 4, [TC], BF16, "qaf")
        cg_rot = Rot(ar, 2, [TC], BF16, "cg")
        pbc_rot = Rot(ar, 5, [TC], BF16, "pbc")
        pfc_rot = Rot(ar, 2, [128], F32, "pfc")
        eo_rot = Rot(ar, 4, [TC], F32, "eo")
        ez_rot = Rot(ar, 4, [TC], F32, "ez")
        e_o1 = ar.alloc([TC], F32)
        e_sq = ar.alloc([TC], F32)
        e_sd = ar.alloc([TC], F32)
        ybc_rot = Rot(ar, 2, [TC], BF16, "ybc")
        hctx, qctx = {}, {}

        def c_load_head(h):
            kas = []
            for c in range(2):
                ka, kar = ka_rot.next()
                kas.append((ka, kar))
                sc.add("sp", lambda e, ka=ka, c=c: e.dma_start(out=ka[0:64, :], in_=CK[(h * 2 + c) * 64:(h * 2 + c + 1) * 64, :]), writes=[kar], dma=True)
                sc.add("sp", lambda e, ka=ka: e.dma_start(out=ka[64:68, :], in_=caugk_d[h]), writes=[(kar, "aug")], dma=True)
            vc, vcr = vc_rot.next()
            sc.add("sp", lambda e: e.dma_start(out=vc, in_=CV[:, h * 4096:(h + 1) * 4096]), writes=[vcr], dma=True)
            hctx[h] = dict(kas=kas, vcv=vc.rearrange("p (b d) -> p b d", b=32), vcr=vcr)

        def c_load_chunk(h, qc):
            tsl = slice(qc * TC, (qc + 1) * TC)
            qs = []
            for c in range(2):
                qbf, qbr = qb_rot.next()
                qaf, qar = qa_rot.next()
                src = CQ[(h * 2 + c) * 64:(h * 2 + c + 1) * 64, tsl]
                sc.add("sp", lambda e, qbf=qbf, src=src: e.dma_start(out=qbf[0:64, :], in_=src), writes=[qbr], dma=True)
                sc.add("sp", lambda e, qbf=qbf: e.dma_start(out=qbf[64:68, :], in_=caugq_d[h, 0]), writes=[(qbr, "aug")], dma=True)
                sc.add("sp", lambda e, qaf=qaf, src=src: e.dma_start(out=qaf[0:64, :], in_=src), writes=[qar], dma=True)
                sc.add("sp", lambda e, qaf=qaf: e.dma_start(out=qaf[64:68, :], in_=caugq_d[h, 1]), writes=[(qar, "aug")], dma=True)
                qs.append((qbf, qbr, qaf, qar))
            cgt, cgr = cg_rot.next()
            sc.add("sp", lambda e: e.dma_start(out=cgt, in_=CG[h * 128:(h + 1) * 128, tsl]), writes=[cgr], dma=True)
            qctx[(h, qc)] = dict(qs=qs, cgt=cgt, cgr=cgr)

        items = []
        for h in range(C_HEADS):
            m = cs[h]
            for qc in range(NT):
                i0 = qc * TC
                for c in range(2):
                    kbs = []
                    for kb in range(32):
                        j0 = kb * 128
                        if j0 + 128 <= i0:
                            if m * (i0 - (j0 + 127)) > SKIP_T:
                                continue
                        elif j0 >= i0 + TC:
                            if m * (j0 - (i0 + TC - 1)) > SKIP_T:
                                continue
                        kbs.append(kb)
                    for ii, kb in enumerate(kbs):
                        items.append(dict(h=h, qc=qc, c=c, kb=kb, ii=ii, first=(ii == 0), last=(ii == len(kbs) - 1), idx=len(items),
                                          chunk_first=(c == 0 and ii == 0)))
        psOb = [bank(3), bank(5)]
        psZb = [bank(4), bank(6)]

        def c_S(it):
            h, qc, c, kb = it["h"], it["qc"], it["c"], it["kb"]
            if it["chunk_first"] and h == 0 and qc == 0:
                c_load_head(0)
                c_load_chunk(0, 0)
            if c == 0 and it["ii"] == 4:
                if qc == 0 and h + 1 < C_HEADS:
                    c_load_head(h + 1)
                nh, nq = (h, qc + 1) if qc + 1 < NT else (h + 1, 0)
                if nh < C_HEADS:
                    c_load_chunk(nh, nq)
            m = cs[h]
            i0 = qc * TC
            ka, kar = hctx[h]["kas"][c]
            qbf, qbr, qaf, qar = qctx[(h, qc)]["qs"][c]
            psS, psSr = bank(it["idx"] % 3)
            j0 = kb * 128
            rds = [kar, (kar, "aug"), qbr, (qbr, "aug"), qar, (qar, "aug")]
            pbt, pbr = pbc_rot.next()
            it.update(pbt=pbt, pbr=pbr)
            if j0 + 128 <= i0 or j0 >= i0 + TC:
                before = j0 + 128 <= i0
                qq = qbf if before else qaf
                bcol = h * CB_W + abs(i0 - j0) // 128
                sc.add("pe", lambda e: e.matmul(psS, lhsT=ka[0:68, j0:j0 + 128], rhs=qq[0:68, :], start=True, stop=True), reads=rds, writes=[psSr])
                sc.add("act", lambda e: e.activation(out=pbt, in_=psS, func=AF.Exp, bias=cbias[:, bcol:bcol + 1]), reads=[psSr, "cbias"], writes=[pbr])
            else:
                sb = (j0 - i0) // 128
                ca, cb_, cc_ = sb * 128, (sb + 1) * 128, TC

                def mmd(e):
                    r = None
                    if sb > 0:
                        r = e.matmul(psS[:, 0:ca], lhsT=ka[0:68, j0:j0 + 128], rhs=qaf[0:68, 0:ca], start=True, stop=True)
                    r = e.matmul(psS[:, ca:cb_], lhsT=ka[0:64, j0:j0 + 128], rhs=qbf[0:64, ca:cb_], start=True, stop=True)
                    if sb < 3:
                        r = e.matmul(psS[:, cb_:cc_], lhsT=ka[0:68, j0:j0 + 128], rhs=qbf[0:68, cb_:cc_], start=True, stop=True)
                    return r
                sc.add("pe", mmd, reads=rds, writes=[psSr])
                pfc, pfr = pfc_rot.next()

                def actd(e):
                    if sb > 0:
                        e.activation(out=pbt[:, 0:ca], in_=psS[:, 0:ca], func=AF.Exp, bias=cbias[:, h * CB_W + sb:h * CB_W + sb + 1])
                    if sb < 3:
                        e.activation(out=pbt[:, cb_:cc_], in_=psS[:, cb_:cc_], func=AF.Exp, bias=cbias[:, h * CB_W + 32 + sb:h * CB_W + 33 + sb])
                    return e.activation(out=pfc, in_=psS[:, ca:cb_], func=AF.Exp)
                sc.add("act", actd, reads=[psSr, "cbias"], writes=[pbr, (pbr, "a"), pfr])
                sc.add("dve", lambda e: e.tensor_tensor(out=pbt[:, ca:cb_], in0=pfc, in1=mdiag[:, h, :], op=ALU.mult), reads=[pfr, "mdiag", (pbr, "a")], writes=[pbr])

        def c_P(it):
            h, qc, c, kb = it["h"], it["qc"], it["c"], it["kb"]
            o_, or_ = psOb[c]
            z_, zr_ = psZb[c]
            vcv, vcr = hctx[h]["vcv"], hctx[h]["vcr"]
            pbt, first, last = it["pbt"], it["first"], it["last"]

            def f(e):
                e.matmul(o_, lhsT=vcv[:, kb, :], rhs=pbt, start=first, stop=last)
                return e.matmul(z_, lhsT=ones_bf, rhs=pbt, start=first, stop=last)
            sc.add("pe", f, reads=[it["pbr"], vcr, "ones_bf"], writes=[or_, zr_])
            if last:
                eo, eor = eo_rot.next()
                ez, ezr = ez_rot.next()
                sc.add("dve", lambda e: e.tensor_copy(out=eo, in_=o_), reads=[or_], writes=[eor])
                sc.add("dve", lambda e: e.tensor_copy(out=ez, in_=z_), reads=[zr_], writes=[ezr])
                qctx[(h, qc)]["ev%d" % c] = (eo, eor, ez, ezr)
                if c == 1:
                    c_epi(h, qc)

        def c_epi(h, qc):
            tsl = slice(qc * TC, (qc + 1) * TC)
            q = qctx[(h, qc)]
            eo0, eo0r, ez0, ez0r = q["ev0"]
            eo1, eo1r, ez1, ez1r = q["ev1"]
            cgt, cgr = q["cgt"], q["cgr"]

            def ep1(e):
                e.reciprocal(out=ez0, in_=ez0)
                e.reciprocal(out=ez1, in_=ez1)
                e.tensor_tensor(out=eo0, in0=eo0, in1=ez0, op=ALU.mult)
                e.tensor_tensor(out=eo1, in0=eo1, in1=ez1, op=ALU.mult)
                return e.scalar_tensor_tensor(out=eo0, in0=eo1, scalar=lamt[:, 4:5], in1=eo0, op0=ALU.mult, op1=ALU.add)
            sc.add("dve", ep1, reads=[eo0r, eo1r, ez0r, ez1r], writes=[eo0r, eo1r, ez0r, ez1r])
            sc.add("act", lambda e: e.activation(out=e_sq, in_=eo0, func=AF.Square), reads=[eo0r], writes=["e_sq"])
            bn, bnr = bank(7)
            sc.add("pe", lambda e: e.matmul(bn, lhsT=ones_f, rhs=e_sq, start=True, stop=True), reads=["e_sq", "ones_f"], writes=[bnr])
            sc.add("act", lambda e: e.activation(out=e_sd, in_=bn, func=AF.Sqrt, scale=1.0 / 128, bias=EPS), reads=[bnr], writes=["e_sd"])
            ybc, ybcr = ybc_rot.next()

            def ep2(e):
                e.reciprocal(out=e_o1, in_=e_sd)
                e.scalar_tensor_tensor(out=eo1, in0=eo0, scalar=lamt[:, 5:6], in1=e_o1, op0=ALU.mult, op1=ALU.mult)
                return e.tensor_tensor(out=ybc, in0=eo1, in1=cgt, op=ALU.mult)
            sc.add("dve", ep2, reads=["e_sd", eo0r, eo1r, cgr], writes=[ybcr, "e_o1", eo1r])
            sc.add("sp", lambda e: e.dma_start(out=Y[YC0 + h * 128:YC0 + (h + 1) * 128, tsl], in_=ybc), reads=[ybcr], dma=True)

        run_pipeline(items, [c_S, c_P], [0, 2])
        sc.barrier()

        ar.reset()
        scale_d = 96.0 ** -0.5
        kd2_rot = Rot(ar, 2, [S], BF16, "kd2")
        vraw2_rot = Rot(ar, 2, [32 * 64], BF16, "vraw2")
        vd_tiles = [ar.alloc([32, 128], BF16) for _ in range(2)]
        for k in range(2):
            sc.add("pool", lambda e, k=k: e.memset(vd_tiles[k][:, :, 64:128], 1.0), writes=[("vd1", k)])
        qd2_rot = Rot(ar, 3, [TC], BF16, "qd2")
        dg_rot = Rot(ar, 3, [TC], BF16, "dg")
        pbd_rot = Rot(ar, 5, [TC], BF16, "pbd")
        rzd_rot = Rot(ar, 2, [TC], F32, "rzd")
        yfd_rot = Rot(ar, 2, [TC], F32, "yfd")
        ybd_rot = Rot(ar, 2, [TC], BF16, "ybd")
        dh, dq = {}, {}

        def d_load_head(h):
            kd, kdr = kd2_rot.next()
            sc.add("sp", lambda e: e.dma_start(out=kd[0:64, :], in_=DK[h * 64:(h + 1) * 64, :]), writes=[kdr], dma=True)
            sc.add("sp", lambda e: e.dma_start(out=kd[64:96, :], in_=DKR), writes=[(kdr, "r")], dma=True)
            vraw, vrr = vraw2_rot.next()
            vk = h % 2
            vt, vtr = vd_tiles[vk], ("vd", vk)
            sc.add("sp", lambda e: e.dma_start(out=vraw, in_=DV[:, h * 2048:(h + 1) * 2048]), writes=[vrr], dma=True)
            sc.add("pool", lambda e: e.tensor_copy(out=vt[:, :, 0:64], in_=vraw.rearrange("p (b d) -> p b d", b=32)),
                   reads=[vrr, ("vd1", vk)], writes=[vtr])
            dh[h] = dict(kd=kd, kdr=kdr, vt=vt, vtr=vtr)

        def d_load_chunk(h, qc):
            tsl = slice(qc * TC, (qc + 1) * TC)
            qd, qdr = qd2_rot.next()
            sc.add("sp", lambda e: e.dma_start(out=qd[0:96, :], in_=DQ[h * 96:(h + 1) * 96, tsl]), writes=[qdr], dma=True)
            dgt, dgr = dg_rot.next()
            sc.add("sp", lambda e: e.dma_start(out=dgt[0:64, :], in_=DG[h * 64:(h + 1) * 64, tsl]), writes=[dgr], dma=True)
            dq[(h, qc)] = dict(qd=qd, qdr=qdr, dgt=dgt, dgr=dgr)

        items = [dict(h=h, qc=qc, kb=kb, idx=(h * NT + qc) * 32 + kb) for h in range(D_HEADS) for qc in range(NT) for kb in range(32)]

        def d_S(it):
            h, qc, kb = it["h"], it["qc"], it["kb"]
            if kb == 0 and qc == 0 and h == 0:
                d_load_head(0)
                d_load_chunk(0, 0)
            if kb == 4:
                if qc == 0 and h + 1 < D_HEADS:
                    d_load_head(h + 1)
                nh, nq = (h, qc + 1) if qc + 1 < NT else (h + 1, 0)
                if nh < D_HEADS:
                    d_load_chunk(nh, nq)
            kd, kdr = dh[h]["kd"], dh[h]["kdr"]
            qd, qdr = dq[(h, qc)]["qd"], dq[(h, qc)]["qdr"]
            psS, psSr = bank(it["idx"] % 3)
            pbt, pbr = pbd_rot.next()
            sc.add("pe", lambda e: e.matmul(psS, lhsT=kd[0:96, kb * 128:(kb + 1) * 128], rhs=qd[0:96, :], start=True, stop=True),
                   reads=[kdr, (kdr, "r"), qdr], writes=[psSr])
            sc.add("act", lambda e: e.activation(out=pbt, in_=psS, func=AF.Exp, scale=scale_d), reads=[psSr], writes=[pbr])
            it.update(pbt=pbt, pbr=pbr)

        def d_P(it):
            h, qc, kb = it["h"], it["qc"], it["kb"]
            vt, vtr = dh[h]["vt"], dh[h]["vtr"]
            psO, psOr = bank(3 + (h * NT + qc) % 2)
            pbt = it["pbt"]
            sc.add("pe", lambda e: e.matmul(psO, lhsT=vt[:, kb, :], rhs=pbt, start=(kb == 0), stop=(kb == 31)), reads=[it["pbr"], vtr], writes=[psOr])
            if kb == 31:
                tsl = slice(qc * TC, (qc + 1) * TC)
                rz, rzr = rzd_rot.next()
                yf, yfr = yfd_rot.next()
                yb, ybr = ybd_rot.next()
                dgt, dgr = dq[(h, qc)]["dgt"], dq[(h, qc)]["dgr"]

                def epd(e):
                    e.reciprocal(out=rz[0:64, :], in_=psO[64:128, :])
                    e.tensor_tensor(out=yf[0:64, :], in0=psO[0:64, :], in1=rz[0:64, :], op=ALU.mult)
                    return e.tensor_tensor(out=yb[0:64, :], in0=yf[0:64, :], in1=dgt[0:64, :], op=ALU.mult)
                sc.add("dve", epd, reads=[psOr, dgr], writes=[rzr, yfr, ybr])
                sc.add("sp", lambda e: e.dma_start(out=Y[YD0 + h * 64:YD0 + (h + 1) * 64, tsl], in_=yb[0:64, :]), reads=[ybr], dma=True)

        run_pipeline(items, [d_S, d_P], [0, 2])
        sc.barrier()

        ar.reset()
        wbr = ar.alloc([NYC, 1024], BF16)
        wout = ar.alloc([8, 1024], BF16)
        wbr_f = wbr.rearrange("p a b -> p (a b)")
        wout_f = wout.rearrange("p a b -> p (a b)")
        nwb = (NYC * 1024 + 4095) // 4096
        for i in range(nwb):
            n = min(4096, NYC * 1024 - i * 4096)
            load_cast(wbr_d[l][:, i * 4096:i * 4096 + n], n, wbr_f[:, i * 4096:i * 4096 + n], ("wbr", i))
        for i in range(2):
            load_cast(wout_d[l][:, i * 4096:(i + 1) * 4096], 4096, wout_f[:, i * 4096:(i + 1) * 4096], ("wout", i))
        wbr_res = [("wbr", i) for i in range(nwb)]
        wout_res = [("wout", i) for i in range(2)]
        yt_rot = Rot(ar, 2, [NYC, TC], BF16, "yt")
        xt2 = ar.alloc([KC, TC], F32)
        out2 = ar.alloc([KC, TC], F32)
        merged = ar.alloc([KC, TC], BF16)
        mfull_rot = Rot(ar, 2, [KC, TC], BF16, "mfull") if SPLIT else None
        g_rot = Rot(ar, 4, [TC], BF16, "g")
        tm_rot = Rot(ar, 4, [TC], F32, "tm")
        macc_rot = Rot(ar, 3, [TC], F32, "macc")
        sqf_rot = Rot(ar, 2, [TC], F32, "sqf")
        rr2 = ar.alloc([TC], F32)
        Yv = Y.rearrange("(c p) s -> p c s", p=128)
        x_src_v2 = x_src.rearrange("(kc p) s -> p kc s", p=128)
        x_dst_v = x_dst.rearrange("(kc p) s -> p kc s", p=128)
        pbk = [0]
        mres = [("merged", oc) for oc in range(8)]

        def f_stage1(t):
            tsl = slice(t * TC, (t + 1) * TC)
            yt, ytr = yt_rot.next()
            sc.add("sp", lambda e: e.dma_start(out=yt, in_=Yv[:, :, tsl]), writes=[ytr], dma=True)
            for oc in range(8):
                macc = maccr = None
                for br in range(4):
                    gt, gr = g_rot.next()
                    sc.add("sp", lambda e, gt=gt, br=br, oc=oc: e.dma_start(out=gt, in_=GT[br * 1024 + oc * 128:br * 1024 + (oc + 1) * 128, tsl]), writes=[gr], dma=True)
                    bk, bkr = bank(pbk[0] % 4)
                    pbk[0] += 1
                    chs = BR_CHUNKS[br]

                    def bmm(e, bk=bk, chs=chs, oc=oc):
                        r = None
                        for ci, (cidx, nr) in enumerate(chs):
                            r = e.matmul(bk, lhsT=wbr[0:nr, cidx, oc * 128:(oc + 1) * 128], rhs=yt[0:nr, cidx, :], start=(ci == 0), stop=(ci == len(chs) - 1))
                        return r
                    sc.add("pe", bmm, reads=[ytr] + wbr_res, writes=[bkr])
                    if br == 0:
                        macc, maccr = macc_rot.next()
                        sc.add("dve", lambda e, bk=bk, gt=gt, macc=macc: e.tensor_tensor(out=macc, in0=bk, in1=gt, op=ALU.mult), reads=[bkr, gr], writes=[maccr])
                    else:
                        tm, tmr = tm_rot.next()
                        sc.add("dve", lambda e, bk=bk, gt=gt, tm=tm: e.tensor_tensor(out=tm, in0=bk, in1=gt, op=ALU.mult), reads=[bkr, gr], writes=[tmr])
                        if br < 3:
                            sc.add("pool", lambda e, tm=tm, macc=macc: e.tensor_tensor(out=macc, in0=macc, in1=tm, op=ALU.add), reads=[tmr, maccr], writes=[maccr])
                        else:
                            sc.add("pool", lambda e, tm=tm, oc=oc, macc=macc: e.tensor_tensor(out=merged[:, oc, :], in0=macc, in1=tm, op=ALU.add), reads=[tmr, maccr], writes=[("merged", oc)])
            if SPLIT:
                pr, q = t // 2, t % 2
                k = pr % 2
                sc.add("sp", lambda e: e.dma_start(out=ARI[k].rearrange("(kc p) s -> p kc s", p=128)[:, :, q * TC:(q + 1) * TC], in_=merged),
                       reads=mres, writes=[("ari", k, q)], dma=True)
                if q == 1:
                    sc.add("pool", lambda e: e.collective_compute("AllReduce", ALU.add, replica_groups=RG, ins=[ARI[k]], outs=[ARO[k]]),
                           reads=[("ari", k, 0), ("ari", k, 1)], writes=[("aro", k)], cc=True)

        def f_stage2(t):
            tsl = slice(t * TC, (t + 1) * TC)
            if SPLIT:
                pr, q = t // 2, t % 2
                k = pr % 2
                msrc, msr = mfull_rot.next()
                sc.add("sp", lambda e: e.dma_start(out=msrc, in_=ARO[k].rearrange("(kc p) s -> p kc s", p=128)[:, :, q * TC:(q + 1) * TC]),
                       reads=[("aro", k)], writes=[msr], dma=True)
                mrd = [msr]
            else:
                msrc, mrd = merged, mres
            sc.add("sp", lambda e: e.dma_start(out=xt2, in_=x_src_v2[:, :, tsl]), writes=["xt2"], dma=True)
            bn, bnr = bank(6)
            for oc2 in range(8):
                bo, bor = bank(4 + oc2 % 2)

                def omm(e, bo=bo, oc2=oc2):
                    r = None
                    for oc in range(8):
                        r = e.matmul(bo, lhsT=wout[:, oc, oc2 * 128:(oc2 + 1) * 128], rhs=msrc[:, oc, :], start=(oc == 0), stop=(oc == 7))
                    return r
                sc.add("pe", omm, reads=mrd + wout_res, writes=[bor])
                sqf, sqfr = sqf_rot.next()

                def oev(e, bo=bo, oc2=oc2, sqf=sqf):
                    e.activation(out=out2[:, oc2, :], in_=bo, func=AF.Copy)
                    return e.activation(out=sqf, in_=bo, func=AF.Square)
                sc.add("act", oev, reads=[bor], writes=[("out2", oc2), sqfr])
                sc.add("pe", lambda e, sqf=sqf, oc2=oc2: e.matmul(bn, lhsT=ones_f, rhs=sqf, start=(oc2 == 0), stop=(oc2 == 7)), reads=[sqfr, "ones_f"], writes=[bnr])
            sc.add("act", lambda e: e.activation(out=rr2, in_=bn, func=AF.Sqrt, scale=1.0 / DM, bias=EPS), reads=[bnr], writes=["rr2"])
            ores = [("out2", i) for i in range(8)]

            def resid(e):
                e.reciprocal(out=rr2, in_=rr2)
                r = None
                for oc2 in range(8):
                    e.scalar_tensor_tensor(out=out2[:, oc2, :], in0=out2[:, oc2, :], scalar=col(SM_GPOST + oc2), in1=rr2, op0=ALU.mult, op1=ALU.mult)
                    r = e.tensor_tensor(out=xt2[:, oc2, :], in0=xt2[:, oc2, :], in1=out2[:, oc2, :], op=ALU.add)
                return r
            sc.add("dve", resid, reads=["rr2", "xt2", "small"] + ores, writes=["xt2", "rr2"] + ores)
            sc.add("sp", lambda e: e.dma_start(out=x_dst_v[:, :, tsl], in_=xt2), reads=["xt2"], dma=True)

        if SPLIT:
            for t in range(NT):
                f_stage1(t)
                if t % 2 == 1 and t >= 3:
                    f_stage2(t - 3)
                    f_stage2(t - 2)
            f_stage2(NT - 2)
            f_stage2(NT - 1)
        else:
            for t in range(NT):
                f_stage1(t)
                f_stage2(t)
        sc.barrier()

    for l in range(L):
        layer(l)
    sc.analyze()
    sc.emit(nc, es)
    es.close()
    return nc


def _bf(x):
    return np.asarray(x, dtype=np.float32).astype(ml_dtypes.bfloat16)


def _parts(par):
    if not SPLIT:
        return list(range(6)), list(range(6)), list(range(4)), list(range(6))
    return ([3 * par + i for i in range(3)], [3 * par + i for i in range(3)], [2 * par + i for i in range(2)],
            [3 * par + i for i in range(3)])


def build_consts(par=0):
    c = {}
    slots, _, cheads, _ = _parts(par)
    cs_all = c_slopes()
    cs = [cs_all[h] for h in cheads]
    caugq = np.zeros((C_HEADS, 2, 4, TC), np.float32)
    caugk = np.zeros((C_HEADS, 4, S), np.float32)
    cbias = np.zeros((128, C_HEADS, CB_W), np.float32)
    ii = np.arange(TC, dtype=np.float64)
    jj = (np.arange(S) % 128).astype(np.float64)
    for h, m in enumerate(cs):
        qb = (-m * ii).astype(np.float32)
        qb_hi = _bf(qb).astype(np.float32)
        qb_lo = _bf(qb - qb_hi).astype(np.float32)
        caugq[h, 0] = np.stack([qb_hi, qb_lo, np.ones(TC), np.ones(TC)])
        caugq[h, 1] = -caugq[h, 0]
        kb = (m * jj).astype(np.float32)
        kb_hi = _bf(kb).astype(np.float32)
        kb_lo = _bf(kb - kb_hi).astype(np.float32)
        caugk[h] = np.stack([np.ones(S), np.ones(S), kb_hi, kb_lo])
        cbias[:, h, 0:32] = (-m * 128.0 * np.arange(32))[None, :]
        cbias[:, h, 32:36] = (m * 128.0 * np.arange(4))[None, :]
    c["caugq"] = _bf(caugq)
    c["caugk"] = _bf(caugk)
    c["cbias"] = cbias.reshape(128, C_HEADS * CB_W)
    p = np.arange(128)[:, None].astype(np.float64)
    f = np.arange(128)[None, :].astype(np.float64)
    md = np.stack([np.exp(-m * np.abs(p - f)) for m in cs], axis=1)
    c["mdiag"] = md.reshape(128, C_HEADS * 128).astype(np.float32)
    sl_all = a_slopes()
    ma = np.zeros((128, 3 * A_SLOTS, 384), np.float64)
    k = np.arange(128)[:, None]
    for g, d in enumerate(A_DIL):
        for si, sg in enumerate(slots):
            for j in range(3):
                q = np.arange(128)[None, :]
                rel = np.abs((j - 1) * 128 + k - q)
                ma[:, g * A_SLOTS + si, j * 128:(j + 1) * 128] = np.where(rel <= 64, np.exp(-sl_all[sg] * d * rel), 0.0)
    c["ma"] = ma.reshape(128, 3 * A_SLOTS * 384).astype(np.float32)
    inv = (10000.0 ** (-np.arange(0, 32, 2, dtype=np.float32) / 32)).astype(np.float32)
    ang = np.arange(S, dtype=np.float32)[:, None] * inv[None, :]
    cos, sin = np.cos(ang).astype(np.float32).T, np.sin(ang).astype(np.float32).T
    c["ropec"] = np.ascontiguousarray(np.concatenate([cos, cos], 0))
    c["ropes"] = np.ascontiguousarray(np.concatenate([-sin, sin], 0))
    return c


def _bchan(par):
    _, blocks, _, _ = _parts(par)
    idx = -np.ones(BW, np.int64)
    for i, b in enumerate(blocks):
        idx[i * 64:(i + 1) * 64] = np.arange(b * 64, (b + 1) * 64)
    return idx


def pack_w_in(w, par):
    slots, _, cheads, dheads = _parts(par)
    L = w.shape[0]
    out = np.zeros((L, DM, IN_W), np.float32)
    O = OFF_ALL

    def put(name, off_in_fam, src_cols):
        n = len(src_cols)
        out[:, :, OFF[name] + off_in_fam:OFF[name] + off_in_fam + n] = w[:, :, src_cols]
    for fam in ("a_q", "a_k", "a_v"):
        for g in range(3):
            cols = np.concatenate([np.arange(O[fam] + g * 384 + s * 64, O[fam] + g * 384 + (s + 1) * 64) for s in slots])
            put(fam, g * AW, cols)
    put("a_g", 0, np.concatenate([np.arange(O["a_g"] + s * 64, O["a_g"] + (s + 1) * 64) for s in slots]))
    bidx = _bchan(par)
    nreal = int((bidx >= 0).sum())
    put("b_x", 0, O["b_x"] + bidx[:nreal])
    put("b_g", 0, O["b_g"] + bidx[:nreal])
    for fam in ("c_q", "c_k", "c_v", "c_g"):
        put(fam, 0, np.concatenate([np.arange(O[fam] + h * 128, O[fam] + (h + 1) * 128) for h in cheads]))
    put("d_cq", 0, np.arange(O["d_cq"], O["d_cq"] + 256))
    put("d_ckv", 0, np.arange(O["d_ckv"], O["d_ckv"] + 128))
    put("d_kr", 0, np.arange(O["d_kr"], O["d_kr"] + 32))
    put("d_g", 0, np.concatenate([np.arange(O["d_g"] + h * 64, O["d_g"] + (h + 1) * 64) for h in dheads]))
    put("gate", 0, np.arange(O["gate"], O["gate"] + 4096))
    return out


def pack_layers(inp, layers, par=0):
    f32 = np.float32
    L = len(layers)
    slots, blocks, cheads, dheads = _parts(par)
    bidx = _bchan(par)
    real = bidx >= 0
    small = np.zeros((L, 128, NSMALL), f32)
    lamv = np.zeros((L, 1, 256), f32)
    lruw = np.zeros((L, 128, 4 * NBC, 128), f32)
    wuqa = np.zeros((L, 128, 2, D_HEADS * 96), f32)
    wuqb = np.zeros((L, 128, 2, D_HEADS * 96), f32)
    wukvk = np.zeros((L, 128, DW), f32)
    wukvv = np.zeros((L, 128, DW), f32)
    wbr = np.zeros((L, 128, NYC, 1024), f32)
    wout = np.zeros((L, 128, 8, 1024), f32)

    def bvec(v):
        o = np.zeros(BW, f32)
        o[real] = v[bidx[real]]
        return o.reshape(NBC, 128).T
    for li, l in enumerate(layers):
        sm = small[li]
        sm[:, SM_GPRE:SM_GPRE + 8] = inp["norm_pre"][l].reshape(8, 128).T
        sm[:, SM_GPOST:SM_GPOST + 8] = inp["norm_post"][l].reshape(8, 128).T
        sm[:, SM_BGATE:SM_BGATE + 32] = inp["b_gate"][l].reshape(32, 128).T
        for j in range(4):
            sm[:, SM_CONVW + j * NBC:SM_CONVW + (j + 1) * NBC] = bvec(inp["conv_w"][l][j])
        sm[:, SM_CONVB:SM_CONVB + NBC] = bvec(inp["conv_b"][l])
        for dr in range(2):
            sm[:, SM_BR + dr * NBC:SM_BR + (dr + 1) * NBC] = bvec(inp["lru_br"][l][dr])
            sm[:, SM_BI + dr * NBC:SM_BI + (dr + 1) * NBC] = bvec(inp["lru_bi"][l][dr])
            sm[:, SM_LAM + dr * NBC:SM_LAM + (dr + 1) * NBC] = bvec(inp["lru_lambda"][l][dr])
        sm[:, SM_SUBLN] = inp["diff_subln"][l]
        sm[:, SM_QN:SM_QN + 2] = inp["mla_q_norm"][l].reshape(2, 128).T
        sm[:, SM_KVN] = inp["mla_kv_norm"][l]
        lam_init = 0.8 - 0.6 * math.exp(-0.3 * l)
        sm[:, SM_LAMINIT] = lam_init
        sm[:, SM_OML] = (1.0 - lam_init)
        lamv[li, 0, 0:64] = inp["diff_lam_q1"][l]
        lamv[li, 0, 64:128] = inp["diff_lam_k1"][l]
        lamv[li, 0, 128:192] = inp["diff_lam_q2"][l]
        lamv[li, 0, 192:256] = inp["diff_lam_k2"][l]
        for gi, w in enumerate((inp["lru_wr"][l], inp["lru_wi"][l])):
            for dr in range(2):
                for i, b in enumerate(blocks):
                    c, bb = i // 2, i % 2
                    lruw[li, bb * 64:(bb + 1) * 64, (gi * 2 + dr) * NBC + c, bb * 64:(bb + 1) * 64] = w[dr, b]
        uq = inp["mla_w_uq"][l].reshape(2, 128, 6, 96)[:, :, dheads, :]
        wuqa[li] = uq.transpose(1, 0, 2, 3).reshape(128, 2, D_HEADS * 96)
        uqb = uq.copy()
        uqb[..., 64:80] = uq[..., 80:96]
        uqb[..., 80:96] = uq[..., 64:80]
        wuqb[li] = uqb.transpose(1, 0, 2, 3).reshape(128, 2, D_HEADS * 96)
        ukv = inp["mla_w_ukv"][l].reshape(128, 6, 128)[:, dheads, :]
        wukvk[li] = ukv[:, :, 0:64].reshape(128, DW)
        wukvv[li] = ukv[:, :, 64:128].reshape(128, DW)
        wy = np.zeros((NYC * 128, 1024), f32)
        wa, wb_, wc, wd = inp["w_br_a"][l], inp["w_br_b"][l], inp["w_br_c"][l], inp["w_br_d"][l]
        for i, s_ in enumerate(slots):
            wy[YA0 + i * 64:YA0 + (i + 1) * 64] = wa[s_ * 64:(s_ + 1) * 64]
        wy[YB0:YB0 + BW][real] = wb_[bidx[real]]
        for i, h in enumerate(cheads):
            wy[YC0 + i * 128:YC0 + (i + 1) * 128] = wc[h * 128:(h + 1) * 128]
        for i, h in enumerate(dheads):
            wy[YD0 + i * 64:YD0 + (i + 1) * 64] = wd[h * 64:(h + 1) * 64]
        wbr[li] = wy.reshape(NYC, 128, 1024).transpose(1, 0, 2)
        wout[li] = inp["w_out"][l].reshape(8, 128, 1024).transpose(1, 0, 2)
    return dict(small=small, lamv=lamv, lruw=lruw.reshape(L, 128, 4 * NBC * 128), wuqa=wuqa.reshape(L, 128, 2 * D_HEADS * 96),
                wuqb=wuqb.reshape(L, 128, 2 * D_HEADS * 96), wukvk=wukvk, wukvv=wukvv, wbr=wbr.reshape(L, 128, NYC * 1024),
                wout=wout.reshape(L, 128, 8 * 1024))


_PROG = {}


def get_prog(nl, debug=False):
    key = (nl, tuple(debug) if debug else None)
    if key not in _PROG:
        _PROG[key] = build_program(nl, debug)
    return _PROG[key]


FUSED = True


def make_in_maps(inp, layers, xT_by_batch):
    npar = 2 if SPLIT else 1
    w_all = np.ascontiguousarray(inp["w_in"][layers[0]:layers[-1] + 1]).astype(np.float32)
    per = []
    for par in range(npar):
        m = dict(w_in=pack_w_in(w_all, par) if SPLIT else w_all)
        m.update(pack_layers(inp, layers, par))
        m.update(build_consts(par))
        per.append(m)
    in_maps = []
    for c in range(8):
        b, par = (c // 2, c % 2) if SPLIT else (c % 4, 0)
        m = dict(xT=xT_by_batch[b])
        m.update(per[par])
        in_maps.append(m)
    return in_maps


def kernel(**inputs):
    inp = {k: np.asarray(v) for k, v in inputs.items()}
    x = inp["x"].astype(np.float32)
    xT = [np.ascontiguousarray(x[b].T) for b in range(4)]
    groups = [list(range(DEPTH))] if FUSED else [[l] for l in range(DEPTH)]
    for layers in groups:
        nc = get_prog(len(layers))
        in_maps = make_in_maps(inp, layers, xT)
        res = run_bass_kernel_spmd(nc, in_maps, core_ids=list(range(8)))
        xT = [np.asarray(res.results[(2 * b) if SPLIT else b]["outT"]) for b in range(4)]
    out = np.stack([xT[b].T for b in range(4)], 0).astype(np.float32)
    return np.ascontiguousarray(out)
```

```python
import math
from contextlib import ExitStack

import numpy as np
import ml_dtypes

import concourse.bass as bass
import concourse.mybir as mybir
from concourse.bass_utils import run_bass_kernel_spmd

F32 = mybir.dt.float32
BF16 = mybir.dt.bfloat16
AF = mybir.ActivationFunctionType
ALU = mybir.AluOpType
AX = mybir.AxisListType

S = 4096
DM = 1024
DEPTH = 4
NT = 8
TC = 512
KC = 8
EPS = 1e-6
A_DIL = (1, 4, 16)
SPLIT = True
A_SLOTS_ALL, C_HEADS_ALL, D_HEADS_ALL = 6, 4, 6
A_SLOTS = 3 if SPLIT else 6
C_HEADS = 2 if SPLIT else 4
D_HEADS = 3 if SPLIT else 6
NBC = 2 if SPLIT else 3
AW, BW, CW, DW = A_SLOTS * 64, NBC * 128, C_HEADS * 128, D_HEADS * 64
_fam = [("a_q", 3 * AW), ("a_k", 3 * AW), ("a_v", 3 * AW), ("a_g", AW), ("b_x", BW), ("b_g", BW), ("c_q", CW), ("c_k", CW),
        ("c_v", CW), ("c_g", CW), ("d_cq", 256), ("d_ckv", 128), ("d_kr", 32), ("d_g", DW), ("gate", 4096)]
OFF = {}
_o = 0
for _n, _w in _fam:
    OFF[_n] = _o
    _o += _w
IN_W = _o
OFF_ALL = dict(a_q=0, a_k=1152, a_v=2304, a_g=3456, b_x=3840, b_g=4224, c_q=4608, c_k=5120,
               c_v=5632, c_g=6144, d_cq=6656, d_ckv=6912, d_kr=7040, d_g=7072, gate=7456)
SKIP_T = 1e30 if SPLIT else 100.0
if SPLIT:
    YA0, YB0, YC0, YD0, NYC = 0, 256, 512, 768, 8
    BR_CHUNKS = [[(0, 128), (1, 64)], [(2, 128), (3, 64)], [(4, 128), (5, 128)], [(6, 128), (7, 64)]]
else:
    YA0, YB0, YC0, YD0, NYC = 0, 384, 768, 1280, 13
    BR_CHUNKS = [[(0, 128), (1, 128), (2, 128)], [(3, 128), (4, 128), (5, 128)], [(6, 128), (7, 128), (8, 128), (9, 128)],
                 [(10, 128), (11, 128), (12, 128)]]
RG = [[0, 1], [2, 3], [4, 5], [6, 7]]
CB_W = 36

SM_GPRE, SM_GPOST, SM_BGATE, SM_CONVW = 0, 8, 16, 48
SM_CONVB = SM_CONVW + 4 * NBC
SM_BR = SM_CONVB + NBC
SM_BI = SM_BR + 2 * NBC
SM_LAM = SM_BI + 2 * NBC
SM_SUBLN = SM_LAM + 2 * NBC
SM_QN, SM_KVN, SM_LAMINIT, SM_OML = SM_SUBLN + 1, SM_SUBLN + 3, SM_SUBLN + 4, SM_SUBLN + 5
NSMALL = SM_SUBLN + 8


def a_slopes():
    return [2.0 ** (-8.0 * (i + 1) / A_SLOTS_ALL) for i in range(A_SLOTS_ALL)]


def c_slopes():
    return [2.0 ** (-8.0 * (i + 1) / C_HEADS_ALL) for i in range(C_HEADS_ALL)]


class Sched:
    ENGS = ("pe", "act", "dve", "pool", "sp")
    NDMA = {"sp": 24, "act": 12, "pool": 4, "cc": 4}
    UNIT = {"sp": 16, "act": 16, "pool": 16, "cc": 1}

    def __init__(self):
        self.ops = []

    def add(self, eng, fn, reads=(), writes=(), dma=False, cc=False):
        self.ops.append(dict(eng=eng, fn=fn, reads=tuple(reads), writes=tuple(writes), dma=(dma or cc),
                             q=("cc" if cc else eng), needs_inc=False, deps=[]))

    def barrier(self):
        self.ops.append(dict(barrier=True))

    def analyze(self):
        ops = self.ops
        last_w, readers = {}, {}
        last_on = {}
        for i, op in enumerate(ops):
            if op.get("barrier"):
                for e, j in last_on.items():
                    ops[j]["needs_inc"] = True
                last_w, readers = {}, {}
                continue
            deps = set()
            for r in op["reads"]:
                if r in last_w:
                    deps.add((last_w[r], "raw"))
            for w in op["writes"]:
                if w in last_w:
                    deps.add((last_w[w], "waw"))
                for rd in readers.get(w, ()):
                    deps.add((rd, "war"))
            keep = set()
            for d, kind in deps:
                if d == i:
                    continue
                p = ops[d]
                if p["dma"]:
                    keep.add(d)
                elif p["eng"] == op["eng"]:
                    if op["dma"]:
                        keep.add(d)
                    elif kind == "raw" and op["eng"] in ("act", "dve", "pool"):
                        keep.add(d)
                else:
                    keep.add(d)
            op["deps"] = sorted(keep)
            for d in keep:
                ops[d]["needs_inc"] = True
            for w in op["writes"]:
                last_w[w] = i
                readers[w] = []
            for r in op["reads"]:
                if r not in op["writes"]:
                    readers.setdefault(r, []).append(i)
            if not op["dma"]:
                last_on[op["eng"]] = i
        tick = {e: 0 for e in self.ENGS}
        dma_cnt = {q: [0] * n for q, n in self.NDMA.items()}
        dma_rr = {q: 0 for q in self.NDMA}
        seen = {e: {} for e in self.ENGS}
        pending = {e: [] for e in self.ENGS}
        for op in ops:
            if op.get("barrier"):
                snap = [(("c", e), tick[e]) for e in self.ENGS if tick[e] > 0]
                for q, cnts in dma_cnt.items():
                    for k, c in enumerate(cnts):
                        if c > 0:
                            snap.append((("d", q, k), self.UNIT[q] * c))
                for e in self.ENGS:
                    pending[e] = [(s_, v) for (s_, v) in snap if s_ != ("c", e)]
                continue
            e = op["eng"]
            waits = list(pending[e])
            pending[e] = []
            for d in op["deps"]:
                p = ops[d]
                waits.append((p["sem"], p["tick"]))
            if op["dma"]:
                q = op["q"]
                k = dma_rr[q]
                dma_rr[q] = (k + 1) % self.NDMA[q]
                if dma_cnt[q][k] > 0:
                    waits.append((("d", q, k), self.UNIT[q] * dma_cnt[q][k]))
                dma_cnt[q][k] += 1
                op["sem"] = ("d", q, k)
                op["tick"] = self.UNIT[q] * dma_cnt[q][k]
            elif op["needs_inc"]:
                tick[e] += 1
                op["sem"] = ("c", e)
                op["tick"] = tick[e]
            fw = []
            for s_, v in waits:
                if seen[e].get(s_, 0) >= v:
                    continue
                seen[e][s_] = v
                fw.append((s_, v))
            mx = {}
            for s_, v in fw:
                mx[s_] = max(mx.get(s_, 0), v)
            op["waits"] = sorted(mx.items(), key=lambda kv: str(kv[0]))
        self.final_dma = {(q, k): self.UNIT[q] * c for q, cnts in dma_cnt.items() for k, c in enumerate(cnts) if c > 0}

    def emit(self, nc, es):
        sems = {}
        for e in self.ENGS:
            sems[("c", e)] = es.enter_context(nc.semaphore("c_" + e))
        for q, n in self.NDMA.items():
            for k in range(n):
                sems[("d", q, k)] = es.enter_context(nc.semaphore("d_%s_%d" % (q, k)))
        blk = es.enter_context(nc.Block())
        ops = self.ops

        def run(engname):
            def body(e):
                for op in ops:
                    if op.get("barrier") or op["eng"] != engname:
                        continue
                    for s_, v in op["waits"]:
                        e.wait_ge(sems[s_], v)
                    ins = op["fn"](e)
                    if op["dma"]:
                        ins.then_inc(sems[op["sem"]], self.UNIT[op["q"]])
                    elif op["needs_inc"]:
                        ins.then_inc(sems[op["sem"]], 1)
                if engname == "sp":
                    for (q, k), v in sorted(self.final_dma.items()):
                        e.wait_ge(sems[("d", q, k)], v)
            return body

        blk.tensor(run("pe"))
        blk.scalar(run("act"))
        blk.vector(run("dve"))
        blk.gpsimd(run("pool"))
        blk.sync(run("sp"))


class Arena:
    def __init__(self, ap, nbytes, name):
        self.ap = ap
        self.nbytes = nbytes
        self.off = 0
        self.name = name
        self.cnt = 0

    def reset(self):
        self.off = 0

    def alloc(self, free_shape, dt, parts=128):
        n = 1
        for d in free_shape:
            n *= d
        nb = n * (4 if dt == F32 else 2)
        nb = (nb + 63) // 64 * 64
        assert self.off + nb <= self.nbytes, "arena %s overflow: %d + %d > %d" % (self.name, self.off, nb, self.nbytes)
        a = self.ap[:, self.off // 4:(self.off + nb) // 4]
        self.off += nb
        if dt == BF16:
            a = a.bitcast(BF16)
        a = a[:, 0:n]
        if len(free_shape) == 2:
            a = a.rearrange("p (a b) -> p a b", a=free_shape[0])
        elif len(free_shape) == 3:
            a = a.rearrange("p (a b c) -> p a b c", a=free_shape[0], b=free_shape[1])
        self.cnt += 1
        return a


class Rot:
    def __init__(self, arena, n, free_shape, dt, name):
        self.tiles = [arena.alloc(free_shape, dt) for _ in range(n)]
        self.name = name
        self.i = 0

    def next(self):
        k = self.i % len(self.tiles)
        self.i += 1
        return self.tiles[k], (self.name, k)


def run_pipeline(items, stages, lags):
    n = len(items)
    mx = max(lags)
    for step in range(n + mx):
        for f, lg in zip(stages, lags):
            i = step - lg
            if 0 <= i < n:
                f(items[i])


def build_program(nlayers, debug=False):
    nc = bass.Bass("TRN2", target_bir_lowering=False)
    L = nlayers

    def din(name, shape, dt=F32):
        return nc.dram_tensor(name, list(shape), dt, kind="ExternalInput").ap()

    def dscr(name, shape, dt=BF16):
        ext = bool(debug) and name in debug
        return nc.dram_tensor(name, list(shape), dt, kind="ExternalOutput" if ext else "Internal").ap()

    xT_in = din("xT", [DM, S])
    w_in = din("w_in", [L, DM, IN_W])
    small_d = din("small", [L, 128, NSMALL])
    lamv_d = din("lamv", [L, 1, 256])
    lruw_d = din("lruw", [L, 128, 4 * NBC * 128])
    wuqa_d = din("wuqa", [L, 128, 2 * D_HEADS * 96])
    wuqb_d = din("wuqb", [L, 128, 2 * D_HEADS * 96])
    wukvk_d = din("wukvk", [L, 128, DW])
    wukvv_d = din("wukvv", [L, 128, DW])
    wbr_d = din("wbr", [L, 128, NYC * 1024])
    cbias_d = din("cbias", [128, C_HEADS * CB_W])
    wout_d = din("wout", [L, 128, 8 * 1024])
    caugq_d = din("caugq", [C_HEADS, 2, 4, TC], BF16)
    caugk_d = din("caugk", [C_HEADS, 4, S], BF16)
    mdiag_d = din("mdiag", [128, C_HEADS * 128])
    ma_d = din("ma", [128, 3 * A_SLOTS * 384])
    ropec_d = din("ropec", [32, S])
    ropes_d = din("ropes", [32, S])
    outT = nc.dram_tensor("outT", [DM, S], F32, kind="ExternalOutput").ap()

    AQ = [dscr("AQ%d" % g, [AW, S]) for g in range(3)]
    AK = [dscr("AK%d" % g, [AW, S]) for g in range(3)]
    AV = [dscr("AV%d" % g, [128, A_SLOTS * 32 * 64]) for g in range(3)]
    AG = dscr("AG", [AW, S])
    BX = dscr("BX", [BW, S], F32)
    BG = dscr("BG", [BW, S])
    CQ = dscr("CQ", [CW, S])
    CK = dscr("CK", [CW, S])
    CV = dscr("CV", [128, C_HEADS * 32 * 128])
    CG = dscr("CG", [CW, S])
    DLAT = dscr("DLAT", [384, S], F32)
    DKRAW = dscr("DKRAW", [64, S], F32)
    DG = dscr("DG", [DW, S])
    GT = dscr("GT", [4096, S])
    DQ = dscr("DQ", [D_HEADS * 96, S])
    DK = dscr("DK", [D_HEADS * 64, S])
    DKR = dscr("DKR", [32, S])
    DV = dscr("DV", [128, D_HEADS * 32 * 64])
    Y = dscr("Y", [NYC * 128, S])
    ARI = [nc.dram_tensor("ARI%d" % i, [DM, 2 * TC], BF16, kind="Internal").ap() for i in range(2)]
    ARO = [nc.dram_tensor("ARO%d" % i, [DM, 2 * TC], BF16, kind="Internal").ap() for i in range(2)]
    XS = [nc.dram_tensor("XS%d" % i, [DM, S], F32, kind="Internal").ap() for i in range(2)]

    sc = Sched()
    es = ExitStack()
    PERS_BYTES = 56 * 1024
    ARENA_BYTES = 136 * 1024
    pers_t = es.enter_context(nc.sbuf_tensor("pers", [128, PERS_BYTES // 4], F32))
    arena_t = es.enter_context(nc.sbuf_tensor("arena", [128, ARENA_BYTES // 4], F32))
    pers = Arena(pers_t[:], PERS_BYTES, "pers")
    ar = Arena(arena_t[:], ARENA_BYTES, "arena")
    banks = [es.enter_context(nc.psum_tensor("bank%d" % i, [128, 512], F32)) for i in range(8)]

    def bank(i):
        return banks[i][:], ("bank", i)

    small = pers.alloc([NSMALL], F32)
    lamv = pers.alloc([256], F32)
    lamt = pers.alloc([16], F32)
    spc = pers.alloc([8], F32)
    ones_bf = pers.alloc([128], BF16)
    ones_f = pers.alloc([128], F32)
    mdiag = pers.alloc([C_HEADS, 128], F32)
    cbias = pers.alloc([C_HEADS * CB_W], F32)
    wst = [pers.alloc([4096], F32) for _ in range(2)]
    wbf = [pers.alloc([4096], BF16) for _ in range(2)]
    wst_i = [0]

    def col(c, n=1):
        return small[:, c:c + n]

    sc.add("dve", lambda e: e.memset(ones_bf, 1.0), writes=["ones_bf"])
    sc.add("dve", lambda e: e.memset(ones_f, 1.0), writes=["ones_f"])
    sc.add("sp", lambda e: e.dma_start(out=mdiag.rearrange("p a b -> p (a b)"), in_=mdiag_d), writes=["mdiag"], dma=True)
    sc.add("sp", lambda e: e.dma_start(out=cbias, in_=cbias_d), writes=["cbias"], dma=True)

    def load_cast(src_ap, ncols_f32, dst_bf, dst_res):
        k = wst_i[0] % 2
        wst_i[0] += 1
        st = wst[k]
        sc.add("sp", lambda e: e.dma_start(out=st[:, 0:ncols_f32], in_=src_ap), writes=[("wst", k)], dma=True)
        sc.add("pool", lambda e: e.tensor_copy(out=dst_bf, in_=st[:, 0:ncols_f32]), reads=[("wst", k)], writes=[dst_res])

    def layer(l):
        x_src = xT_in if l == 0 else XS[(l - 1) % 2]
        x_dst = outT if l == L - 1 else XS[l % 2]
        w_l = w_in[l]
        w_l_v = w_l.rearrange("(kc p) n -> p kc n", p=128)

        sc.add("sp", lambda e, l=l: e.dma_start(out=small, in_=small_d[l]), writes=["small"], dma=True)
        sc.add("sp", lambda e, l=l: e.dma_start(out=lamv, in_=lamv_d[l].partition_broadcast(128)), writes=["lamv"], dma=True)
        ar.reset()
        tmpl = ar.alloc([128], F32)

        sc.add("dve", lambda e: e.tensor_tensor(out=tmpl[:, 0:64], in0=lamv[:, 0:64], in1=lamv[:, 64:128], op=ALU.mult), reads=["lamv"], writes=["tmpl0"])
        sc.add("dve", lambda e: e.tensor_tensor(out=tmpl[:, 64:128], in0=lamv[:, 128:192], in1=lamv[:, 192:256], op=ALU.mult), reads=["lamv"], writes=["tmpl1"])
        sc.add("dve", lambda e: e.reduce_sum(out=lamt[:, 0:1], in_=tmpl[:, 0:64], axis=AX.X), reads=["tmpl0"], writes=["lamt0"])
        sc.add("dve", lambda e: e.reduce_sum(out=lamt[:, 1:2], in_=tmpl[:, 64:128], axis=AX.X), reads=["tmpl1"], writes=["lamt1"])
        sc.add("act", lambda e: e.activation(out=lamt[:, 2:4], in_=lamt[:, 0:2], func=AF.Exp), reads=["lamt0", "lamt1"], writes=["lamt23"])
        sc.add("dve", lambda e: e.tensor_tensor(out=lamt[:, 6:7], in0=lamt[:, 3:4], in1=lamt[:, 2:3], op=ALU.subtract), reads=["lamt23"], writes=["lamt6"])
        sc.add("dve", lambda e: e.tensor_tensor(out=lamt[:, 4:5], in0=lamt[:, 6:7], in1=col(SM_LAMINIT), op=ALU.subtract), reads=["lamt6", "small"], writes=["lamt4"])
        sc.add("dve", lambda e: e.tensor_tensor(out=lamt[:, 5:6], in0=col(SM_SUBLN), in1=col(SM_OML), op=ALU.mult), reads=["small"], writes=["lamt5"])
        sc.add("act", lambda e: e.activation(out=spc[:, 0:6], in_=col(SM_LAM, 6), func=AF.Exp, scale=-1.0), reads=["small"], writes=["spc_a"])
        sc.add("act", lambda e: e.activation(out=lamt[:, 8:14], in_=spc[:, 0:6], func=AF.Ln, bias=1.0), reads=["spc_a"], writes=["spc_b"])
        sc.add("dve", lambda e: e.tensor_scalar(out=spc[:, 0:6], in0=lamt[:, 8:14], scalar1=-8.0, scalar2=None, op0=ALU.mult),
               reads=["spc_b"], writes=["spc"])
        sc.barrier()

        ar.reset()
        hT = ar.alloc([KC, S], BF16)
        p0mark = ar.off
        xt_rot = Rot(ar, 2, [KC, TC], F32, "xt")
        sq_t = ar.alloc([KC, TC], F32)
        rstd_rot = Rot(ar, 2, [TC], F32, "rstd")
        x_src_v = x_src.rearrange("(kc p) s -> p kc s", p=128)
        for t in range(NT):
            xt, xr = xt_rot.next()
            sc.add("sp", lambda e, xt=xt, t=t: e.dma_start(out=xt, in_=x_src_v[:, :, t * TC:(t + 1) * TC]), writes=[xr], dma=True)
            sc.add("act", lambda e, xt=xt: e.activation(out=sq_t, in_=xt, func=AF.Square), reads=[xr], writes=["sq_t"])
            bk, br = bank(t % 2)

            def ssmm(e, bk=bk):
                for kc in range(KC):
                    r = e.matmul(bk, lhsT=ones_f, rhs=sq_t[:, kc, :], start=(kc == 0), stop=(kc == KC - 1))
                return r
            sc.add("pe", ssmm, reads=["sq_t", "ones_f"], writes=[br])
            rs, rr = rstd_rot.next()
            sc.add("act", lambda e, bk=bk, rs=rs: e.activation(out=rs, in_=bk, func=AF.Sqrt, scale=1.0 / DM, bias=EPS), reads=[br], writes=[rr])
            sc.add("dve", lambda e, rs=rs: e.reciprocal(out=rs, in_=rs), reads=[rr], writes=[rr])

            def hmk(e, xt=xt, rs=rs, t=t):
                for kc in range(KC):
                    r = e.scalar_tensor_tensor(out=hT[:, kc, t * TC:(t + 1) * TC], in0=xt[:, kc, :], scalar=col(SM_GPRE + kc),
                                               in1=rs, op0=ALU.mult, op1=ALU.mult)
                return r
            sc.add("dve", hmk, reads=[xr, rr, "small"], writes=[("hT", t)])
        sc.barrier()

        ar.off = p0mark
        ob_rot_a = Rot(ar, 3, [TC], BF16, "ob_a")
        ob_rot_d = Rot(ar, 3, [TC], BF16, "ob_d")
        of_rot = Rot(ar, 2, [TC], F32, "of")
        vstage = ar.alloc([4 * 32 * 128], BF16)
        hT_res = [("hT", t) for t in range(NT)]
        bank_i = [0]

        def nbank():
            b = bank_i[0] % 8
            bank_i[0] += 1
            return bank(b)

        fm_jobs = []
        for g in range(3):
            fm_jobs.append((OFF["a_q"] + g * AW, AW, A_DIL[g], "scale", AQ[g], 0.125))
        for g in range(3):
            fm_jobs.append((OFF["a_k"] + g * AW, AW, A_DIL[g], "copy", AK[g], None))
        fm_jobs.append((OFF["a_g"], AW, 1, "silu", AG, None))
        fm_jobs.append((OFF["b_x"], BW, 1, "copyf", BX, None))
        fm_jobs.append((OFF["b_g"], BW, 1, "silu", BG, None))
        fm_jobs.append((OFF["c_q"], CW, 1, "scale", CQ, 0.125))
        fm_jobs.append((OFF["c_k"], CW, 1, "copy", CK, None))
        fm_jobs.append((OFF["c_g"], CW, 1, "silu", CG, None))
        fm_jobs.append((OFF["d_cq"], 384, 1, "copyf", DLAT, None))
        fm_jobs.append((OFF["d_kr"], 64, 1, "copyf_kr", DKRAW, None))
        fm_jobs.append((OFF["d_g"], DW, 1, "silu", DG, None))
        for i in range(8):
            fm_jobs.append((OFF["gate"] + i * 512, 512, 1, "gate", GT[i * 512:(i + 1) * 512, :], i))
        tm_jobs = []
        for g in range(3):
            tm_jobs.append((OFF["a_v"] + g * AW, AW, A_DIL[g], AV[g], A_SLOTS, 64))
        tm_jobs.append((OFF["c_v"], CW, 1, CV, C_HEADS, 128))
        jobs = [("fm", j) for j in fm_jobs] + [("tm", j) for j in tm_jobs]

        def load_job(ji):
            kind, j = jobs[ji]
            c0, n = j[0], j[1]
            k = ji % 2
            st = wst[k].rearrange("p (kc n) -> p kc n", kc=KC)
            wb = wbf[k].rearrange("p (kc n) -> p kc n", kc=KC)
            if kind == "fm" and j[3] == "copyf_kr":
                sc.add("sp", lambda e: e.dma_start(out=st[:, :, 0:32], in_=w_l_v[:, :, c0:c0 + 32]), writes=[("wst", k)], dma=True)
                sc.add("sp", lambda e: e.dma_start(out=st[:, :, 32:48], in_=w_l_v[:, :, c0 + 16:c0 + 32]), writes=[("wst", k, 1)], dma=True)
                sc.add("sp", lambda e: e.dma_start(out=st[:, :, 48:64], in_=w_l_v[:, :, c0:c0 + 16]), writes=[("wst", k, 2)], dma=True)
                rd = [("wst", k), ("wst", k, 1), ("wst", k, 2)]
            else:
                sc.add("sp", lambda e: e.dma_start(out=st[:, :, 0:n], in_=w_l_v[:, :, c0:c0 + n]), writes=[("wst", k)], dma=True)
                rd = [("wst", k)]
            sc.add("pool", lambda e: e.tensor_copy(out=wb[:, :, 0:n], in_=st[:, :, 0:n]), reads=rd, writes=[("wbf", k)])

        def rhs_view(d, kc, t):
            if d == 1:
                return [(hT[:, kc, t * TC:(t + 1) * TC], None)]
            hv = hT[:, kc, :].rearrange("p (l r) -> p r l", r=d)
            if d == 4:
                return [(hv[:, t // 2, (t % 2) * 512:(t % 2) * 512 + 512], None)]
            return [(hv[:, 2 * t:2 * t + 2, :], 2)]

        def lhs_view(d, kc, b):
            if d == 1:
                return hT[:, kc, b * 128:(b + 1) * 128]
            hv = hT[:, kc, :].rearrange("p (l r) -> p r l", r=d)
            nbs = 32 // d
            return hv[:, b // nbs, (b % nbs) * 128:(b % nbs) * 128 + 128]

        def compute_fm(ji):
            _, (c0, n, d, kind, dst, extra) = jobs[ji]
            k = ji % 2
            wb = wbf[k].rearrange("p (kc n) -> p kc n", kc=KC)
            ncb = (n + 127) // 128
            for cb in range(ncb):
                m = min(128, n - cb * 128)
                for t in range(NT):
                    bk, bres = nbank()

                    def mm(e, bk=bk, cb=cb, m=m, t=t):
                        r = None
                        for kc in range(KC):
                            (rv, a), = rhs_view(d, kc, t)
                            o = bk[0:m, :]
                            if a is not None:
                                o = o.rearrange("p (a b) -> p a b", a=a)
                            r = e.matmul(o, lhsT=wb[:, kc, cb * 128:cb * 128 + m], rhs=rv, start=(kc == 0), stop=(kc == KC - 1))
                        return r
                    sc.add("pe", mm, reads=[("wbf", k)] + hT_res, writes=[bres])
                    drows = dst[cb * 128:cb * 128 + m, t * TC:(t + 1) * TC]
                    if kind in ("silu", "gate"):
                        ot, ores = ob_rot_a.next()
                        if kind == "silu":
                            sc.add("act", lambda e, ot=ot, bk=bk, m=m: e.activation(out=ot[0:m, :], in_=bk[0:m, :], func=AF.Silu),
                                   reads=[bres], writes=[ores])
                        else:
                            bcol = SM_BGATE + (extra // 2) * 8 + (extra % 2) * 4 + cb
                            sc.add("act", lambda e, ot=ot, bk=bk, bcol=bcol: e.activation(out=ot, in_=bk, func=AF.Sigmoid, bias=col(bcol)),
                                   reads=[bres, "small"], writes=[ores])
                        sc.add("act", lambda e, ot=ot, drows=drows, m=m: e.dma_start(out=drows, in_=ot[0:m, :]), reads=[ores], dma=True)
                    elif kind in ("copy", "scale"):
                        ot, ores = ob_rot_d.next()
                        if kind == "copy":
                            sc.add("dve", lambda e, ot=ot, bk=bk, m=m: e.tensor_copy(out=ot[0:m, :], in_=bk[0:m, :]), reads=[bres], writes=[ores])
                        else:
                            sc.add("dve", lambda e, ot=ot, bk=bk, m=m: e.tensor_scalar(out=ot[0:m, :], in0=bk[0:m, :], scalar1=extra, scalar2=None, op0=ALU.mult),
                                   reads=[bres], writes=[ores])
                        sc.add("sp", lambda e, ot=ot, drows=drows, m=m: e.dma_start(out=drows, in_=ot[0:m, :]), reads=[ores], dma=True)
                    else:
                        ot, ores = of_rot.next()
                        sc.add("dve", lambda e, ot=ot, bk=bk, m=m: e.tensor_copy(out=ot[0:m, :], in_=bk[0:m, :]), reads=[bres], writes=[ores])
                        sc.add("sp", lambda e, ot=ot, drows=drows, m=m: e.dma_start(out=drows, in_=ot[0:m, :]), reads=[ores], dma=True)

        def compute_tm(ji):
            _, (c0, n, d, dst, nh, hd) = jobs[ji]
            k = ji % 2
            wb = wbf[k].rearrange("p (kc n) -> p kc n", kc=KC)
            vs = vstage[:, 0:nh * 32 * hd].rearrange("p (h b d) -> p h b d", h=nh, b=32)
            for b in range(32):
                bk, bres = nbank()

                def mm(e, bk=bk, b=b):
                    r = None
                    for kc in range(KC):
                        r = e.matmul(bk[:, 0:n], lhsT=lhs_view(d, kc, b), rhs=wb[:, kc, 0:n], start=(kc == 0), stop=(kc == KC - 1))
                    return r
                sc.add("pe", mm, reads=[("wbf", k)] + hT_res, writes=[bres])
                sc.add("dve", lambda e, bk=bk, b=b: e.tensor_copy(out=vs[:, :, b, :], in_=bk[:, 0:n].rearrange("p (h d) -> p h d", h=nh)),
                       reads=[bres], writes=[("vstage", b)])
            sc.add("sp", lambda e: e.dma_start(out=dst, in_=vstage[:, 0:nh * 32 * hd]), reads=[("vstage", b) for b in range(32)], dma=True)

        load_job(0)
        for ji in range(len(jobs)):
            if ji + 1 < len(jobs):
                load_job(ji + 1)
            if jobs[ji][0] == "fm":
                compute_fm(ji)
            else:
                compute_tm(ji)
        sc.barrier()

        ar.reset()
        wuqa = ar.alloc([2, D_HEADS * 96], BF16)
        wuqb = ar.alloc([2, D_HEADS * 96], BF16)
        wukvk = ar.alloc([DW], BF16)
        wukvv = ar.alloc([DW], BF16)
        load_cast(wuqa_d[l], 2 * D_HEADS * 96, wuqa.rearrange("p a b -> p (a b)"), "wuqa")
        load_cast(wuqb_d[l], 2 * D_HEADS * 96, wuqb.rearrange("p a b -> p (a b)"), "wuqb")
        load_cast(wukvk_d[l], DW, wukvk, "wukvk")
        load_cast(wukvv_d[l], DW, wukvv, "wukvv")
        lat_rot = Rot(ar, 2, [3, TC], F32, "lat")
        kra_rot = Rot(ar, 2, [TC], F32, "kra")
        krb_rot = Rot(ar, 2, [TC], F32, "krb")
        cc_rot = Rot(ar, 2, [TC], F32, "cc")
        ss_rot = Rot(ar, 2, [TC], F32, "ss")
        sq2 = ar.alloc([3, TC], F32)
        rq_rot = Rot(ar, 2, [TC], F32, "rq")
        rkv_rot = Rot(ar, 2, [TC], F32, "rkv")
        cqn_rot = Rot(ar, 2, [3, TC], BF16, "cqn")
        qd_rot = Rot(ar, 3, [TC], BF16, "qd")
        kd_rot = Rot(ar, 3, [TC], BF16, "kd")
        t1_rot = Rot(ar, 2, [TC], F32, "t1")
        t2_rot = Rot(ar, 2, [TC], F32, "t2")
        krr_rot = Rot(ar, 2, [TC], BF16, "krr")
        dvst = ar.alloc([D_HEADS, 32, 64], BF16)
        DLv = DLAT.rearrange("(c p) s -> p c s", p=128)
        for t in range(NT):
            tsl = slice(t * TC, (t + 1) * TC)
            lat, latr = lat_rot.next()
            kra, krar = kra_rot.next()
            krb, krbr = krb_rot.next()
            cct, ccr = cc_rot.next()
            sst, ssr = ss_rot.next()
            sc.add("sp", lambda e, lat=lat, tsl=tsl: e.dma_start(out=lat, in_=DLv[:, :, tsl]), writes=[latr], dma=True)
            sc.add("sp", lambda e, kra=kra, tsl=tsl: e.dma_start(out=kra[64:96, :], in_=DKRAW[0:32, tsl]), writes=[krar], dma=True)
            sc.add("sp", lambda e, krb=krb, tsl=tsl: e.dma_start(out=krb[64:96, :], in_=DKRAW[32:64, tsl]), writes=[krbr], dma=True)
            sc.add("sp", lambda e, cct=cct, tsl=tsl: e.dma_start(out=cct[64:96, :], in_=ropec_d[:, tsl]), writes=[ccr], dma=True)
            sc.add("sp", lambda e, sst=sst, tsl=tsl: e.dma_start(out=sst[64:96, :], in_=ropes_d[:, tsl]), writes=[ssr], dma=True)
            sc.add("act", lambda e, lat=lat: e.activation(out=sq2, in_=lat, func=AF.Square), reads=[latr], writes=["sq2"])
            b0, b0r = bank(0)
            b1, b1r = bank(1)

            def ssq(e, b0=b0, b1=b1):
                e.matmul(b0, lhsT=ones_f, rhs=sq2[:, 0, :], start=True, stop=False)
                e.matmul(b0, lhsT=ones_f, rhs=sq2[:, 1, :], start=False, stop=True)
                return e.matmul(b1, lhsT=ones_f, rhs=sq2[:, 2, :], start=True, stop=True)
            sc.add("pe", ssq, reads=["sq2", "ones_f"], writes=[b0r, b1r])
            rq, rqr = rq_rot.next()
            rkv, rkvr = rkv_rot.next()

            def rsq(e, rq=rq, rkv=rkv, b0=b0, b1=b1):
                e.activation(out=rq, in_=b0, func=AF.Sqrt, scale=1.0 / 256, bias=EPS)
                return e.activation(out=rkv, in_=b1, func=AF.Sqrt, scale=1.0 / 128, bias=EPS)
            sc.add("act", rsq, reads=[b0r, b1r], writes=[rqr, rkvr])
            cqn, cqnr = cqn_rot.next()

            def nrm(e, rq=rq, rkv=rkv, lat=lat, cqn=cqn):
                e.reciprocal(out=rq, in_=rq)
                e.reciprocal(out=rkv, in_=rkv)
                e.scalar_tensor_tensor(out=cqn[:, 0, :], in0=lat[:, 0, :], scalar=col(SM_QN), in1=rq, op0=ALU.mult, op1=ALU.mult)
                e.scalar_tensor_tensor(out=cqn[:, 1, :], in0=lat[:, 1, :], scalar=col(SM_QN + 1), in1=rq, op0=ALU.mult, op1=ALU.mult)
                return e.scalar_tensor_tensor(out=cqn[:, 2, :], in0=lat[:, 2, :], scalar=col(SM_KVN), in1=rkv, op0=ALU.mult, op1=ALU.mult)
            sc.add("dve", nrm, reads=[rqr, rkvr, latr, "small"], writes=[cqnr, rqr, rkvr])
            t1, t1r = t1_rot.next()
            t2, t2r = t2_rot.next()
            krr, krrr = krr_rot.next()

            def krope(e, kra=kra, krb=krb, cct=cct, sst=sst, t1=t1, t2=t2, krr=krr):
                e.tensor_tensor(out=t1[64:96, :], in0=krb[64:96, :], in1=sst[64:96, :], op=ALU.mult)
                e.tensor_tensor(out=t2[64:96, :], in0=kra[64:96, :], in1=cct[64:96, :], op=ALU.mult)
                return e.tensor_tensor(out=krr[64:96, :], in0=t1[64:96, :], in1=t2[64:96, :], op=ALU.add)
            sc.add("dve", krope, reads=[krar, krbr, ccr, ssr], writes=[t1r, t2r, krrr])
            sc.add("sp", lambda e, krr=krr, tsl=tsl: e.dma_start(out=DKR[:, tsl], in_=krr[64:96, :]), reads=[krrr], dma=True)
            for h in range(D_HEADS):
                ba, bar_ = bank(2 + (h % 2) * 3)
                bb, bbr = bank(3 + (h % 2) * 3)
                bkk, bkr = bank(4 + (h % 2) * 3)

                def upq(e, ba=ba, bb=bb, bkk=bkk, h=h, cqn=cqn):
                    for c in range(2):
                        e.matmul(ba[0:96, :], lhsT=wuqa[:, c, h * 96:(h + 1) * 96], rhs=cqn[:, c, :], start=(c == 0), stop=(c == 1))
                    for c in range(2):
                        e.matmul(bb[0:96, :], lhsT=wuqb[:, c, h * 96:(h + 1) * 96], rhs=cqn[:, c, :], start=(c == 0), stop=(c == 1))
                    return e.matmul(bkk[0:64, :], lhsT=wukvk[:, h * 64:(h + 1) * 64], rhs=cqn[:, 2, :], start=True, stop=True)
                sc.add("pe", upq, reads=[cqnr, "wuqa", "wuqb", "wukvk"], writes=[bar_, bbr, bkr])
                qd, qdr = qd_rot.next()
                kd, kdr = kd_rot.next()
                t1, t1r = t1_rot.next()
                t2, t2r = t2_rot.next()

                def qrope(e, ba=ba, bb=bb, qd=qd, t1=t1, t2=t2, cct=cct, sst=sst):
                    e.tensor_copy(out=qd[0:64, :], in_=ba[0:64, :])
                    e.tensor_tensor(out=t1[64:96, :], in0=bb[64:96, :], in1=sst[64:96, :], op=ALU.mult)
                    e.tensor_tensor(out=t2[64:96, :], in0=ba[64:96, :], in1=cct[64:96, :], op=ALU.mult)
                    return e.tensor_tensor(out=qd[64:96, :], in0=t1[64:96, :], in1=t2[64:96, :], op=ALU.add)
                sc.add("dve", qrope, reads=[bar_, bbr, ccr, ssr], writes=[qdr, t1r, t2r])
                sc.add("sp", lambda e, qd=qd, h=h, tsl=tsl: e.dma_start(out=DQ[h * 96:(h + 1) * 96, tsl], in_=qd[0:96, :]), reads=[qdr], dma=True)
                sc.add("act", lambda e, kd=kd, bkk=bkk: e.activation(out=kd[0:64, :], in_=bkk[0:64, :], func=AF.Copy), reads=[bkr], writes=[kdr])
                sc.add("act", lambda e, kd=kd, h=h, tsl=tsl: e.dma_start(out=DK[h * 64:(h + 1) * 64, tsl], in_=kd[0:64, :]), reads=[kdr], dma=True)
            for tb in range(4):
                bv, bvr = bank(tb % 2)
                b = t * 4 + tb
                sc.add("pe", lambda e, bv=bv, tb=tb, cqn=cqn: e.matmul(bv[:, 0:DW], lhsT=cqn[:, 2, tb * 128:(tb + 1) * 128], rhs=wukvv, start=True, stop=True),
                       reads=[cqnr, "wukvv"], writes=[bvr])
                sc.add("act", lambda e, bv=bv, b=b: e.activation(out=dvst[:, :, b, :], in_=bv[:, 0:DW].rearrange("p (h d) -> p h d", h=D_HEADS), func=AF.Copy),
                       reads=[bvr], writes=[("dvst", b)])
        sc.add("sp", lambda e: e.dma_start(out=DV, in_=dvst.rearrange("p h b d -> p (h b d)")), reads=[("dvst", b) for b in range(32)], dma=True)
        sc.barrier()

        ar.reset()
        ma = ar.alloc([3 * A_SLOTS, 384], F32)
        sc.add("sp", lambda e: e.dma_start(out=ma.rearrange("p a b -> p (a b)"), in_=ma_d), writes=["ma"], dma=True)
        acc = ar.alloc([S], F32)
        kt_rot = Rot(ar, 2, [S], BF16, "kt")
        qt_rot = Rot(ar, 2, [S], BF16, "qt")
        vraw_rot = Rot(ar, 2, [32 * 64], BF16, "vraw")
        vt_tiles = [ar.alloc([32, 128], BF16) for _ in range(2)]
        for k in range(2):
            sc.add("pool", lambda e, k=k: e.memset(vt_tiles[k][:, :, 64:128], 1.0), writes=[("vt1", k)])
        pf_rot = Rot(ar, 3, [384], F32, "pf")
        pb_rot = Rot(ar, 4, [384], BF16, "pb")
        ag_rot = Rot(ar, 2, [TC], BF16, "ag")
        rz_rot = Rot(ar, 2, [TC], F32, "rz")
        yf_rot = Rot(ar, 2, [TC], F32, "yf")
        yb_rot = Rot(ar, 2, [TC], BF16, "yb")
        groups = [(s_, g) for s_ in range(A_SLOTS) for g in range(3)]
        gctx = {}

        def a_load(gi):
            s_, g = groups[gi]
            kt, ktr = kt_rot.next()
            qt, qtr = qt_rot.next()
            vraw, vrr = vraw_rot.next()
            vk = gi % 2
            vt, vtr = vt_tiles[vk], ("vt", vk)
            sc.add("sp", lambda e: e.dma_start(out=kt[0:64, :], in_=AK[g][s_ * 64:(s_ + 1) * 64, :]), writes=[ktr], dma=True)
            sc.add("sp", lambda e: e.dma_start(out=qt[0:64, :], in_=AQ[g][s_ * 64:(s_ + 1) * 64, :]), writes=[qtr], dma=True)
            sc.add("sp", lambda e: e.dma_start(out=vraw, in_=AV[g][:, s_ * 2048:(s_ + 1) * 2048]), writes=[vrr], dma=True)
            sc.add("pool", lambda e: e.tensor_copy(out=vt[:, :, 0:64], in_=vraw.rearrange("p (b d) -> p b d", b=32)),
                   reads=[vrr, ("vt1", vk)], writes=[vtr])
            gctx[gi] = dict(kt=kt, ktr=ktr, qt=qt, qtr=qtr, vt=vt, vtr=vtr)

        items = []
        for gi, (s_, g) in enumerate(groups):
            for qb in range(32):
                items.append(dict(gi=gi, s_=s_, g=g, qb=qb, idx=len(items)))

        def a_S(it):
            gi, qb, g = it["gi"], it["qb"], it["g"]
            if qb == 0 and gi == 0:
                a_load(0)
            if qb == 16 and gi + 1 < len(groups):
                a_load(gi + 1)
            c = gctx[gi]
            d = A_DIL[g]
            nbs = 32 // d
            lb = qb % nbs
            js = [j for j in range(3) if 0 <= lb - 1 + j < nbs]
            psS, psSr = bank(it["idx"] % 3)
            kt, qt = c["kt"], c["qt"]

            def f(e):
                r = None
                for j in js:
                    kb = qb - 1 + j
                    r = e.matmul(psS[:, j * 128:(j + 1) * 128], lhsT=kt[0:64, kb * 128:(kb + 1) * 128], rhs=qt[0:64, qb * 128:(qb + 1) * 128],
                                 start=True, stop=True)
                return r
            sc.add("pe", f, reads=[c["ktr"], c["qtr"]], writes=[psSr])
            it.update(js=js, psS=psS, psSr=psSr, d=d, nbs=nbs, lb=lb)

        def a_E(it):
            js, psS = it["js"], it["psS"]
            c0, c1 = js[0] * 128, (js[-1] + 1) * 128
            pf, pfr = pf_rot.next()
            sc.add("act", lambda e: e.activation(out=pf[:, c0:c1], in_=psS[:, c0:c1], func=AF.Exp), reads=[it["psSr"]], writes=[pfr])
            it.update(pf=pf, pfr=pfr, c0=c0, c1=c1)

        def a_M(it):
            pf, c0, c1 = it["pf"], it["c0"], it["c1"]
            mi = it["g"] * A_SLOTS + it["s_"]
            pb, pbr = pb_rot.next()
            sc.add("dve", lambda e: e.tensor_tensor(out=pb[:, c0:c1], in0=pf[:, c0:c1], in1=ma[:, mi, c0:c1], op=ALU.mult),
                   reads=[it["pfr"], "ma"], writes=[pbr])
            it.update(pb=pb, pbr=pbr)

        def a_P(it):
            c = gctx[it["gi"]]
            vt, pb, js, qb = c["vt"], it["pb"], it["js"], it["qb"]
            psO, psOr = bank(3 + it["idx"] % 3)

            def pv(e):
                r = None
                for j in js:
                    kb = qb - 1 + j
                    r = e.matmul(psO[:, 0:128], lhsT=vt[:, kb, :], rhs=pb[:, j * 128:(j + 1) * 128], start=(j == js[0]), stop=(j == js[-1]))
                return r
            sc.add("pe", pv, reads=[it["pbr"], c["vtr"]], writes=[psOr])
            it.update(psO=psO, psOr=psOr)

        def a_A(it):
            d, nbs, qb, g, s_ = it["d"], it["nbs"], it["qb"], it["g"], it["s_"]
            psO = it["psO"]
            rr_, lb_ = qb // nbs, qb % nbs
            if d == 1:
                av = acc[:, qb * 128:(qb + 1) * 128]
            else:
                av = acc.rearrange("p (l r) -> p r l", r=d)[:, rr_, lb_ * 128:(lb_ + 1) * 128]
            if g == 0:
                sc.add("dve", lambda e: e.tensor_copy(out=av, in_=psO[:, 0:128]), reads=[it["psOr"]], writes=["acc"])
            else:
                sc.add("dve", lambda e: e.tensor_tensor(out=av, in0=psO[:, 0:128], in1=av, op=ALU.add), reads=[it["psOr"], "acc"], writes=["acc"])
            if g == 2 and qb == 31:
                for t in range(NT):
                    a_epi(s_, t)

        def a_epi(s_, t):
            tsl = slice(t * TC, (t + 1) * TC)
            agt, agr = ag_rot.next()
            rz, rzr = rz_rot.next()
            yf, yfr = yf_rot.next()
            yb, ybr = yb_rot.next()
            sc.add("sp", lambda e: e.dma_start(out=agt[0:64, :], in_=AG[s_ * 64:(s_ + 1) * 64, tsl]), writes=[agr], dma=True)

            def epi(e):
                e.reciprocal(out=rz[0:64, :], in_=acc[64:128, tsl])
                e.tensor_tensor(out=yf[0:64, :], in0=acc[0:64, tsl], in1=rz[0:64, :], op=ALU.mult)
                return e.tensor_tensor(out=yb[0:64, :], in0=yf[0:64, :], in1=agt[0:64, :], op=ALU.mult)
            sc.add("dve", epi, reads=["acc", agr], writes=[rzr, yfr, ybr])
            sc.add("sp", lambda e: e.dma_start(out=Y[YA0 + s_ * 64:YA0 + (s_ + 1) * 64, tsl], in_=yb[0:64, :]), reads=[ybr], dma=True)

        run_pipeline(items, [a_S, a_E, a_M, a_P, a_A], [0, 1, 2, 3, 4])
        sc.barrier()

        ar.reset()
        lw = ar.alloc([4 * NBC, 128], BF16)
        load_cast(lruw_d[l], 4 * NBC * 128, lw.rearrange("p a b -> p (a b)"), "lw")
        xp = ar.alloc([S + 4], F32)
        xc = ar.alloc([S], F32)
        Rb = ar.alloc([S], F32)
        Ib = ar.alloc([S], F32)
        Ab_ = ar.alloc([S], F32)
        Hf = ar.alloc([S], F32)
        Hb = ar.alloc([S], F32)
        xcb = ar.alloc([S], BF16)
        bgt = ar.alloc([S], BF16)
        sc.add("dve", lambda e: e.memset(xp[:, 0:1], 0.0), writes=["xp_pad0"])
        sc.add("dve", lambda e: e.memset(xp[:, S + 1:S + 4], 0.0), writes=["xp_pad1"])
        for c in range(NBC):
            sc.add("sp", lambda e, c=c: e.dma_start(out=xp[:, 1:S + 1], in_=BX[c * 128:(c + 1) * 128, :]), writes=["xp"], dma=True)
            sc.add("sp", lambda e, c=c: e.dma_start(out=bgt, in_=BG[c * 128:(c + 1) * 128, :]), writes=["bgt"], dma=True)

            def conv(e, c=c):
                e.tensor_scalar(out=xc, in0=xp[:, 0:S], scalar1=col(SM_CONVW + 0 * NBC + c), scalar2=col(SM_CONVB + c), op0=ALU.mult, op1=ALU.add)
                for j in range(1, 4):
                    r = e.scalar_tensor_tensor(out=xc, in0=xp[:, j:j + S], scalar=col(SM_CONVW + j * NBC + c), in1=xc, op0=ALU.mult, op1=ALU.add)
                return r
            sc.add("dve", conv, reads=["xp", "xp_pad0", "xp_pad1", "small"], writes=["xc"])
            sc.add("pool", lambda e: e.tensor_copy(out=xcb, in_=xc), reads=["xc"], writes=["xcb"])
            for dr in range(2):
                for t in range(NT):
                    tsl = slice(t * TC, (t + 1) * TC)
                    bR, bRr = bank((2 * t) % 8)
                    bI, bIr = bank((2 * t + 1) % 8)

                    def gmm(e, bR=bR, bI=bI, tsl=tsl, c=c, dr=dr):
                        e.matmul(bR, lhsT=lw[:, (0 * 2 + dr) * NBC + c, :], rhs=xcb[:, tsl], start=True, stop=True)
                        return e.matmul(bI, lhsT=lw[:, (1 * 2 + dr) * NBC + c, :], rhs=xcb[:, tsl], start=True, stop=True)
                    sc.add("pe", gmm, reads=["xcb", "lw"], writes=[bRr, bIr])
                    sc.add("dve", lambda e, bR=bR, tsl=tsl, c=c, dr=dr: e.tensor_scalar(out=Rb[:, tsl], in0=bR, scalar1=col(SM_BR + dr * NBC + c), scalar2=None, op0=ALU.add),
                           reads=[bRr, "small"], writes=[("Rb", t)])
                    sc.add("dve", lambda e, bI=bI, tsl=tsl, c=c, dr=dr: e.tensor_scalar(out=Ib[:, tsl], in0=bI, scalar1=col(SM_BI + dr * NBC + c), scalar2=None, op0=ALU.add),
                           reads=[bIr, "small"], writes=[("Ib", t)])
                Rres = [("Rb", t) for t in range(NT)]
                Ires = [("Ib", t) for t in range(NT)]

                def gates(e, c=c, dr=dr):
                    e.activation(out=Rb, in_=Rb, func=AF.Sigmoid)
                    e.activation(out=Ib, in_=Ib, func=AF.Sigmoid)
                    e.activation(out=Ab_, in_=Rb, func=AF.Exp, scale=spc[:, dr * NBC + c:dr * NBC + c + 1])
                    e.activation(out=Rb, in_=Ab_, func=AF.Square)
                    return e.activation(out=Rb, in_=Rb, func=AF.Sqrt, scale=-1.0, bias=1.0)
                sc.add("act", gates, reads=Rres + Ires + ["spc", "Hscan%d" % dr], writes=["Rw", "Ig", "Ab"])
                Hd = Hf if dr == 0 else Hb

                def premul(e):
                    e.tensor_tensor(out=Ib, in0=Ib, in1=xc, op=ALU.mult)
                    return e.tensor_tensor(out=Ib, in0=Ib, in1=Rb, op=ALU.mult)
                sc.add("dve", premul, reads=["Rw", "Ig", "Ab", "xc"], writes=["U", "Hscan%d" % (1 - dr)] + Rres + Ires)
                order = list(range(NT)) if dr == 0 else list(range(NT - 1, -1, -1))
                for oi, t in enumerate(order):
                    tsl = slice(t * TC, (t + 1) * TC)
                    if oi == 0:
                        init = 0.0
                    elif dr == 0:
                        init = Hd[:, t * TC - 1:t * TC]
                    else:
                        init = Hd[:, (t + 1) * TC:(t + 1) * TC + 1]
                    if dr == 0:
                        sc.add("dve", lambda e, Hd=Hd, tsl=tsl, init=init: e.tensor_tensor_scan(out=Hd[:, tsl], data0=Ab_[:, tsl], data1=Ib[:, tsl], initial=init, op0=ALU.mult, op1=ALU.add),
                               reads=["U", "Ab"] + ([("Hc", dr, order[oi - 1])] if oi else []), writes=[("Hc", dr, t)])
                    else:
                        sc.add("dve", lambda e, Hd=Hd, tsl=tsl, init=init: e.tensor_tensor_scan(out=Hd[:, tsl][:, ::-1], data0=Ab_[:, tsl][:, ::-1], data1=Ib[:, tsl][:, ::-1], initial=init, op0=ALU.mult, op1=ALU.add),
                               reads=["U", "Ab"] + ([("Hc", dr, order[oi - 1])] if oi else []), writes=[("Hc", dr, t)])

            def fin(e):
                e.tensor_tensor(out=Hf, in0=Hf, in1=Hb, op=ALU.add)
                return e.tensor_tensor(out=bgt, in0=Hf, in1=bgt, op=ALU.mult)
            sc.add("dve", fin, reads=[("Hc", 0, NT - 1), ("Hc", 1, 0), "bgt"], writes=["bgt"])
            sc.add("sp", lambda e, c=c: e.dma_start(out=Y[YB0 + c * 128:YB0 + (c + 1) * 128, :], in_=bgt), reads=["bgt"], dma=True)
        sc.barrier()

        ar.reset()
        cs = c_slopes() if not SPLIT else [min(c_slopes()[h], c_slopes()[h + 2]) for h in range(2)]
        ka_rot = Rot(ar, 4, [S], BF16, "ka")
        vc_rot = Rot(ar, 2, [32 * 128], BF16, "vc")
        qb_rot = Rot(ar, 4, [TC], BF16, "qbf")
        qa_rot = Rot(ar, 4, [TC], BF16, "qaf")
        cg_rot = Rot(ar, 2, [TC], BF16, "cg")
        pbc_rot = Rot(ar, 5, [TC], BF16, "pbc")
        pfc_rot = Rot(ar, 2, [128], F32, "pfc")
        eo_rot = Rot(ar, 4, [TC], F32, "eo")
        ez_rot = Rot(ar, 4, [TC], F32, "ez")
        e_o1 = ar.alloc([TC], F32)
        e_sq = ar.alloc([TC], F32)
        e_sd = ar.alloc([TC], F32)
        ybc_rot = Rot(ar, 2, [TC], BF16, "ybc")
        hctx, qctx = {}, {}

        def c_load_head(h):
            kas = []
            for c in range(2):
                ka, kar = ka_rot.next()
                kas.append((ka, kar))
                sc.add("sp", lambda e, ka=ka, c=c: e.dma_start(out=ka[0:64, :], in_=CK[(h * 2 + c) * 64:(h * 2 + c + 1) * 64, :]), writes=[kar], dma=True)
                sc.add("sp", lambda e, ka=ka: e.dma_start(out=ka[64:68, :], in_=caugk_d[h]), writes=[(kar, "aug")], dma=True)
            vc, vcr = vc_rot.next()
            sc.add("sp", lambda e: e.dma_start(out=vc, in_=CV[:, h * 4096:(h + 1) * 4096]), writes=[vcr], dma=True)
            hctx[h] = dict(kas=kas, vcv=vc.rearrange("p (b d) -> p b d", b=32), vcr=vcr)

        def c_load_chunk(h, qc):
            tsl = slice(qc * TC, (qc + 1) * TC)
            qs = []
            for c in range(2):
                qbf, qbr = qb_rot.next()
                qaf, qar = qa_rot.next()
                src = CQ[(h * 2 + c) * 64:(h * 2 + c + 1) * 64, tsl]
                sc.add("sp", lambda e, qbf=qbf, src=src: e.dma_start(out=qbf[0:64, :], in_=src), writes=[qbr], dma=True)
                sc.add("sp", lambda e, qbf=qbf: e.dma_start(out=qbf[64:68, :], in_=caugq_d[h, 0]), writes=[(qbr, "aug")], dma=True)
                sc.add("sp", lambda e, qaf=qaf, src=src: e.dma_start(out=qaf[0:64, :], in_=src), writes=[qar], dma=True)
                sc.add("sp", lambda e, qaf=qaf: e.dma_start(out=qaf[64:68, :], in_=caugq_d[h, 1]), writes=[(qar, "aug")], dma=True)
                qs.append((qbf, qbr, qaf, qar))
            cgt, cgr = cg_rot.next()
            sc.add("sp", lambda e: e.dma_start(out=cgt, in_=CG[h * 128:(h + 1) * 128, tsl]), writes=[cgr], dma=True)
            qctx[(h, qc)] = dict(qs=qs, cgt=cgt, cgr=cgr)

        items = []
        for h in range(C_HEADS):
            m = cs[h]
            for qc in range(NT):
                i0 = qc * TC
                for c in range(2):
                    kbs = []
                    for kb in range(32):
                        j0 = kb * 128
                        if j0 + 128 <= i0:
                            if m * (i0 - (j0 + 127)) > SKIP_T:
                                continue
                        elif j0 >= i0 + TC:
                            if m * (j0 - (i0 + TC - 1)) > SKIP_T:
                                continue
                        kbs.append(kb)
                    for ii, kb in enumerate(kbs):
                        items.append(dict(h=h, qc=qc, c=c, kb=kb, ii=ii, first=(ii == 0), last=(ii == len(kbs) - 1), idx=len(items),
                                          chunk_first=(c == 0 and ii == 0)))
        psOb = [bank(3), bank(5)]
        psZb = [bank(4), bank(6)]

        def c_S(it):
            h, qc, c, kb = it["h"], it["qc"], it["c"], it["kb"]
            if it["chunk_first"] and h == 0 and qc == 0:
                c_load_head(0)
                c_load_chunk(0, 0)
            if c == 0 and it["ii"] == 4:
                if qc == 0 and h + 1 < C_HEADS:
                    c_load_head(h + 1)
                nh, nq = (h, qc + 1) if qc + 1 < NT else (h + 1, 0)
                if nh < C_HEADS:
                    c_load_chunk(nh, nq)
            m = cs[h]
            i0 = qc * TC
            ka, kar = hctx[h]["kas"][c]
            qbf, qbr, qaf, qar = qctx[(h, qc)]["qs"][c]
            psS, psSr = bank(it["idx"] % 3)
            j0 = kb * 128
            rds = [kar, (kar, "aug"), qbr, (qbr, "aug"), qar, (qar, "aug")]
            pbt, pbr = pbc_rot.next()
            it.update(pbt=pbt, pbr=pbr)
            if j0 + 128 <= i0 or j0 >= i0 + TC:
                before = j0 + 128 <= i0
                qq = qbf if before else qaf
                bcol = h * CB_W + abs(i0 - j0) // 128
                sc.add("pe", lambda e: e.matmul(psS, lhsT=ka[0:68, j0:j0 + 128], rhs=qq[0:68, :], start=True, stop=True), reads=rds, writes=[psSr])
                sc.add("act", lambda e: e.activation(out=pbt, in_=psS, func=AF.Exp, bias=cbias[:, bcol:bcol + 1]), reads=[psSr, "cbias"], writes=[pbr])
            else:
                sb = (j0 - i0) // 128
                ca, cb_, cc_ = sb * 128, (sb + 1) * 128, TC

                def mmd(e):
                    r = None
                    if sb > 0:
                        r = e.matmul(psS[:, 0:ca], lhsT=ka[0:68, j0:j0 + 128], rhs=qaf[0:68, 0:ca], start=True, stop=True)
                    r = e.matmul(psS[:, ca:cb_], lhsT=ka[0:64, j0:j0 + 128], rhs=qbf[0:64, ca:cb_], start=True, stop=True)
                    if sb < 3:
                        r = e.matmul(psS[:, cb_:cc_], lhsT=ka[0:68, j0:j0 + 128], rhs=qbf[0:68, cb_:cc_], start=True, stop=True)
                    return r
                sc.add("pe", mmd, reads=rds, writes=[psSr])
                pfc, pfr = pfc_rot.next()

                def actd(e):
                    if sb > 0:
                        e.activation(out=pbt[:, 0:ca], in_=psS[:, 0:ca], func=AF.Exp, bias=cbias[:, h * CB_W + sb:h * CB_W + sb + 1])
                    if sb < 3:
                        e.activation(out=pbt[:, cb_:cc_], in_=psS[:, cb_:cc_], func=AF.Exp, bias=cbias[:, h * CB_W + 32 + sb:h * CB_W + 33 + sb])
                    return e.activation(out=pfc, in_=psS[:, ca:cb_], func=AF.Exp)
                sc.add("act", actd, reads=[psSr, "cbias"], writes=[pbr, (pbr, "a"), pfr])
                sc.add("dve", lambda e: e.tensor_tensor(out=pbt[:, ca:cb_], in0=pfc, in1=mdiag[:, h, :], op=ALU.mult), reads=[pfr, "mdiag", (pbr, "a")], writes=[pbr])

        def c_P(it):
            h, qc, c, kb = it["h"], it["qc"], it["c"], it["kb"]
            o_, or_ = psOb[c]
            z_, zr_ = psZb[c]
            vcv, vcr = hctx[h]["vcv"], hctx[h]["vcr"]
            pbt, first, last = it["pbt"], it["first"], it["last"]

            def f(e):
                e.matmul(o_, lhsT=vcv[:, kb, :], rhs=pbt, start=first, stop=last)
                return e.matmul(z_, lhsT=ones_bf, rhs=pbt, start=first, stop=last)
            sc.add("pe", f, reads=[it["pbr"], vcr, "ones_bf"], writes=[or_, zr_])
            if last:
                eo, eor = eo_rot.next()
                ez, ezr = ez_rot.next()
                sc.add("dve", lambda e: e.tensor_copy(out=eo, in_=o_), reads=[or_], writes=[eor])
                sc.add("dve", lambda e: e.tensor_copy(out=ez, in_=z_), reads=[zr_], writes=[ezr])
                qctx[(h, qc)]["ev%d" % c] = (eo, eor, ez, ezr)
                if c == 1:
                    c_epi(h, qc)

        def c_epi(h, qc):
            tsl = slice(qc * TC, (qc + 1) * TC)
            q = qctx[(h, qc)]
            eo0, eo0r, ez0, ez0r = q["ev0"]
            eo1, eo1r, ez1, ez1r = q["ev1"]
            cgt, cgr = q["cgt"], q["cgr"]

            def ep1(e):
                e.reciprocal(out=ez0, in_=ez0)
                e.reciprocal(out=ez1, in_=ez1)
                e.tensor_tensor(out=eo0, in0=eo0, in1=ez0, op=ALU.mult)
                e.tensor_tensor(out=eo1, in0=eo1, in1=ez1, op=ALU.mult)
                return e.scalar_tensor_tensor(out=eo0, in0=eo1, scalar=lamt[:, 4:5], in1=eo0, op0=ALU.mult, op1=ALU.add)
            sc.add("dve", ep1, reads=[eo0r, eo1r, ez0r, ez1r], writes=[eo0r, eo1r, ez0r, ez1r])
            sc.add("act", lambda e: e.activation(out=e_sq, in_=eo0, func=AF.Square), reads=[eo0r], writes=["e_sq"])
            bn, bnr = bank(7)
            sc.add("pe", lambda e: e.matmul(bn, lhsT=ones_f, rhs=e_sq, start=True, stop=True), reads=["e_sq", "ones_f"], writes=[bnr])
            sc.add("act", lambda e: e.activation(out=e_sd, in_=bn, func=AF.Sqrt, scale=1.0 / 128, bias=EPS), reads=[bnr], writes=["e_sd"])
            ybc, ybcr = ybc_rot.next()

            def ep2(e):
                e.reciprocal(out=e_o1, in_=e_sd)
                e.scalar_tensor_tensor(out=eo1, in0=eo0, scalar=lamt[:, 5:6], in1=e_o1, op0=ALU.mult, op1=ALU.mult)
                return e.tensor_tensor(out=ybc, in0=eo1, in1=cgt, op=ALU.mult)
            sc.add("dve", ep2, reads=["e_sd", eo0r, eo1r, cgr], writes=[ybcr, "e_o1", eo1r])
            sc.add("sp", lambda e: e.dma_start(out=Y[YC0 + h * 128:YC0 + (h + 1) * 128, tsl], in_=ybc), reads=[ybcr], dma=True)

        run_pipeline(items, [c_S, c_P], [0, 2])
        sc.barrier()

        ar.reset()
        scale_d = 96.0 ** -0.5
        kd2_rot = Rot(ar, 2, [S], BF16, "kd2")
        vraw2_rot = Rot(ar, 2, [32 * 64], BF16, "vraw2")
        vd_tiles = [ar.alloc([32, 128], BF16) for _ in range(2)]
        for k in range(2):
            sc.add("pool", lambda e, k=k: e.memset(vd_tiles[k][:, :, 64:128], 1.0), writes=[("vd1", k)])
        qd2_rot = Rot(ar, 3, [TC], BF16, "qd2")
        dg_rot = Rot(ar, 3, [TC], BF16, "dg")
        pbd_rot = Rot(ar, 5, [TC], BF16, "pbd")
        rzd_rot = Rot(ar, 2, [TC], F32, "rzd")
        yfd_rot = Rot(ar, 2, [TC], F32, "yfd")
        ybd_rot = Rot(ar, 2, [TC], BF16, "ybd")
        dh, dq = {}, {}
        FTOP = (NYC * 1024 * 2 + 8 * 1024 * 2)
        ftop = Arena(arena_t[:, (ARENA_BYTES - FTOP) // 4:ARENA_BYTES // 4], FTOP, "ftop")
        assert ar.off <= ARENA_BYTES - FTOP, ar.off
        wbr = ftop.alloc([NYC, 1024], BF16)
        wout = ftop.alloc([8, 1024], BF16)
        wbr_f = wbr.rearrange("p a b -> p (a b)")
        wout_f = wout.rearrange("p a b -> p (a b)")
        nwb = (NYC * 1024 + 4095) // 4096

        def f_weights():
            for i in range(nwb):
                n = min(4096, NYC * 1024 - i * 4096)
                load_cast(wbr_d[l][:, i * 4096:i * 4096 + n], n, wbr_f[:, i * 4096:i * 4096 + n], ("wbr", i))
            for i in range(2):
                load_cast(wout_d[l][:, i * 4096:(i + 1) * 4096], 4096, wout_f[:, i * 4096:(i + 1) * 4096], ("wout", i))
        wbr_res = [("wbr", i) for i in range(nwb)]
        wout_res = [("wout", i) for i in range(2)]

        def d_load_head(h):
            kd, kdr = kd2_rot.next()
            sc.add("sp", lambda e: e.dma_start(out=kd[0:64, :], in_=DK[h * 64:(h + 1) * 64, :]), writes=[kdr], dma=True)
            sc.add("sp", lambda e: e.dma_start(out=kd[64:96, :], in_=DKR), writes=[(kdr, "r")], dma=True)
            vraw, vrr = vraw2_rot.next()
            vk = h % 2
            vt, vtr = vd_tiles[vk], ("vd", vk)
            sc.add("sp", lambda e: e.dma_start(out=vraw, in_=DV[:, h * 2048:(h + 1) * 2048]), writes=[vrr], dma=True)
            sc.add("pool", lambda e: e.tensor_copy(out=vt[:, :, 0:64], in_=vraw.rearrange("p (b d) -> p b d", b=32)),
                   reads=[vrr, ("vd1", vk)], writes=[vtr])
            dh[h] = dict(kd=kd, kdr=kdr, vt=vt, vtr=vtr)

        def d_load_chunk(h, qc):
            tsl = slice(qc * TC, (qc + 1) * TC)
            qd, qdr = qd2_rot.next()
            sc.add("sp", lambda e: e.dma_start(out=qd[0:96, :], in_=DQ[h * 96:(h + 1) * 96, tsl]), writes=[qdr], dma=True)
            dgt, dgr = dg_rot.next()
            sc.add("sp", lambda e: e.dma_start(out=dgt[0:64, :], in_=DG[h * 64:(h + 1) * 64, tsl]), writes=[dgr], dma=True)
            dq[(h, qc)] = dict(qd=qd, qdr=qdr, dgt=dgt, dgr=dgr)

        items = [dict(h=h, qc=qc, kb=kb, idx=(h * NT + qc) * 32 + kb) for h in range(D_HEADS) for qc in range(NT) for kb in range(32)]

        def d_S(it):
            h, qc, kb = it["h"], it["qc"], it["kb"]
            if kb == 0 and qc == 0 and h == 0:
                d_load_head(0)
                d_load_chunk(0, 0)
            if it["idx"] == 64:
                f_weights()
            if kb == 4:
                if qc == 0 and h + 1 < D_HEADS:
                    d_load_head(h + 1)
                nh, nq = (h, qc + 1) if qc + 1 < NT else (h + 1, 0)
                if nh < D_HEADS:
                    d_load_chunk(nh, nq)
            kd, kdr = dh[h]["kd"], dh[h]["kdr"]
            qd, qdr = dq[(h, qc)]["qd"], dq[(h, qc)]["qdr"]
            psS, psSr = bank(it["idx"] % 3)
            pbt, pbr = pbd_rot.next()
            sc.add("pe", lambda e: e.matmul(psS, lhsT=kd[0:96, kb * 128:(kb + 1) * 128], rhs=qd[0:96, :], start=True, stop=True),
                   reads=[kdr, (kdr, "r"), qdr], writes=[psSr])
            sc.add("act", lambda e: e.activation(out=pbt, in_=psS, func=AF.Exp, scale=scale_d), reads=[psSr], writes=[pbr])
            it.update(pbt=pbt, pbr=pbr)

        def d_P(it):
            h, qc, kb = it["h"], it["qc"], it["kb"]
            vt, vtr = dh[h]["vt"], dh[h]["vtr"]
            psO, psOr = bank(3 + (h * NT + qc) % 2)
            pbt = it["pbt"]
            sc.add("pe", lambda e: e.matmul(psO, lhsT=vt[:, kb, :], rhs=pbt, start=(kb == 0), stop=(kb == 31)), reads=[it["pbr"], vtr], writes=[psOr])
            if kb == 31:
                tsl = slice(qc * TC, (qc + 1) * TC)
                rz, rzr = rzd_rot.next()
                yf, yfr = yfd_rot.next()
                yb, ybr = ybd_rot.next()
                dgt, dgr = dq[(h, qc)]["dgt"], dq[(h, qc)]["dgr"]

                def epd(e):
                    e.reciprocal(out=rz[0:64, :], in_=psO[64:128, :])
                    e.tensor_tensor(out=yf[0:64, :], in0=psO[0:64, :], in1=rz[0:64, :], op=ALU.mult)
                    return e.tensor_tensor(out=yb[0:64, :], in0=yf[0:64, :], in1=dgt[0:64, :], op=ALU.mult)
                sc.add("dve", epd, reads=[psOr, dgr], writes=[rzr, yfr, ybr])
                sc.add("sp", lambda e: e.dma_start(out=Y[YD0 + h * 64:YD0 + (h + 1) * 64, tsl], in_=yb[0:64, :]), reads=[ybr], dma=True)

        run_pipeline(items, [d_S, d_P], [0, 2])
        sc.barrier()

        ar.reset()
        assert True
        yt_rot = Rot(ar, 2, [NYC, TC], BF16, "yt")
        xt2 = ar.alloc([KC, TC], F32)
        out2 = ar.alloc([KC, TC], F32)
        merged = ar.alloc([KC, TC], BF16)
        mfull_rot = Rot(ar, 2, [KC, TC], BF16, "mfull") if SPLIT else None
        g_rot = Rot(ar, 4, [TC], BF16, "g")
        tm_rot = Rot(ar, 4, [TC], F32, "tm")
        macc_rot = Rot(ar, 3, [TC], F32, "macc")
        sqf_rot = Rot(ar, 2, [TC], F32, "sqf")
        rr2 = ar.alloc([TC], F32)
        assert ar.off <= ARENA_BYTES - FTOP, ar.off
        Yv = Y.rearrange("(c p) s -> p c s", p=128)
        x_src_v2 = x_src.rearrange("(kc p) s -> p kc s", p=128)
        x_dst_v = x_dst.rearrange("(kc p) s -> p kc s", p=128)
        pbk = [0]
        mres = [("merged", oc) for oc in range(8)]

        def f_stage1(t):
            tsl = slice(t * TC, (t + 1) * TC)
            yt, ytr = yt_rot.next()
            sc.add("sp", lambda e: e.dma_start(out=yt, in_=Yv[:, :, tsl]), writes=[ytr], dma=True)
            for oc in range(8):
                macc = maccr = None
                for br in range(4):
                    gt, gr = g_rot.next()
                    sc.add("sp", lambda e, gt=gt, br=br, oc=oc: e.dma_start(out=gt, in_=GT[br * 1024 + oc * 128:br * 1024 + (oc + 1) * 128, tsl]), writes=[gr], dma=True)
                    bk, bkr = bank(pbk[0] % 4)
                    pbk[0] += 1
                    chs = BR_CHUNKS[br]

                    def bmm(e, bk=bk, chs=chs, oc=oc):
                        r = None
                        for ci, (cidx, nr) in enumerate(chs):
                            r = e.matmul(bk, lhsT=wbr[0:nr, cidx, oc * 128:(oc + 1) * 128], rhs=yt[0:nr, cidx, :], start=(ci == 0), stop=(ci == len(chs) - 1))
                        return r
                    sc.add("pe", bmm, reads=[ytr] + wbr_res, writes=[bkr])
                    if br == 0:
                        macc, maccr = macc_rot.next()
                        sc.add("dve", lambda e, bk=bk, gt=gt, macc=macc: e.tensor_tensor(out=macc, in0=bk, in1=gt, op=ALU.mult), reads=[bkr, gr], writes=[maccr])
                    else:
                        tm, tmr = tm_rot.next()
                        sc.add("dve", lambda e, bk=bk, gt=gt, tm=tm: e.tensor_tensor(out=tm, in0=bk, in1=gt, op=ALU.mult), reads=[bkr, gr], writes=[tmr])
                        if br < 3:
                            sc.add("pool", lambda e, tm=tm, macc=macc: e.tensor_tensor(out=macc, in0=macc, in1=tm, op=ALU.add), reads=[tmr, maccr], writes=[maccr])
                        else:
                            sc.add("pool", lambda e, tm=tm, oc=oc, macc=macc: e.tensor_tensor(out=merged[:, oc, :], in0=macc, in1=tm, op=ALU.add), reads=[tmr, maccr], writes=[("merged", oc)])
            if SPLIT:
                pr, q = t // 2, t % 2
                k = pr % 2
                sc.add("sp", lambda e: e.dma_start(out=ARI[k].rearrange("(kc p) s -> p kc s", p=128)[:, :, q * TC:(q + 1) * TC], in_=merged),
                       reads=mres, writes=[("ari", k, q)], dma=True)
                if q == 1:
                    sc.add("pool", lambda e: e.collective_compute("AllReduce", ALU.add, replica_groups=RG, ins=[ARI[k]], outs=[ARO[k]]),
                           reads=[("ari", k, 0), ("ari", k, 1)], writes=[("aro", k)], cc=True)

        def f_stage2(t):
            tsl = slice(t * TC, (t + 1) * TC)
            if SPLIT:
                pr, q = t // 2, t % 2
                k = pr % 2
                msrc, msr = mfull_rot.next()
                sc.add("sp", lambda e: e.dma_start(out=msrc, in_=ARO[k].rearrange("(kc p) s -> p kc s", p=128)[:, :, q * TC:(q + 1) * TC]),
                       reads=[("aro", k)], writes=[msr], dma=True)
                mrd = [msr]
            else:
                msrc, mrd = merged, mres
            sc.add("sp", lambda e: e.dma_start(out=xt2, in_=x_src_v2[:, :, tsl]), writes=["xt2"], dma=True)
            bn, bnr = bank(6)
            for oc2 in range(8):
                bo, bor = bank(4 + oc2 % 2)

                def omm(e, bo=bo, oc2=oc2):
                    r = None
                    for oc in range(8):
                        r = e.matmul(bo, lhsT=wout[:, oc, oc2 * 128:(oc2 + 1) * 128], rhs=msrc[:, oc, :], start=(oc == 0), stop=(oc == 7))
                    return r
                sc.add("pe", omm, reads=mrd + wout_res, writes=[bor])
                sqf, sqfr = sqf_rot.next()

                def oev(e, bo=bo, oc2=oc2, sqf=sqf):
                    e.activation(out=out2[:, oc2, :], in_=bo, func=AF.Copy)
                    return e.activation(out=sqf, in_=bo, func=AF.Square)
                sc.add("act", oev, reads=[bor], writes=[("out2", oc2), sqfr])
                sc.add("pe", lambda e, sqf=sqf, oc2=oc2: e.matmul(bn, lhsT=ones_f, rhs=sqf, start=(oc2 == 0), stop=(oc2 == 7)), reads=[sqfr, "ones_f"], writes=[bnr])
            sc.add("act", lambda e: e.activation(out=rr2, in_=bn, func=AF.Sqrt, scale=1.0 / DM, bias=EPS), reads=[bnr], writes=["rr2"])
            ores = [("out2", i) for i in range(8)]

            def resid(e):
                e.reciprocal(out=rr2, in_=rr2)
                r = None
                for oc2 in range(8):
                    e.scalar_tensor_tensor(out=out2[:, oc2, :], in0=out2[:, oc2, :], scalar=col(SM_GPOST + oc2), in1=rr2, op0=ALU.mult, op1=ALU.mult)
                    r = e.tensor_tensor(out=xt2[:, oc2, :], in0=xt2[:, oc2, :], in1=out2[:, oc2, :], op=ALU.add)
                return r
            sc.add("dve", resid, reads=["rr2", "xt2", "small"] + ores, writes=["xt2", "rr2"] + ores)
            sc.add("sp", lambda e: e.dma_start(out=x_dst_v[:, :, tsl], in_=xt2), reads=["xt2"], dma=True)

        if SPLIT:
            for t in range(NT):
                f_stage1(t)
                if t % 2 == 1 and t >= 3:
                    f_stage2(t - 3)
                    f_stage2(t - 2)
            f_stage2(NT - 2)
            f_stage2(NT - 1)
        else:
            for t in range(NT):
                f_stage1(t)
                f_stage2(t)
        sc.barrier()

    for l in range(L):
        layer(l)
    sc.analyze()
    sc.emit(nc, es)
    es.close()
    return nc


def _bf(x):
    return np.asarray(x, dtype=np.float32).astype(ml_dtypes.bfloat16)


def _parts(par):
    if not SPLIT:
        return list(range(6)), list(range(6)), list(range(4)), list(range(6))
    return ([3 * par + i for i in range(3)], [3 * par + i for i in range(3)], [2 * par + i for i in range(2)],
            [3 * par + i for i in range(3)])


def build_consts(par=0):
    c = {}
    slots, _, cheads, _ = _parts(par)
    cs_all = c_slopes()
    cs = [cs_all[h] for h in cheads]
    caugq = np.zeros((C_HEADS, 2, 4, TC), np.float32)
    caugk = np.zeros((C_HEADS, 4, S), np.float32)
    cbias = np.zeros((128, C_HEADS, CB_W), np.float32)
    ii = np.arange(TC, dtype=np.float64)
    jj = (np.arange(S) % 128).astype(np.float64)
    for h, m in enumerate(cs):
        qb = (-m * ii).astype(np.float32)
        qb_hi = _bf(qb).astype(np.float32)
        qb_lo = _bf(qb - qb_hi).astype(np.float32)
        caugq[h, 0] = np.stack([qb_hi, qb_lo, np.ones(TC), np.ones(TC)])
        caugq[h, 1] = -caugq[h, 0]
        kb = (m * jj).astype(np.float32)
        kb_hi = _bf(kb).astype(np.float32)
        kb_lo = _bf(kb - kb_hi).astype(np.float32)
        caugk[h] = np.stack([np.ones(S), np.ones(S), kb_hi, kb_lo])
        cbias[:, h, 0:32] = (-m * 128.0 * np.arange(32))[None, :]
        cbias[:, h, 32:36] = (m * 128.0 * np.arange(4))[None, :]
    c["caugq"] = _bf(caugq)
    c["caugk"] = _bf(caugk)
    c["cbias"] = cbias.reshape(128, C_HEADS * CB_W)
    p = np.arange(128)[:, None].astype(np.float64)
    f = np.arange(128)[None, :].astype(np.float64)
    md = np.stack([np.exp(-m * np.abs(p - f)) for m in cs], axis=1)
    c["mdiag"] = md.reshape(128, C_HEADS * 128).astype(np.float32)
    sl_all = a_slopes()
    ma = np.zeros((128, 3 * A_SLOTS, 384), np.float64)
    k = np.arange(128)[:, None]
    for g, d in enumerate(A_DIL):
        for si, sg in enumerate(slots):
            for j in range(3):
                q = np.arange(128)[None, :]
                rel = np.abs((j - 1) * 128 + k - q)
                ma[:, g * A_SLOTS + si, j * 128:(j + 1) * 128] = np.where(rel <= 64, np.exp(-sl_all[sg] * d * rel), 0.0)
    c["ma"] = ma.reshape(128, 3 * A_SLOTS * 384).astype(np.float32)
    inv = (10000.0 ** (-np.arange(0, 32, 2, dtype=np.float32) / 32)).astype(np.float32)
    ang = np.arange(S, dtype=np.float32)[:, None] * inv[None, :]
    cos, sin = np.cos(ang).astype(np.float32).T, np.sin(ang).astype(np.float32).T
    c["ropec"] = np.ascontiguousarray(np.concatenate([cos, cos], 0))
    c["ropes"] = np.ascontiguousarray(np.concatenate([-sin, sin], 0))
    return c


def _bchan(par):
    _, blocks, _, _ = _parts(par)
    idx = -np.ones(BW, np.int64)
    for i, b in enumerate(blocks):
        idx[i * 64:(i + 1) * 64] = np.arange(b * 64, (b + 1) * 64)
    return idx


def pack_w_in(w, par):
    slots, _, cheads, dheads = _parts(par)
    L = w.shape[0]
    out = np.zeros((L, DM, IN_W), np.float32)
    O = OFF_ALL

    def put(name, off_in_fam, src_cols):
        n = len(src_cols)
        out[:, :, OFF[name] + off_in_fam:OFF[name] + off_in_fam + n] = w[:, :, src_cols]
    for fam in ("a_q", "a_k", "a_v"):
        for g in range(3):
            cols = np.concatenate([np.arange(O[fam] + g * 384 + s * 64, O[fam] + g * 384 + (s + 1) * 64) for s in slots])
            put(fam, g * AW, cols)
    put("a_g", 0, np.concatenate([np.arange(O["a_g"] + s * 64, O["a_g"] + (s + 1) * 64) for s in slots]))
    bidx = _bchan(par)
    nreal = int((bidx >= 0).sum())
    put("b_x", 0, O["b_x"] + bidx[:nreal])
    put("b_g", 0, O["b_g"] + bidx[:nreal])
    for fam in ("c_q", "c_k", "c_v", "c_g"):
        put(fam, 0, np.concatenate([np.arange(O[fam] + h * 128, O[fam] + (h + 1) * 128) for h in cheads]))
    put("d_cq", 0, np.arange(O["d_cq"], O["d_cq"] + 256))
    put("d_ckv", 0, np.arange(O["d_ckv"], O["d_ckv"] + 128))
    put("d_kr", 0, np.arange(O["d_kr"], O["d_kr"] + 32))
    put("d_g", 0, np.concatenate([np.arange(O["d_g"] + h * 64, O["d_g"] + (h + 1) * 64) for h in dheads]))
    put("gate", 0, np.arange(O["gate"], O["gate"] + 4096))
    return out


def pack_layers(inp, layers, par=0):
    f32 = np.float32
    L = len(layers)
    slots, blocks, cheads, dheads = _parts(par)
    bidx = _bchan(par)
    real = bidx >= 0
    small = np.zeros((L, 128, NSMALL), f32)
    lamv = np.zeros((L, 1, 256), f32)
    lruw = np.zeros((L, 128, 4 * NBC, 128), f32)
    wuqa = np.zeros((L, 128, 2, D_HEADS * 96), f32)
    wuqb = np.zeros((L, 128, 2, D_HEADS * 96), f32)
    wukvk = np.zeros((L, 128, DW), f32)
    wukvv = np.zeros((L, 128, DW), f32)
    wbr = np.zeros((L, 128, NYC, 1024), f32)
    wout = np.zeros((L, 128, 8, 1024), f32)

    def bvec(v):
        o = np.zeros(BW, f32)
        o[real] = v[bidx[real]]
        return o.reshape(NBC, 128).T
    for li, l in enumerate(layers):
        sm = small[li]
        sm[:, SM_GPRE:SM_GPRE + 8] = inp["norm_pre"][l].reshape(8, 128).T
        sm[:, SM_GPOST:SM_GPOST + 8] = inp["norm_post"][l].reshape(8, 128).T
        sm[:, SM_BGATE:SM_BGATE + 32] = inp["b_gate"][l].reshape(32, 128).T
        for j in range(4):
            sm[:, SM_CONVW + j * NBC:SM_CONVW + (j + 1) * NBC] = bvec(inp["conv_w"][l][j])
        sm[:, SM_CONVB:SM_CONVB + NBC] = bvec(inp["conv_b"][l])
        for dr in range(2):
            sm[:, SM_BR + dr * NBC:SM_BR + (dr + 1) * NBC] = bvec(inp["lru_br"][l][dr])
            sm[:, SM_BI + dr * NBC:SM_BI + (dr + 1) * NBC] = bvec(inp["lru_bi"][l][dr])
            sm[:, SM_LAM + dr * NBC:SM_LAM + (dr + 1) * NBC] = bvec(inp["lru_lambda"][l][dr])
        sm[:, SM_SUBLN] = inp["diff_subln"][l]
        sm[:, SM_QN:SM_QN + 2] = inp["mla_q_norm"][l].reshape(2, 128).T
        sm[:, SM_KVN] = inp["mla_kv_norm"][l]
        lam_init = 0.8 - 0.6 * math.exp(-0.3 * l)
        sm[:, SM_LAMINIT] = lam_init
        sm[:, SM_OML] = (1.0 - lam_init)
        lamv[li, 0, 0:64] = inp["diff_lam_q1"][l]
        lamv[li, 0, 64:128] = inp["diff_lam_k1"][l]
        lamv[li, 0, 128:192] = inp["diff_lam_q2"][l]
        lamv[li, 0, 192:256] = inp["diff_lam_k2"][l]
        for gi, w in enumerate((inp["lru_wr"][l], inp["lru_wi"][l])):
            for dr in range(2):
                for i, b in enumerate(blocks):
                    c, bb = i // 2, i % 2
                    lruw[li, bb * 64:(bb + 1) * 64, (gi * 2 + dr) * NBC + c, bb * 64:(bb + 1) * 64] = w[dr, b]
        uq = inp["mla_w_uq"][l].reshape(2, 128, 6, 96)[:, :, dheads, :]
        wuqa[li] = uq.transpose(1, 0, 2, 3).reshape(128, 2, D_HEADS * 96)
        uqb = uq.copy()
        uqb[..., 64:80] = uq[..., 80:96]
        uqb[..., 80:96] = uq[..., 64:80]
        wuqb[li] = uqb.transpose(1, 0, 2, 3).reshape(128, 2, D_HEADS * 96)
        ukv = inp["mla_w_ukv"][l].reshape(128, 6, 128)[:, dheads, :]
        wukvk[li] = ukv[:, :, 0:64].reshape(128, DW)
        wukvv[li] = ukv[:, :, 64:128].reshape(128, DW)
        wy = np.zeros((NYC * 128, 1024), f32)
        wa, wb_, wc, wd = inp["w_br_a"][l], inp["w_br_b"][l], inp["w_br_c"][l], inp["w_br_d"][l]
        for i, s_ in enumerate(slots):
            wy[YA0 + i * 64:YA0 + (i + 1) * 64] = wa[s_ * 64:(s_ + 1) * 64]
        wy[YB0:YB0 + BW][real] = wb_[bidx[real]]
        for i, h in enumerate(cheads):
            wy[YC0 + i * 128:YC0 + (i + 1) * 128] = wc[h * 128:(h + 1) * 128]
        for i, h in enumerate(dheads):
            wy[YD0 + i * 64:YD0 + (i + 1) * 64] = wd[h * 64:(h + 1) * 64]
        wbr[li] = wy.reshape(NYC, 128, 1024).transpose(1, 0, 2)
        wout[li] = inp["w_out"][l].reshape(8, 128, 1024).transpose(1, 0, 2)
    return dict(small=small, lamv=lamv, lruw=lruw.reshape(L, 128, 4 * NBC * 128), wuqa=wuqa.reshape(L, 128, 2 * D_HEADS * 96),
                wuqb=wuqb.reshape(L, 128, 2 * D_HEADS * 96), wukvk=wukvk, wukvv=wukvv, wbr=wbr.reshape(L, 128, NYC * 1024),
                wout=wout.reshape(L, 128, 8 * 1024))


_PROG = {}


def get_prog(nl, debug=False):
    key = (nl, tuple(debug) if debug else None)
    if key not in _PROG:
        _PROG[key] = build_program(nl, debug)
    return _PROG[key]


FUSED = True


def make_in_maps(inp, layers, xT_by_batch):
    npar = 2 if SPLIT else 1
    w_all = np.ascontiguousarray(inp["w_in"][layers[0]:layers[-1] + 1]).astype(np.float32)
    per = []
    for par in range(npar):
        m = dict(w_in=pack_w_in(w_all, par) if SPLIT else w_all)
        m.update(pack_layers(inp, layers, par))
        m.update(build_consts(par))
        per.append(m)
    in_maps = []
    for c in range(8):
        b, par = (c // 2, c % 2) if SPLIT else (c % 4, 0)
        m = dict(xT=xT_by_batch[b])
        m.update(per[par])
        in_maps.append(m)
    return in_maps


def kernel(**inputs):
    inp = {k: np.asarray(v) for k, v in inputs.items()}
    x = inp["x"].astype(np.float32)
    xT = [np.ascontiguousarray(x[b].T) for b in range(4)]
    groups = [list(range(DEPTH))] if FUSED else [[l] for l in range(DEPTH)]
    for layers in groups:
        nc = get_prog(len(layers))
        in_maps = make_in_maps(inp, layers, xT)
        res = run_bass_kernel_spmd(nc, in_maps, core_ids=list(range(8)))
        xT = [np.asarray(res.results[(2 * b) if SPLIT else b]["outT"]) for b in range(4)]
    out = np.stack([xT[b].T for b in range(4)], 0).astype(np.float32)
    return np.ascontiguousarray(out)
```

```python
import math
from contextlib import ExitStack

import numpy as np
import ml_dtypes

import concourse.bass as bass
import concourse.mybir as mybir
from concourse.bass_utils import run_bass_kernel_spmd

F32 = mybir.dt.float32
BF16 = mybir.dt.bfloat16
AF = mybir.ActivationFunctionType
ALU = mybir.AluOpType
AX = mybir.AxisListType

S = 4096
DM = 1024
DEPTH = 4
NT = 8
TC = 512
KC = 8
EPS = 1e-6
A_DIL = (1, 4, 16)
SPLIT = True
A_SLOTS_ALL, C_HEADS_ALL, D_HEADS_ALL = 6, 4, 6
A_SLOTS = 3 if SPLIT else 6
C_HEADS = 2 if SPLIT else 4
D_HEADS = 3 if SPLIT else 6
NBC = 2 if SPLIT else 3
AW, BW, CW, DW = A_SLOTS * 64, NBC * 128, C_HEADS * 128, D_HEADS * 64
_fam = [("a_q", 3 * AW), ("a_k", 3 * AW), ("a_v", 3 * AW), ("a_g", AW), ("b_x", BW), ("b_g", BW), ("c_q", CW), ("c_k", CW),
        ("c_v", CW), ("c_g", CW), ("d_cq", 256), ("d_ckv", 128), ("d_kr", 32), ("d_g", DW), ("gate", 4096)]
OFF = {}
_o = 0
for _n, _w in _fam:
    OFF[_n] = _o
    _o += _w
IN_W = _o
OFF_ALL = dict(a_q=0, a_k=1152, a_v=2304, a_g=3456, b_x=3840, b_g=4224, c_q=4608, c_k=5120,
               c_v=5632, c_g=6144, d_cq=6656, d_ckv=6912, d_kr=7040, d_g=7072, gate=7456)
SKIP_T = 1e30 if SPLIT else 100.0
if SPLIT:
    YA0, YB0, YC0, YD0, NYC = 0, 256, 512, 768, 8
    BR_CHUNKS = [[(0, 128), (1, 64)], [(2, 128), (3, 64)], [(4, 128), (5, 128)], [(6, 128), (7, 64)]]
else:
    YA0, YB0, YC0, YD0, NYC = 0, 384, 768, 1280, 13
    BR_CHUNKS = [[(0, 128), (1, 128), (2, 128)], [(3, 128), (4, 128), (5, 128)], [(6, 128), (7, 128), (8, 128), (9, 128)],
                 [(10, 128), (11, 128), (12, 128)]]
RG = [[0, 1], [2, 3], [4, 5], [6, 7]]
CB_W = 36

SM_GPRE, SM_GPOST, SM_BGATE, SM_CONVW = 0, 8, 16, 48
SM_CONVB = SM_CONVW + 4 * NBC
SM_BR = SM_CONVB + NBC
SM_BI = SM_BR + 2 * NBC
SM_LAM = SM_BI + 2 * NBC
SM_SUBLN = SM_LAM + 2 * NBC
SM_QN, SM_KVN, SM_LAMINIT, SM_OML = SM_SUBLN + 1, SM_SUBLN + 3, SM_SUBLN + 4, SM_SUBLN + 5
NSMALL = SM_SUBLN + 8


def a_slopes():
    return [2.0 ** (-8.0 * (i + 1) / A_SLOTS_ALL) for i in range(A_SLOTS_ALL)]


def c_slopes():
    return [2.0 ** (-8.0 * (i + 1) / C_HEADS_ALL) for i in range(C_HEADS_ALL)]


class Sched:
    ENGS = ("pe", "act", "dve", "pool", "sp")
    NDMA = {"sp": 24, "act": 12, "pool": 4, "cc": 4}
    UNIT = {"sp": 16, "act": 16, "pool": 16, "cc": 1}

    def __init__(self):
        self.ops = []

    def add(self, eng, fn, reads=(), writes=(), dma=False, cc=False):
        self.ops.append(dict(eng=eng, fn=fn, reads=tuple(reads), writes=tuple(writes), dma=(dma or cc),
                             q=("cc" if cc else eng), needs_inc=False, deps=[]))

    def barrier(self):
        self.ops.append(dict(barrier=True))

    def analyze(self):
        ops = self.ops
        last_w, readers = {}, {}
        last_on = {}
        for i, op in enumerate(ops):
            if op.get("barrier"):
                for e, j in last_on.items():
                    ops[j]["needs_inc"] = True
                last_w, readers = {}, {}
                continue
            deps = set()
            for r in op["reads"]:
                if r in last_w:
                    deps.add((last_w[r], "raw"))
            for w in op["writes"]:
                if w in last_w:
                    deps.add((last_w[w], "waw"))
                for rd in readers.get(w, ()):
                    deps.add((rd, "war"))
            keep = set()
            for d, kind in deps:
                if d == i:
                    continue
                p = ops[d]
                if p["dma"]:
                    keep.add(d)
                elif p["eng"] == op["eng"]:
                    if op["dma"]:
                        keep.add(d)
                    elif kind == "raw" and op["eng"] in ("act", "dve", "pool"):
                        keep.add(d)
                else:
                    keep.add(d)
            op["deps"] = sorted(keep)
            for d in keep:
                ops[d]["needs_inc"] = True
            for w in op["writes"]:
                last_w[w] = i
                readers[w] = []
            for r in op["reads"]:
                if r not in op["writes"]:
                    readers.setdefault(r, []).append(i)
            if not op["dma"]:
                last_on[op["eng"]] = i
        tick = {e: 0 for e in self.ENGS}
        dma_cnt = {q: [0] * n for q, n in self.NDMA.items()}
        dma_rr = {q: 0 for q in self.NDMA}
        seen = {e: {} for e in self.ENGS}
        pending = {e: [] for e in self.ENGS}
        for op in ops:
            if op.get("barrier"):
                snap = [(("c", e), tick[e]) for e in self.ENGS if tick[e] > 0]
                for q, cnts in dma_cnt.items():
                    for k, c in enumerate(cnts):
                        if c > 0:
                            snap.append((("d", q, k), self.UNIT[q] * c))
                for e in self.ENGS:
                    pending[e] = [(s_, v) for (s_, v) in snap if s_ != ("c", e)]
                continue
            e = op["eng"]
            waits = list(pending[e])
            pending[e] = []
            for d in op["deps"]:
                p = ops[d]
                waits.append((p["sem"], p["tick"]))
            if op["dma"]:
                q = op["q"]
                k = dma_rr[q]
                dma_rr[q] = (k + 1) % self.NDMA[q]
                if dma_cnt[q][k] > 0:
                    waits.append((("d", q, k), self.UNIT[q] * dma_cnt[q][k]))
                dma_cnt[q][k] += 1
                op["sem"] = ("d", q, k)
                op["tick"] = self.UNIT[q] * dma_cnt[q][k]
            elif op["needs_inc"]:
                tick[e] += 1
                op["sem"] = ("c", e)
                op["tick"] = tick[e]
            fw = []
            for s_, v in waits:
                if seen[e].get(s_, 0) >= v:
                    continue
                seen[e][s_] = v
                fw.append((s_, v))
            mx = {}
            for s_, v in fw:
                mx[s_] = max(mx.get(s_, 0), v)
            op["waits"] = sorted(mx.items(), key=lambda kv: str(kv[0]))
        self.final_dma = {(q, k): self.UNIT[q] * c for q, cnts in dma_cnt.items() for k, c in enumerate(cnts) if c > 0}

    def emit(self, nc, es):
        sems = {}
        for e in self.ENGS:
            sems[("c", e)] = es.enter_context(nc.semaphore("c_" + e))
        for q, n in self.NDMA.items():
            for k in range(n):
                sems[("d", q, k)] = es.enter_context(nc.semaphore("d_%s_%d" % (q, k)))
        blk = es.enter_context(nc.Block())
        ops = self.ops

        def run(engname):
            def body(e):
                for op in ops:
                    if op.get("barrier") or op["eng"] != engname:
                        continue
                    for s_, v in op["waits"]:
                        e.wait_ge(sems[s_], v)
                    ins = op["fn"](e)
                    if op["dma"]:
                        ins.then_inc(sems[op["sem"]], self.UNIT[op["q"]])
                    elif op["needs_inc"]:
                        ins.then_inc(sems[op["sem"]], 1)
                if engname == "sp":
                    for (q, k), v in sorted(self.final_dma.items()):
                        e.wait_ge(sems[("d", q, k)], v)
            return body

        blk.tensor(run("pe"))
        blk.scalar(run("act"))
        blk.vector(run("dve"))
        blk.gpsimd(run("pool"))
        blk.sync(run("sp"))


class Arena:
    def __init__(self, ap, nbytes, name):
        self.ap = ap
        self.nbytes = nbytes
        self.off = 0
        self.name = name
        self.cnt = 0

    def reset(self):
        self.off = 0

    def alloc(self, free_shape, dt, parts=128):
        n = 1
        for d in free_shape:
            n *= d
        nb = n * (4 if dt == F32 else 2)
        nb = (nb + 63) // 64 * 64
        assert self.off + nb <= self.nbytes, "arena %s overflow: %d + %d > %d" % (self.name, self.off, nb, self.nbytes)
        a = self.ap[:, self.off // 4:(self.off + nb) // 4]
        self.off += nb
        if dt == BF16:
            a = a.bitcast(BF16)
        a = a[:, 0:n]
        if len(free_shape) == 2:
            a = a.rearrange("p (a b) -> p a b", a=free_shape[0])
        elif len(free_shape) == 3:
            a = a.rearrange("p (a b c) -> p a b c", a=free_shape[0], b=free_shape[1])
        self.cnt += 1
        return a


class Rot:
    def __init__(self, arena, n, free_shape, dt, name):
        self.tiles = [arena.alloc(free_shape, dt) for _ in range(n)]
        self.name = name
        self.i = 0

    def next(self):
        k = self.i % len(self.tiles)
        self.i += 1
        return self.tiles[k], (self.name, k)


def run_pipeline(items, stages, lags):
    n = len(items)
    mx = max(lags)
    for step in range(n + mx):
        for f, lg in zip(stages, lags):
            i = step - lg
            if 0 <= i < n:
                f(items[i])


def build_program(nlayers, debug=False):
    nc = bass.Bass("TRN2", target_bir_lowering=False)
    L = nlayers

    def din(name, shape, dt=F32):
        return nc.dram_tensor(name, list(shape), dt, kind="ExternalInput").ap()

    def dscr(name, shape, dt=BF16):
        ext = bool(debug) and name in debug
        return nc.dram_tensor(name, list(shape), dt, kind="ExternalOutput" if ext else "Internal").ap()

    xT_in = din("xT", [DM, S])
    w_in = din("w_in", [L, DM, IN_W])
    small_d = din("small", [L, 128, NSMALL])
    lamv_d = din("lamv", [L, 1, 256])
    lruw_d = din("lruw", [L, 128, 4 * NBC * 128])
    wuqa_d = din("wuqa", [L, 128, 2 * D_HEADS * 96])
    wuqb_d = din("wuqb", [L, 128, 2 * D_HEADS * 96])
    wukvk_d = din("wukvk", [L, 128, DW])
    wukvv_d = din("wukvv", [L, 128, DW])
    wbr_d = din("wbr", [L, 128, NYC * 1024])
    cbias_d = din("cbias", [128, C_HEADS * CB_W])
    wout_d = din("wout", [L, 128, 8 * 1024])
    caugq_d = din("caugq", [C_HEADS, 2, 4, TC], BF16)
    caugk_d = din("caugk", [C_HEADS, 4, S], BF16)
    mdiag_d = din("mdiag", [128, C_HEADS * 128])
    ma_d = din("ma", [128, 3 * A_SLOTS * 384])
    ropec_d = din("ropec", [32, S])
    ropes_d = din("ropes", [32, S])
    outT = nc.dram_tensor("outT", [DM, S], F32, kind="ExternalOutput").ap()

    AQ = [dscr("AQ%d" % g, [AW, S]) for g in range(3)]
    AK = [dscr("AK%d" % g, [AW, S]) for g in range(3)]
    AV = [dscr("AV%d" % g, [128, A_SLOTS * 32 * 64]) for g in range(3)]
    AG = dscr("AG", [AW, S])
    BX = dscr("BX", [BW, S], F32)
    BG = dscr("BG", [BW, S])
    CQ = dscr("CQ", [CW, S])
    CK = dscr("CK", [CW, S])
    CV = dscr("CV", [128, C_HEADS * 32 * 128])
    CG = dscr("CG", [CW, S])
    DLAT = dscr("DLAT", [384, S], F32)
    DKRAW = dscr("DKRAW", [64, S], F32)
    DG = dscr("DG", [DW, S])
    GT = dscr("GT", [4096, S])
    DQ = dscr("DQ", [D_HEADS * 96, S])
    DK = dscr("DK", [D_HEADS * 64, S])
    DKR = dscr("DKR", [32, S])
    DV = dscr("DV", [128, D_HEADS * 32 * 64])
    Y = dscr("Y", [NYC * 128, S])
    ARI = [nc.dram_tensor("ARI%d" % i, [DM, 2 * TC], BF16, kind="Internal").ap() for i in range(2)]
    ARO = [nc.dram_tensor("ARO%d" % i, [DM, 2 * TC], BF16, kind="Internal").ap() for i in range(2)]
    XS = [nc.dram_tensor("XS%d" % i, [DM, S], F32, kind="Internal").ap() for i in range(2)]

    sc = Sched()
    es = ExitStack()
    PERS_BYTES = 56 * 1024
    ARENA_BYTES = 136 * 1024
    pers_t = es.enter_context(nc.sbuf_tensor("pers", [128, PERS_BYTES // 4], F32))
    arena_t = es.enter_context(nc.sbuf_tensor("arena", [128, ARENA_BYTES // 4], F32))
    pers = Arena(pers_t[:], PERS_BYTES, "pers")
    ar = Arena(arena_t[:], ARENA_BYTES, "arena")
    banks = [es.enter_context(nc.psum_tensor("bank%d" % i, [128, 512], F32)) for i in range(8)]

    def bank(i):
        return banks[i][:], ("bank", i)

    small = pers.alloc([NSMALL], F32)
    lamv = pers.alloc([256], F32)
    lamt = pers.alloc([16], F32)
    spc = pers.alloc([8], F32)
    ones_bf = pers.alloc([128], BF16)
    ones_f = pers.alloc([128], F32)
    mdiag = pers.alloc([C_HEADS, 128], F32)
    cbias = pers.alloc([C_HEADS * CB_W], F32)
    wst = [pers.alloc([4096], F32) for _ in range(2)]
    wbf = [pers.alloc([4096], BF16) for _ in range(2)]
    wst_i = [0]

    def col(c, n=1):
        return small[:, c:c + n]

    sc.add("dve", lambda e: e.memset(ones_bf, 1.0), writes=["ones_bf"])
    sc.add("dve", lambda e: e.memset(ones_f, 1.0), writes=["ones_f"])
    sc.add("sp", lambda e: e.dma_start(out=mdiag.rearrange("p a b -> p (a b)"), in_=mdiag_d), writes=["mdiag"], dma=True)
    sc.add("sp", lambda e: e.dma_start(out=cbias, in_=cbias_d), writes=["cbias"], dma=True)

    def load_cast(src_ap, ncols_f32, dst_bf, dst_res):
        k = wst_i[0] % 2
        wst_i[0] += 1
        st = wst[k]
        sc.add("sp", lambda e: e.dma_start(out=st[:, 0:ncols_f32], in_=src_ap), writes=[("wst", k)], dma=True)
        sc.add("pool", lambda e: e.tensor_copy(out=dst_bf, in_=st[:, 0:ncols_f32]), reads=[("wst", k)], writes=[dst_res])

    def layer(l):
        x_src = xT_in if l == 0 else XS[(l - 1) % 2]
        x_dst = outT if l == L - 1 else XS[l % 2]
        w_l = w_in[l]
        w_l_v = w_l.rearrange("(kc p) n -> p kc n", p=128)

        sc.add("sp", lambda e, l=l: e.dma_start(out=small, in_=small_d[l]), writes=["small"], dma=True)
        sc.add("sp", lambda e, l=l: e.dma_start(out=lamv, in_=lamv_d[l].partition_broadcast(128)), writes=["lamv"], dma=True)
        ar.reset()
        tmpl = ar.alloc([128], F32)

        sc.add("dve", lambda e: e.tensor_tensor(out=tmpl[:, 0:64], in0=lamv[:, 0:64], in1=lamv[:, 64:128], op=ALU.mult), reads=["lamv"], writes=["tmpl0"])
        sc.add("dve", lambda e: e.tensor_tensor(out=tmpl[:, 64:128], in0=lamv[:, 128:192], in1=lamv[:, 192:256], op=ALU.mult), reads=["lamv"], writes=["tmpl1"])
        sc.add("dve", lambda e: e.reduce_sum(out=lamt[:, 0:1], in_=tmpl[:, 0:64], axis=AX.X), reads=["tmpl0"], writes=["lamt0"])
        sc.add("dve", lambda e: e.reduce_sum(out=lamt[:, 1:2], in_=tmpl[:, 64:128], axis=AX.X), reads=["tmpl1"], writes=["lamt1"])
        sc.add("act", lambda e: e.activation(out=lamt[:, 2:4], in_=lamt[:, 0:2], func=AF.Exp), reads=["lamt0", "lamt1"], writes=["lamt23"])
        sc.add("dve", lambda e: e.tensor_tensor(out=lamt[:, 6:7], in0=lamt[:, 3:4], in1=lamt[:, 2:3], op=ALU.subtract), reads=["lamt23"], writes=["lamt6"])
        sc.add("dve", lambda e: e.tensor_tensor(out=lamt[:, 4:5], in0=lamt[:, 6:7], in1=col(SM_LAMINIT), op=ALU.subtract), reads=["lamt6", "small"], writes=["lamt4"])
        sc.add("dve", lambda e: e.tensor_tensor(out=lamt[:, 5:6], in0=col(SM_SUBLN), in1=col(SM_OML), op=ALU.mult), reads=["small"], writes=["lamt5"])
        sc.add("act", lambda e: e.activation(out=spc[:, 0:6], in_=col(SM_LAM, 6), func=AF.Exp, scale=-1.0), reads=["small"], writes=["spc_a"])
        sc.add("act", lambda e: e.activation(out=lamt[:, 8:14], in_=spc[:, 0:6], func=AF.Ln, bias=1.0), reads=["spc_a"], writes=["spc_b"])
        sc.add("dve", lambda e: e.tensor_scalar(out=spc[:, 0:6], in0=lamt[:, 8:14], scalar1=-8.0, scalar2=None, op0=ALU.mult),
               reads=["spc_b"], writes=["spc"])
        sc.barrier()

        ar.reset()
        hT = ar.alloc([KC, S], BF16)
        p0mark = ar.off
        xt_rot = Rot(ar, 2, [KC, TC], F32, "xt")
        sq_t = ar.alloc([KC, TC], F32)
        rstd_rot = Rot(ar, 2, [TC], F32, "rstd")
        x_src_v = x_src.rearrange("(kc p) s -> p kc s", p=128)
        for t in range(NT):
            xt, xr = xt_rot.next()
            sc.add("sp", lambda e, xt=xt, t=t: e.dma_start(out=xt, in_=x_src_v[:, :, t * TC:(t + 1) * TC]), writes=[xr], dma=True)
            sc.add("act", lambda e, xt=xt: e.activation(out=sq_t, in_=xt, func=AF.Square), reads=[xr], writes=["sq_t"])
            bk, br = bank(t % 2)

            def ssmm(e, bk=bk):
                for kc in range(KC):
                    r = e.matmul(bk, lhsT=ones_f, rhs=sq_t[:, kc, :], start=(kc == 0), stop=(kc == KC - 1))
                return r
            sc.add("pe", ssmm, reads=["sq_t", "ones_f"], writes=[br])
            rs, rr = rstd_rot.next()
            def rstd_op(e, bk=bk, rs=rs):
                e.activation(out=rs, in_=bk, func=AF.Ln, scale=1.0 / DM, bias=EPS)
                return e.activation(out=rs, in_=rs, func=AF.Exp, scale=-0.5)
            sc.add("act", rstd_op, reads=[br], writes=[rr])

            def hmk(e, xt=xt, rs=rs, t=t):
                for kc in range(KC):
                    r = e.scalar_tensor_tensor(out=hT[:, kc, t * TC:(t + 1) * TC], in0=xt[:, kc, :], scalar=col(SM_GPRE + kc),
                                               in1=rs, op0=ALU.mult, op1=ALU.mult)
                return r
            sc.add("dve", hmk, reads=[xr, rr, "small"], writes=[("hT", t)])
        sc.barrier()

        ar.off = p0mark
        ob_rot_a = Rot(ar, 3, [TC], BF16, "ob_a")
        ob_rot_d = Rot(ar, 3, [TC], BF16, "ob_d")
        of_rot = Rot(ar, 2, [TC], F32, "of")
        vstage = ar.alloc([4 * 32 * 128], BF16)
        hT_res = [("hT", t) for t in range(NT)]
        bank_i = [0]

        def nbank():
            b = bank_i[0] % 8
            bank_i[0] += 1
            return bank(b)

        fm_jobs = []
        for g in range(3):
            fm_jobs.append((OFF["a_q"] + g * AW, AW, A_DIL[g], "scale", AQ[g], 0.125))
        for g in range(3):
            fm_jobs.append((OFF["a_k"] + g * AW, AW, A_DIL[g], "copy", AK[g], None))
        fm_jobs.append((OFF["a_g"], AW, 1, "silu", AG, None))
        fm_jobs.append((OFF["b_x"], BW, 1, "copyf", BX, None))
        fm_jobs.append((OFF["b_g"], BW, 1, "silu", BG, None))
        fm_jobs.append((OFF["c_q"], CW, 1, "scale", CQ, 0.125))
        fm_jobs.append((OFF["c_k"], CW, 1, "copy", CK, None))
        fm_jobs.append((OFF["c_g"], CW, 1, "silu", CG, None))
        fm_jobs.append((OFF["d_cq"], 384, 1, "copyf", DLAT, None))
        fm_jobs.append((OFF["d_kr"], 64, 1, "copyf_kr", DKRAW, None))
        fm_jobs.append((OFF["d_g"], DW, 1, "silu", DG, None))
        for i in range(8):
            fm_jobs.append((OFF["gate"] + i * 512, 512, 1, "gate", GT[i * 512:(i + 1) * 512, :], i))
        tm_jobs = []
        for g in range(3):
            tm_jobs.append((OFF["a_v"] + g * AW, AW, A_DIL[g], AV[g], A_SLOTS, 64))
        tm_jobs.append((OFF["c_v"], CW, 1, CV, C_HEADS, 128))
        jobs = [("fm", j) for j in fm_jobs] + [("tm", j) for j in tm_jobs]

        def load_job(ji):
            kind, j = jobs[ji]
            c0, n = j[0], j[1]
            k = ji % 2
            st = wst[k].rearrange("p (kc n) -> p kc n", kc=KC)
            wb = wbf[k].rearrange("p (kc n) -> p kc n", kc=KC)
            if kind == "fm" and j[3] == "copyf_kr":
                sc.add("sp", lambda e: e.dma_start(out=st[:, :, 0:32], in_=w_l_v[:, :, c0:c0 + 32]), writes=[("wst", k)], dma=True)
                sc.add("sp", lambda e: e.dma_start(out=st[:, :, 32:48], in_=w_l_v[:, :, c0 + 16:c0 + 32]), writes=[("wst", k, 1)], dma=True)
                sc.add("sp", lambda e: e.dma_start(out=st[:, :, 48:64], in_=w_l_v[:, :, c0:c0 + 16]), writes=[("wst", k, 2)], dma=True)
                rd = [("wst", k), ("wst", k, 1), ("wst", k, 2)]
            else:
                sc.add("sp", lambda e: e.dma_start(out=st[:, :, 0:n], in_=w_l_v[:, :, c0:c0 + n]), writes=[("wst", k)], dma=True)
                rd = [("wst", k)]
            sc.add("pool", lambda e: e.tensor_copy(out=wb[:, :, 0:n], in_=st[:, :, 0:n]), reads=rd, writes=[("wbf", k)])

        def rhs_view(d, kc, t):
            if d == 1:
                return [(hT[:, kc, t * TC:(t + 1) * TC], None)]
            hv = hT[:, kc, :].rearrange("p (l r) -> p r l", r=d)
            if d == 4:
                return [(hv[:, t // 2, (t % 2) * 512:(t % 2) * 512 + 512], None)]
            return [(hv[:, 2 * t:2 * t + 2, :], 2)]

        def lhs_view(d, kc, b):
            if d == 1:
                return hT[:, kc, b * 128:(b + 1) * 128]
            hv = hT[:, kc, :].rearrange("p (l r) -> p r l", r=d)
            nbs = 32 // d
            return hv[:, b // nbs, (b % nbs) * 128:(b % nbs) * 128 + 128]

        def compute_fm(ji):
            _, (c0, n, d, kind, dst, extra) = jobs[ji]
            k = ji % 2
            wb = wbf[k].rearrange("p (kc n) -> p kc n", kc=KC)
            ncb = (n + 127) // 128
            for cb in range(ncb):
                m = min(128, n - cb * 128)
                for t in range(NT):
                    bk, bres = nbank()

                    def mm(e, bk=bk, cb=cb, m=m, t=t):
                        r = None
                        for kc in range(KC):
                            (rv, a), = rhs_view(d, kc, t)
                            o = bk[0:m, :]
                            if a is not None:
                                o = o.rearrange("p (a b) -> p a b", a=a)
                            r = e.matmul(o, lhsT=wb[:, kc, cb * 128:cb * 128 + m], rhs=rv, start=(kc == 0), stop=(kc == KC - 1))
                        return r
                    sc.add("pe", mm, reads=[("wbf", k)] + hT_res, writes=[bres])
                    drows = dst[cb * 128:cb * 128 + m, t * TC:(t + 1) * TC]
                    if kind in ("silu", "gate"):
                        ot, ores = ob_rot_a.next()
                        if kind == "silu":
                            sc.add("act", lambda e, ot=ot, bk=bk, m=m: e.activation(out=ot[0:m, :], in_=bk[0:m, :], func=AF.Silu),
                                   reads=[bres], writes=[ores])
                        else:
                            bcol = SM_BGATE + (extra // 2) * 8 + (extra % 2) * 4 + cb
                            sc.add("act", lambda e, ot=ot, bk=bk, bcol=bcol: e.activation(out=ot, in_=bk, func=AF.Sigmoid, bias=col(bcol)),
                                   reads=[bres, "small"], writes=[ores])
                        sc.add("act", lambda e, ot=ot, drows=drows, m=m: e.dma_start(out=drows, in_=ot[0:m, :]), reads=[ores], dma=True)
                    elif kind in ("copy", "scale"):
                        ot, ores = ob_rot_d.next()
                        if kind == "copy":
                            sc.add("dve", lambda e, ot=ot, bk=bk, m=m: e.tensor_copy(out=ot[0:m, :], in_=bk[0:m, :]), reads=[bres], writes=[ores])
                        else:
                            sc.add("dve", lambda e, ot=ot, bk=bk, m=m: e.tensor_scalar(out=ot[0:m, :], in0=bk[0:m, :], scalar1=extra, scalar2=None, op0=ALU.mult),
                                   reads=[bres], writes=[ores])
                        sc.add("sp", lambda e, ot=ot, drows=drows, m=m: e.dma_start(out=drows, in_=ot[0:m, :]), reads=[ores], dma=True)
                    else:
                        ot, ores = of_rot.next()
                        sc.add("dve", lambda e, ot=ot, bk=bk, m=m: e.tensor_copy(out=ot[0:m, :], in_=bk[0:m, :]), reads=[bres], writes=[ores])
                        sc.add("sp", lambda e, ot=ot, drows=drows, m=m: e.dma_start(out=drows, in_=ot[0:m, :]), reads=[ores], dma=True)

        def compute_tm(ji):
            _, (c0, n, d, dst, nh, hd) = jobs[ji]
            k = ji % 2
            wb = wbf[k].rearrange("p (kc n) -> p kc n", kc=KC)
            vs = vstage[:, 0:nh * 32 * hd].rearrange("p (h b d) -> p h b d", h=nh, b=32)
            for b in range(32):
                bk, bres = nbank()

                def mm(e, bk=bk, b=b):
                    r = None
                    for kc in range(KC):
                        r = e.matmul(bk[:, 0:n], lhsT=lhs_view(d, kc, b), rhs=wb[:, kc, 0:n], start=(kc == 0), stop=(kc == KC - 1))
                    return r
                sc.add("pe", mm, reads=[("wbf", k)] + hT_res, writes=[bres])
                sc.add("dve", lambda e, bk=bk, b=b: e.tensor_copy(out=vs[:, :, b, :], in_=bk[:, 0:n].rearrange("p (h d) -> p h d", h=nh)),
                       reads=[bres], writes=[("vstage", b)])
            sc.add("sp", lambda e: e.dma_start(out=dst, in_=vstage[:, 0:nh * 32 * hd]), reads=[("vstage", b) for b in range(32)], dma=True)

        load_job(0)
        for ji in range(len(jobs)):
            if ji + 1 < len(jobs):
                load_job(ji + 1)
            if jobs[ji][0] == "fm":
                compute_fm(ji)
            else:
                compute_tm(ji)
        sc.barrier()

        ar.reset()
        wuqa = ar.alloc([2, D_HEADS * 96], BF16)
        wuqb = ar.alloc([2, D_HEADS * 96], BF16)
        wukvk = ar.alloc([DW], BF16)
        wukvv = ar.alloc([DW], BF16)
        load_cast(wuqa_d[l], 2 * D_HEADS * 96, wuqa.rearrange("p a b -> p (a b)"), "wuqa")
        load_cast(wuqb_d[l], 2 * D_HEADS * 96, wuqb.rearrange("p a b -> p (a b)"), "wuqb")
        load_cast(wukvk_d[l], DW, wukvk, "wukvk")
        load_cast(wukvv_d[l], DW, wukvv, "wukvv")
        lat_rot = Rot(ar, 2, [3, TC], F32, "lat")
        kra_rot = Rot(ar, 2, [TC], F32, "kra")
        krb_rot = Rot(ar, 2, [TC], F32, "krb")
        cc_rot = Rot(ar, 2, [TC], F32, "cc")
        ss_rot = Rot(ar, 2, [TC], F32, "ss")
        sq2 = ar.alloc([3, TC], F32)
        rq_rot = Rot(ar, 2, [TC], F32, "rq")
        rkv_rot = Rot(ar, 2, [TC], F32, "rkv")
        cqn_rot = Rot(ar, 2, [3, TC], BF16, "cqn")
        qd_rot = Rot(ar, 3, [TC], BF16, "qd")
        kd_rot = Rot(ar, 3, [TC], BF16, "kd")
        t1_rot = Rot(ar, 2, [TC], F32, "t1")
        t2_rot = Rot(ar, 2, [TC], F32, "t2")
        krr_rot = Rot(ar, 2, [TC], BF16, "krr")
        dvst = ar.alloc([D_HEADS, 32, 64], BF16)
        DLv = DLAT.rearrange("(c p) s -> p c s", p=128)
        for t in range(NT):
            tsl = slice(t * TC, (t + 1) * TC)
            lat, latr = lat_rot.next()
            kra, krar = kra_rot.next()
            krb, krbr = krb_rot.next()
            cct, ccr = cc_rot.next()
            sst, ssr = ss_rot.next()
            sc.add("sp", lambda e, lat=lat, tsl=tsl: e.dma_start(out=lat, in_=DLv[:, :, tsl]), writes=[latr], dma=True)
            sc.add("sp", lambda e, kra=kra, tsl=tsl: e.dma_start(out=kra[64:96, :], in_=DKRAW[0:32, tsl]), writes=[krar], dma=True)
            sc.add("sp", lambda e, krb=krb, tsl=tsl: e.dma_start(out=krb[64:96, :], in_=DKRAW[32:64, tsl]), writes=[krbr], dma=True)
            sc.add("sp", lambda e, cct=cct, tsl=tsl: e.dma_start(out=cct[64:96, :], in_=ropec_d[:, tsl]), writes=[ccr], dma=True)
            sc.add("sp", lambda e, sst=sst, tsl=tsl: e.dma_start(out=sst[64:96, :], in_=ropes_d[:, tsl]), writes=[ssr], dma=True)
            sc.add("act", lambda e, lat=lat: e.activation(out=sq2, in_=lat, func=AF.Square), reads=[latr], writes=["sq2"])
            b0, b0r = bank(0)
            b1, b1r = bank(1)

            def ssq(e, b0=b0, b1=b1):
                e.matmul(b0, lhsT=ones_f, rhs=sq2[:, 0, :], start=True, stop=False)
                e.matmul(b0, lhsT=ones_f, rhs=sq2[:, 1, :], start=False, stop=True)
                return e.matmul(b1, lhsT=ones_f, rhs=sq2[:, 2, :], start=True, stop=True)
            sc.add("pe", ssq, reads=["sq2", "ones_f"], writes=[b0r, b1r])
            rq, rqr = rq_rot.next()
            rkv, rkvr = rkv_rot.next()

            def rsq(e, rq=rq, rkv=rkv, b0=b0, b1=b1):
                e.activation(out=rq, in_=b0, func=AF.Ln, scale=1.0 / 256, bias=EPS)
                e.activation(out=rkv, in_=b1, func=AF.Ln, scale=1.0 / 128, bias=EPS)
                e.activation(out=rq, in_=rq, func=AF.Exp, scale=-0.5)
                return e.activation(out=rkv, in_=rkv, func=AF.Exp, scale=-0.5)
            sc.add("act", rsq, reads=[b0r, b1r], writes=[rqr, rkvr])
            cqn, cqnr = cqn_rot.next()

            def nrm(e, rq=rq, rkv=rkv, lat=lat, cqn=cqn):
                e.scalar_tensor_tensor(out=cqn[:, 0, :], in0=lat[:, 0, :], scalar=col(SM_QN), in1=rq, op0=ALU.mult, op1=ALU.mult)
                e.scalar_tensor_tensor(out=cqn[:, 1, :], in0=lat[:, 1, :], scalar=col(SM_QN + 1), in1=rq, op0=ALU.mult, op1=ALU.mult)
                return e.scalar_tensor_tensor(out=cqn[:, 2, :], in0=lat[:, 2, :], scalar=col(SM_KVN), in1=rkv, op0=ALU.mult, op1=ALU.mult)
            sc.add("dve", nrm, reads=[rqr, rkvr, latr, "small"], writes=[cqnr, rqr, rkvr])
            t1, t1r = t1_rot.next()
            t2, t2r = t2_rot.next()
            krr, krrr = krr_rot.next()

            def krope(e, kra=kra, krb=krb, cct=cct, sst=sst, t1=t1, t2=t2, krr=krr):
                e.tensor_tensor(out=t1[64:96, :], in0=krb[64:96, :], in1=sst[64:96, :], op=ALU.mult)
                e.tensor_tensor(out=t2[64:96, :], in0=kra[64:96, :], in1=cct[64:96, :], op=ALU.mult)
                return e.tensor_tensor(out=krr[64:96, :], in0=t1[64:96, :], in1=t2[64:96, :], op=ALU.add)
            sc.add("dve", krope, reads=[krar, krbr, ccr, ssr], writes=[t1r, t2r, krrr])
            sc.add("sp", lambda e, krr=krr, tsl=tsl: e.dma_start(out=DKR[:, tsl], in_=krr[64:96, :]), reads=[krrr], dma=True)
            for h in range(D_HEADS):
                ba, bar_ = bank(2 + (h % 2) * 3)
                bb, bbr = bank(3 + (h % 2) * 3)
                bkk, bkr = bank(4 + (h % 2) * 3)

                def upq(e, ba=ba, bb=bb, bkk=bkk, h=h, cqn=cqn):
                    for c in range(2):
                        e.matmul(ba[0:96, :], lhsT=wuqa[:, c, h * 96:(h + 1) * 96], rhs=cqn[:, c, :], start=(c == 0), stop=(c == 1))
                    for c in range(2):
                        e.matmul(bb[0:96, :], lhsT=wuqb[:, c, h * 96:(h + 1) * 96], rhs=cqn[:, c, :], start=(c == 0), stop=(c == 1))
                    return e.matmul(bkk[0:64, :], lhsT=wukvk[:, h * 64:(h + 1) * 64], rhs=cqn[:, 2, :], start=True, stop=True)
                sc.add("pe", upq, reads=[cqnr, "wuqa", "wuqb", "wukvk"], writes=[bar_, bbr, bkr])
                qd, qdr = qd_rot.next()
                kd, kdr = kd_rot.next()
                t1, t1r = t1_rot.next()
                t2, t2r = t2_rot.next()

                def qrope(e, ba=ba, bb=bb, qd=qd, t1=t1, t2=t2, cct=cct, sst=sst):
                    e.tensor_copy(out=qd[0:64, :], in_=ba[0:64, :])
                    e.tensor_tensor(out=t1[64:96, :], in0=bb[64:96, :], in1=sst[64:96, :], op=ALU.mult)
                    e.tensor_tensor(out=t2[64:96, :], in0=ba[64:96, :], in1=cct[64:96, :], op=ALU.mult)
                    return e.tensor_tensor(out=qd[64:96, :], in0=t1[64:96, :], in1=t2[64:96, :], op=ALU.add)
                sc.add("dve", qrope, reads=[bar_, bbr, ccr, ssr], writes=[qdr, t1r, t2r])
                sc.add("sp", lambda e, qd=qd, h=h, tsl=tsl: e.dma_start(out=DQ[h * 96:(h + 1) * 96, tsl], in_=qd[0:96, :]), reads=[qdr], dma=True)
                sc.add("act", lambda e, kd=kd, bkk=bkk: e.activation(out=kd[0:64, :], in_=bkk[0:64, :], func=AF.Copy), reads=[bkr], writes=[kdr])
                sc.add("act", lambda e, kd=kd, h=h, tsl=tsl: e.dma_start(out=DK[h * 64:(h + 1) * 64, tsl], in_=kd[0:64, :]), reads=[kdr], dma=True)
            for tb in range(4):
                bv, bvr = bank(tb % 2)
                b = t * 4 + tb
                sc.add("pe", lambda e, bv=bv, tb=tb, cqn=cqn: e.matmul(bv[:, 0:DW], lhsT=cqn[:, 2, tb * 128:(tb + 1) * 128], rhs=wukvv, start=True, stop=True),
                       reads=[cqnr, "wukvv"], writes=[bvr])
                sc.add("act", lambda e, bv=bv, b=b: e.activation(out=dvst[:, :, b, :], in_=bv[:, 0:DW].rearrange("p (h d) -> p h d", h=D_HEADS), func=AF.Copy),
                       reads=[bvr], writes=[("dvst", b)])
        sc.add("sp", lambda e: e.dma_start(out=DV, in_=dvst.rearrange("p h b d -> p (h b d)")), reads=[("dvst", b) for b in range(32)], dma=True)
        sc.barrier()

        ar.reset()
        ma = ar.alloc([3 * A_SLOTS, 384], F32)
        sc.add("sp", lambda e: e.dma_start(out=ma.rearrange("p a b -> p (a b)"), in_=ma_d), writes=["ma"], dma=True)
        acc = ar.alloc([S], F32)
        kt_rot = Rot(ar, 2, [S], BF16, "kt")
        qt_rot = Rot(ar, 2, [S], BF16, "qt")
        vraw_rot = Rot(ar, 2, [32 * 64], BF16, "vraw")
        vt_tiles = [ar.alloc([32, 128], BF16) for _ in range(2)]
        for k in range(2):
            sc.add("pool", lambda e, k=k: e.memset(vt_tiles[k][:, :, 64:128], 1.0), writes=[("vt1", k)])
        pf_rot = Rot(ar, 3, [384], F32, "pf")
        pb_rot = Rot(ar, 4, [384], BF16, "pb")
        ag_rot = Rot(ar, 2, [TC], BF16, "ag")
        rz_rot = Rot(ar, 2, [TC], F32, "rz")
        yf_rot = Rot(ar, 2, [TC], F32, "yf")
        yb_rot = Rot(ar, 2, [TC], BF16, "yb")
        groups = [(s_, g) for s_ in range(A_SLOTS) for g in range(3)]
        gctx = {}

        def a_load(gi):
            s_, g = groups[gi]
            kt, ktr = kt_rot.next()
            qt, qtr = qt_rot.next()
            vraw, vrr = vraw_rot.next()
            vk = gi % 2
            vt, vtr = vt_tiles[vk], ("vt", vk)
            sc.add("sp", lambda e: e.dma_start(out=kt[0:64, :], in_=AK[g][s_ * 64:(s_ + 1) * 64, :]), writes=[ktr], dma=True)
            sc.add("sp", lambda e: e.dma_start(out=qt[0:64, :], in_=AQ[g][s_ * 64:(s_ + 1) * 64, :]), writes=[qtr], dma=True)
            sc.add("sp", lambda e: e.dma_start(out=vraw, in_=AV[g][:, s_ * 2048:(s_ + 1) * 2048]), writes=[vrr], dma=True)
            sc.add("pool", lambda e: e.tensor_copy(out=vt[:, :, 0:64], in_=vraw.rearrange("p (b d) -> p b d", b=32)),
                   reads=[vrr, ("vt1", vk)], writes=[vtr])
            gctx[gi] = dict(kt=kt, ktr=ktr, qt=qt, qtr=qtr, vt=vt, vtr=vtr)

        items = []
        for gi, (s_, g) in enumerate(groups):
            for qb in range(32):
                items.append(dict(gi=gi, s_=s_, g=g, qb=qb, idx=len(items)))

        def a_S(it):
            gi, qb, g = it["gi"], it["qb"], it["g"]
            if qb == 0 and gi == 0:
                a_load(0)
            if qb == 16 and gi + 1 < len(groups):
                a_load(gi + 1)
            c = gctx[gi]
            d = A_DIL[g]
            nbs = 32 // d
            lb = qb % nbs
            js = [j for j in range(3) if 0 <= lb - 1 + j < nbs]
            psS, psSr = bank(it["idx"] % 3)
            kt, qt = c["kt"], c["qt"]

            def f(e):
                r = None
                for j in js:
                    kb = qb - 1 + j
                    r = e.matmul(psS[:, j * 128:(j + 1) * 128], lhsT=kt[0:64, kb * 128:(kb + 1) * 128], rhs=qt[0:64, qb * 128:(qb + 1) * 128],
                                 start=True, stop=True)
                return r
            sc.add("pe", f, reads=[c["ktr"], c["qtr"]], writes=[psSr])
            it.update(js=js, psS=psS, psSr=psSr, d=d, nbs=nbs, lb=lb)

        def a_E(it):
            js, psS = it["js"], it["psS"]
            c0, c1 = js[0] * 128, (js[-1] + 1) * 128
            pf, pfr = pf_rot.next()
            sc.add("act", lambda e: e.activation(out=pf[:, c0:c1], in_=psS[:, c0:c1], func=AF.Exp), reads=[it["psSr"]], writes=[pfr])
            it.update(pf=pf, pfr=pfr, c0=c0, c1=c1)

        def a_M(it):
            pf, c0, c1 = it["pf"], it["c0"], it["c1"]
            mi = it["g"] * A_SLOTS + it["s_"]
            pb, pbr = pb_rot.next()
            sc.add("dve", lambda e: e.tensor_tensor(out=pb[:, c0:c1], in0=pf[:, c0:c1], in1=ma[:, mi, c0:c1], op=ALU.mult),
                   reads=[it["pfr"], "ma"], writes=[pbr])
            it.update(pb=pb, pbr=pbr)

        def a_P(it):
            c = gctx[it["gi"]]
            vt, pb, js, qb = c["vt"], it["pb"], it["js"], it["qb"]
            psO, psOr = bank(3 + it["idx"] % 3)

            def pv(e):
                r = None
                for j in js:
                    kb = qb - 1 + j
                    r = e.matmul(psO[:, 0:128], lhsT=vt[:, kb, :], rhs=pb[:, j * 128:(j + 1) * 128], start=(j == js[0]), stop=(j == js[-1]))
                return r
            sc.add("pe", pv, reads=[it["pbr"], c["vtr"]], writes=[psOr])
            it.update(psO=psO, psOr=psOr)

        def a_A(it):
            d, nbs, qb, g, s_ = it["d"], it["nbs"], it["qb"], it["g"], it["s_"]
            psO = it["psO"]
            rr_, lb_ = qb // nbs, qb % nbs
            if d == 1:
                av = acc[:, qb * 128:(qb + 1) * 128]
            else:
                av = acc.rearrange("p (l r) -> p r l", r=d)[:, rr_, lb_ * 128:(lb_ + 1) * 128]
            if g == 0:
                sc.add("dve", lambda e: e.tensor_copy(out=av, in_=psO[:, 0:128]), reads=[it["psOr"]], writes=["acc"])
            else:
                sc.add("dve", lambda e: e.tensor_tensor(out=av, in0=psO[:, 0:128], in1=av, op=ALU.add), reads=[it["psOr"], "acc"], writes=["acc"])
            if g == 2 and qb == 31:
                for t in range(NT):
                    a_epi(s_, t)

        def a_epi(s_, t):
            tsl = slice(t * TC, (t + 1) * TC)
            agt, agr = ag_rot.next()
            rz, rzr = rz_rot.next()
            yf, yfr = yf_rot.next()
            yb, ybr = yb_rot.next()
            sc.add("sp", lambda e: e.dma_start(out=agt[0:64, :], in_=AG[s_ * 64:(s_ + 1) * 64, tsl]), writes=[agr], dma=True)

            def epi(e):
                e.reciprocal(out=rz[0:64, :], in_=acc[64:128, tsl])
                e.tensor_tensor(out=yf[0:64, :], in0=acc[0:64, tsl], in1=rz[0:64, :], op=ALU.mult)
                return e.tensor_tensor(out=yb[0:64, :], in0=yf[0:64, :], in1=agt[0:64, :], op=ALU.mult)
            sc.add("dve", epi, reads=["acc", agr], writes=[rzr, yfr, ybr])
            sc.add("sp", lambda e: e.dma_start(out=Y[YA0 + s_ * 64:YA0 + (s_ + 1) * 64, tsl], in_=yb[0:64, :]), reads=[ybr], dma=True)

        run_pipeline(items, [a_S, a_E, a_M, a_P, a_A], [0, 1, 2, 3, 4])
        sc.barrier()

        ar.reset()
        lw = ar.alloc([4 * NBC, 128], BF16)
        load_cast(lruw_d[l], 4 * NBC * 128, lw.rearrange("p a b -> p (a b)"), "lw")
        xp = ar.alloc([S + 4], F32)
        xc = ar.alloc([S], F32)
        Rb = ar.alloc([S], F32)
        Ib = ar.alloc([S], F32)
        Ab_ = ar.alloc([S], F32)
        Hf = ar.alloc([S], F32)
        Hb = ar.alloc([S], F32)
        xcb = ar.alloc([S], BF16)
        bgt = ar.alloc([S], BF16)
        sc.add("dve", lambda e: e.memset(xp[:, 0:1], 0.0), writes=["xp_pad0"])
        sc.add("dve", lambda e: e.memset(xp[:, S + 1:S + 4], 0.0), writes=["xp_pad1"])
        for c in range(NBC):
            sc.add("sp", lambda e, c=c: e.dma_start(out=xp[:, 1:S + 1], in_=BX[c * 128:(c + 1) * 128, :]), writes=["xp"], dma=True)
            sc.add("sp", lambda e, c=c: e.dma_start(out=bgt, in_=BG[c * 128:(c + 1) * 128, :]), writes=["bgt"], dma=True)

            def conv(e, c=c):
                e.tensor_scalar(out=xc, in0=xp[:, 0:S], scalar1=col(SM_CONVW + 0 * NBC + c), scalar2=col(SM_CONVB + c), op0=ALU.mult, op1=ALU.add)
                for j in range(1, 4):
                    r = e.scalar_tensor_tensor(out=xc, in0=xp[:, j:j + S], scalar=col(SM_CONVW + j * NBC + c), in1=xc, op0=ALU.mult, op1=ALU.add)
                return r
            sc.add("dve", conv, reads=["xp", "xp_pad0", "xp_pad1", "small"], writes=["xc"])
            sc.add("pool", lambda e: e.tensor_copy(out=xcb, in_=xc), reads=["xc"], writes=["xcb"])
            for dr in range(2):
                for t in range(NT):
                    tsl = slice(t * TC, (t + 1) * TC)
                    bR, bRr = bank((2 * t) % 8)
                    bI, bIr = bank((2 * t + 1) % 8)

                    def gmm(e, bR=bR, bI=bI, tsl=tsl, c=c, dr=dr):
                        e.matmul(bR, lhsT=lw[:, (0 * 2 + dr) * NBC + c, :], rhs=xcb[:, tsl], start=True, stop=True)
                        return e.matmul(bI, lhsT=lw[:, (1 * 2 + dr) * NBC + c, :], rhs=xcb[:, tsl], start=True, stop=True)
                    sc.add("pe", gmm, reads=["xcb", "lw"], writes=[bRr, bIr])
                    sc.add("dve", lambda e, bR=bR, tsl=tsl, c=c, dr=dr: e.tensor_scalar(out=Rb[:, tsl], in0=bR, scalar1=col(SM_BR + dr * NBC + c), scalar2=None, op0=ALU.add),
                           reads=[bRr, "small"], writes=[("Rb", t)])
                    sc.add("dve", lambda e, bI=bI, tsl=tsl, c=c, dr=dr: e.tensor_scalar(out=Ib[:, tsl], in0=bI, scalar1=col(SM_BI + dr * NBC + c), scalar2=None, op0=ALU.add),
                           reads=[bIr, "small"], writes=[("Ib", t)])
                Rres = [("Rb", t) for t in range(NT)]
                Ires = [("Ib", t) for t in range(NT)]

                def gates(e, c=c, dr=dr):
                    e.activation(out=Rb, in_=Rb, func=AF.Sigmoid)
                    e.activation(out=Ib, in_=Ib, func=AF.Sigmoid)
                    e.activation(out=Ab_, in_=Rb, func=AF.Exp, scale=spc[:, dr * NBC + c:dr * NBC + c + 1])
                    e.activation(out=Rb, in_=Ab_, func=AF.Square)
                    return e.activation(out=Rb, in_=Rb, func=AF.Sqrt, scale=-1.0, bias=1.0)
                sc.add("act", gates, reads=Rres + Ires + ["spc", "Hscan%d" % dr], writes=["Rw", "Ig", "Ab"])
                Hd = Hf if dr == 0 else Hb

                def premul(e):
                    e.tensor_tensor(out=Ib, in0=Ib, in1=xc, op=ALU.mult)
                    return e.tensor_tensor(out=Ib, in0=Ib, in1=Rb, op=ALU.mult)
                sc.add("dve", premul, reads=["Rw", "Ig", "Ab", "xc"], writes=["U", "Hscan%d" % (1 - dr)] + Rres + Ires)
                order = list(range(NT)) if dr == 0 else list(range(NT - 1, -1, -1))
                for oi, t in enumerate(order):
                    tsl = slice(t * TC, (t + 1) * TC)
                    if oi == 0:
                        init = 0.0
                    elif dr == 0:
                        init = Hd[:, t * TC - 1:t * TC]
                    else:
                        init = Hd[:, (t + 1) * TC:(t + 1) * TC + 1]
                    if dr == 0:
                        sc.add("dve", lambda e, Hd=Hd, tsl=tsl, init=init: e.tensor_tensor_scan(out=Hd[:, tsl], data0=Ab_[:, tsl], data1=Ib[:, tsl], initial=init, op0=ALU.mult, op1=ALU.add),
                               reads=["U", "Ab"] + ([("Hc", dr, order[oi - 1])] if oi else []), writes=[("Hc", dr, t)])
                    else:
                        sc.add("dve", lambda e, Hd=Hd, tsl=tsl, init=init: e.tensor_tensor_scan(out=Hd[:, tsl][:, ::-1], data0=Ab_[:, tsl][:, ::-1], data1=Ib[:, tsl][:, ::-1], initial=init, op0=ALU.mult, op1=ALU.add),
                               reads=["U", "Ab"] + ([("Hc", dr, order[oi - 1])] if oi else []), writes=[("Hc", dr, t)])

            def fin(e):
                e.tensor_tensor(out=Hf, in0=Hf, in1=Hb, op=ALU.add)
                return e.tensor_tensor(out=bgt, in0=Hf, in1=bgt, op=ALU.mult)
            sc.add("dve", fin, reads=[("Hc", 0, NT - 1), ("Hc", 1, 0), "bgt"], writes=["bgt"])
            sc.add("sp", lambda e, c=c: e.dma_start(out=Y[YB0 + c * 128:YB0 + (c + 1) * 128, :], in_=bgt), reads=["bgt"], dma=True)
        sc.barrier()

        ar.reset()
        cs = c_slopes() if not SPLIT else [min(c_slopes()[h], c_slopes()[h + 2]) for h in range(2)]
        ka_rot = Rot(ar, 4, [S], BF16, "ka")
        vc_rot = Rot(ar, 2, [32 * 128], BF16, "vc")
        qb_rot = Rot(ar, 4, [TC], BF16, "qbf")
        qa_rot = Rot(ar, 4, [TC], BF16, "qaf")
        cg_rot = Rot(ar, 2, [TC], BF16, "cg")
        pbc_rot = Rot(ar, 5, [TC], BF16, "pbc")
        pfc_rot = Rot(ar, 2, [128], F32, "pfc")
        eo_rot = Rot(ar, 4, [TC], F32, "eo")
        ez_rot = Rot(ar, 4, [TC], F32, "ez")
        e_o1 = ar.alloc([TC], F32)
        e_sq = ar.alloc([TC], F32)
        e_sd = ar.alloc([TC], F32)
        ybc_rot = Rot(ar, 2, [TC], BF16, "ybc")
        hctx, qctx = {}, {}

        def c_load_head(h):
            kas = []
            for c in range(2):
                ka, kar = ka_rot.next()
                kas.append((ka, kar))
                sc.add("sp", lambda e, ka=ka, c=c: e.dma_start(out=ka[0:64, :], in_=CK[(h * 2 + c) * 64:(h * 2 + c + 1) * 64, :]), writes=[kar], dma=True)
                sc.add("sp", lambda e, ka=ka: e.dma_start(out=ka[64:68, :], in_=caugk_d[h]), writes=[(kar, "aug")], dma=True)
            vc, vcr = vc_rot.next()
            sc.add("sp", lambda e: e.dma_start(out=vc, in_=CV[:, h * 4096:(h + 1) * 4096]), writes=[vcr], dma=True)
            hctx[h] = dict(kas=kas, vcv=vc.rearrange("p (b d) -> p b d", b=32), vcr=vcr)

        def c_load_chunk(h, qc):
            tsl = slice(qc * TC, (qc + 1) * TC)
            qs = []
            for c in range(2):
                qbf, qbr = qb_rot.next()
                qaf, qar = qa_rot.next()
                src = CQ[(h * 2 + c) * 64:(h * 2 + c + 1) * 64, tsl]
                sc.add("sp", lambda e, qbf=qbf, src=src: e.dma_start(out=qbf[0:64, :], in_=src), writes=[qbr], dma=True)
                sc.add("sp", lambda e, qbf=qbf: e.dma_start(out=qbf[64:68, :], in_=caugq_d[h, 0]), writes=[(qbr, "aug")], dma=True)
                sc.add("sp", lambda e, qaf=qaf, src=src: e.dma_start(out=qaf[0:64, :], in_=src), writes=[qar], dma=True)
                sc.add("sp", lambda e, qaf=qaf: e.dma_start(out=qaf[64:68, :], in_=caugq_d[h, 1]), writes=[(qar, "aug")], dma=True)
                qs.append((qbf, qbr, qaf, qar))
            cgt, cgr = cg_rot.next()
            sc.add("sp", lambda e: e.dma_start(out=cgt, in_=CG[h * 128:(h + 1) * 128, tsl]), writes=[cgr], dma=True)
            qctx[(h, qc)] = dict(qs=qs, cgt=cgt, cgr=cgr)

        items = []
        for h in range(C_HEADS):
            m = cs[h]
            for qc in range(NT):
                i0 = qc * TC
                for c in range(2):
                    kbs = []
                    for kb in range(32):
                        j0 = kb * 128
                        if j0 + 128 <= i0:
                            if m * (i0 - (j0 + 127)) > SKIP_T:
                                continue
                        elif j0 >= i0 + TC:
                            if m * (j0 - (i0 + TC - 1)) > SKIP_T:
                                continue
                        kbs.append(kb)
                    for ii, kb in enumerate(kbs):
                        items.append(dict(h=h, qc=qc, c=c, kb=kb, ii=ii, first=(ii == 0), last=(ii == len(kbs) - 1), idx=len(items),
                                          chunk_first=(c == 0 and ii == 0)))
        psOb = [bank(3), bank(5)]
        psZb = [bank(4), bank(6)]

        def c_S(it):
            h, qc, c, kb = it["h"], it["qc"], it["c"], it["kb"]
            if it["chunk_first"] and h == 0 and qc == 0:
                c_load_head(0)
                c_load_chunk(0, 0)
            if c == 0 and it["ii"] == 4:
                if qc == 0 and h + 1 < C_HEADS:
                    c_load_head(h + 1)
                nh, nq = (h, qc + 1) if qc + 1 < NT else (h + 1, 0)
                if nh < C_HEADS:
                    c_load_chunk(nh, nq)
            m = cs[h]
            i0 = qc * TC
            ka, kar = hctx[h]["kas"][c]
            qbf, qbr, qaf, qar = qctx[(h, qc)]["qs"][c]
            psS, psSr = bank(it["idx"] % 3)
            j0 = kb * 128
            rds = [kar, (kar, "aug"), qbr, (qbr, "aug"), qar, (qar, "aug")]
            pbt, pbr = pbc_rot.next()
            it.update(pbt=pbt, pbr=pbr)
            if j0 + 128 <= i0 or j0 >= i0 + TC:
                before = j0 + 128 <= i0
                qq = qbf if before else qaf
                bcol = h * CB_W + abs(i0 - j0) // 128
                sc.add("pe", lambda e: e.matmul(psS, lhsT=ka[0:68, j0:j0 + 128], rhs=qq[0:68, :], start=True, stop=True), reads=rds, writes=[psSr])
                sc.add("act", lambda e: e.activation(out=pbt, in_=psS, func=AF.Exp, bias=cbias[:, bcol:bcol + 1]), reads=[psSr, "cbias"], writes=[pbr])
            else:
                sb = (j0 - i0) // 128
                ca, cb_, cc_ = sb * 128, (sb + 1) * 128, TC

                def mmd(e):
                    r = None
                    if sb > 0:
                        r = e.matmul(psS[:, 0:ca], lhsT=ka[0:68, j0:j0 + 128], rhs=qaf[0:68, 0:ca], start=True, stop=True)
                    r = e.matmul(psS[:, ca:cb_], lhsT=ka[0:64, j0:j0 + 128], rhs=qbf[0:64, ca:cb_], start=True, stop=True)
                    if sb < 3:
                        r = e.matmul(psS[:, cb_:cc_], lhsT=ka[0:68, j0:j0 + 128], rhs=qbf[0:68, cb_:cc_], start=True, stop=True)
                    return r
                sc.add("pe", mmd, reads=rds, writes=[psSr])
                pfc, pfr = pfc_rot.next()

                def actd(e):
                    if sb > 0:
                        e.activation(out=pbt[:, 0:ca], in_=psS[:, 0:ca], func=AF.Exp, bias=cbias[:, h * CB_W + sb:h * CB_W + sb + 1])
                    if sb < 3:
                        e.activation(out=pbt[:, cb_:cc_], in_=psS[:, cb_:cc_], func=AF.Exp, bias=cbias[:, h * CB_W + 32 + sb:h * CB_W + 33 + sb])
                    return e.activation(out=pfc, in_=psS[:, ca:cb_], func=AF.Exp)
                sc.add("act", actd, reads=[psSr, "cbias"], writes=[pbr, (pbr, "a"), pfr])
                sc.add("dve", lambda e: e.tensor_tensor(out=pbt[:, ca:cb_], in0=pfc, in1=mdiag[:, h, :], op=ALU.mult), reads=[pfr, "mdiag", (pbr, "a")], writes=[pbr])

        def c_P(it):
            h, qc, c, kb = it["h"], it["qc"], it["c"], it["kb"]
            o_, or_ = psOb[c]
            z_, zr_ = psZb[c]
            vcv, vcr = hctx[h]["vcv"], hctx[h]["vcr"]
            pbt, first, last = it["pbt"], it["first"], it["last"]

            def f(e):
                e.matmul(o_, lhsT=vcv[:, kb, :], rhs=pbt, start=first, stop=last)
                return e.matmul(z_, lhsT=ones_bf, rhs=pbt, start=first, stop=last)
            sc.add("pe", f, reads=[it["pbr"], vcr, "ones_bf"], writes=[or_, zr_])
            if last:
                eo, eor = eo_rot.next()
                ez, ezr = ez_rot.next()
                sc.add("dve", lambda e: e.tensor_copy(out=eo, in_=o_), reads=[or_], writes=[eor])
                sc.add("dve", lambda e: e.tensor_copy(out=ez, in_=z_), reads=[zr_], writes=[ezr])
                qctx[(h, qc)]["ev%d" % c] = (eo, eor, ez, ezr)
                if c == 1:
                    c_epi(h, qc)

        def c_epi(h, qc):
            tsl = slice(qc * TC, (qc + 1) * TC)
            q = qctx[(h, qc)]
            eo0, eo0r, ez0, ez0r = q["ev0"]
            eo1, eo1r, ez1, ez1r = q["ev1"]
            cgt, cgr = q["cgt"], q["cgr"]

            def ep1(e):
                e.reciprocal(out=ez0, in_=ez0)
                e.reciprocal(out=ez1, in_=ez1)
                e.tensor_tensor(out=eo0, in0=eo0, in1=ez0, op=ALU.mult)
                e.tensor_tensor(out=eo1, in0=eo1, in1=ez1, op=ALU.mult)
                return e.scalar_tensor_tensor(out=eo0, in0=eo1, scalar=lamt[:, 4:5], in1=eo0, op0=ALU.mult, op1=ALU.add)
            sc.add("dve", ep1, reads=[eo0r, eo1r, ez0r, ez1r], writes=[eo0r, eo1r, ez0r, ez1r])
            sc.add("act", lambda e: e.activation(out=e_sq, in_=eo0, func=AF.Square), reads=[eo0r], writes=["e_sq"])
            bn, bnr = bank(7)
            sc.add("pe", lambda e: e.matmul(bn, lhsT=ones_f, rhs=e_sq, start=True, stop=True), reads=["e_sq", "ones_f"], writes=[bnr])
            sc.add("act", lambda e: e.activation(out=e_sd, in_=bn, func=AF.Sqrt, scale=1.0 / 128, bias=EPS), reads=[bnr], writes=["e_sd"])
            ybc, ybcr = ybc_rot.next()

            def ep2(e):
                e.reciprocal(out=e_o1, in_=e_sd)
                e.scalar_tensor_tensor(out=eo1, in0=eo0, scalar=lamt[:, 5:6], in1=e_o1, op0=ALU.mult, op1=ALU.mult)
                return e.tensor_tensor(out=ybc, in0=eo1, in1=cgt, op=ALU.mult)
            sc.add("dve", ep2, reads=["e_sd", eo0r, eo1r, cgr], writes=[ybcr, "e_o1", eo1r])
            sc.add("sp", lambda e: e.dma_start(out=Y[YC0 + h * 128:YC0 + (h + 1) * 128, tsl], in_=ybc), reads=[ybcr], dma=True)

        run_pipeline(items, [c_S, c_P], [0, 2])
        sc.barrier()

        ar.reset()
        scale_d = 96.0 ** -0.5
        kd2_rot = Rot(ar, 2, [S], BF16, "kd2")
        vraw2_rot = Rot(ar, 2, [32 * 64], BF16, "vraw2")
        vd_tiles = [ar.alloc([32, 128], BF16) for _ in range(2)]
        for k in range(2):
            sc.add("pool", lambda e, k=k: e.memset(vd_tiles[k][:, :, 64:128], 1.0), writes=[("vd1", k)])
        qd2_rot = Rot(ar, 3, [TC], BF16, "qd2")
        dg_rot = Rot(ar, 3, [TC], BF16, "dg")
        pbd_rot = Rot(ar, 5, [TC], BF16, "pbd")
        rzd_rot = Rot(ar, 2, [TC], F32, "rzd")
        yfd_rot = Rot(ar, 2, [TC], F32, "yfd")
        ybd_rot = Rot(ar, 2, [TC], BF16, "ybd")
        dh, dq = {}, {}

        def d_load_head(h):
            kd, kdr = kd2_rot.next()
            sc.add("sp", lambda e: e.dma_start(out=kd[0:64, :], in_=DK[h * 64:(h + 1) * 64, :]), writes=[kdr], dma=True)
            sc.add("sp", lambda e: e.dma_start(out=kd[64:96, :], in_=DKR), writes=[(kdr, "r")], dma=True)
            vraw, vrr = vraw2_rot.next()
            vk = h % 2
            vt, vtr = vd_tiles[vk], ("vd", vk)
            sc.add("sp", lambda e: e.dma_start(out=vraw, in_=DV[:, h * 2048:(h + 1) * 2048]), writes=[vrr], dma=True)
            sc.add("pool", lambda e: e.tensor_copy(out=vt[:, :, 0:64], in_=vraw.rearrange("p (b d) -> p b d", b=32)),
                   reads=[vrr, ("vd1", vk)], writes=[vtr])
            dh[h] = dict(kd=kd, kdr=kdr, vt=vt, vtr=vtr)

        def d_load_chunk(h, qc):
            tsl = slice(qc * TC, (qc + 1) * TC)
            qd, qdr = qd2_rot.next()
            sc.add("sp", lambda e: e.dma_start(out=qd[0:96, :], in_=DQ[h * 96:(h + 1) * 96, tsl]), writes=[qdr], dma=True)
            dgt, dgr = dg_rot.next()
            sc.add("sp", lambda e: e.dma_start(out=dgt[0:64, :], in_=DG[h * 64:(h + 1) * 64, tsl]), writes=[dgr], dma=True)
            dq[(h, qc)] = dict(qd=qd, qdr=qdr, dgt=dgt, dgr=dgr)

        items = [dict(h=h, qc=qc, kb=kb, idx=(h * NT + qc) * 32 + kb) for h in range(D_HEADS) for qc in range(NT) for kb in range(32)]

        def d_S(it):
            h, qc, kb = it["h"], it["qc"], it["kb"]
            if kb == 0 and qc == 0 and h == 0:
                d_load_head(0)
                d_load_chunk(0, 0)
            if kb == 4:
                if qc == 0 and h + 1 < D_HEADS:
                    d_load_head(h + 1)
                nh, nq = (h, qc + 1) if qc + 1 < NT else (h + 1, 0)
                if nh < D_HEADS:
                    d_load_chunk(nh, nq)
            kd, kdr = dh[h]["kd"], dh[h]["kdr"]
            qd, qdr = dq[(h, qc)]["qd"], dq[(h, qc)]["qdr"]
            psS, psSr = bank(it["idx"] % 3)
            pbt, pbr = pbd_rot.next()
            sc.add("pe", lambda e: e.matmul(psS, lhsT=kd[0:96, kb * 128:(kb + 1) * 128], rhs=qd[0:96, :], start=True, stop=True),
                   reads=[kdr, (kdr, "r"), qdr], writes=[psSr])
            sc.add("act", lambda e: e.activation(out=pbt, in_=psS, func=AF.Exp, scale=scale_d), reads=[psSr], writes=[pbr])
            it.update(pbt=pbt, pbr=pbr)

        def d_P(it):
            h, qc, kb = it["h"], it["qc"], it["kb"]
            vt, vtr = dh[h]["vt"], dh[h]["vtr"]
            psO, psOr = bank(3 + (h * NT + qc) % 2)
            pbt = it["pbt"]
            sc.add("pe", lambda e: e.matmul(psO, lhsT=vt[:, kb, :], rhs=pbt, start=(kb == 0), stop=(kb == 31)), reads=[it["pbr"], vtr], writes=[psOr])
            if kb == 31:
                tsl = slice(qc * TC, (qc + 1) * TC)
                rz, rzr = rzd_rot.next()
                yf, yfr = yfd_rot.next()
                yb, ybr = ybd_rot.next()
                dgt, dgr = dq[(h, qc)]["dgt"], dq[(h, qc)]["dgr"]

                def epd(e):
                    e.reciprocal(out=rz[0:64, :], in_=psO[64:128, :])
                    e.tensor_tensor(out=yf[0:64, :], in0=psO[0:64, :], in1=rz[0:64, :], op=ALU.mult)
                    return e.tensor_tensor(out=yb[0:64, :], in0=yf[0:64, :], in1=dgt[0:64, :], op=ALU.mult)
                sc.add("dve", epd, reads=[psOr, dgr], writes=[rzr, yfr, ybr])
                sc.add("sp", lambda e: e.dma_start(out=Y[YD0 + h * 64:YD0 + (h + 1) * 64, tsl], in_=yb[0:64, :]), reads=[ybr], dma=True)

        run_pipeline(items, [d_S, d_P], [0, 2])
        sc.barrier()

        ar.reset()
        wbr = ar.alloc([NYC, 1024], BF16)
        wout = ar.alloc([8, 1024], BF16)
        wbr_f = wbr.rearrange("p a b -> p (a b)")
        wout_f = wout.rearrange("p a b -> p (a b)")
        nwb = (NYC * 1024 + 4095) // 4096
        for i in range(nwb):
            n = min(4096, NYC * 1024 - i * 4096)
            load_cast(wbr_d[l][:, i * 4096:i * 4096 + n], n, wbr_f[:, i * 4096:i * 4096 + n], ("wbr", i))
        for i in range(2):
            load_cast(wout_d[l][:, i * 4096:(i + 1) * 4096], 4096, wout_f[:, i * 4096:(i + 1) * 4096], ("wout", i))
        wbr_res = [("wbr", i) for i in range(nwb)]
        wout_res = [("wout", i) for i in range(2)]
        yt_rot = Rot(ar, 2, [NYC, TC], BF16, "yt")
        xt2 = ar.alloc([KC, TC], F32)
        out2 = ar.alloc([KC, TC], F32)
        merged = ar.alloc([KC, TC], BF16)
        mfull_rot = Rot(ar, 2, [KC, TC], BF16, "mfull") if SPLIT else None
        g_rot = Rot(ar, 4, [TC], BF16, "g")
        tm_rot = Rot(ar, 4, [TC], F32, "tm")
        macc_rot = Rot(ar, 3, [TC], F32, "macc")
        sqf_rot = Rot(ar, 2, [TC], F32, "sqf")
        rr2 = ar.alloc([TC], F32)
        Yv = Y.rearrange("(c p) s -> p c s", p=128)
        x_src_v2 = x_src.rearrange("(kc p) s -> p kc s", p=128)
        x_dst_v = x_dst.rearrange("(kc p) s -> p kc s", p=128)
        pbk = [0]
        mres = [("merged", oc) for oc in range(8)]

        def f_stage1(t):
            tsl = slice(t * TC, (t + 1) * TC)
            yt, ytr = yt_rot.next()
            sc.add("sp", lambda e: e.dma_start(out=yt, in_=Yv[:, :, tsl]), writes=[ytr], dma=True)
            for oc in range(8):
                macc = maccr = None
                for br in range(4):
                    gt, gr = g_rot.next()
                    sc.add("sp", lambda e, gt=gt, br=br, oc=oc: e.dma_start(out=gt, in_=GT[br * 1024 + oc * 128:br * 1024 + (oc + 1) * 128, tsl]), writes=[gr], dma=True)
                    bk, bkr = bank(pbk[0] % 4)
                    pbk[0] += 1
                    chs = BR_CHUNKS[br]

                    def bmm(e, bk=bk, chs=chs, oc=oc):
                        r = None
                        for ci, (cidx, nr) in enumerate(chs):
                            r = e.matmul(bk, lhsT=wbr[0:nr, cidx, oc * 128:(oc + 1) * 128], rhs=yt[0:nr, cidx, :], start=(ci == 0), stop=(ci == len(chs) - 1))
                        return r
                    sc.add("pe", bmm, reads=[ytr] + wbr_res, writes=[bkr])
                    if br == 0:
                        macc, maccr = macc_rot.next()
                        sc.add("dve", lambda e, bk=bk, gt=gt, macc=macc: e.tensor_tensor(out=macc, in0=bk, in1=gt, op=ALU.mult), reads=[bkr, gr], writes=[maccr])
                    else:
                        tm, tmr = tm_rot.next()
                        sc.add("dve", lambda e, bk=bk, gt=gt, tm=tm: e.tensor_tensor(out=tm, in0=bk, in1=gt, op=ALU.mult), reads=[bkr, gr], writes=[tmr])
                        if br < 3:
                            sc.add("pool", lambda e, tm=tm, macc=macc: e.tensor_tensor(out=macc, in0=macc, in1=tm, op=ALU.add), reads=[tmr, maccr], writes=[maccr])
                        else:
                            sc.add("pool", lambda e, tm=tm, oc=oc, macc=macc: e.tensor_tensor(out=merged[:, oc, :], in0=macc, in1=tm, op=ALU.add), reads=[tmr, maccr], writes=[("merged", oc)])
            if SPLIT:
                pr, q = t // 2, t % 2
                k = pr % 2
                sc.add("sp", lambda e: e.dma_start(out=ARI[k].rearrange("(kc p) s -> p kc s", p=128)[:, :, q * TC:(q + 1) * TC], in_=merged),
                       reads=mres, writes=[("ari", k, q)], dma=True)
                if q == 1:
                    sc.add("pool", lambda e: e.collective_compute("AllReduce", ALU.add, replica_groups=RG, ins=[ARI[k]], outs=[ARO[k]]),
                           reads=[("ari", k, 0), ("ari", k, 1)], writes=[("aro", k)], cc=True)

        def f_stage2(t):
            tsl = slice(t * TC, (t + 1) * TC)
            if SPLIT:
                pr, q = t // 2, t % 2
                k = pr % 2
                msrc, msr = mfull_rot.next()
                sc.add("sp", lambda e: e.dma_start(out=msrc, in_=ARO[k].rearrange("(kc p) s -> p kc s", p=128)[:, :, q * TC:(q + 1) * TC]),
                       reads=[("aro", k)], writes=[msr], dma=True)
                mrd = [msr]
            else:
                msrc, mrd = merged, mres
            sc.add("sp", lambda e: e.dma_start(out=xt2, in_=x_src_v2[:, :, tsl]), writes=["xt2"], dma=True)
            bn, bnr = bank(6)
            for oc2 in range(8):
                bo, bor = bank(4 + oc2 % 2)

                def omm(e, bo=bo, oc2=oc2):
                    r = None
                    for oc in range(8):
                        r = e.matmul(bo, lhsT=wout[:, oc, oc2 * 128:(oc2 + 1) * 128], rhs=msrc[:, oc, :], start=(oc == 0), stop=(oc == 7))
                    return r
                sc.add("pe", omm, reads=mrd + wout_res, writes=[bor])
                sqf, sqfr = sqf_rot.next()

                def oev(e, bo=bo, oc2=oc2, sqf=sqf):
                    e.activation(out=out2[:, oc2, :], in_=bo, func=AF.Copy)
                    return e.activation(out=sqf, in_=bo, func=AF.Square)
                sc.add("act", oev, reads=[bor], writes=[("out2", oc2), sqfr])
                sc.add("pe", lambda e, sqf=sqf, oc2=oc2: e.matmul(bn, lhsT=ones_f, rhs=sqf, start=(oc2 == 0), stop=(oc2 == 7)), reads=[sqfr, "ones_f"], writes=[bnr])
            def rr2_op(e):
                e.activation(out=rr2, in_=bn, func=AF.Ln, scale=1.0 / DM, bias=EPS)
                return e.activation(out=rr2, in_=rr2, func=AF.Exp, scale=-0.5)
            sc.add("act", rr2_op, reads=[bnr], writes=["rr2"])
            ores = [("out2", i) for i in range(8)]

            def resid(e):
                r = None
                for oc2 in range(8):
                    e.scalar_tensor_tensor(out=out2[:, oc2, :], in0=out2[:, oc2, :], scalar=col(SM_GPOST + oc2), in1=rr2, op0=ALU.mult, op1=ALU.mult)
                    r = e.tensor_tensor(out=xt2[:, oc2, :], in0=xt2[:, oc2, :], in1=out2[:, oc2, :], op=ALU.add)
                return r
            sc.add("dve", resid, reads=["rr2", "xt2", "small"] + ores, writes=["xt2", "rr2"] + ores)
            sc.add("sp", lambda e: e.dma_start(out=x_dst_v[:, :, tsl], in_=xt2), reads=["xt2"], dma=True)

        if SPLIT:
            for t in range(NT):
                f_stage1(t)
                if t % 2 == 1 and t >= 3:
                    f_stage2(t - 3)
                    f_stage2(t - 2)
            f_stage2(NT - 2)
            f_stage2(NT - 1)
        else:
            for t in range(NT):
                f_stage1(t)
                f_stage2(t)
        sc.barrier()

    for l in range(L):
        layer(l)
    sc.analyze()
    sc.emit(nc, es)
    es.close()
    return nc


def _bf(x):
    return np.asarray(x, dtype=np.float32).astype(ml_dtypes.bfloat16)


def _parts(par):
    if not SPLIT:
        return list(range(6)), list(range(6)), list(range(4)), list(range(6))
    return ([3 * par + i for i in range(3)], [3 * par + i for i in range(3)], [2 * par + i for i in range(2)],
            [3 * par + i for i in range(3)])


def build_consts(par=0):
    c = {}
    slots, _, cheads, _ = _parts(par)
    cs_all = c_slopes()
    cs = [cs_all[h] for h in cheads]
    caugq = np.zeros((C_HEADS, 2, 4, TC), np.float32)
    caugk = np.zeros((C_HEADS, 4, S), np.float32)
    cbias = np.zeros((128, C_HEADS, CB_W), np.float32)
    ii = np.arange(TC, dtype=np.float64)
    jj = (np.arange(S) % 128).astype(np.float64)
    for h, m in enumerate(cs):
        qb = (-m * ii).astype(np.float32)
        qb_hi = _bf(qb).astype(np.float32)
        qb_lo = _bf(qb - qb_hi).astype(np.float32)
        caugq[h, 0] = np.stack([qb_hi, qb_lo, np.ones(TC), np.ones(TC)])
        caugq[h, 1] = -caugq[h, 0]
        kb = (m * jj).astype(np.float32)
        kb_hi = _bf(kb).astype(np.float32)
        kb_lo = _bf(kb - kb_hi).astype(np.float32)
        caugk[h] = np.stack([np.ones(S), np.ones(S), kb_hi, kb_lo])
        cbias[:, h, 0:32] = (-m * 128.0 * np.arange(32))[None, :]
        cbias[:, h, 32:36] = (m * 128.0 * np.arange(4))[None, :]
    c["caugq"] = _bf(caugq)
    c["caugk"] = _bf(caugk)
    c["cbias"] = cbias.reshape(128, C_HEADS * CB_W)
    p = np.arange(128)[:, None].astype(np.float64)
    f = np.arange(128)[None, :].astype(np.float64)
    md = np.stack([np.exp(-m * np.abs(p - f)) for m in cs], axis=1)
    c["mdiag"] = md.reshape(128, C_HEADS * 128).astype(np.float32)
    sl_all = a_slopes()
    ma = np.zeros((128, 3 * A_SLOTS, 384), np.float64)
    k = np.arange(128)[:, None]
    for g, d in enumerate(A_DIL):
        for si, sg in enumerate(slots):
            for j in range(3):
                q = np.arange(128)[None, :]
                rel = np.abs((j - 1) * 128 + k - q)
                ma[:, g * A_SLOTS + si, j * 128:(j + 1) * 128] = np.where(rel <= 64, np.exp(-sl_all[sg] * d * rel), 0.0)
    c["ma"] = ma.reshape(128, 3 * A_SLOTS * 384).astype(np.float32)
    inv = (10000.0 ** (-np.arange(0, 32, 2, dtype=np.float32) / 32)).astype(np.float32)
    ang = np.arange(S, dtype=np.float32)[:, None] * inv[None, :]
    cos, sin = np.cos(ang).astype(np.float32).T, np.sin(ang).astype(np.float32).T
    c["ropec"] = np.ascontiguousarray(np.concatenate([cos, cos], 0))
    c["ropes"] = np.ascontiguousarray(np.concatenate([-sin, sin], 0))
    return c


def _bchan(par):
    _, blocks, _, _ = _parts(par)
    idx = -np.ones(BW, np.int64)
    for i, b in enumerate(blocks):
        idx[i * 64:(i + 1) * 64] = np.arange(b * 64, (b + 1) * 64)
    return idx


def pack_w_in(w, par):
    slots, _, cheads, dheads = _parts(par)
    L = w.shape[0]
    out = np.zeros((L, DM, IN_W), np.float32)
    O = OFF_ALL

    def put(name, off_in_fam, src_cols):
        n = len(src_cols)
        out[:, :, OFF[name] + off_in_fam:OFF[name] + off_in_fam + n] = w[:, :, src_cols]
    for fam in ("a_q", "a_k", "a_v"):
        for g in range(3):
            cols = np.concatenate([np.arange(O[fam] + g * 384 + s * 64, O[fam] + g * 384 + (s + 1) * 64) for s in slots])
            put(fam, g * AW, cols)
    put("a_g", 0, np.concatenate([np.arange(O["a_g"] + s * 64, O["a_g"] + (s + 1) * 64) for s in slots]))
    bidx = _bchan(par)
    nreal = int((bidx >= 0).sum())
    put("b_x", 0, O["b_x"] + bidx[:nreal])
    put("b_g", 0, O["b_g"] + bidx[:nreal])
    for fam in ("c_q", "c_k", "c_v", "c_g"):
        put(fam, 0, np.concatenate([np.arange(O[fam] + h * 128, O[fam] + (h + 1) * 128) for h in cheads]))
    put("d_cq", 0, np.arange(O["d_cq"], O["d_cq"] + 256))
    put("d_ckv", 0, np.arange(O["d_ckv"], O["d_ckv"] + 128))
    put("d_kr", 0, np.arange(O["d_kr"], O["d_kr"] + 32))
    put("d_g", 0, np.concatenate([np.arange(O["d_g"] + h * 64, O["d_g"] + (h + 1) * 64) for h in dheads]))
    put("gate", 0, np.arange(O["gate"], O["gate"] + 4096))
    return out


def pack_layers(inp, layers, par=0):
    f32 = np.float32
    L = len(layers)
    slots, blocks, cheads, dheads = _parts(par)
    bidx = _bchan(par)
    real = bidx >= 0
    small = np.zeros((L, 128, NSMALL), f32)
    lamv = np.zeros((L, 1, 256), f32)
    lruw = np.zeros((L, 128, 4 * NBC, 128), f32)
    wuqa = np.zeros((L, 128, 2, D_HEADS * 96), f32)
    wuqb = np.zeros((L, 128, 2, D_HEADS * 96), f32)
    wukvk = np.zeros((L, 128, DW), f32)
    wukvv = np.zeros((L, 128, DW), f32)
    wbr = np.zeros((L, 128, NYC, 1024), f32)
    wout = np.zeros((L, 128, 8, 1024), f32)

    def bvec(v):
        o = np.zeros(BW, f32)
        o[real] = v[bidx[real]]
        return o.reshape(NBC, 128).T
    for li, l in enumerate(layers):
        sm = small[li]
        sm[:, SM_GPRE:SM_GPRE + 8] = inp["norm_pre"][l].reshape(8, 128).T
        sm[:, SM_GPOST:SM_GPOST + 8] = inp["norm_post"][l].reshape(8, 128).T
        sm[:, SM_BGATE:SM_BGATE + 32] = inp["b_gate"][l].reshape(32, 128).T
        for j in range(4):
            sm[:, SM_CONVW + j * NBC:SM_CONVW + (j + 1) * NBC] = bvec(inp["conv_w"][l][j])
        sm[:, SM_CONVB:SM_CONVB + NBC] = bvec(inp["conv_b"][l])
        for dr in range(2):
            sm[:, SM_BR + dr * NBC:SM_BR + (dr + 1) * NBC] = bvec(inp["lru_br"][l][dr])
            sm[:, SM_BI + dr * NBC:SM_BI + (dr + 1) * NBC] = bvec(inp["lru_bi"][l][dr])
            sm[:, SM_LAM + dr * NBC:SM_LAM + (dr + 1) * NBC] = bvec(inp["lru_lambda"][l][dr])
        sm[:, SM_SUBLN] = inp["diff_subln"][l]
        sm[:, SM_QN:SM_QN + 2] = inp["mla_q_norm"][l].reshape(2, 128).T
        sm[:, SM_KVN] = inp["mla_kv_norm"][l]
        lam_init = 0.8 - 0.6 * math.exp(-0.3 * l)
        sm[:, SM_LAMINIT] = lam_init
        sm[:, SM_OML] = (1.0 - lam_init)
        lamv[li, 0, 0:64] = inp["diff_lam_q1"][l]
        lamv[li, 0, 64:128] = inp["diff_lam_k1"][l]
        lamv[li, 0, 128:192] = inp["diff_lam_q2"][l]
        lamv[li, 0, 192:256] = inp["diff_lam_k2"][l]
        for gi, w in enumerate((inp["lru_wr"][l], inp["lru_wi"][l])):
            for dr in range(2):
                for i, b in enumerate(blocks):
                    c, bb = i // 2, i % 2
                    lruw[li, bb * 64:(bb + 1) * 64, (gi * 2 + dr) * NBC + c, bb * 64:(bb + 1) * 64] = w[dr, b]
        uq = inp["mla_w_uq"][l].reshape(2, 128, 6, 96)[:, :, dheads, :]
        wuqa[li] = uq.transpose(1, 0, 2, 3).reshape(128, 2, D_HEADS * 96)
        uqb = uq.copy()
        uqb[..., 64:80] = uq[..., 80:96]
        uqb[..., 80:96] = uq[..., 64:80]
        wuqb[li] = uqb.transpose(1, 0, 2, 3).reshape(128, 2, D_HEADS * 96)
        ukv = inp["mla_w_ukv"][l].reshape(128, 6, 128)[:, dheads, :]
        wukvk[li] = ukv[:, :, 0:64].reshape(128, DW)
        wukvv[li] = ukv[:, :, 64:128].reshape(128, DW)
        wy = np.zeros((NYC * 128, 1024), f32)
        wa, wb_, wc, wd = inp["w_br_a"][l], inp["w_br_b"][l], inp["w_br_c"][l], inp["w_br_d"][l]
        for i, s_ in enumerate(slots):
            wy[YA0 + i * 64:YA0 + (i + 1) * 64] = wa[s_ * 64:(s_ + 1) * 64]
        wy[YB0:YB0 + BW][real] = wb_[bidx[real]]
        for i, h in enumerate(cheads):
            wy[YC0 + i * 128:YC0 + (i + 1) * 128] = wc[h * 128:(h + 1) * 128]
        for i, h in enumerate(dheads):
            wy[YD0 + i * 64:YD0 + (i + 1) * 64] = wd[h * 64:(h + 1) * 64]
        wbr[li] = wy.reshape(NYC, 128, 1024).transpose(1, 0, 2)
        wout[li] = inp["w_out"][l].reshape(8, 128, 1024).transpose(1, 0, 2)
    return dict(small=small, lamv=lamv, lruw=lruw.reshape(L, 128, 4 * NBC * 128), wuqa=wuqa.reshape(L, 128, 2 * D_HEADS * 96),
                wuqb=wuqb.reshape(L, 128, 2 * D_HEADS * 96), wukvk=wukvk, wukvv=wukvv, wbr=wbr.reshape(L, 128, NYC * 1024),
                wout=wout.reshape(L, 128, 8 * 1024))


_PROG = {}


def get_prog(nl, debug=False):
    key = (nl, tuple(debug) if debug else None)
    if key not in _PROG:
        _PROG[key] = build_program(nl, debug)
    return _PROG[key]


FUSED = True


def make_in_maps(inp, layers, xT_by_batch):
    npar = 2 if SPLIT else 1
    w_all = np.ascontiguousarray(inp["w_in"][layers[0]:layers[-1] + 1]).astype(np.float32)
    per = []
    for par in range(npar):
        m = dict(w_in=pack_w_in(w_all, par) if SPLIT else w_all)
        m.update(pack_layers(inp, layers, par))
        m.update(build_consts(par))
        per.append(m)
    in_maps = []
    for c in range(8):
        b, par = (c // 2, c % 2) if SPLIT else (c % 4, 0)
        m = dict(xT=xT_by_batch[b])
        m.update(per[par])
        in_maps.append(m)
    return in_maps


def kernel(**inputs):
    inp = {k: np.asarray(v) for k, v in inputs.items()}
    x = inp["x"].astype(np.float32)
    xT = [np.ascontiguousarray(x[b].T) for b in range(4)]
    groups = [list(range(DEPTH))] if FUSED else [[l] for l in range(DEPTH)]
    for layers in groups:
        nc = get_prog(len(layers))
        in_maps = make_in_maps(inp, layers, xT)
        res = run_bass_kernel_spmd(nc, in_maps, core_ids=list(range(8)))
        xT = [np.asarray(res.results[(2 * b) if SPLIT else b]["outT"]) for b in range(4)]
    out = np.stack([xT[b].T for b in range(4)], 0).astype(np.float32)
    return np.ascontiguousarray(out)
```

```python
import math
from contextlib import ExitStack

import numpy as np
import ml_dtypes

import concourse.bass as bass
import concourse.mybir as mybir
from concourse.bass_utils import run_bass_kernel_spmd

F32 = mybir.dt.float32
BF16 = mybir.dt.bfloat16
AF = mybir.ActivationFunctionType
ALU = mybir.AluOpType
AX = mybir.AxisListType

S = 4096
DM = 1024
DEPTH = 4
NT = 8
TC = 512
KC = 8
EPS = 1e-6
A_DIL = (1, 4, 16)
SPLIT = True
A_SLOTS_ALL, C_HEADS_ALL, D_HEADS_ALL = 6, 4, 6
A_SLOTS = 3 if SPLIT else 6
C_HEADS = 2 if SPLIT else 4
D_HEADS = 3 if SPLIT else 6
NBC = 2 if SPLIT else 3
AW, BW, CW, DW = A_SLOTS * 64, NBC * 128, C_HEADS * 128, D_HEADS * 64
_fam = [("a_q", 3 * AW), ("a_k", 3 * AW), ("a_v", 3 * AW), ("a_g", AW), ("b_x", BW), ("b_g", BW), ("c_q", CW), ("c_k", CW),
        ("c_v", CW), ("c_g", CW), ("d_cq", 256), ("d_ckv", 128), ("d_kr", 32), ("d_g", DW), ("gate", 4096)]
OFF = {}
_o = 0
for _n, _w in _fam:
    OFF[_n] = _o
    _o += _w
IN_W = _o
OFF_ALL = dict(a_q=0, a_k=1152, a_v=2304, a_g=3456, b_x=3840, b_g=4224, c_q=4608, c_k=5120,
               c_v=5632, c_g=6144, d_cq=6656, d_ckv=6912, d_kr=7040, d_g=7072, gate=7456)
SKIP_T = 1e30 if SPLIT else 100.0
if SPLIT:
    YA0, YB0, YC0, YD0, NYC = 0, 256, 512, 768, 8
    BR_CHUNKS = [[(0, 128), (1, 64)], [(2, 128), (3, 64)], [(4, 128), (5, 128)], [(6, 128), (7, 64)]]
else:
    YA0, YB0, YC0, YD0, NYC = 0, 384, 768, 1280, 13
    BR_CHUNKS = [[(0, 128), (1, 128), (2, 128)], [(3, 128), (4, 128), (5, 128)], [(6, 128), (7, 128), (8, 128), (9, 128)],
                 [(10, 128), (11, 128), (12, 128)]]
RG = [[0, 1], [2, 3], [4, 5], [6, 7]]
CB_W = 36

SM_GPRE, SM_GPOST, SM_BGATE, SM_CONVW = 0, 8, 16, 48
SM_CONVB = SM_CONVW + 4 * NBC
SM_BR = SM_CONVB + NBC
SM_BI = SM_BR + 2 * NBC
SM_LAM = SM_BI + 2 * NBC
SM_SUBLN = SM_LAM + 2 * NBC
SM_QN, SM_KVN, SM_LAMINIT, SM_OML = SM_SUBLN + 1, SM_SUBLN + 3, SM_SUBLN + 4, SM_SUBLN + 5
NSMALL = SM_SUBLN + 8


def a_slopes():
    return [2.0 ** (-8.0 * (i + 1) / A_SLOTS_ALL) for i in range(A_SLOTS_ALL)]


def c_slopes():
    return [2.0 ** (-8.0 * (i + 1) / C_HEADS_ALL) for i in range(C_HEADS_ALL)]


class Sched:
    ENGS = ("pe", "act", "dve", "pool", "sp")
    NDMA = {"sp": 24, "act": 12, "pool": 4, "cc": 4}
    UNIT = {"sp": 16, "act": 16, "pool": 16, "cc": 1}

    def __init__(self):
        self.ops = []

    def add(self, eng, fn, reads=(), writes=(), dma=False, cc=False):
        self.ops.append(dict(eng=eng, fn=fn, reads=tuple(reads), writes=tuple(writes), dma=(dma or cc),
                             q=("cc" if cc else eng), needs_inc=False, deps=[]))

    def barrier(self):
        self.ops.append(dict(barrier=True))

    def analyze(self):
        ops = self.ops
        last_w, readers = {}, {}
        last_on = {}
        for i, op in enumerate(ops):
            if op.get("barrier"):
                for e, j in last_on.items():
                    ops[j]["needs_inc"] = True
                last_w, readers = {}, {}
                continue
            deps = set()
            for r in op["reads"]:
                if r in last_w:
                    deps.add((last_w[r], "raw"))
            for w in op["writes"]:
                if w in last_w:
                    deps.add((last_w[w], "waw"))
                for rd in readers.get(w, ()):
                    deps.add((rd, "war"))
            keep = set()
            for d, kind in deps:
                if d == i:
                    continue
                p = ops[d]
                if p["dma"]:
                    keep.add(d)
                elif p["eng"] == op["eng"]:
                    if op["dma"]:
                        keep.add(d)
                    elif kind == "raw" and op["eng"] in ("act", "dve", "pool"):
                        keep.add(d)
                else:
                    keep.add(d)
            op["deps"] = sorted(keep)
            for d in keep:
                ops[d]["needs_inc"] = True
            for w in op["writes"]:
                last_w[w] = i
                readers[w] = []
            for r in op["reads"]:
                if r not in op["writes"]:
                    readers.setdefault(r, []).append(i)
            if not op["dma"]:
                last_on[op["eng"]] = i
        tick = {e: 0 for e in self.ENGS}
        dma_cnt = {q: [0] * n for q, n in self.NDMA.items()}
        dma_rr = {q: 0 for q in self.NDMA}
        seen = {e: {} for e in self.ENGS}
        pending = {e: [] for e in self.ENGS}
        for op in ops:
            if op.get("barrier"):
                snap = [(("c", e), tick[e]) for e in self.ENGS if tick[e] > 0]
                for q, cnts in dma_cnt.items():
                    for k, c in enumerate(cnts):
                        if c > 0:
                            snap.append((("d", q, k), self.UNIT[q] * c))
                for e in self.ENGS:
                    pending[e] = [(s_, v) for (s_, v) in snap if s_ != ("c", e)]
                continue
            e = op["eng"]
            waits = list(pending[e])
            pending[e] = []
            for d in op["deps"]:
                p = ops[d]
                waits.append((p["sem"], p["tick"]))
            if op["dma"]:
                q = op["q"]
                k = dma_rr[q]
                dma_rr[q] = (k + 1) % self.NDMA[q]
                if dma_cnt[q][k] > 0:
                    waits.append((("d", q, k), self.UNIT[q] * dma_cnt[q][k]))
                dma_cnt[q][k] += 1
                op["sem"] = ("d", q, k)
                op["tick"] = self.UNIT[q] * dma_cnt[q][k]
            elif op["needs_inc"]:
                tick[e] += 1
                op["sem"] = ("c", e)
                op["tick"] = tick[e]
            fw = []
            for s_, v in waits:
                if seen[e].get(s_, 0) >= v:
                    continue
                seen[e][s_] = v
                fw.append((s_, v))
            mx = {}
            for s_, v in fw:
                mx[s_] = max(mx.get(s_, 0), v)
            op["waits"] = sorted(mx.items(), key=lambda kv: str(kv[0]))
        self.final_dma = {(q, k): self.UNIT[q] * c for q, cnts in dma_cnt.items() for k, c in enumerate(cnts) if c > 0}

    def emit(self, nc, es):
        sems = {}
        for e in self.ENGS:
            sems[("c", e)] = es.enter_context(nc.semaphore("c_" + e))
        for q, n in self.NDMA.items():
            for k in range(n):
                sems[("d", q, k)] = es.enter_context(nc.semaphore("d_%s_%d" % (q, k)))
        blk = es.enter_context(nc.Block())
        ops = self.ops

        def run(engname):
            def body(e):
                for op in ops:
                    if op.get("barrier") or op["eng"] != engname:
                        continue
                    for s_, v in op["waits"]:
                        e.wait_ge(sems[s_], v)
                    ins = op["fn"](e)
                    if op["dma"]:
                        ins.then_inc(sems[op["sem"]], self.UNIT[op["q"]])
                    elif op["needs_inc"]:
                        ins.then_inc(sems[op["sem"]], 1)
                if engname == "sp":
                    for (q, k), v in sorted(self.final_dma.items()):
                        e.wait_ge(sems[("d", q, k)], v)
            return body

        blk.tensor(run("pe"))
        blk.scalar(run("act"))
        blk.vector(run("dve"))
        blk.gpsimd(run("pool"))
        blk.sync(run("sp"))


class Arena:
    def __init__(self, ap, nbytes, name):
        self.ap = ap
        self.nbytes = nbytes
        self.off = 0
        self.name = name
        self.cnt = 0

    def reset(self):
        self.off = 0

    def alloc(self, free_shape, dt, parts=128):
        n = 1
        for d in free_shape:
            n *= d
        nb = n * (4 if dt == F32 else 2)
        nb = (nb + 63) // 64 * 64
        assert self.off + nb <= self.nbytes, "arena %s overflow: %d + %d > %d" % (self.name, self.off, nb, self.nbytes)
        a = self.ap[:, self.off // 4:(self.off + nb) // 4]
        self.off += nb
        if dt == BF16:
            a = a.bitcast(BF16)
        a = a[:, 0:n]
        if len(free_shape) == 2:
            a = a.rearrange("p (a b) -> p a b", a=free_shape[0])
        elif len(free_shape) == 3:
            a = a.rearrange("p (a b c) -> p a b c", a=free_shape[0], b=free_shape[1])
        self.cnt += 1
        return a


class Rot:
    def __init__(self, arena, n, free_shape, dt, name):
        self.tiles = [arena.alloc(free_shape, dt) for _ in range(n)]
        self.name = name
        self.i = 0

    def next(self):
        k = self.i % len(self.tiles)
        self.i += 1
        return self.tiles[k], (self.name, k)


def run_pipeline(items, stages, lags):
    n = len(items)
    mx = max(lags)
    for step in range(n + mx):
        for f, lg in zip(stages, lags):
            i = step - lg
            if 0 <= i < n:
                f(items[i])


def build_program(nlayers, debug=False):
    nc = bass.Bass("TRN2", target_bir_lowering=False)
    L = nlayers

    def din(name, shape, dt=F32):
        return nc.dram_tensor(name, list(shape), dt, kind="ExternalInput").ap()

    def dscr(name, shape, dt=BF16):
        ext = bool(debug) and name in debug
        return nc.dram_tensor(name, list(shape), dt, kind="ExternalOutput" if ext else "Internal").ap()

    xT_in = din("xT", [DM, S])
    w_in = din("w_in", [L, DM, IN_W])
    small_d = din("small", [L, 128, NSMALL])
    lamv_d = din("lamv", [L, 1, 256])
    lruw_d = din("lruw", [L, 128, 4 * NBC * 128])
    wuqa_d = din("wuqa", [L, 128, 2 * D_HEADS * 96])
    wuqb_d = din("wuqb", [L, 128, 2 * D_HEADS * 96])
    wukvk_d = din("wukvk", [L, 128, DW])
    wukvv_d = din("wukvv", [L, 128, DW])
    wbr_d = din("wbr", [L, 128, NYC * 1024])
    cbias_d = din("cbias", [128, C_HEADS * CB_W])
    wout_d = din("wout", [L, 128, 8 * 1024])
    caugq_d = din("caugq", [C_HEADS, 2, 4, TC], BF16)
    caugk_d = din("caugk", [C_HEADS, 4, S], BF16)
    mdiag_d = din("mdiag", [128, C_HEADS * 128])
    ma_d = din("ma", [128, 3 * A_SLOTS * 384])
    ropec_d = din("ropec", [32, S])
    ropes_d = din("ropes", [32, S])
    outT = nc.dram_tensor("outT", [DM, S], F32, kind="ExternalOutput").ap()

    AQ = [dscr("AQ%d" % g, [AW, S]) for g in range(3)]
    AK = [dscr("AK%d" % g, [AW, S]) for g in range(3)]
    AV = [dscr("AV%d" % g, [128, A_SLOTS * 32 * 64]) for g in range(3)]
    AG = dscr("AG", [AW, S])
    BX = dscr("BX", [BW, S], F32)
    BG = dscr("BG", [BW, S])
    CQ = dscr("CQ", [CW, S])
    CK = dscr("CK", [CW, S])
    CV = dscr("CV", [128, C_HEADS * 32 * 128])
    CG = dscr("CG", [CW, S])
    DLAT = dscr("DLAT", [384, S], F32)
    DKRAW = dscr("DKRAW", [64, S], F32)
    DG = dscr("DG", [DW, S])
    GT = dscr("GT", [4096, S])
    DQ = dscr("DQ", [D_HEADS * 96, S])
    DK = dscr("DK", [D_HEADS * 64, S])
    DKR = dscr("DKR", [32, S])
    DV = dscr("DV", [128, D_HEADS * 32 * 64])
    Y = dscr("Y", [NYC * 128, S])
    ARI = [nc.dram_tensor("ARI%d" % i, [DM, 2 * TC], BF16, kind="Internal").ap() for i in range(2)]
    ARO = [nc.dram_tensor("ARO%d" % i, [DM, 2 * TC], BF16, kind="Internal").ap() for i in range(2)]
    XS = [nc.dram_tensor("XS%d" % i, [DM, S], F32, kind="Internal").ap() for i in range(2)]

    sc = Sched()
    es = ExitStack()
    PERS_BYTES = 56 * 1024
    ARENA_BYTES = 136 * 1024
    pers_t = es.enter_context(nc.sbuf_tensor("pers", [128, PERS_BYTES // 4], F32))
    arena_t = es.enter_context(nc.sbuf_tensor("arena", [128, ARENA_BYTES // 4], F32))
    pers = Arena(pers_t[:], PERS_BYTES, "pers")
    ar = Arena(arena_t[:], ARENA_BYTES, "arena")
    banks = [es.enter_context(nc.psum_tensor("bank%d" % i, [128, 512], F32)) for i in range(8)]

    def bank(i):
        return banks[i][:], ("bank", i)

    small = pers.alloc([NSMALL], F32)
    lamv = pers.alloc([256], F32)
    lamt = pers.alloc([16], F32)
    spc = pers.alloc([8], F32)
    ones_bf = pers.alloc([128], BF16)
    ones_f = pers.alloc([128], F32)
    mdiag = pers.alloc([C_HEADS, 128], F32)
    cbias = pers.alloc([C_HEADS * CB_W], F32)
    wst = [pers.alloc([4096], F32) for _ in range(2)]
    wbf = [pers.alloc([4096], BF16) for _ in range(2)]
    wst_i = [0]

    def col(c, n=1):
        return small[:, c:c + n]

    sc.add("dve", lambda e: e.memset(ones_bf, 1.0), writes=["ones_bf"])
    sc.add("dve", lambda e: e.memset(ones_f, 1.0), writes=["ones_f"])
    sc.add("sp", lambda e: e.dma_start(out=mdiag.rearrange("p a b -> p (a b)"), in_=mdiag_d), writes=["mdiag"], dma=True)
    sc.add("sp", lambda e: e.dma_start(out=cbias, in_=cbias_d), writes=["cbias"], dma=True)

    def load_cast(src_ap, ncols_f32, dst_bf, dst_res):
        k = wst_i[0] % 2
        wst_i[0] += 1
        st = wst[k]
        sc.add("sp", lambda e: e.dma_start(out=st[:, 0:ncols_f32], in_=src_ap), writes=[("wst", k)], dma=True)
        sc.add("pool", lambda e: e.tensor_copy(out=dst_bf, in_=st[:, 0:ncols_f32]), reads=[("wst", k)], writes=[dst_res])

    def layer(l):
        x_src = xT_in if l == 0 else XS[(l - 1) % 2]
        x_dst = outT if l == L - 1 else XS[l % 2]
        w_l = w_in[l]
        w_l_v = w_l.rearrange("(kc p) n -> p kc n", p=128)

        sc.add("sp", lambda e, l=l: e.dma_start(out=small, in_=small_d[l]), writes=["small"], dma=True)
        sc.add("sp", lambda e, l=l: e.dma_start(out=lamv, in_=lamv_d[l].partition_broadcast(128)), writes=["lamv"], dma=True)
        ar.reset()
        tmpl = ar.alloc([128], F32)

        sc.add("dve", lambda e: e.tensor_tensor(out=tmpl[:, 0:64], in0=lamv[:, 0:64], in1=lamv[:, 64:128], op=ALU.mult), reads=["lamv"], writes=["tmpl0"])
        sc.add("dve", lambda e: e.tensor_tensor(out=tmpl[:, 64:128], in0=lamv[:, 128:192], in1=lamv[:, 192:256], op=ALU.mult), reads=["lamv"], writes=["tmpl1"])
        sc.add("dve", lambda e: e.reduce_sum(out=lamt[:, 0:1], in_=tmpl[:, 0:64], axis=AX.X), reads=["tmpl0"], writes=["lamt0"])
        sc.add("dve", lambda e: e.reduce_sum(out=lamt[:, 1:2], in_=tmpl[:, 64:128], axis=AX.X), reads=["tmpl1"], writes=["lamt1"])
        sc.add("act", lambda e: e.activation(out=lamt[:, 2:4], in_=lamt[:, 0:2], func=AF.Exp), reads=["lamt0", "lamt1"], writes=["lamt23"])
        sc.add("dve", lambda e: e.tensor_tensor(out=lamt[:, 6:7], in0=lamt[:, 3:4], in1=lamt[:, 2:3], op=ALU.subtract), reads=["lamt23"], writes=["lamt6"])
        sc.add("dve", lambda e: e.tensor_tensor(out=lamt[:, 4:5], in0=lamt[:, 6:7], in1=col(SM_LAMINIT), op=ALU.subtract), reads=["lamt6", "small"], writes=["lamt4"])
        sc.add("dve", lambda e: e.tensor_tensor(out=lamt[:, 5:6], in0=col(SM_SUBLN), in1=col(SM_OML), op=ALU.mult), reads=["small"], writes=["lamt5"])
        sc.add("act", lambda e: e.activation(out=spc[:, 0:6], in_=col(SM_LAM, 6), func=AF.Exp, scale=-1.0), reads=["small"], writes=["spc_a"])
        sc.add("act", lambda e: e.activation(out=lamt[:, 8:14], in_=spc[:, 0:6], func=AF.Ln, bias=1.0), reads=["spc_a"], writes=["spc_b"])
        sc.add("dve", lambda e: e.tensor_scalar(out=spc[:, 0:6], in0=lamt[:, 8:14], scalar1=-8.0, scalar2=None, op0=ALU.mult),
               reads=["spc_b"], writes=["spc"])
        sc.barrier()

        ar.reset()
        hT = ar.alloc([KC, S], BF16)
        p0mark = ar.off
        xt_rot = Rot(ar, 2, [KC, TC], F32, "xt")
        sq_t = ar.alloc([KC, TC], F32)
        rstd_rot = Rot(ar, 2, [TC], F32, "rstd")
        x_src_v = x_src.rearrange("(kc p) s -> p kc s", p=128)
        for t in range(NT):
            xt, xr = xt_rot.next()
            sc.add("sp", lambda e, xt=xt, t=t: e.dma_start(out=xt, in_=x_src_v[:, :, t * TC:(t + 1) * TC]), writes=[xr], dma=True)
            sc.add("act", lambda e, xt=xt: e.activation(out=sq_t, in_=xt, func=AF.Square), reads=[xr], writes=["sq_t"])
            bk, br = bank(t % 2)

            def ssmm(e, bk=bk):
                for kc in range(KC):
                    r = e.matmul(bk, lhsT=ones_f, rhs=sq_t[:, kc, :], start=(kc == 0), stop=(kc == KC - 1))
                return r
            sc.add("pe", ssmm, reads=["sq_t", "ones_f"], writes=[br])
            rs, rr = rstd_rot.next()
            def rstd_op(e, bk=bk, rs=rs):
                e.activation(out=rs, in_=bk, func=AF.Ln, scale=1.0 / DM, bias=EPS)
                return e.activation(out=rs, in_=rs, func=AF.Exp, scale=-0.5)
            sc.add("act", rstd_op, reads=[br], writes=[rr])

            def hmk(e, xt=xt, rs=rs, t=t):
                for kc in range(KC):
                    r = e.scalar_tensor_tensor(out=hT[:, kc, t * TC:(t + 1) * TC], in0=xt[:, kc, :], scalar=col(SM_GPRE + kc),
                                               in1=rs, op0=ALU.mult, op1=ALU.mult)
                return r
            sc.add("dve", hmk, reads=[xr, rr, "small"], writes=[("hT", t)])
        sc.barrier()

        ar.off = p0mark
        ob_rot_a = Rot(ar, 3, [TC], BF16, "ob_a")
        ob_rot_d = Rot(ar, 3, [TC], BF16, "ob_d")
        of_rot = Rot(ar, 2, [TC], F32, "of")
        vstage = ar.alloc([4 * 32 * 128], BF16)
        hT_res = [("hT", t) for t in range(NT)]
        bank_i = [0]

        def nbank():
            b = bank_i[0] % 8
            bank_i[0] += 1
            return bank(b)

        fm_jobs = []
        for g in range(3):
            fm_jobs.append((OFF["a_q"] + g * AW, AW, A_DIL[g], "scale", AQ[g], 0.125))
        for g in range(3):
            fm_jobs.append((OFF["a_k"] + g * AW, AW, A_DIL[g], "copy", AK[g], None))
        fm_jobs.append((OFF["a_g"], AW, 1, "silu", AG, None))
        fm_jobs.append((OFF["b_x"], BW, 1, "copyf", BX, None))
        fm_jobs.append((OFF["b_g"], BW, 1, "silu", BG, None))
        fm_jobs.append((OFF["c_q"], CW, 1, "scale", CQ, 0.125))
        fm_jobs.append((OFF["c_k"], CW, 1, "copy", CK, None))
        fm_jobs.append((OFF["c_g"], CW, 1, "silu", CG, None))
        fm_jobs.append((OFF["d_cq"], 384, 1, "copyf", DLAT, None))
        fm_jobs.append((OFF["d_kr"], 64, 1, "copyf_kr", DKRAW, None))
        fm_jobs.append((OFF["d_g"], DW, 1, "silu", DG, None))
        for i in range(8):
            fm_jobs.append((OFF["gate"] + i * 512, 512, 1, "gate", GT[i * 512:(i + 1) * 512, :], i))
        tm_jobs = []
        for g in range(3):
            tm_jobs.append((OFF["a_v"] + g * AW, AW, A_DIL[g], AV[g], A_SLOTS, 64))
        tm_jobs.append((OFF["c_v"], CW, 1, CV, C_HEADS, 128))
        jobs = [("fm", j) for j in fm_jobs] + [("tm", j) for j in tm_jobs]

        def load_job(ji):
            kind, j = jobs[ji]
            c0, n = j[0], j[1]
            k = ji % 2
            st = wst[k].rearrange("p (kc n) -> p kc n", kc=KC)
            wb = wbf[k].rearrange("p (kc n) -> p kc n", kc=KC)
            if kind == "fm" and j[3] == "copyf_kr":
                sc.add("sp", lambda e: e.dma_start(out=st[:, :, 0:32], in_=w_l_v[:, :, c0:c0 + 32]), writes=[("wst", k)], dma=True)
                sc.add("sp", lambda e: e.dma_start(out=st[:, :, 32:48], in_=w_l_v[:, :, c0 + 16:c0 + 32]), writes=[("wst", k, 1)], dma=True)
                sc.add("sp", lambda e: e.dma_start(out=st[:, :, 48:64], in_=w_l_v[:, :, c0:c0 + 16]), writes=[("wst", k, 2)], dma=True)
                rd = [("wst", k), ("wst", k, 1), ("wst", k, 2)]
            else:
                sc.add("sp", lambda e: e.dma_start(out=st[:, :, 0:n], in_=w_l_v[:, :, c0:c0 + n]), writes=[("wst", k)], dma=True)
                rd = [("wst", k)]
            sc.add("pool", lambda e: e.tensor_copy(out=wb[:, :, 0:n], in_=st[:, :, 0:n]), reads=rd, writes=[("wbf", k)])

        def rhs_view(d, kc, t):
            if d == 1:
                return [(hT[:, kc, t * TC:(t + 1) * TC], None)]
            hv = hT[:, kc, :].rearrange("p (l r) -> p r l", r=d)
            if d == 4:
                return [(hv[:, t // 2, (t % 2) * 512:(t % 2) * 512 + 512], None)]
            return [(hv[:, 2 * t:2 * t + 2, :], 2)]

        def lhs_view(d, kc, b):
            if d == 1:
                return hT[:, kc, b * 128:(b + 1) * 128]
            hv = hT[:, kc, :].rearrange("p (l r) -> p r l", r=d)
            nbs = 32 // d
            return hv[:, b // nbs, (b % nbs) * 128:(b % nbs) * 128 + 128]

        def compute_fm(ji):
            _, (c0, n, d, kind, dst, extra) = jobs[ji]
            k = ji % 2
            wb = wbf[k].rearrange("p (kc n) -> p kc n", kc=KC)
            ncb = (n + 127) // 128
            for cb in range(ncb):
                m = min(128, n - cb * 128)
                for t in range(NT):
                    bk, bres = nbank()

                    def mm(e, bk=bk, cb=cb, m=m, t=t):
                        r = None
                        for kc in range(KC):
                            (rv, a), = rhs_view(d, kc, t)
                            o = bk[0:m, :]
                            if a is not None:
                                o = o.rearrange("p (a b) -> p a b", a=a)
                            r = e.matmul(o, lhsT=wb[:, kc, cb * 128:cb * 128 + m], rhs=rv, start=(kc == 0), stop=(kc == KC - 1))
                        return r
                    sc.add("pe", mm, reads=[("wbf", k)] + hT_res, writes=[bres])
                    drows = dst[cb * 128:cb * 128 + m, t * TC:(t + 1) * TC]
                    if kind in ("silu", "gate"):
                        ot, ores = ob_rot_a.next()
                        if kind == "silu":
                            sc.add("act", lambda e, ot=ot, bk=bk, m=m: e.activation(out=ot[0:m, :], in_=bk[0:m, :], func=AF.Silu),
                                   reads=[bres], writes=[ores])
                        else:
                            bcol = SM_BGATE + (extra // 2) * 8 + (extra % 2) * 4 + cb
                            sc.add("act", lambda e, ot=ot, bk=bk, bcol=bcol: e.activation(out=ot, in_=bk, func=AF.Sigmoid, bias=col(bcol)),
                                   reads=[bres, "small"], writes=[ores])
                        sc.add("act", lambda e, ot=ot, drows=drows, m=m: e.dma_start(out=drows, in_=ot[0:m, :]), reads=[ores], dma=True)
                    elif kind in ("copy", "scale"):
                        ot, ores = ob_rot_d.next()
                        if kind == "copy":
                            sc.add("dve", lambda e, ot=ot, bk=bk, m=m: e.tensor_copy(out=ot[0:m, :], in_=bk[0:m, :]), reads=[bres], writes=[ores])
                        else:
                            sc.add("dve", lambda e, ot=ot, bk=bk, m=m: e.tensor_scalar(out=ot[0:m, :], in0=bk[0:m, :], scalar1=extra, scalar2=None, op0=ALU.mult),
                                   reads=[bres], writes=[ores])
                        sc.add("sp", lambda e, ot=ot, drows=drows, m=m: e.dma_start(out=drows, in_=ot[0:m, :]), reads=[ores], dma=True)
                    else:
                        ot, ores = of_rot.next()
                        sc.add("dve", lambda e, ot=ot, bk=bk, m=m: e.tensor_copy(out=ot[0:m, :], in_=bk[0:m, :]), reads=[bres], writes=[ores])
                        sc.add("sp", lambda e, ot=ot, drows=drows, m=m: e.dma_start(out=drows, in_=ot[0:m, :]), reads=[ores], dma=True)

        def compute_tm(ji):
            _, (c0, n, d, dst, nh, hd) = jobs[ji]
            k = ji % 2
            wb = wbf[k].rearrange("p (kc n) -> p kc n", kc=KC)
            vs = vstage[:, 0:nh * 32 * hd].rearrange("p (h b d) -> p h b d", h=nh, b=32)
            for b in range(32):
                bk, bres = nbank()

                def mm(e, bk=bk, b=b):
                    r = None
                    for kc in range(KC):
                        r = e.matmul(bk[:, 0:n], lhsT=lhs_view(d, kc, b), rhs=wb[:, kc, 0:n], start=(kc == 0), stop=(kc == KC - 1))
                    return r
                sc.add("pe", mm, reads=[("wbf", k)] + hT_res, writes=[bres])
                sc.add("dve", lambda e, bk=bk, b=b: e.tensor_copy(out=vs[:, :, b, :], in_=bk[:, 0:n].rearrange("p (h d) -> p h d", h=nh)),
                       reads=[bres], writes=[("vstage", b)])
            sc.add("sp", lambda e: e.dma_start(out=dst, in_=vstage[:, 0:nh * 32 * hd]), reads=[("vstage", b) for b in range(32)], dma=True)

        load_job(0)
        for ji in range(len(jobs)):
            if ji + 1 < len(jobs):
                load_job(ji + 1)
            if jobs[ji][0] == "fm":
                compute_fm(ji)
            else:
                compute_tm(ji)
        sc.barrier()

        ar.reset()
        wuqa = ar.alloc([2, D_HEADS * 96], BF16)
        wuqb = ar.alloc([2, D_HEADS * 96], BF16)
        wukvk = ar.alloc([DW], BF16)
        wukvv = ar.alloc([DW], BF16)
        load_cast(wuqa_d[l], 2 * D_HEADS * 96, wuqa.rearrange("p a b -> p (a b)"), "wuqa")
        load_cast(wuqb_d[l], 2 * D_HEADS * 96, wuqb.rearrange("p a b -> p (a b)"), "wuqb")
        load_cast(wukvk_d[l], DW, wukvk, "wukvk")
        load_cast(wukvv_d[l], DW, wukvv, "wukvv")
        lat_rot = Rot(ar, 2, [3, TC], F32, "lat")
        kra_rot = Rot(ar, 2, [TC], F32, "kra")
        krb_rot = Rot(ar, 2, [TC], F32, "krb")
        cc_rot = Rot(ar, 2, [TC], F32, "cc")
        ss_rot = Rot(ar, 2, [TC], F32, "ss")
        sq2 = ar.alloc([3, TC], F32)
        rq_rot = Rot(ar, 2, [TC], F32, "rq")
        rkv_rot = Rot(ar, 2, [TC], F32, "rkv")
        cqn_rot = Rot(ar, 2, [3, TC], BF16, "cqn")
        qd_rot = Rot(ar, 3, [TC], BF16, "qd")
        kd_rot = Rot(ar, 3, [TC], BF16, "kd")
        t1_rot = Rot(ar, 2, [TC], F32, "t1")
        t2_rot = Rot(ar, 2, [TC], F32, "t2")
        krr_rot = Rot(ar, 2, [TC], BF16, "krr")
        dvst = ar.alloc([D_HEADS, 32, 64], BF16)
        DLv = DLAT.rearrange("(c p) s -> p c s", p=128)
        for t in range(NT):
            tsl = slice(t * TC, (t + 1) * TC)
            lat, latr = lat_rot.next()
            kra, krar = kra_rot.next()
            krb, krbr = krb_rot.next()
            cct, ccr = cc_rot.next()
            sst, ssr = ss_rot.next()
            sc.add("sp", lambda e, lat=lat, tsl=tsl: e.dma_start(out=lat, in_=DLv[:, :, tsl]), writes=[latr], dma=True)
            sc.add("sp", lambda e, kra=kra, tsl=tsl: e.dma_start(out=kra[64:96, :], in_=DKRAW[0:32, tsl]), writes=[krar], dma=True)
            sc.add("sp", lambda e, krb=krb, tsl=tsl: e.dma_start(out=krb[64:96, :], in_=DKRAW[32:64, tsl]), writes=[krbr], dma=True)
            sc.add("sp", lambda e, cct=cct, tsl=tsl: e.dma_start(out=cct[64:96, :], in_=ropec_d[:, tsl]), writes=[ccr], dma=True)
            sc.add("sp", lambda e, sst=sst, tsl=tsl: e.dma_start(out=sst[64:96, :], in_=ropes_d[:, tsl]), writes=[ssr], dma=True)
            sc.add("act", lambda e, lat=lat: e.activation(out=sq2, in_=lat, func=AF.Square), reads=[latr], writes=["sq2"])
            b0, b0r = bank(0)
            b1, b1r = bank(1)

            def ssq(e, b0=b0, b1=b1):
                e.matmul(b0, lhsT=ones_f, rhs=sq2[:, 0, :], start=True, stop=False)
                e.matmul(b0, lhsT=ones_f, rhs=sq2[:, 1, :], start=False, stop=True)
                return e.matmul(b1, lhsT=ones_f, rhs=sq2[:, 2, :], start=True, stop=True)
            sc.add("pe", ssq, reads=["sq2", "ones_f"], writes=[b0r, b1r])
            rq, rqr = rq_rot.next()
            rkv, rkvr = rkv_rot.next()

            def rsq(e, rq=rq, rkv=rkv, b0=b0, b1=b1):
                e.activation(out=rq, in_=b0, func=AF.Ln, scale=1.0 / 256, bias=EPS)
                e.activation(out=rkv, in_=b1, func=AF.Ln, scale=1.0 / 128, bias=EPS)
                e.activation(out=rq, in_=rq, func=AF.Exp, scale=-0.5)
                return e.activation(out=rkv, in_=rkv, func=AF.Exp, scale=-0.5)
            sc.add("act", rsq, reads=[b0r, b1r], writes=[rqr, rkvr])
            cqn, cqnr = cqn_rot.next()

            def nrm(e, rq=rq, rkv=rkv, lat=lat, cqn=cqn):
                e.scalar_tensor_tensor(out=cqn[:, 0, :], in0=lat[:, 0, :], scalar=col(SM_QN), in1=rq, op0=ALU.mult, op1=ALU.mult)
                e.scalar_tensor_tensor(out=cqn[:, 1, :], in0=lat[:, 1, :], scalar=col(SM_QN + 1), in1=rq, op0=ALU.mult, op1=ALU.mult)
                return e.scalar_tensor_tensor(out=cqn[:, 2, :], in0=lat[:, 2, :], scalar=col(SM_KVN), in1=rkv, op0=ALU.mult, op1=ALU.mult)
            sc.add("dve", nrm, reads=[rqr, rkvr, latr, "small"], writes=[cqnr, rqr, rkvr])
            t1, t1r = t1_rot.next()
            t2, t2r = t2_rot.next()
            krr, krrr = krr_rot.next()

            def krope(e, kra=kra, krb=krb, cct=cct, sst=sst, t1=t1, t2=t2, krr=krr):
                e.tensor_tensor(out=t1[64:96, :], in0=krb[64:96, :], in1=sst[64:96, :], op=ALU.mult)
                e.tensor_tensor(out=t2[64:96, :], in0=kra[64:96, :], in1=cct[64:96, :], op=ALU.mult)
                return e.tensor_tensor(out=krr[64:96, :], in0=t1[64:96, :], in1=t2[64:96, :], op=ALU.add)
            sc.add("dve", krope, reads=[krar, krbr, ccr, ssr], writes=[t1r, t2r, krrr])
            sc.add("sp", lambda e, krr=krr, tsl=tsl: e.dma_start(out=DKR[:, tsl], in_=krr[64:96, :]), reads=[krrr], dma=True)
            for h in range(D_HEADS):
                ba, bar_ = bank(2 + (h % 2) * 3)
                bb, bbr = bank(3 + (h % 2) * 3)
                bkk, bkr = bank(4 + (h % 2) * 3)

                def upq(e, ba=ba, bb=bb, bkk=bkk, h=h, cqn=cqn):
                    for c in range(2):
                        e.matmul(ba[0:96, :], lhsT=wuqa[:, c, h * 96:(h + 1) * 96], rhs=cqn[:, c, :], start=(c == 0), stop=(c == 1))
                    for c in range(2):
                        e.matmul(bb[0:96, :], lhsT=wuqb[:, c, h * 96:(h + 1) * 96], rhs=cqn[:, c, :], start=(c == 0), stop=(c == 1))
                    return e.matmul(bkk[0:64, :], lhsT=wukvk[:, h * 64:(h + 1) * 64], rhs=cqn[:, 2, :], start=True, stop=True)
                sc.add("pe", upq, reads=[cqnr, "wuqa", "wuqb", "wukvk"], writes=[bar_, bbr, bkr])
                qd, qdr = qd_rot.next()
                kd, kdr = kd_rot.next()
                t1, t1r = t1_rot.next()
                t2, t2r = t2_rot.next()

                def qrope(e, ba=ba, bb=bb, qd=qd, t1=t1, t2=t2, cct=cct, sst=sst):
                    e.tensor_copy(out=qd[0:64, :], in_=ba[0:64, :])
                    e.tensor_tensor(out=t1[64:96, :], in0=bb[64:96, :], in1=sst[64:96, :], op=ALU.mult)
                    e.tensor_tensor(out=t2[64:96, :], in0=ba[64:96, :], in1=cct[64:96, :], op=ALU.mult)
                    return e.tensor_tensor(out=qd[64:96, :], in0=t1[64:96, :], in1=t2[64:96, :], op=ALU.add)
                sc.add("dve", qrope, reads=[bar_, bbr, ccr, ssr], writes=[qdr, t1r, t2r])
                sc.add("sp", lambda e, qd=qd, h=h, tsl=tsl: e.dma_start(out=DQ[h * 96:(h + 1) * 96, tsl], in_=qd[0:96, :]), reads=[qdr], dma=True)
                sc.add("act", lambda e, kd=kd, bkk=bkk: e.activation(out=kd[0:64, :], in_=bkk[0:64, :], func=AF.Copy), reads=[bkr], writes=[kdr])
                sc.add("act", lambda e, kd=kd, h=h, tsl=tsl: e.dma_start(out=DK[h * 64:(h + 1) * 64, tsl], in_=kd[0:64, :]), reads=[kdr], dma=True)
            for tb in range(4):
                bv, bvr = bank(tb % 2)
                b = t * 4 + tb
                sc.add("pe", lambda e, bv=bv, tb=tb, cqn=cqn: e.matmul(bv[:, 0:DW], lhsT=cqn[:, 2, tb * 128:(tb + 1) * 128], rhs=wukvv, start=True, stop=True),
                       reads=[cqnr, "wukvv"], writes=[bvr])
                sc.add("act", lambda e, bv=bv, b=b: e.activation(out=dvst[:, :, b, :], in_=bv[:, 0:DW].rearrange("p (h d) -> p h d", h=D_HEADS), func=AF.Copy),
                       reads=[bvr], writes=[("dvst", b)])
        sc.add("sp", lambda e: e.dma_start(out=DV, in_=dvst.rearrange("p h b d -> p (h b d)")), reads=[("dvst", b) for b in range(32)], dma=True)
        sc.barrier()

        ar.reset()
        ma = ar.alloc([3 * A_SLOTS, 384], F32)
        sc.add("sp", lambda e: e.dma_start(out=ma.rearrange("p a b -> p (a b)"), in_=ma_d), writes=["ma"], dma=True)
        acc = ar.alloc([S], F32)
        kt_rot = Rot(ar, 2, [S], BF16, "kt")
        qt_rot = Rot(ar, 2, [S], BF16, "qt")
        vraw_rot = Rot(ar, 2, [32 * 64], BF16, "vraw")
        vt_tiles = [ar.alloc([32, 128], BF16) for _ in range(2)]
        for k in range(2):
            sc.add("pool", lambda e, k=k: e.memset(vt_tiles[k][:, :, 64:128], 1.0), writes=[("vt1", k)])
        pf_rot = Rot(ar, 3, [384], F32, "pf")
        pb_rot = Rot(ar, 4, [384], BF16, "pb")
        ag_rot = Rot(ar, 2, [TC], BF16, "ag")
        rz_rot = Rot(ar, 2, [TC], F32, "rz")
        yf_rot = Rot(ar, 2, [TC], F32, "yf")
        yb_rot = Rot(ar, 2, [TC], BF16, "yb")
        groups = [(s_, g) for s_ in range(A_SLOTS) for g in range(3)]
        gctx = {}

        def a_load(gi):
            s_, g = groups[gi]
            kt, ktr = kt_rot.next()
            qt, qtr = qt_rot.next()
            vraw, vrr = vraw_rot.next()
            vk = gi % 2
            vt, vtr = vt_tiles[vk], ("vt", vk)
            sc.add("sp", lambda e: e.dma_start(out=kt[0:64, :], in_=AK[g][s_ * 64:(s_ + 1) * 64, :]), writes=[ktr], dma=True)
            sc.add("sp", lambda e: e.dma_start(out=qt[0:64, :], in_=AQ[g][s_ * 64:(s_ + 1) * 64, :]), writes=[qtr], dma=True)
            sc.add("sp", lambda e: e.dma_start(out=vraw, in_=AV[g][:, s_ * 2048:(s_ + 1) * 2048]), writes=[vrr], dma=True)
            sc.add("pool", lambda e: e.tensor_copy(out=vt[:, :, 0:64], in_=vraw.rearrange("p (b d) -> p b d", b=32)),
                   reads=[vrr, ("vt1", vk)], writes=[vtr])
            gctx[gi] = dict(kt=kt, ktr=ktr, qt=qt, qtr=qtr, vt=vt, vtr=vtr)

        items = []
        for gi, (s_, g) in enumerate(groups):
            for qb in range(32):
                items.append(dict(gi=gi, s_=s_, g=g, qb=qb, idx=len(items)))

        def a_S(it):
            gi, qb, g = it["gi"], it["qb"], it["g"]
            if qb == 0 and gi == 0:
                a_load(0)
            if qb == 16 and gi + 1 < len(groups):
                a_load(gi + 1)
            c = gctx[gi]
            d = A_DIL[g]
            nbs = 32 // d
            lb = qb % nbs
            js = [j for j in range(3) if 0 <= lb - 1 + j < nbs]
            psS, psSr = bank(it["idx"] % 3)
            kt, qt = c["kt"], c["qt"]

            def f(e):
                r = None
                for j in js:
                    kb = qb - 1 + j
                    r = e.matmul(psS[:, j * 128:(j + 1) * 128], lhsT=kt[0:64, kb * 128:(kb + 1) * 128], rhs=qt[0:64, qb * 128:(qb + 1) * 128],
                                 start=True, stop=True)
                return r
            sc.add("pe", f, reads=[c["ktr"], c["qtr"]], writes=[psSr])
            it.update(js=js, psS=psS, psSr=psSr, d=d, nbs=nbs, lb=lb)

        def a_E(it):
            js, psS = it["js"], it["psS"]
            c0, c1 = js[0] * 128, (js[-1] + 1) * 128
            pf, pfr = pf_rot.next()
            sc.add("act", lambda e: e.activation(out=pf[:, c0:c1], in_=psS[:, c0:c1], func=AF.Exp), reads=[it["psSr"]], writes=[pfr])
            it.update(pf=pf, pfr=pfr, c0=c0, c1=c1)

        def a_M(it):
            pf, c0, c1 = it["pf"], it["c0"], it["c1"]
            mi = it["g"] * A_SLOTS + it["s_"]
            pb, pbr = pb_rot.next()
            sc.add("dve", lambda e: e.tensor_tensor(out=pb[:, c0:c1], in0=pf[:, c0:c1], in1=ma[:, mi, c0:c1], op=ALU.mult),
                   reads=[it["pfr"], "ma"], writes=[pbr])
            it.update(pb=pb, pbr=pbr)

        def a_P(it):
            c = gctx[it["gi"]]
            vt, pb, js, qb = c["vt"], it["pb"], it["js"], it["qb"]
            psO, psOr = bank(3 + it["idx"] % 3)

            def pv(e):
                r = None
                for j in js:
                    kb = qb - 1 + j
                    r = e.matmul(psO[:, 0:128], lhsT=vt[:, kb, :], rhs=pb[:, j * 128:(j + 1) * 128], start=(j == js[0]), stop=(j == js[-1]))
                return r
            sc.add("pe", pv, reads=[it["pbr"], c["vtr"]], writes=[psOr])
            it.update(psO=psO, psOr=psOr)

        def a_A(it):
            d, nbs, qb, g, s_ = it["d"], it["nbs"], it["qb"], it["g"], it["s_"]
            psO = it["psO"]
            rr_, lb_ = qb // nbs, qb % nbs
            if d == 1:
                av = acc[:, qb * 128:(qb + 1) * 128]
            else:
                av = acc.rearrange("p (l r) -> p r l", r=d)[:, rr_, lb_ * 128:(lb_ + 1) * 128]
            if g == 0:
                sc.add("dve", lambda e: e.tensor_copy(out=av, in_=psO[:, 0:128]), reads=[it["psOr"]], writes=["acc"])
            else:
                sc.add("dve", lambda e: e.tensor_tensor(out=av, in0=psO[:, 0:128], in1=av, op=ALU.add), reads=[it["psOr"], "acc"], writes=["acc"])
            if g == 2 and qb == 31:
                a_epi_all(s_)

        def a_epi_all(s_):
            ctx = {}

            def st_copy(t):
                tsl = slice(t * TC, (t + 1) * TC)
                agt, agr = ag_rot.next()
                rz, rzr = rz_rot.next()
                yf, yfr = yf_rot.next()
                yb, ybr = yb_rot.next()
                sc.add("sp", lambda e: e.dma_start(out=agt[0:64, :], in_=AG[s_ * 64:(s_ + 1) * 64, tsl]), writes=[agr], dma=True)
                sc.add("dve", lambda e: e.tensor_copy(out=rz[0:64, :], in_=acc[64:128, tsl]), reads=["acc"], writes=[rzr])

                def rz_act(e):
                    e.activation(out=rz[0:64, :], in_=rz[0:64, :], func=AF.Ln)
                    return e.activation(out=rz[0:64, :], in_=rz[0:64, :], func=AF.Exp, scale=-1.0)
                sc.add("act", rz_act, reads=[rzr], writes=[rzr])
                ctx[t] = (tsl, agt, agr, rz, rzr, yf, yfr, yb, ybr)

            def st_mul(t):
                tsl, agt, agr, rz, rzr, yf, yfr, yb, ybr = ctx[t]

                def epi(e):
                    e.tensor_tensor(out=yf[0:64, :], in0=acc[0:64, tsl], in1=rz[0:64, :], op=ALU.mult)
                    return e.tensor_tensor(out=yb[0:64, :], in0=yf[0:64, :], in1=agt[0:64, :], op=ALU.mult)
                sc.add("dve", epi, reads=["acc", agr, rzr], writes=[yfr, ybr])
                sc.add("sp", lambda e: e.dma_start(out=Y[YA0 + s_ * 64:YA0 + (s_ + 1) * 64, tsl], in_=yb[0:64, :]), reads=[ybr], dma=True)

            for t in range(NT + 1):
                if t < NT:
                    st_copy(t)
                if t >= 1:
                    st_mul(t - 1)

        run_pipeline(items, [a_S, a_E, a_M, a_P, a_A], [0, 1, 2, 3, 4])
        sc.barrier()

        ar.reset()
        lw = ar.alloc([4 * NBC, 128], BF16)
        load_cast(lruw_d[l], 4 * NBC * 128, lw.rearrange("p a b -> p (a b)"), "lw")
        xp = ar.alloc([S + 4], F32)
        xc = ar.alloc([S], F32)
        Rb = ar.alloc([S], F32)
        Ib = ar.alloc([S], F32)
        Ab_ = ar.alloc([S], F32)
        Hf = ar.alloc([S], F32)
        Hb = ar.alloc([S], F32)
        xcb = ar.alloc([S], BF16)
        bgt = ar.alloc([S], BF16)
        sc.add("dve", lambda e: e.memset(xp[:, 0:1], 0.0), writes=["xp_pad0"])
        sc.add("dve", lambda e: e.memset(xp[:, S + 1:S + 4], 0.0), writes=["xp_pad1"])
        for c in range(NBC):
            sc.add("sp", lambda e, c=c: e.dma_start(out=xp[:, 1:S + 1], in_=BX[c * 128:(c + 1) * 128, :]), writes=["xp"], dma=True)
            sc.add("sp", lambda e, c=c: e.dma_start(out=bgt, in_=BG[c * 128:(c + 1) * 128, :]), writes=["bgt"], dma=True)

            def conv(e, c=c):
                e.tensor_scalar(out=xc, in0=xp[:, 0:S], scalar1=col(SM_CONVW + 0 * NBC + c), scalar2=col(SM_CONVB + c), op0=ALU.mult, op1=ALU.add)
                for j in range(1, 4):
                    r = e.scalar_tensor_tensor(out=xc, in0=xp[:, j:j + S], scalar=col(SM_CONVW + j * NBC + c), in1=xc, op0=ALU.mult, op1=ALU.add)
                return r
            sc.add("dve", conv, reads=["xp", "xp_pad0", "xp_pad1", "small"], writes=["xc"])
            sc.add("pool", lambda e: e.tensor_copy(out=xcb, in_=xc), reads=["xc"], writes=["xcb"])
            for dr in range(2):
                for t in range(NT):
                    tsl = slice(t * TC, (t + 1) * TC)
                    bR, bRr = bank((2 * t) % 8)
                    bI, bIr = bank((2 * t + 1) % 8)

                    def gmm(e, bR=bR, bI=bI, tsl=tsl, c=c, dr=dr):
                        e.matmul(bR, lhsT=lw[:, (0 * 2 + dr) * NBC + c, :], rhs=xcb[:, tsl], start=True, stop=True)
                        return e.matmul(bI, lhsT=lw[:, (1 * 2 + dr) * NBC + c, :], rhs=xcb[:, tsl], start=True, stop=True)
                    sc.add("pe", gmm, reads=["xcb", "lw"], writes=[bRr, bIr])
                    sc.add("dve", lambda e, bR=bR, tsl=tsl, c=c, dr=dr: e.tensor_scalar(out=Rb[:, tsl], in0=bR, scalar1=col(SM_BR + dr * NBC + c), scalar2=None, op0=ALU.add),
                           reads=[bRr, "small"], writes=[("Rb", t)])
                    sc.add("dve", lambda e, bI=bI, tsl=tsl, c=c, dr=dr: e.tensor_scalar(out=Ib[:, tsl], in0=bI, scalar1=col(SM_BI + dr * NBC + c), scalar2=None, op0=ALU.add),
                           reads=[bIr, "small"], writes=[("Ib", t)])
                Rres = [("Rb", t) for t in range(NT)]
                Ires = [("Ib", t) for t in range(NT)]

                def gates(e, c=c, dr=dr):
                    e.activation(out=Rb, in_=Rb, func=AF.Sigmoid)
                    e.activation(out=Ib, in_=Ib, func=AF.Sigmoid)
                    e.activation(out=Ab_, in_=Rb, func=AF.Exp, scale=spc[:, dr * NBC + c:dr * NBC + c + 1])
                    e.activation(out=Rb, in_=Ab_, func=AF.Square)
                    return e.activation(out=Rb, in_=Rb, func=AF.Sqrt, scale=-1.0, bias=1.0)
                sc.add("act", gates, reads=Rres + Ires + ["spc", "Hscan%d" % dr], writes=["Rw", "Ig", "Ab"])
                Hd = Hf if dr == 0 else Hb

                def premul(e):
                    e.tensor_tensor(out=Ib, in0=Ib, in1=xc, op=ALU.mult)
                    return e.tensor_tensor(out=Ib, in0=Ib, in1=Rb, op=ALU.mult)
                sc.add("dve", premul, reads=["Rw", "Ig", "Ab", "xc"], writes=["U", "Hscan%d" % (1 - dr)] + Rres + Ires)
                order = list(range(NT)) if dr == 0 else list(range(NT - 1, -1, -1))
                for oi, t in enumerate(order):
                    tsl = slice(t * TC, (t + 1) * TC)
                    if oi == 0:
                        init = 0.0
                    elif dr == 0:
                        init = Hd[:, t * TC - 1:t * TC]
                    else:
                        init = Hd[:, (t + 1) * TC:(t + 1) * TC + 1]
                    if dr == 0:
                        sc.add("dve", lambda e, Hd=Hd, tsl=tsl, init=init: e.tensor_tensor_scan(out=Hd[:, tsl], data0=Ab_[:, tsl], data1=Ib[:, tsl], initial=init, op0=ALU.mult, op1=ALU.add),
                               reads=["U", "Ab"] + ([("Hc", dr, order[oi - 1])] if oi else []), writes=[("Hc", dr, t)])
                    else:
                        sc.add("dve", lambda e, Hd=Hd, tsl=tsl, init=init: e.tensor_tensor_scan(out=Hd[:, tsl][:, ::-1], data0=Ab_[:, tsl][:, ::-1], data1=Ib[:, tsl][:, ::-1], initial=init, op0=ALU.mult, op1=ALU.add),
                               reads=["U", "Ab"] + ([("Hc", dr, order[oi - 1])] if oi else []), writes=[("Hc", dr, t)])

            def fin(e):
                e.tensor_tensor(out=Hf, in0=Hf, in1=Hb, op=ALU.add)
                return e.tensor_tensor(out=bgt, in0=Hf, in1=bgt, op=ALU.mult)
            sc.add("dve", fin, reads=[("Hc", 0, NT - 1), ("Hc", 1, 0), "bgt"], writes=["bgt"])
            sc.add("sp", lambda e, c=c: e.dma_start(out=Y[YB0 + c * 128:YB0 + (c + 1) * 128, :], in_=bgt), reads=["bgt"], dma=True)
        sc.barrier()

        ar.reset()
        cs = c_slopes() if not SPLIT else [min(c_slopes()[h], c_slopes()[h + 2]) for h in range(2)]
        ka_rot = Rot(ar, 4, [S], BF16, "ka")
        vc_rot = Rot(ar, 2, [32 * 128], BF16, "vc")
        qb_rot = Rot(ar, 4, [TC], BF16, "qbf")
        qa_rot = Rot(ar, 4, [TC], BF16, "qaf")
        cg_rot = Rot(ar, 2, [TC], BF16, "cg")
        pbc_rot = Rot(ar, 5, [TC], BF16, "pbc")
        pfc_rot = Rot(ar, 2, [128], F32, "pfc")
        eo_rot = Rot(ar, 4, [TC], F32, "eo")
        ez_rot = Rot(ar, 4, [TC], F32, "ez")
        e_o1 = ar.alloc([TC], F32)
        e_sq = ar.alloc([TC], F32)
        e_sd = ar.alloc([TC], F32)
        ybc_rot = Rot(ar, 2, [TC], BF16, "ybc")
        hctx, qctx = {}, {}

        def c_load_head(h):
            kas = []
            for c in range(2):
                ka, kar = ka_rot.next()
                kas.append((ka, kar))
                sc.add("sp", lambda e, ka=ka, c=c: e.dma_start(out=ka[0:64, :], in_=CK[(h * 2 + c) * 64:(h * 2 + c + 1) * 64, :]), writes=[kar], dma=True)
                sc.add("sp", lambda e, ka=ka: e.dma_start(out=ka[64:68, :], in_=caugk_d[h]), writes=[(kar, "aug")], dma=True)
            vc, vcr = vc_rot.next()
            sc.add("sp", lambda e: e.dma_start(out=vc, in_=CV[:, h * 4096:(h + 1) * 4096]), writes=[vcr], dma=True)
            hctx[h] = dict(kas=kas, vcv=vc.rearrange("p (b d) -> p b d", b=32), vcr=vcr)

        def c_load_chunk(h, qc):
            tsl = slice(qc * TC, (qc + 1) * TC)
            qs = []
            for c in range(2):
                qbf, qbr = qb_rot.next()
                qaf, qar = qa_rot.next()
                src = CQ[(h * 2 + c) * 64:(h * 2 + c + 1) * 64, tsl]
                sc.add("sp", lambda e, qbf=qbf, src=src: e.dma_start(out=qbf[0:64, :], in_=src), writes=[qbr], dma=True)
                sc.add("sp", lambda e, qbf=qbf: e.dma_start(out=qbf[64:68, :], in_=caugq_d[h, 0]), writes=[(qbr, "aug")], dma=True)
                sc.add("sp", lambda e, qaf=qaf, src=src: e.dma_start(out=qaf[0:64, :], in_=src), writes=[qar], dma=True)
                sc.add("sp", lambda e, qaf=qaf: e.dma_start(out=qaf[64:68, :], in_=caugq_d[h, 1]), writes=[(qar, "aug")], dma=True)
                qs.append((qbf, qbr, qaf, qar))
            cgt, cgr = cg_rot.next()
            sc.add("sp", lambda e: e.dma_start(out=cgt, in_=CG[h * 128:(h + 1) * 128, tsl]), writes=[cgr], dma=True)
            qctx[(h, qc)] = dict(qs=qs, cgt=cgt, cgr=cgr)

        items = []
        for h in range(C_HEADS):
            m = cs[h]
            for qc in range(NT):
                i0 = qc * TC
                for c in range(2):
                    kbs = []
                    for kb in range(32):
                        j0 = kb * 128
                        if j0 + 128 <= i0:
                            if m * (i0 - (j0 + 127)) > SKIP_T:
                                continue
                        elif j0 >= i0 + TC:
                            if m * (j0 - (i0 + TC - 1)) > SKIP_T:
                                continue
                        kbs.append(kb)
                    for ii, kb in enumerate(kbs):
                        items.append(dict(h=h, qc=qc, c=c, kb=kb, ii=ii, first=(ii == 0), last=(ii == len(kbs) - 1), idx=len(items),
                                          chunk_first=(c == 0 and ii == 0)))
        psOb = [bank(3), bank(5)]
        psZb = [bank(4), bank(6)]

        def c_S(it):
            h, qc, c, kb = it["h"], it["qc"], it["c"], it["kb"]
            if it["chunk_first"] and h == 0 and qc == 0:
                c_load_head(0)
                c_load_chunk(0, 0)
            if c == 0 and it["ii"] == 4:
                if qc == 0 and h + 1 < C_HEADS:
                    c_load_head(h + 1)
                nh, nq = (h, qc + 1) if qc + 1 < NT else (h + 1, 0)
                if nh < C_HEADS:
                    c_load_chunk(nh, nq)
            m = cs[h]
            i0 = qc * TC
            ka, kar = hctx[h]["kas"][c]
            qbf, qbr, qaf, qar = qctx[(h, qc)]["qs"][c]
            psS, psSr = bank(it["idx"] % 3)
            j0 = kb * 128
            rds = [kar, (kar, "aug"), qbr, (qbr, "aug"), qar, (qar, "aug")]
            pbt, pbr = pbc_rot.next()
            it.update(pbt=pbt, pbr=pbr)
            if j0 + 128 <= i0 or j0 >= i0 + TC:
                before = j0 + 128 <= i0
                qq = qbf if before else qaf
                bcol = h * CB_W + abs(i0 - j0) // 128
                sc.add("pe", lambda e: e.matmul(psS, lhsT=ka[0:68, j0:j0 + 128], rhs=qq[0:68, :], start=True, stop=True), reads=rds, writes=[psSr])
                sc.add("act", lambda e: e.activation(out=pbt, in_=psS, func=AF.Exp, bias=cbias[:, bcol:bcol + 1]), reads=[psSr, "cbias"], writes=[pbr])
            else:
                sb = (j0 - i0) // 128
                ca, cb_, cc_ = sb * 128, (sb + 1) * 128, TC

                def mmd(e):
                    r = None
                    if sb > 0:
                        r = e.matmul(psS[:, 0:ca], lhsT=ka[0:68, j0:j0 + 128], rhs=qaf[0:68, 0:ca], start=True, stop=True)
                    r = e.matmul(psS[:, ca:cb_], lhsT=ka[0:64, j0:j0 + 128], rhs=qbf[0:64, ca:cb_], start=True, stop=True)
                    if sb < 3:
                        r = e.matmul(psS[:, cb_:cc_], lhsT=ka[0:68, j0:j0 + 128], rhs=qbf[0:68, cb_:cc_], start=True, stop=True)
                    return r
                sc.add("pe", mmd, reads=rds, writes=[psSr])
                pfc, pfr = pfc_rot.next()

                def actd(e):
                    if sb > 0:
                        e.activation(out=pbt[:, 0:ca], in_=psS[:, 0:ca], func=AF.Exp, bias=cbias[:, h * CB_W + sb:h * CB_W + sb + 1])
                    if sb < 3:
                        e.activation(out=pbt[:, cb_:cc_], in_=psS[:, cb_:cc_], func=AF.Exp, bias=cbias[:, h * CB_W + 32 + sb:h * CB_W + 33 + sb])
                    return e.activation(out=pfc, in_=psS[:, ca:cb_], func=AF.Exp)
                sc.add("act", actd, reads=[psSr, "cbias"], writes=[pbr, (pbr, "a"), pfr])
                sc.add("dve", lambda e: e.tensor_tensor(out=pbt[:, ca:cb_], in0=pfc, in1=mdiag[:, h, :], op=ALU.mult), reads=[pfr, "mdiag", (pbr, "a")], writes=[pbr])

        def c_P(it):
            h, qc, c, kb = it["h"], it["qc"], it["c"], it["kb"]
            o_, or_ = psOb[c]
            z_, zr_ = psZb[c]
            vcv, vcr = hctx[h]["vcv"], hctx[h]["vcr"]
            pbt, first, last = it["pbt"], it["first"], it["last"]

            def f(e):
                e.matmul(o_, lhsT=vcv[:, kb, :], rhs=pbt, start=first, stop=last)
                return e.matmul(z_, lhsT=ones_bf, rhs=pbt, start=first, stop=last)
            sc.add("pe", f, reads=[it["pbr"], vcr, "ones_bf"], writes=[or_, zr_])
            if last:
                eo, eor = eo_rot.next()
                ez, ezr = ez_rot.next()
                sc.add("dve", lambda e: e.tensor_copy(out=eo, in_=o_), reads=[or_], writes=[eor])
                sc.add("dve", lambda e: e.tensor_copy(out=ez, in_=z_), reads=[zr_], writes=[ezr])
                qctx[(h, qc)]["ev%d" % c] = (eo, eor, ez, ezr)
                if c == 1:
                    c_epi(h, qc)

        def c_epi(h, qc):
            tsl = slice(qc * TC, (qc + 1) * TC)
            q = qctx[(h, qc)]
            eo0, eo0r, ez0, ez0r = q["ev0"]
            eo1, eo1r, ez1, ez1r = q["ev1"]
            cgt, cgr = q["cgt"], q["cgr"]

            def ep1(e):
                e.reciprocal(out=ez0, in_=ez0)
                e.reciprocal(out=ez1, in_=ez1)
                e.tensor_tensor(out=eo0, in0=eo0, in1=ez0, op=ALU.mult)
                e.tensor_tensor(out=eo1, in0=eo1, in1=ez1, op=ALU.mult)
                return e.scalar_tensor_tensor(out=eo0, in0=eo1, scalar=lamt[:, 4:5], in1=eo0, op0=ALU.mult, op1=ALU.add)
            sc.add("dve", ep1, reads=[eo0r, eo1r, ez0r, ez1r], writes=[eo0r, eo1r, ez0r, ez1r])
            sc.add("act", lambda e: e.activation(out=e_sq, in_=eo0, func=AF.Square), reads=[eo0r], writes=["e_sq"])
            bn, bnr = bank(7)
            sc.add("pe", lambda e: e.matmul(bn, lhsT=ones_f, rhs=e_sq, start=True, stop=True), reads=["e_sq", "ones_f"], writes=[bnr])
            sc.add("act", lambda e: e.activation(out=e_sd, in_=bn, func=AF.Sqrt, scale=1.0 / 128, bias=EPS), reads=[bnr], writes=["e_sd"])
            ybc, ybcr = ybc_rot.next()

            def ep2(e):
                e.reciprocal(out=e_o1, in_=e_sd)
                e.scalar_tensor_tensor(out=eo1, in0=eo0, scalar=lamt[:, 5:6], in1=e_o1, op0=ALU.mult, op1=ALU.mult)
                return e.tensor_tensor(out=ybc, in0=eo1, in1=cgt, op=ALU.mult)
            sc.add("dve", ep2, reads=["e_sd", eo0r, eo1r, cgr], writes=[ybcr, "e_o1", eo1r])
            sc.add("sp", lambda e: e.dma_start(out=Y[YC0 + h * 128:YC0 + (h + 1) * 128, tsl], in_=ybc), reads=[ybcr], dma=True)

        run_pipeline(items, [c_S, c_P], [0, 2])
        sc.barrier()

        ar.reset()
        scale_d = 96.0 ** -0.5
        kd2_rot = Rot(ar, 2, [S], BF16, "kd2")
        vraw2_rot = Rot(ar, 2, [32 * 64], BF16, "vraw2")
        vd_tiles = [ar.alloc([32, 128], BF16) for _ in range(2)]
        for k in range(2):
            sc.add("pool", lambda e, k=k: e.memset(vd_tiles[k][:, :, 64:128], 1.0), writes=[("vd1", k)])
        qd2_rot = Rot(ar, 3, [TC], BF16, "qd2")
        dg_rot = Rot(ar, 3, [TC], BF16, "dg")
        pbd_rot = Rot(ar, 5, [TC], BF16, "pbd")
        rzd_rot = Rot(ar, 2, [TC], F32, "rzd")
        yfd_rot = Rot(ar, 2, [TC], F32, "yfd")
        ybd_rot = Rot(ar, 2, [TC], BF16, "ybd")
        dh, dq = {}, {}

        def d_load_head(h):
            kd, kdr = kd2_rot.next()
            sc.add("sp", lambda e: e.dma_start(out=kd[0:64, :], in_=DK[h * 64:(h + 1) * 64, :]), writes=[kdr], dma=True)
            sc.add("sp", lambda e: e.dma_start(out=kd[64:96, :], in_=DKR), writes=[(kdr, "r")], dma=True)
            vraw, vrr = vraw2_rot.next()
            vk = h % 2
            vt, vtr = vd_tiles[vk], ("vd", vk)
            sc.add("sp", lambda e: e.dma_start(out=vraw, in_=DV[:, h * 2048:(h + 1) * 2048]), writes=[vrr], dma=True)
            sc.add("pool", lambda e: e.tensor_copy(out=vt[:, :, 0:64], in_=vraw.rearrange("p (b d) -> p b d", b=32)),
                   reads=[vrr, ("vd1", vk)], writes=[vtr])
            dh[h] = dict(kd=kd, kdr=kdr, vt=vt, vtr=vtr)

        def d_load_chunk(h, qc):
            tsl = slice(qc * TC, (qc + 1) * TC)
            qd, qdr = qd2_rot.next()
            sc.add("sp", lambda e: e.dma_start(out=qd[0:96, :], in_=DQ[h * 96:(h + 1) * 96, tsl]), writes=[qdr], dma=True)
            dgt, dgr = dg_rot.next()
            sc.add("sp", lambda e: e.dma_start(out=dgt[0:64, :], in_=DG[h * 64:(h + 1) * 64, tsl]), writes=[dgr], dma=True)
            dq[(h, qc)] = dict(qd=qd, qdr=qdr, dgt=dgt, dgr=dgr)

        items = [dict(h=h, qc=qc, kb=kb, idx=(h * NT + qc) * 32 + kb) for h in range(D_HEADS) for qc in range(NT) for kb in range(32)]

        def d_S(it):
            h, qc, kb = it["h"], it["qc"], it["kb"]
            if kb == 0 and qc == 0 and h == 0:
                d_load_head(0)
                d_load_chunk(0, 0)
            if kb == 4:
                if qc == 0 and h + 1 < D_HEADS:
                    d_load_head(h + 1)
                nh, nq = (h, qc + 1) if qc + 1 < NT else (h + 1, 0)
                if nh < D_HEADS:
                    d_load_chunk(nh, nq)
            kd, kdr = dh[h]["kd"], dh[h]["kdr"]
            qd, qdr = dq[(h, qc)]["qd"], dq[(h, qc)]["qdr"]
            psS, psSr = bank(it["idx"] % 3)
            pbt, pbr = pbd_rot.next()
            sc.add("pe", lambda e: e.matmul(psS, lhsT=kd[0:96, kb * 128:(kb + 1) * 128], rhs=qd[0:96, :], start=True, stop=True),
                   reads=[kdr, (kdr, "r"), qdr], writes=[psSr])
            sc.add("act", lambda e: e.activation(out=pbt, in_=psS, func=AF.Exp, scale=scale_d), reads=[psSr], writes=[pbr])
            it.update(pbt=pbt, pbr=pbr)

        def d_P(it):
            h, qc, kb = it["h"], it["qc"], it["kb"]
            vt, vtr = dh[h]["vt"], dh[h]["vtr"]
            psO, psOr = bank(3 + (h * NT + qc) % 2)
            pbt = it["pbt"]
            sc.add("pe", lambda e: e.matmul(psO, lhsT=vt[:, kb, :], rhs=pbt, start=(kb == 0), stop=(kb == 31)), reads=[it["pbr"], vtr], writes=[psOr])
            if kb == 31:
                tsl = slice(qc * TC, (qc + 1) * TC)
                rz, rzr = rzd_rot.next()
                yf, yfr = yfd_rot.next()
                yb, ybr = ybd_rot.next()
                dgt, dgr = dq[(h, qc)]["dgt"], dq[(h, qc)]["dgr"]

                def epd(e):
                    e.reciprocal(out=rz[0:64, :], in_=psO[64:128, :])
                    e.tensor_tensor(out=yf[0:64, :], in0=psO[0:64, :], in1=rz[0:64, :], op=ALU.mult)
                    return e.tensor_tensor(out=yb[0:64, :], in0=yf[0:64, :], in1=dgt[0:64, :], op=ALU.mult)
                sc.add("dve", epd, reads=[psOr, dgr], writes=[rzr, yfr, ybr])
                sc.add("sp", lambda e: e.dma_start(out=Y[YD0 + h * 64:YD0 + (h + 1) * 64, tsl], in_=yb[0:64, :]), reads=[ybr], dma=True)

        run_pipeline(items, [d_S, d_P], [0, 2])
        sc.barrier()

        ar.reset()
        wbr = ar.alloc([NYC, 1024], BF16)
        wout = ar.alloc([8, 1024], BF16)
        wbr_f = wbr.rearrange("p a b -> p (a b)")
        wout_f = wout.rearrange("p a b -> p (a b)")
        nwb = (NYC * 1024 + 4095) // 4096
        for i in range(nwb):
            n = min(4096, NYC * 1024 - i * 4096)
            load_cast(wbr_d[l][:, i * 4096:i * 4096 + n], n, wbr_f[:, i * 4096:i * 4096 + n], ("wbr", i))
        for i in range(2):
            load_cast(wout_d[l][:, i * 4096:(i + 1) * 4096], 4096, wout_f[:, i * 4096:(i + 1) * 4096], ("wout", i))
        wbr_res = [("wbr", i) for i in range(nwb)]
        wout_res = [("wout", i) for i in range(2)]
        yt_rot = Rot(ar, 2, [NYC, TC], BF16, "yt")
        xt2 = ar.alloc([KC, TC], F32)
        out2 = ar.alloc([KC, TC], F32)
        merged = ar.alloc([KC, TC], BF16)
        mfull_rot = Rot(ar, 2, [KC, TC], BF16, "mfull") if SPLIT else None
        g_rot = Rot(ar, 4, [TC], BF16, "g")
        tm_rot = Rot(ar, 4, [TC], F32, "tm")
        macc_rot = Rot(ar, 3, [TC], F32, "macc")
        sqf_rot = Rot(ar, 2, [TC], F32, "sqf")
        rr2 = ar.alloc([TC], F32)
        Yv = Y.rearrange("(c p) s -> p c s", p=128)
        x_src_v2 = x_src.rearrange("(kc p) s -> p kc s", p=128)
        x_dst_v = x_dst.rearrange("(kc p) s -> p kc s", p=128)
        pbk = [0]
        mres = [("merged", oc) for oc in range(8)]

        def f_stage1(t):
            tsl = slice(t * TC, (t + 1) * TC)
            yt, ytr = yt_rot.next()
            sc.add("sp", lambda e: e.dma_start(out=yt, in_=Yv[:, :, tsl]), writes=[ytr], dma=True)
            for oc in range(8):
                macc = maccr = None
                for br in range(4):
                    gt, gr = g_rot.next()
                    sc.add("sp", lambda e, gt=gt, br=br, oc=oc: e.dma_start(out=gt, in_=GT[br * 1024 + oc * 128:br * 1024 + (oc + 1) * 128, tsl]), writes=[gr], dma=True)
                    bk, bkr = bank(pbk[0] % 4)
                    pbk[0] += 1
                    chs = BR_CHUNKS[br]

                    def bmm(e, bk=bk, chs=chs, oc=oc):
                        r = None
                        for ci, (cidx, nr) in enumerate(chs):
                            r = e.matmul(bk, lhsT=wbr[0:nr, cidx, oc * 128:(oc + 1) * 128], rhs=yt[0:nr, cidx, :], start=(ci == 0), stop=(ci == len(chs) - 1))
                        return r
                    sc.add("pe", bmm, reads=[ytr] + wbr_res, writes=[bkr])
                    if br == 0:
                        macc, maccr = macc_rot.next()
                        sc.add("dve", lambda e, bk=bk, gt=gt, macc=macc: e.tensor_tensor(out=macc, in0=bk, in1=gt, op=ALU.mult), reads=[bkr, gr], writes=[maccr])
                    else:
                        tm, tmr = tm_rot.next()
                        sc.add("dve", lambda e, bk=bk, gt=gt, tm=tm: e.tensor_tensor(out=tm, in0=bk, in1=gt, op=ALU.mult), reads=[bkr, gr], writes=[tmr])
                        if br < 3:
                            sc.add("pool", lambda e, tm=tm, macc=macc: e.tensor_tensor(out=macc, in0=macc, in1=tm, op=ALU.add), reads=[tmr, maccr], writes=[maccr])
                        else:
                            sc.add("pool", lambda e, tm=tm, oc=oc, macc=macc: e.tensor_tensor(out=merged[:, oc, :], in0=macc, in1=tm, op=ALU.add), reads=[tmr, maccr], writes=[("merged", oc)])
            if SPLIT:
                pr, q = t // 2, t % 2
                k = pr % 2
                sc.add("sp", lambda e: e.dma_start(out=ARI[k].rearrange("(kc p) s -> p kc s", p=128)[:, :, q * TC:(q + 1) * TC], in_=merged),
                       reads=mres, writes=[("ari", k, q)], dma=True)
                if q == 1:
                    sc.add("pool", lambda e: e.collective_compute("AllReduce", ALU.add, replica_groups=RG, ins=[ARI[k]], outs=[ARO[k]]),
                           reads=[("ari", k, 0), ("ari", k, 1)], writes=[("aro", k)], cc=True)

        def f_stage2(t):
            tsl = slice(t * TC, (t + 1) * TC)
            if SPLIT:
                pr, q = t // 2, t % 2
                k = pr % 2
                msrc, msr = mfull_rot.next()
                sc.add("sp", lambda e: e.dma_start(out=msrc, in_=ARO[k].rearrange("(kc p) s -> p kc s", p=128)[:, :, q * TC:(q + 1) * TC]),
                       reads=[("aro", k)], writes=[msr], dma=True)
                mrd = [msr]
            else:
                msrc, mrd = merged, mres
            sc.add("sp", lambda e: e.dma_start(out=xt2, in_=x_src_v2[:, :, tsl]), writes=["xt2"], dma=True)
            bn, bnr = bank(6)
            for oc2 in range(8):
                bo, bor = bank(4 + oc2 % 2)

                def omm(e, bo=bo, oc2=oc2):
                    r = None
                    for oc in range(8):
                        r = e.matmul(bo, lhsT=wout[:, oc, oc2 * 128:(oc2 + 1) * 128], rhs=msrc[:, oc, :], start=(oc == 0), stop=(oc == 7))
                    return r
                sc.add("pe", omm, reads=mrd + wout_res, writes=[bor])
                sqf, sqfr = sqf_rot.next()

                def oev(e, bo=bo, oc2=oc2, sqf=sqf):
                    e.activation(out=out2[:, oc2, :], in_=bo, func=AF.Copy)
                    return e.activation(out=sqf, in_=bo, func=AF.Square)
                sc.add("act", oev, reads=[bor], writes=[("out2", oc2), sqfr])
                sc.add("pe", lambda e, sqf=sqf, oc2=oc2: e.matmul(bn, lhsT=ones_f, rhs=sqf, start=(oc2 == 0), stop=(oc2 == 7)), reads=[sqfr, "ones_f"], writes=[bnr])
            def rr2_op(e):
                e.activation(out=rr2, in_=bn, func=AF.Ln, scale=1.0 / DM, bias=EPS)
                return e.activation(out=rr2, in_=rr2, func=AF.Exp, scale=-0.5)
            sc.add("act", rr2_op, reads=[bnr], writes=["rr2"])
            ores = [("out2", i) for i in range(8)]

            def resid(e):
                r = None
                for oc2 in range(8):
                    e.scalar_tensor_tensor(out=out2[:, oc2, :], in0=out2[:, oc2, :], scalar=col(SM_GPOST + oc2), in1=rr2, op0=ALU.mult, op1=ALU.mult)
                    r = e.tensor_tensor(out=xt2[:, oc2, :], in0=xt2[:, oc2, :], in1=out2[:, oc2, :], op=ALU.add)
                return r
            sc.add("dve", resid, reads=["rr2", "xt2", "small"] + ores, writes=["xt2", "rr2"] + ores)
            sc.add("sp", lambda e: e.dma_start(out=x_dst_v[:, :, tsl], in_=xt2), reads=["xt2"], dma=True)

        if SPLIT:
            for t in range(NT):
                f_stage1(t)
                if t % 2 == 1 and t >= 3:
                    f_stage2(t - 3)
                    f_stage2(t - 2)
            f_stage2(NT - 2)
            f_stage2(NT - 1)
        else:
            for t in range(NT):
                f_stage1(t)
                f_stage2(t)
        sc.barrier()

    for l in range(L):
        layer(l)
    sc.analyze()
    sc.emit(nc, es)
    es.close()
    return nc


def _bf(x):
    return np.asarray(x, dtype=np.float32).astype(ml_dtypes.bfloat16)


def _parts(par):
    if not SPLIT:
        return list(range(6)), list(range(6)), list(range(4)), list(range(6))
    return ([3 * par + i for i in range(3)], [3 * par + i for i in range(3)], [2 * par + i for i in range(2)],
            [3 * par + i for i in range(3)])


def build_consts(par=0):
    c = {}
    slots, _, cheads, _ = _parts(par)
    cs_all = c_slopes()
    cs = [cs_all[h] for h in cheads]
    caugq = np.zeros((C_HEADS, 2, 4, TC), np.float32)
    caugk = np.zeros((C_HEADS, 4, S), np.float32)
    cbias = np.zeros((128, C_HEADS, CB_W), np.float32)
    ii = np.arange(TC, dtype=np.float64)
    jj = (np.arange(S) % 128).astype(np.float64)
    for h, m in enumerate(cs):
        qb = (-m * ii).astype(np.float32)
        qb_hi = _bf(qb).astype(np.float32)
        qb_lo = _bf(qb - qb_hi).astype(np.float32)
        caugq[h, 0] = np.stack([qb_hi, qb_lo, np.ones(TC), np.ones(TC)])
        caugq[h, 1] = -caugq[h, 0]
        kb = (m * jj).astype(np.float32)
        kb_hi = _bf(kb).astype(np.float32)
        kb_lo = _bf(kb - kb_hi).astype(np.float32)
        caugk[h] = np.stack([np.ones(S), np.ones(S), kb_hi, kb_lo])
        cbias[:, h, 0:32] = (-m * 128.0 * np.arange(32))[None, :]
        cbias[:, h, 32:36] = (m * 128.0 * np.arange(4))[None, :]
    c["caugq"] = _bf(caugq)
    c["caugk"] = _bf(caugk)
    c["cbias"] = cbias.reshape(128, C_HEADS * CB_W)
    p = np.arange(128)[:, None].astype(np.float64)
    f = np.arange(128)[None, :].astype(np.float64)
    md = np.stack([np.exp(-m * np.abs(p - f)) for m in cs], axis=1)
    c["mdiag"] = md.reshape(128, C_HEADS * 128).astype(np.float32)
    sl_all = a_slopes()
    ma = np.zeros((128, 3 * A_SLOTS, 384), np.float64)
    k = np.arange(128)[:, None]
    for g, d in enumerate(A_DIL):
        for si, sg in enumerate(slots):
            for j in range(3):
                q = np.arange(128)[None, :]
                rel = np.abs((j - 1) * 128 + k - q)
                ma[:, g * A_SLOTS + si, j * 128:(j + 1) * 128] = np.where(rel <= 64, np.exp(-sl_all[sg] * d * rel), 0.0)
    c["ma"] = ma.reshape(128, 3 * A_SLOTS * 384).astype(np.float32)
    inv = (10000.0 ** (-np.arange(0, 32, 2, dtype=np.float32) / 32)).astype(np.float32)
    ang = np.arange(S, dtype=np.float32)[:, None] * inv[None, :]
    cos, sin = np.cos(ang).astype(np.float32).T, np.sin(ang).astype(np.float32).T
    c["ropec"] = np.ascontiguousarray(np.concatenate([cos, cos], 0))
    c["ropes"] = np.ascontiguousarray(np.concatenate([-sin, sin], 0))
    return c


def _bchan(par):
    _, blocks, _, _ = _parts(par)
    idx = -np.ones(BW, np.int64)
    for i, b in enumerate(blocks):
        idx[i * 64:(i + 1) * 64] = np.arange(b * 64, (b + 1) * 64)
    return idx


def pack_w_in(w, par):
    slots, _, cheads, dheads = _parts(par)
    L = w.shape[0]
    out = np.zeros((L, DM, IN_W), np.float32)
    O = OFF_ALL

    def put(name, off_in_fam, src_cols):
        n = len(src_cols)
        out[:, :, OFF[name] + off_in_fam:OFF[name] + off_in_fam + n] = w[:, :, src_cols]
    for fam in ("a_q", "a_k", "a_v"):
        for g in range(3):
            cols = np.concatenate([np.arange(O[fam] + g * 384 + s * 64, O[fam] + g * 384 + (s + 1) * 64) for s in slots])
            put(fam, g * AW, cols)
    put("a_g", 0, np.concatenate([np.arange(O["a_g"] + s * 64, O["a_g"] + (s + 1) * 64) for s in slots]))
    bidx = _bchan(par)
    nreal = int((bidx >= 0).sum())
    put("b_x", 0, O["b_x"] + bidx[:nreal])
    put("b_g", 0, O["b_g"] + bidx[:nreal])
    for fam in ("c_q", "c_k", "c_v", "c_g"):
        put(fam, 0, np.concatenate([np.arange(O[fam] + h * 128, O[fam] + (h + 1) * 128) for h in cheads]))
    put("d_cq", 0, np.arange(O["d_cq"], O["d_cq"] + 256))
    put("d_ckv", 0, np.arange(O["d_ckv"], O["d_ckv"] + 128))
    put("d_kr", 0, np.arange(O["d_kr"], O["d_kr"] + 32))
    put("d_g", 0, np.concatenate([np.arange(O["d_g"] + h * 64, O["d_g"] + (h + 1) * 64) for h in dheads]))
    put("gate", 0, np.arange(O["gate"], O["gate"] + 4096))
    return out


def pack_layers(inp, layers, par=0):
    f32 = np.float32
    L = len(layers)
    slots, blocks, cheads, dheads = _parts(par)
    bidx = _bchan(par)
    real = bidx >= 0
    small = np.zeros((L, 128, NSMALL), f32)
    lamv = np.zeros((L, 1, 256), f32)
    lruw = np.zeros((L, 128, 4 * NBC, 128), f32)
    wuqa = np.zeros((L, 128, 2, D_HEADS * 96), f32)
    wuqb = np.zeros((L, 128, 2, D_HEADS * 96), f32)
    wukvk = np.zeros((L, 128, DW), f32)
    wukvv = np.zeros((L, 128, DW), f32)
    wbr = np.zeros((L, 128, NYC, 1024), f32)
    wout = np.zeros((L, 128, 8, 1024), f32)

    def bvec(v):
        o = np.zeros(BW, f32)
        o[real] = v[bidx[real]]
        return o.reshape(NBC, 128).T
    for li, l in enumerate(layers):
        sm = small[li]
        sm[:, SM_GPRE:SM_GPRE + 8] = inp["norm_pre"][l].reshape(8, 128).T
        sm[:, SM_GPOST:SM_GPOST + 8] = inp["norm_post"][l].reshape(8, 128).T
        sm[:, SM_BGATE:SM_BGATE + 32] = inp["b_gate"][l].reshape(32, 128).T
        for j in range(4):
            sm[:, SM_CONVW + j * NBC:SM_CONVW + (j + 1) * NBC] = bvec(inp["conv_w"][l][j])
        sm[:, SM_CONVB:SM_CONVB + NBC] = bvec(inp["conv_b"][l])
        for dr in range(2):
            sm[:, SM_BR + dr * NBC:SM_BR + (dr + 1) * NBC] = bvec(inp["lru_br"][l][dr])
            sm[:, SM_BI + dr * NBC:SM_BI + (dr + 1) * NBC] = bvec(inp["lru_bi"][l][dr])
            sm[:, SM_LAM + dr * NBC:SM_LAM + (dr + 1) * NBC] = bvec(inp["lru_lambda"][l][dr])
        sm[:, SM_SUBLN] = inp["diff_subln"][l]
        sm[:, SM_QN:SM_QN + 2] = inp["mla_q_norm"][l].reshape(2, 128).T
        sm[:, SM_KVN] = inp["mla_kv_norm"][l]
        lam_init = 0.8 - 0.6 * math.exp(-0.3 * l)
        sm[:, SM_LAMINIT] = lam_init
        sm[:, SM_OML] = (1.0 - lam_init)
        lamv[li, 0, 0:64] = inp["diff_lam_q1"][l]
        lamv[li, 0, 64:128] = inp["diff_lam_k1"][l]
        lamv[li, 0, 128:192] = inp["diff_lam_q2"][l]
        lamv[li, 0, 192:256] = inp["diff_lam_k2"][l]
        for gi, w in enumerate((inp["lru_wr"][l], inp["lru_wi"][l])):
            for dr in range(2):
                for i, b in enumerate(blocks):
                    c, bb = i // 2, i % 2
                    lruw[li, bb * 64:(bb + 1) * 64, (gi * 2 + dr) * NBC + c, bb * 64:(bb + 1) * 64] = w[dr, b]
        uq = inp["mla_w_uq"][l].reshape(2, 128, 6, 96)[:, :, dheads, :]
        wuqa[li] = uq.transpose(1, 0, 2, 3).reshape(128, 2, D_HEADS * 96)
        uqb = uq.copy()
        uqb[..., 64:80] = uq[..., 80:96]
        uqb[..., 80:96] = uq[..., 64:80]
        wuqb[li] = uqb.transpose(1, 0, 2, 3).reshape(128, 2, D_HEADS * 96)
        ukv = inp["mla_w_ukv"][l].reshape(128, 6, 128)[:, dheads, :]
        wukvk[li] = ukv[:, :, 0:64].reshape(128, DW)
        wukvv[li] = ukv[:, :, 64:128].reshape(128, DW)
        wy = np.zeros((NYC * 128, 1024), f32)
        wa, wb_, wc, wd = inp["w_br_a"][l], inp["w_br_b"][l], inp["w_br_c"][l], inp["w_br_d"][l]
        for i, s_ in enumerate(slots):
            wy[YA0 + i * 64:YA0 + (i + 1) * 64] = wa[s_ * 64:(s_ + 1) * 64]
        wy[YB0:YB0 + BW][real] = wb_[bidx[real]]
        for i, h in enumerate(cheads):
            wy[YC0 + i * 128:YC0 + (i + 1) * 128] = wc[h * 128:(h + 1) * 128]
        for i, h in enumerate(dheads):
            wy[YD0 + i * 64:YD0 + (i + 1) * 64] = wd[h * 64:(h + 1) * 64]
        wbr[li] = wy.reshape(NYC, 128, 1024).transpose(1, 0, 2)
        wout[li] = inp["w_out"][l].reshape(8, 128, 1024).transpose(1, 0, 2)
    return dict(small=small, lamv=lamv, lruw=lruw.reshape(L, 128, 4 * NBC * 128), wuqa=wuqa.reshape(L, 128, 2 * D_HEADS * 96),
                wuqb=wuqb.reshape(L, 128, 2 * D_HEADS * 96), wukvk=wukvk, wukvv=wukvv, wbr=wbr.reshape(L, 128, NYC * 1024),
                wout=wout.reshape(L, 128, 8 * 1024))


_PROG = {}


def get_prog(nl, debug=False):
    key = (nl, tuple(debug) if debug else None)
    if key not in _PROG:
        _PROG[key] = build_program(nl, debug)
    return _PROG[key]


FUSED = True


def make_in_maps(inp, layers, xT_by_batch):
    npar = 2 if SPLIT else 1
    w_all = np.ascontiguousarray(inp["w_in"][layers[0]:layers[-1] + 1]).astype(np.float32)
    per = []
    for par in range(npar):
        m = dict(w_in=pack_w_in(w_all, par) if SPLIT else w_all)
        m.update(pack_layers(inp, layers, par))
        m.update(build_consts(par))
        per.append(m)
    in_maps = []
    for c in range(8):
        b, par = (c // 2, c % 2) if SPLIT else (c % 4, 0)
        m = dict(xT=xT_by_batch[b])
        m.update(per[par])
        in_maps.append(m)
    return in_maps


def kernel(**inputs):
    inp = {k: np.asarray(v) for k, v in inputs.items()}
    x = inp["x"].astype(np.float32)
    xT = [np.ascontiguousarray(x[b].T) for b in range(4)]
    groups = [list(range(DEPTH))] if FUSED else [[l] for l in range(DEPTH)]
    for layers in groups:
        nc = get_prog(len(layers))
        in_maps = make_in_maps(inp, layers, xT)
        res = run_bass_kernel_spmd(nc, in_maps, core_ids=list(range(8)))
        xT = [np.asarray(res.results[(2 * b) if SPLIT else b]["outT"]) for b in range(4)]
    out = np.stack([xT[b].T for b in range(4)], 0).astype(np.float32)
    return np.ascontiguousarray(out)
```

```python
import math
from contextlib import ExitStack

import numpy as np
import ml_dtypes

import concourse.bass as bass
import concourse.mybir as mybir
from concourse.bass_utils import run_bass_kernel_spmd

F32 = mybir.dt.float32
BF16 = mybir.dt.bfloat16
AF = mybir.ActivationFunctionType
ALU = mybir.AluOpType
AX = mybir.AxisListType

S = 4096
DM = 1024
DEPTH = 4
NT = 8
TC = 512
KC = 8
EPS = 1e-6
A_DIL = (1, 4, 16)
SPLIT = True
A_SLOTS_ALL, C_HEADS_ALL, D_HEADS_ALL = 6, 4, 6
A_SLOTS = 3 if SPLIT else 6
C_HEADS = 2 if SPLIT else 4
D_HEADS = 3 if SPLIT else 6
NBC = 2 if SPLIT else 3
AW, BW, CW, DW = A_SLOTS * 64, NBC * 128, C_HEADS * 128, D_HEADS * 64
_fam = [("a_q", 3 * AW), ("a_k", 3 * AW), ("a_v", 3 * AW), ("a_g", AW), ("b_x", BW), ("b_g", BW), ("c_q", CW), ("c_k", CW),
        ("c_v", CW), ("c_g", CW), ("d_cq", 256), ("d_ckv", 128), ("d_kr", 32), ("d_g", DW), ("gate", 4096)]
OFF = {}
_o = 0
for _n, _w in _fam:
    OFF[_n] = _o
    _o += _w
IN_W = _o
OFF_ALL = dict(a_q=0, a_k=1152, a_v=2304, a_g=3456, b_x=3840, b_g=4224, c_q=4608, c_k=5120,
               c_v=5632, c_g=6144, d_cq=6656, d_ckv=6912, d_kr=7040, d_g=7072, gate=7456)
SKIP_T = 1e30 if SPLIT else 100.0
if SPLIT:
    YA0, YB0, YC0, YD0, NYC = 0, 256, 512, 768, 8
    BR_CHUNKS = [[(0, 128), (1, 64)], [(2, 128), (3, 64)], [(4, 128), (5, 128)], [(6, 128), (7, 64)]]
else:
    YA0, YB0, YC0, YD0, NYC = 0, 384, 768, 1280, 13
    BR_CHUNKS = [[(0, 128), (1, 128), (2, 128)], [(3, 128), (4, 128), (5, 128)], [(6, 128), (7, 128), (8, 128), (9, 128)],
                 [(10, 128), (11, 128), (12, 128)]]
RG = [[0, 1], [2, 3], [4, 5], [6, 7]]
CB_W = 36

SM_GPRE, SM_GPOST, SM_BGATE, SM_CONVW = 0, 8, 16, 48
SM_CONVB = SM_CONVW + 4 * NBC
SM_BR = SM_CONVB + NBC
SM_BI = SM_BR + 2 * NBC
SM_LAM = SM_BI + 2 * NBC
SM_SUBLN = SM_LAM + 2 * NBC
SM_QN, SM_KVN, SM_LAMINIT, SM_OML = SM_SUBLN + 1, SM_SUBLN + 3, SM_SUBLN + 4, SM_SUBLN + 5
NSMALL = SM_SUBLN + 8


def a_slopes():
    return [2.0 ** (-8.0 * (i + 1) / A_SLOTS_ALL) for i in range(A_SLOTS_ALL)]


def c_slopes():
    return [2.0 ** (-8.0 * (i + 1) / C_HEADS_ALL) for i in range(C_HEADS_ALL)]


class Sched:
    ENGS = ("pe", "act", "dve", "pool", "sp")
    NDMA = {"sp": 24, "act": 12, "pool": 4, "cc": 4}
    UNIT = {"sp": 16, "act": 16, "pool": 16, "cc": 1}

    def __init__(self):
        self.ops = []

    def add(self, eng, fn, reads=(), writes=(), dma=False, cc=False):
        self.ops.append(dict(eng=eng, fn=fn, reads=tuple(reads), writes=tuple(writes), dma=(dma or cc),
                             q=("cc" if cc else eng), needs_inc=False, deps=[]))

    def barrier(self):
        self.ops.append(dict(barrier=True))

    def analyze(self):
        ops = self.ops
        last_w, readers = {}, {}
        last_on = {}
        for i, op in enumerate(ops):
            if op.get("barrier"):
                for e, j in last_on.items():
                    ops[j]["needs_inc"] = True
                last_w, readers = {}, {}
                continue
            deps = set()
            for r in op["reads"]:
                if r in last_w:
                    deps.add((last_w[r], "raw"))
            for w in op["writes"]:
                if w in last_w:
                    deps.add((last_w[w], "waw"))
                for rd in readers.get(w, ()):
                    deps.add((rd, "war"))
            keep = set()
            for d, kind in deps:
                if d == i:
                    continue
                p = ops[d]
                if p["dma"]:
                    keep.add(d)
                elif p["eng"] == op["eng"]:
                    if op["dma"]:
                        keep.add(d)
                    elif kind == "raw" and op["eng"] in ("act", "dve", "pool"):
                        keep.add(d)
                else:
                    keep.add(d)
            op["deps"] = sorted(keep)
            for d in keep:
                ops[d]["needs_inc"] = True
            for w in op["writes"]:
                last_w[w] = i
                readers[w] = []
            for r in op["reads"]:
                if r not in op["writes"]:
                    readers.setdefault(r, []).append(i)
            if not op["dma"]:
                last_on[op["eng"]] = i
        tick = {e: 0 for e in self.ENGS}
        dma_cnt = {q: [0] * n for q, n in self.NDMA.items()}
        dma_rr = {q: 0 for q in self.NDMA}
        seen = {e: {} for e in self.ENGS}
        pending = {e: [] for e in self.ENGS}
        for op in ops:
            if op.get("barrier"):
                snap = [(("c", e), tick[e]) for e in self.ENGS if tick[e] > 0]
                for q, cnts in dma_cnt.items():
                    for k, c in enumerate(cnts):
                        if c > 0:
                            snap.append((("d", q, k), self.UNIT[q] * c))
                for e in self.ENGS:
                    pending[e] = [(s_, v) for (s_, v) in snap if s_ != ("c", e)]
                continue
            e = op["eng"]
            waits = list(pending[e])
            pending[e] = []
            for d in op["deps"]:
                p = ops[d]
                waits.append((p["sem"], p["tick"]))
            if op["dma"]:
                q = op["q"]
                k = dma_rr[q]
                dma_rr[q] = (k + 1) % self.NDMA[q]
                if dma_cnt[q][k] > 0:
                    waits.append((("d", q, k), self.UNIT[q] * dma_cnt[q][k]))
                dma_cnt[q][k] += 1
                op["sem"] = ("d", q, k)
                op["tick"] = self.UNIT[q] * dma_cnt[q][k]
            elif op["needs_inc"]:
                tick[e] += 1
                op["sem"] = ("c", e)
                op["tick"] = tick[e]
            fw = []
            for s_, v in waits:
                if seen[e].get(s_, 0) >= v:
                    continue
                seen[e][s_] = v
                fw.append((s_, v))
            mx = {}
            for s_, v in fw:
                mx[s_] = max(mx.get(s_, 0), v)
            op["waits"] = sorted(mx.items(), key=lambda kv: str(kv[0]))
        self.final_dma = {(q, k): self.UNIT[q] * c for q, cnts in dma_cnt.items() for k, c in enumerate(cnts) if c > 0}

    def emit(self, nc, es):
        sems = {}
        for e in self.ENGS:
            sems[("c", e)] = es.enter_context(nc.semaphore("c_" + e))
        for q, n in self.NDMA.items():
            for k in range(n):
                sems[("d", q, k)] = es.enter_context(nc.semaphore("d_%s_%d" % (q, k)))
        blk = es.enter_context(nc.Block())
        ops = self.ops

        def run(engname):
            def body(e):
                for op in ops:
                    if op.get("barrier") or op["eng"] != engname:
                        continue
                    for s_, v in op["waits"]:
                        e.wait_ge(sems[s_], v)
                    ins = op["fn"](e)
                    if op["dma"]:
                        ins.then_inc(sems[op["sem"]], self.UNIT[op["q"]])
                    elif op["needs_inc"]:
                        ins.then_inc(sems[op["sem"]], 1)
                if engname == "sp":
                    for (q, k), v in sorted(self.final_dma.items()):
                        e.wait_ge(sems[("d", q, k)], v)
            return body

        blk.tensor(run("pe"))
        blk.scalar(run("act"))
        blk.vector(run("dve"))
        blk.gpsimd(run("pool"))
        blk.sync(run("sp"))


class Arena:
    def __init__(self, ap, nbytes, name):
        self.ap = ap
        self.nbytes = nbytes
        self.off = 0
        self.name = name
        self.cnt = 0

    def reset(self):
        self.off = 0

    def alloc(self, free_shape, dt, parts=128):
        n = 1
        for d in free_shape:
            n *= d
        nb = n * (4 if dt == F32 else 2)
        nb = (nb + 63) // 64 * 64
        assert self.off + nb <= self.nbytes, "arena %s overflow: %d + %d > %d" % (self.name, self.off, nb, self.nbytes)
        a = self.ap[:, self.off // 4:(self.off + nb) // 4]
        self.off += nb
        if dt == BF16:
            a = a.bitcast(BF16)
        a = a[:, 0:n]
        if len(free_shape) == 2:
            a = a.rearrange("p (a b) -> p a b", a=free_shape[0])
        elif len(free_shape) == 3:
            a = a.rearrange("p (a b c) -> p a b c", a=free_shape[0], b=free_shape[1])
        self.cnt += 1
        return a


class Rot:
    def __init__(self, arena, n, free_shape, dt, name):
        self.tiles = [arena.alloc(free_shape, dt) for _ in range(n)]
        self.name = name
        self.i = 0

    def next(self):
        k = self.i % len(self.tiles)
        self.i += 1
        return self.tiles[k], (self.name, k)


def run_pipeline(items, stages, lags):
    n = len(items)
    mx = max(lags)
    for step in range(n + mx):
        for f, lg in zip(stages, lags):
            i = step - lg
            if 0 <= i < n:
                f(items[i])


def build_program(nlayers, debug=False):
    nc = bass.Bass("TRN2", target_bir_lowering=False)
    L = nlayers

    def din(name, shape, dt=F32):
        return nc.dram_tensor(name, list(shape), dt, kind="ExternalInput").ap()

    def dscr(name, shape, dt=BF16):
        ext = bool(debug) and name in debug
        return nc.dram_tensor(name, list(shape), dt, kind="ExternalOutput" if ext else "Internal").ap()

    xT_in = din("xT", [DM, S])
    w_in = din("w_in", [L, DM, IN_W])
    small_d = din("small", [L, 128, NSMALL])
    lamv_d = din("lamv", [L, 1, 256])
    lruw_d = din("lruw", [L, 128, 4 * NBC * 128])
    wuqa_d = din("wuqa", [L, 128, 2 * D_HEADS * 96])
    wuqb_d = din("wuqb", [L, 128, 2 * D_HEADS * 96])
    wukvk_d = din("wukvk", [L, 128, DW])
    wukvv_d = din("wukvv", [L, 128, DW])
    wbr_d = din("wbr", [L, 128, NYC * 1024])
    cbias_d = din("cbias", [128, C_HEADS * CB_W])
    wout_d = din("wout", [L, 128, 8 * 1024])
    caugq_d = din("caugq", [C_HEADS, 2, 4, TC], BF16)
    caugk_d = din("caugk", [C_HEADS, 4, S], BF16)
    mdiag_d = din("mdiag", [128, C_HEADS * 128])
    ma_d = din("ma", [128, 3 * A_SLOTS * 384])
    ropec_d = din("ropec", [32, S])
    ropes_d = din("ropes", [32, S])
    outT = nc.dram_tensor("outT", [DM, S], F32, kind="ExternalOutput").ap()

    AQ = [dscr("AQ%d" % g, [AW, S]) for g in range(3)]
    AK = [dscr("AK%d" % g, [AW, S]) for g in range(3)]
    AV = [dscr("AV%d" % g, [128, A_SLOTS * 32 * 64]) for g in range(3)]
    AG = dscr("AG", [AW, S])
    BX = dscr("BX", [BW, S], F32)
    BG = dscr("BG", [BW, S])
    CQ = dscr("CQ", [CW, S])
    CK = dscr("CK", [CW, S])
    CV = dscr("CV", [128, C_HEADS * 32 * 128])
    CG = dscr("CG", [CW, S])
    DLAT = dscr("DLAT", [384, S], F32)
    DKRAW = dscr("DKRAW", [64, S], F32)
    DG = dscr("DG", [DW, S])
    GT = dscr("GT", [4096, S])
    DQ = dscr("DQ", [D_HEADS * 96, S])
    DK = dscr("DK", [D_HEADS * 64, S])
    DKR = dscr("DKR", [32, S])
    DV = dscr("DV", [128, D_HEADS * 32 * 64])
    Y = dscr("Y", [NYC * 128, S])
    ARI = [nc.dram_tensor("ARI%d" % i, [DM, 2 * TC], BF16, kind="Internal").ap() for i in range(2)]
    ARO = [nc.dram_tensor("ARO%d" % i, [DM, 2 * TC], BF16, kind="Internal").ap() for i in range(2)]
    XS = [nc.dram_tensor("XS%d" % i, [DM, S], F32, kind="Internal").ap() for i in range(2)]

    sc = Sched()
    es = ExitStack()
    PERS_BYTES = 56 * 1024
    ARENA_BYTES = 136 * 1024
    pers_t = es.enter_context(nc.sbuf_tensor("pers", [128, PERS_BYTES // 4], F32))
    arena_t = es.enter_context(nc.sbuf_tensor("arena", [128, ARENA_BYTES // 4], F32))
    pers = Arena(pers_t[:], PERS_BYTES, "pers")
    ar = Arena(arena_t[:], ARENA_BYTES, "arena")
    banks = [es.enter_context(nc.psum_tensor("bank%d" % i, [128, 512], F32)) for i in range(8)]

    def bank(i):
        return banks[i][:], ("bank", i)

    small = pers.alloc([NSMALL], F32)
    lamv = pers.alloc([256], F32)
    lamt = pers.alloc([16], F32)
    spc = pers.alloc([8], F32)
    ones_bf = pers.alloc([128], BF16)
    ones_f = pers.alloc([128], F32)
    mdiag = pers.alloc([C_HEADS, 128], F32)
    cbias = pers.alloc([C_HEADS * CB_W], F32)
    wst = [pers.alloc([4096], F32) for _ in range(2)]
    wbf = [pers.alloc([4096], BF16) for _ in range(2)]
    wst_i = [0]

    def col(c, n=1):
        return small[:, c:c + n]

    sc.add("dve", lambda e: e.memset(ones_bf, 1.0), writes=["ones_bf"])
    sc.add("dve", lambda e: e.memset(ones_f, 1.0), writes=["ones_f"])
    sc.add("sp", lambda e: e.dma_start(out=mdiag.rearrange("p a b -> p (a b)"), in_=mdiag_d), writes=["mdiag"], dma=True)
    sc.add("sp", lambda e: e.dma_start(out=cbias, in_=cbias_d), writes=["cbias"], dma=True)

    def load_cast(src_ap, ncols_f32, dst_bf, dst_res):
        k = wst_i[0] % 2
        wst_i[0] += 1
        st = wst[k]
        sc.add("sp", lambda e: e.dma_start(out=st[:, 0:ncols_f32], in_=src_ap), writes=[("wst", k)], dma=True)
        sc.add("pool", lambda e: e.tensor_copy(out=dst_bf, in_=st[:, 0:ncols_f32]), reads=[("wst", k)], writes=[dst_res])

    def layer(l):
        x_src = xT_in if l == 0 else XS[(l - 1) % 2]
        x_dst = outT if l == L - 1 else XS[l % 2]
        w_l = w_in[l]
        w_l_v = w_l.rearrange("(kc p) n -> p kc n", p=128)

        sc.add("sp", lambda e, l=l: e.dma_start(out=small, in_=small_d[l]), writes=["small"], dma=True)
        sc.add("sp", lambda e, l=l: e.dma_start(out=lamv, in_=lamv_d[l].partition_broadcast(128)), writes=["lamv"], dma=True)
        ar.reset()
        tmpl = ar.alloc([128], F32)

        sc.add("dve", lambda e: e.tensor_tensor(out=tmpl[:, 0:64], in0=lamv[:, 0:64], in1=lamv[:, 64:128], op=ALU.mult), reads=["lamv"], writes=["tmpl0"])
        sc.add("dve", lambda e: e.tensor_tensor(out=tmpl[:, 64:128], in0=lamv[:, 128:192], in1=lamv[:, 192:256], op=ALU.mult), reads=["lamv"], writes=["tmpl1"])
        sc.add("dve", lambda e: e.reduce_sum(out=lamt[:, 0:1], in_=tmpl[:, 0:64], axis=AX.X), reads=["tmpl0"], writes=["lamt0"])
        sc.add("dve", lambda e: e.reduce_sum(out=lamt[:, 1:2], in_=tmpl[:, 64:128], axis=AX.X), reads=["tmpl1"], writes=["lamt1"])
        sc.add("act", lambda e: e.activation(out=lamt[:, 2:4], in_=lamt[:, 0:2], func=AF.Exp), reads=["lamt0", "lamt1"], writes=["lamt23"])
        sc.add("dve", lambda e: e.tensor_tensor(out=lamt[:, 6:7], in0=lamt[:, 3:4], in1=lamt[:, 2:3], op=ALU.subtract), reads=["lamt23"], writes=["lamt6"])
        sc.add("dve", lambda e: e.tensor_tensor(out=lamt[:, 4:5], in0=lamt[:, 6:7], in1=col(SM_LAMINIT), op=ALU.subtract), reads=["lamt6", "small"], writes=["lamt4"])
        sc.add("dve", lambda e: e.tensor_tensor(out=lamt[:, 5:6], in0=col(SM_SUBLN), in1=col(SM_OML), op=ALU.mult), reads=["small"], writes=["lamt5"])
        sc.add("act", lambda e: e.activation(out=spc[:, 0:6], in_=col(SM_LAM, 6), func=AF.Exp, scale=-1.0), reads=["small"], writes=["spc_a"])
        sc.add("act", lambda e: e.activation(out=lamt[:, 8:14], in_=spc[:, 0:6], func=AF.Ln, bias=1.0), reads=["spc_a"], writes=["spc_b"])
        sc.add("dve", lambda e: e.tensor_scalar(out=spc[:, 0:6], in0=lamt[:, 8:14], scalar1=-8.0, scalar2=None, op0=ALU.mult),
               reads=["spc_b"], writes=["spc"])
        sc.barrier()

        ar.reset()
        hT = ar.alloc([KC, S], BF16)
        p0mark = ar.off
        xt_rot = Rot(ar, 2, [KC, TC], F32, "xt")
        sq_t = ar.alloc([KC, TC], F32)
        rstd_rot = Rot(ar, 2, [TC], F32, "rstd")
        x_src_v = x_src.rearrange("(kc p) s -> p kc s", p=128)
        for t in range(NT):
            xt, xr = xt_rot.next()
            sc.add("sp", lambda e, xt=xt, t=t: e.dma_start(out=xt, in_=x_src_v[:, :, t * TC:(t + 1) * TC]), writes=[xr], dma=True)
            sc.add("act", lambda e, xt=xt: e.activation(out=sq_t, in_=xt, func=AF.Square), reads=[xr], writes=["sq_t"])
            bk, br = bank(t % 2)

            def ssmm(e, bk=bk):
                for kc in range(KC):
                    r = e.matmul(bk, lhsT=ones_f, rhs=sq_t[:, kc, :], start=(kc == 0), stop=(kc == KC - 1))
                return r
            sc.add("pe", ssmm, reads=["sq_t", "ones_f"], writes=[br])
            rs, rr = rstd_rot.next()
            def rstd_op(e, bk=bk, rs=rs):
                e.activation(out=rs, in_=bk, func=AF.Ln, scale=1.0 / DM, bias=EPS)
                return e.activation(out=rs, in_=rs, func=AF.Exp, scale=-0.5)
            sc.add("act", rstd_op, reads=[br], writes=[rr])

            def hmk(e, xt=xt, rs=rs, t=t):
                for kc in range(KC):
                    r = e.scalar_tensor_tensor(out=hT[:, kc, t * TC:(t + 1) * TC], in0=xt[:, kc, :], scalar=col(SM_GPRE + kc),
                                               in1=rs, op0=ALU.mult, op1=ALU.mult)
                return r
            sc.add("dve", hmk, reads=[xr, rr, "small"], writes=[("hT", t)])
        sc.barrier()

        ar.off = p0mark
        ob_rot_a = Rot(ar, 3, [TC], BF16, "ob_a")
        ob_rot_d = Rot(ar, 3, [TC], BF16, "ob_d")
        of_rot = Rot(ar, 2, [TC], F32, "of")
        vstage = ar.alloc([4 * 32 * 128], BF16)
        hT_res = [("hT", t) for t in range(NT)]
        bank_i = [0]

        def nbank():
            b = bank_i[0] % 8
            bank_i[0] += 1
            return bank(b)

        fm_jobs = []
        for g in range(3):
            fm_jobs.append((OFF["a_q"] + g * AW, AW, A_DIL[g], "scale", AQ[g], 0.125))
        for g in range(3):
            fm_jobs.append((OFF["a_k"] + g * AW, AW, A_DIL[g], "copy", AK[g], None))
        fm_jobs.append((OFF["a_g"], AW, 1, "silu", AG, None))
        fm_jobs.append((OFF["b_x"], BW, 1, "copyf", BX, None))
        fm_jobs.append((OFF["b_g"], BW, 1, "silu", BG, None))
        fm_jobs.append((OFF["c_q"], CW, 1, "scale", CQ, 0.125))
        fm_jobs.append((OFF["c_k"], CW, 1, "copy", CK, None))
        fm_jobs.append((OFF["c_g"], CW, 1, "silu", CG, None))
        fm_jobs.append((OFF["d_cq"], 384, 1, "copyf", DLAT, None))
        fm_jobs.append((OFF["d_kr"], 64, 1, "copyf_kr", DKRAW, None))
        fm_jobs.append((OFF["d_g"], DW, 1, "silu", DG, None))
        for i in range(8):
            fm_jobs.append((OFF["gate"] + i * 512, 512, 1, "gate", GT[i * 512:(i + 1) * 512, :], i))
        tm_jobs = []
        for g in range(3):
            tm_jobs.append((OFF["a_v"] + g * AW, AW, A_DIL[g], AV[g], A_SLOTS, 64))
        tm_jobs.append((OFF["c_v"], CW, 1, CV, C_HEADS, 128))
        jobs = [("fm", j) for j in fm_jobs] + [("tm", j) for j in tm_jobs]

        def load_job(ji):
            kind, j = jobs[ji]
            c0, n = j[0], j[1]
            k = ji % 2
            st = wst[k].rearrange("p (kc n) -> p kc n", kc=KC)
            wb = wbf[k].rearrange("p (kc n) -> p kc n", kc=KC)
            if kind == "fm" and j[3] == "copyf_kr":
                sc.add("sp", lambda e: e.dma_start(out=st[:, :, 0:32], in_=w_l_v[:, :, c0:c0 + 32]), writes=[("wst", k)], dma=True)
                sc.add("sp", lambda e: e.dma_start(out=st[:, :, 32:48], in_=w_l_v[:, :, c0 + 16:c0 + 32]), writes=[("wst", k, 1)], dma=True)
                sc.add("sp", lambda e: e.dma_start(out=st[:, :, 48:64], in_=w_l_v[:, :, c0:c0 + 16]), writes=[("wst", k, 2)], dma=True)
                rd = [("wst", k), ("wst", k, 1), ("wst", k, 2)]
            else:
                sc.add("sp", lambda e: e.dma_start(out=st[:, :, 0:n], in_=w_l_v[:, :, c0:c0 + n]), writes=[("wst", k)], dma=True)
                rd = [("wst", k)]
            sc.add("pool", lambda e: e.tensor_copy(out=wb[:, :, 0:n], in_=st[:, :, 0:n]), reads=rd, writes=[("wbf", k)])

        def rhs_view(d, kc, t):
            if d == 1:
                return [(hT[:, kc, t * TC:(t + 1) * TC], None)]
            hv = hT[:, kc, :].rearrange("p (l r) -> p r l", r=d)
            if d == 4:
                return [(hv[:, t // 2, (t % 2) * 512:(t % 2) * 512 + 512], None)]
            return [(hv[:, 2 * t:2 * t + 2, :], 2)]

        def lhs_view(d, kc, b):
            if d == 1:
                return hT[:, kc, b * 128:(b + 1) * 128]
            hv = hT[:, kc, :].rearrange("p (l r) -> p r l", r=d)
            nbs = 32 // d
            return hv[:, b // nbs, (b % nbs) * 128:(b % nbs) * 128 + 128]

        def compute_fm(ji):
            _, (c0, n, d, kind, dst, extra) = jobs[ji]
            k = ji % 2
            wb = wbf[k].rearrange("p (kc n) -> p kc n", kc=KC)
            ncb = (n + 127) // 128
            for cb in range(ncb):
                m = min(128, n - cb * 128)
                for t in range(NT):
                    bk, bres = nbank()

                    def mm(e, bk=bk, cb=cb, m=m, t=t):
                        r = None
                        for kc in range(KC):
                            (rv, a), = rhs_view(d, kc, t)
                            o = bk[0:m, :]
                            if a is not None:
                                o = o.rearrange("p (a b) -> p a b", a=a)
                            r = e.matmul(o, lhsT=wb[:, kc, cb * 128:cb * 128 + m], rhs=rv, start=(kc == 0), stop=(kc == KC - 1))
                        return r
                    sc.add("pe", mm, reads=[("wbf", k)] + hT_res, writes=[bres])
                    drows = dst[cb * 128:cb * 128 + m, t * TC:(t + 1) * TC]
                    if kind in ("silu", "gate"):
                        ot, ores = ob_rot_a.next()
                        if kind == "silu":
                            sc.add("act", lambda e, ot=ot, bk=bk, m=m: e.activation(out=ot[0:m, :], in_=bk[0:m, :], func=AF.Silu),
                                   reads=[bres], writes=[ores])
                        else:
                            bcol = SM_BGATE + (extra // 2) * 8 + (extra % 2) * 4 + cb
                            sc.add("act", lambda e, ot=ot, bk=bk, bcol=bcol: e.activation(out=ot, in_=bk, func=AF.Sigmoid, bias=col(bcol)),
                                   reads=[bres, "small"], writes=[ores])
                        sc.add("act", lambda e, ot=ot, drows=drows, m=m: e.dma_start(out=drows, in_=ot[0:m, :]), reads=[ores], dma=True)
                    elif kind in ("copy", "scale"):
                        ot, ores = ob_rot_d.next()
                        if kind == "copy":
                            sc.add("dve", lambda e, ot=ot, bk=bk, m=m: e.tensor_copy(out=ot[0:m, :], in_=bk[0:m, :]), reads=[bres], writes=[ores])
                        else:
                            sc.add("dve", lambda e, ot=ot, bk=bk, m=m: e.tensor_scalar(out=ot[0:m, :], in0=bk[0:m, :], scalar1=extra, scalar2=None, op0=ALU.mult),
                                   reads=[bres], writes=[ores])
                        sc.add("sp", lambda e, ot=ot, drows=drows, m=m: e.dma_start(out=drows, in_=ot[0:m, :]), reads=[ores], dma=True)
                    else:
                        ot, ores = of_rot.next()
                        sc.add("dve", lambda e, ot=ot, bk=bk, m=m: e.tensor_copy(out=ot[0:m, :], in_=bk[0:m, :]), reads=[bres], writes=[ores])
                        sc.add("sp", lambda e, ot=ot, drows=drows, m=m: e.dma_start(out=drows, in_=ot[0:m, :]), reads=[ores], dma=True)

        def compute_tm(ji):
            _, (c0, n, d, dst, nh, hd) = jobs[ji]
            k = ji % 2
            wb = wbf[k].rearrange("p (kc n) -> p kc n", kc=KC)
            vs = vstage[:, 0:nh * 32 * hd].rearrange("p (h b d) -> p h b d", h=nh, b=32)
            for b in range(32):
                bk, bres = nbank()

                def mm(e, bk=bk, b=b):
                    r = None
                    for kc in range(KC):
                        r = e.matmul(bk[:, 0:n], lhsT=lhs_view(d, kc, b), rhs=wb[:, kc, 0:n], start=(kc == 0), stop=(kc == KC - 1))
                    return r
                sc.add("pe", mm, reads=[("wbf", k)] + hT_res, writes=[bres])
                sc.add("dve", lambda e, bk=bk, b=b: e.tensor_copy(out=vs[:, :, b, :], in_=bk[:, 0:n].rearrange("p (h d) -> p h d", h=nh)),
                       reads=[bres], writes=[("vstage", b)])
            sc.add("sp", lambda e: e.dma_start(out=dst, in_=vstage[:, 0:nh * 32 * hd]), reads=[("vstage", b) for b in range(32)], dma=True)

        load_job(0)
        for ji in range(len(jobs)):
            if ji + 1 < len(jobs):
                load_job(ji + 1)
            if jobs[ji][0] == "fm":
                compute_fm(ji)
            else:
                compute_tm(ji)
        sc.barrier()

        ar.reset()
        wuqa = ar.alloc([2, D_HEADS * 96], BF16)
        wuqb = ar.alloc([2, D_HEADS * 96], BF16)
        wukvk = ar.alloc([DW], BF16)
        wukvv = ar.alloc([DW], BF16)
        load_cast(wuqa_d[l], 2 * D_HEADS * 96, wuqa.rearrange("p a b -> p (a b)"), "wuqa")
        load_cast(wuqb_d[l], 2 * D_HEADS * 96, wuqb.rearrange("p a b -> p (a b)"), "wuqb")
        load_cast(wukvk_d[l], DW, wukvk, "wukvk")
        load_cast(wukvv_d[l], DW, wukvv, "wukvv")
        lat_rot = Rot(ar, 2, [3, TC], F32, "lat")
        kra_rot = Rot(ar, 2, [TC], F32, "kra")
        krb_rot = Rot(ar, 2, [TC], F32, "krb")
        cc_rot = Rot(ar, 2, [TC], F32, "cc")
        ss_rot = Rot(ar, 2, [TC], F32, "ss")
        sq2 = ar.alloc([3, TC], F32)
        rq_rot = Rot(ar, 2, [TC], F32, "rq")
        rkv_rot = Rot(ar, 2, [TC], F32, "rkv")
        cqn_rot = Rot(ar, 2, [3, TC], BF16, "cqn")
        qd_rot = Rot(ar, 3, [TC], BF16, "qd")
        kd_rot = Rot(ar, 3, [TC], BF16, "kd")
        t1_rot = Rot(ar, 2, [TC], F32, "t1")
        t2_rot = Rot(ar, 2, [TC], F32, "t2")
        krr_rot = Rot(ar, 2, [TC], BF16, "krr")
        dvst = ar.alloc([D_HEADS, 32, 64], BF16)
        DLv = DLAT.rearrange("(c p) s -> p c s", p=128)
        for t in range(NT):
            tsl = slice(t * TC, (t + 1) * TC)
            lat, latr = lat_rot.next()
            kra, krar = kra_rot.next()
            krb, krbr = krb_rot.next()
            cct, ccr = cc_rot.next()
            sst, ssr = ss_rot.next()
            sc.add("sp", lambda e, lat=lat, tsl=tsl: e.dma_start(out=lat, in_=DLv[:, :, tsl]), writes=[latr], dma=True)
            sc.add("sp", lambda e, kra=kra, tsl=tsl: e.dma_start(out=kra[64:96, :], in_=DKRAW[0:32, tsl]), writes=[krar], dma=True)
            sc.add("sp", lambda e, krb=krb, tsl=tsl: e.dma_start(out=krb[64:96, :], in_=DKRAW[32:64, tsl]), writes=[krbr], dma=True)
            sc.add("sp", lambda e, cct=cct, tsl=tsl: e.dma_start(out=cct[64:96, :], in_=ropec_d[:, tsl]), writes=[ccr], dma=True)
            sc.add("sp", lambda e, sst=sst, tsl=tsl: e.dma_start(out=sst[64:96, :], in_=ropes_d[:, tsl]), writes=[ssr], dma=True)
            sc.add("act", lambda e, lat=lat: e.activation(out=sq2, in_=lat, func=AF.Square), reads=[latr], writes=["sq2"])
            b0, b0r = bank(0)
            b1, b1r = bank(1)

            def ssq(e, b0=b0, b1=b1):
                e.matmul(b0, lhsT=ones_f, rhs=sq2[:, 0, :], start=True, stop=False)
                e.matmul(b0, lhsT=ones_f, rhs=sq2[:, 1, :], start=False, stop=True)
                return e.matmul(b1, lhsT=ones_f, rhs=sq2[:, 2, :], start=True, stop=True)
            sc.add("pe", ssq, reads=["sq2", "ones_f"], writes=[b0r, b1r])
            rq, rqr = rq_rot.next()
            rkv, rkvr = rkv_rot.next()

            def rsq(e, rq=rq, rkv=rkv, b0=b0, b1=b1):
                e.activation(out=rq, in_=b0, func=AF.Ln, scale=1.0 / 256, bias=EPS)
                e.activation(out=rkv, in_=b1, func=AF.Ln, scale=1.0 / 128, bias=EPS)
                e.activation(out=rq, in_=rq, func=AF.Exp, scale=-0.5)
                return e.activation(out=rkv, in_=rkv, func=AF.Exp, scale=-0.5)
            sc.add("act", rsq, reads=[b0r, b1r], writes=[rqr, rkvr])
            cqn, cqnr = cqn_rot.next()

            def nrm(e, rq=rq, rkv=rkv, lat=lat, cqn=cqn):
                e.scalar_tensor_tensor(out=cqn[:, 0, :], in0=lat[:, 0, :], scalar=col(SM_QN), in1=rq, op0=ALU.mult, op1=ALU.mult)
                e.scalar_tensor_tensor(out=cqn[:, 1, :], in0=lat[:, 1, :], scalar=col(SM_QN + 1), in1=rq, op0=ALU.mult, op1=ALU.mult)
                return e.scalar_tensor_tensor(out=cqn[:, 2, :], in0=lat[:, 2, :], scalar=col(SM_KVN), in1=rkv, op0=ALU.mult, op1=ALU.mult)
            sc.add("dve", nrm, reads=[rqr, rkvr, latr, "small"], writes=[cqnr, rqr, rkvr])
            t1, t1r = t1_rot.next()
            t2, t2r = t2_rot.next()
            krr, krrr = krr_rot.next()

            def krope(e, kra=kra, krb=krb, cct=cct, sst=sst, t1=t1, t2=t2, krr=krr):
                e.tensor_tensor(out=t1[64:96, :], in0=krb[64:96, :], in1=sst[64:96, :], op=ALU.mult)
                e.tensor_tensor(out=t2[64:96, :], in0=kra[64:96, :], in1=cct[64:96, :], op=ALU.mult)
                return e.tensor_tensor(out=krr[64:96, :], in0=t1[64:96, :], in1=t2[64:96, :], op=ALU.add)
            sc.add("dve", krope, reads=[krar, krbr, ccr, ssr], writes=[t1r, t2r, krrr])
            sc.add("sp", lambda e, krr=krr, tsl=tsl: e.dma_start(out=DKR[:, tsl], in_=krr[64:96, :]), reads=[krrr], dma=True)
            for h in range(D_HEADS):
                ba, bar_ = bank(2 + (h % 2) * 3)
                bb, bbr = bank(3 + (h % 2) * 3)
                bkk, bkr = bank(4 + (h % 2) * 3)

                def upq(e, ba=ba, bb=bb, bkk=bkk, h=h, cqn=cqn):
                    for c in range(2):
                        e.matmul(ba[0:96, :], lhsT=wuqa[:, c, h * 96:(h + 1) * 96], rhs=cqn[:, c, :], start=(c == 0), stop=(c == 1))
                    for c in range(2):
                        e.matmul(bb[0:96, :], lhsT=wuqb[:, c, h * 96:(h + 1) * 96], rhs=cqn[:, c, :], start=(c == 0), stop=(c == 1))
                    return e.matmul(bkk[0:64, :], lhsT=wukvk[:, h * 64:(h + 1) * 64], rhs=cqn[:, 2, :], start=True, stop=True)
                sc.add("pe", upq, reads=[cqnr, "wuqa", "wuqb", "wukvk"], writes=[bar_, bbr, bkr])
                qd, qdr = qd_rot.next()
                kd, kdr = kd_rot.next()
                t1, t1r = t1_rot.next()
                t2, t2r = t2_rot.next()

                def qrope(e, ba=ba, bb=bb, qd=qd, t1=t1, t2=t2, cct=cct, sst=sst):
                    e.tensor_copy(out=qd[0:64, :], in_=ba[0:64, :])
                    e.tensor_tensor(out=t1[64:96, :], in0=bb[64:96, :], in1=sst[64:96, :], op=ALU.mult)
                    e.tensor_tensor(out=t2[64:96, :], in0=ba[64:96, :], in1=cct[64:96, :], op=ALU.mult)
                    return e.tensor_tensor(out=qd[64:96, :], in0=t1[64:96, :], in1=t2[64:96, :], op=ALU.add)
                sc.add("dve", qrope, reads=[bar_, bbr, ccr, ssr], writes=[qdr, t1r, t2r])
                sc.add("sp", lambda e, qd=qd, h=h, tsl=tsl: e.dma_start(out=DQ[h * 96:(h + 1) * 96, tsl], in_=qd[0:96, :]), reads=[qdr], dma=True)
                sc.add("act", lambda e, kd=kd, bkk=bkk: e.activation(out=kd[0:64, :], in_=bkk[0:64, :], func=AF.Copy), reads=[bkr], writes=[kdr])
                sc.add("act", lambda e, kd=kd, h=h, tsl=tsl: e.dma_start(out=DK[h * 64:(h + 1) * 64, tsl], in_=kd[0:64, :]), reads=[kdr], dma=True)
            for tb in range(4):
                bv, bvr = bank(tb % 2)
                b = t * 4 + tb
                sc.add("pe", lambda e, bv=bv, tb=tb, cqn=cqn: e.matmul(bv[:, 0:DW], lhsT=cqn[:, 2, tb * 128:(tb + 1) * 128], rhs=wukvv, start=True, stop=True),
                       reads=[cqnr, "wukvv"], writes=[bvr])
                sc.add("act", lambda e, bv=bv, b=b: e.activation(out=dvst[:, :, b, :], in_=bv[:, 0:DW].rearrange("p (h d) -> p h d", h=D_HEADS), func=AF.Copy),
                       reads=[bvr], writes=[("dvst", b)])
        sc.add("sp", lambda e: e.dma_start(out=DV, in_=dvst.rearrange("p h b d -> p (h b d)")), reads=[("dvst", b) for b in range(32)], dma=True)
        sc.barrier()

        ar.reset()
        ma = ar.alloc([3 * A_SLOTS, 384], F32)
        sc.add("sp", lambda e: e.dma_start(out=ma.rearrange("p a b -> p (a b)"), in_=ma_d), writes=["ma"], dma=True)
        acc = ar.alloc([S], F32)
        kt_rot = Rot(ar, 2, [S], BF16, "kt")
        qt_rot = Rot(ar, 2, [S], BF16, "qt")
        vraw_rot = Rot(ar, 2, [32 * 64], BF16, "vraw")
        vt_tiles = [ar.alloc([32, 128], BF16) for _ in range(2)]
        for k in range(2):
            sc.add("pool", lambda e, k=k: e.memset(vt_tiles[k][:, :, 64:128], 1.0), writes=[("vt1", k)])
        pf_rot = Rot(ar, 3, [384], F32, "pf")
        pb_rot = Rot(ar, 4, [384], BF16, "pb")
        ag_rot = Rot(ar, 2, [TC], BF16, "ag")
        rz_rot = Rot(ar, 2, [TC], F32, "rz")
        yf_rot = Rot(ar, 2, [TC], F32, "yf")
        yb_rot = Rot(ar, 2, [TC], BF16, "yb")
        groups = [(s_, g) for s_ in range(A_SLOTS) for g in range(3)]
        gctx = {}

        def a_load(gi):
            s_, g = groups[gi]
            kt, ktr = kt_rot.next()
            qt, qtr = qt_rot.next()
            vraw, vrr = vraw_rot.next()
            vk = gi % 2
            vt, vtr = vt_tiles[vk], ("vt", vk)
            sc.add("sp", lambda e: e.dma_start(out=kt[0:64, :], in_=AK[g][s_ * 64:(s_ + 1) * 64, :]), writes=[ktr], dma=True)
            sc.add("sp", lambda e: e.dma_start(out=qt[0:64, :], in_=AQ[g][s_ * 64:(s_ + 1) * 64, :]), writes=[qtr], dma=True)
            sc.add("sp", lambda e: e.dma_start(out=vraw, in_=AV[g][:, s_ * 2048:(s_ + 1) * 2048]), writes=[vrr], dma=True)
            sc.add("pool", lambda e: e.tensor_copy(out=vt[:, :, 0:64], in_=vraw.rearrange("p (b d) -> p b d", b=32)),
                   reads=[vrr, ("vt1", vk)], writes=[vtr])
            gctx[gi] = dict(kt=kt, ktr=ktr, qt=qt, qtr=qtr, vt=vt, vtr=vtr)

        items = []
        for gi, (s_, g) in enumerate(groups):
            for qb in range(32):
                items.append(dict(gi=gi, s_=s_, g=g, qb=qb, idx=len(items)))

        def a_S(it):
            gi, qb, g = it["gi"], it["qb"], it["g"]
            if qb == 0 and gi == 0:
                a_load(0)
            if qb == 16 and gi + 1 < len(groups):
                a_load(gi + 1)
            c = gctx[gi]
            d = A_DIL[g]
            nbs = 32 // d
            lb = qb % nbs
            js = [j for j in range(3) if 0 <= lb - 1 + j < nbs]
            psS, psSr = bank(it["idx"] % 3)
            kt, qt = c["kt"], c["qt"]

            def f(e):
                r = None
                for j in js:
                    kb = qb - 1 + j
                    r = e.matmul(psS[:, j * 128:(j + 1) * 128], lhsT=kt[0:64, kb * 128:(kb + 1) * 128], rhs=qt[0:64, qb * 128:(qb + 1) * 128],
                                 start=True, stop=True)
                return r
            sc.add("pe", f, reads=[c["ktr"], c["qtr"]], writes=[psSr])
            it.update(js=js, psS=psS, psSr=psSr, d=d, nbs=nbs, lb=lb)

        def a_E(it):
            js, psS = it["js"], it["psS"]
            c0, c1 = js[0] * 128, (js[-1] + 1) * 128
            pf, pfr = pf_rot.next()
            sc.add("act", lambda e: e.activation(out=pf[:, c0:c1], in_=psS[:, c0:c1], func=AF.Exp), reads=[it["psSr"]], writes=[pfr])
            it.update(pf=pf, pfr=pfr, c0=c0, c1=c1)

        def a_M(it):
            pf, c0, c1 = it["pf"], it["c0"], it["c1"]
            mi = it["g"] * A_SLOTS + it["s_"]
            pb, pbr = pb_rot.next()
            sc.add("dve", lambda e: e.tensor_tensor(out=pb[:, c0:c1], in0=pf[:, c0:c1], in1=ma[:, mi, c0:c1], op=ALU.mult),
                   reads=[it["pfr"], "ma"], writes=[pbr])
            it.update(pb=pb, pbr=pbr)

        def a_P(it):
            c = gctx[it["gi"]]
            vt, pb, js, qb = c["vt"], it["pb"], it["js"], it["qb"]
            psO, psOr = bank(3 + it["idx"] % 3)

            def pv(e):
                r = None
                for j in js:
                    kb = qb - 1 + j
                    r = e.matmul(psO[:, 0:128], lhsT=vt[:, kb, :], rhs=pb[:, j * 128:(j + 1) * 128], start=(j == js[0]), stop=(j == js[-1]))
                return r
            sc.add("pe", pv, reads=[it["pbr"], c["vtr"]], writes=[psOr])
            it.update(psO=psO, psOr=psOr)

        def a_A(it):
            d, nbs, qb, g, s_ = it["d"], it["nbs"], it["qb"], it["g"], it["s_"]
            psO = it["psO"]
            rr_, lb_ = qb // nbs, qb % nbs
            if d == 1:
                av = acc[:, qb * 128:(qb + 1) * 128]
            else:
                av = acc.rearrange("p (l r) -> p r l", r=d)[:, rr_, lb_ * 128:(lb_ + 1) * 128]
            if g == 0:
                sc.add("dve", lambda e: e.tensor_copy(out=av, in_=psO[:, 0:128]), reads=[it["psOr"]], writes=["acc"])
            else:
                sc.add("dve", lambda e: e.tensor_tensor(out=av, in0=psO[:, 0:128], in1=av, op=ALU.add), reads=[it["psOr"], "acc"], writes=["acc"])
            if g == 2 and qb == 31:
                a_epi_all(s_)

        def a_epi_all(s_):
            ctx = {}

            def st_copy(t):
                tsl = slice(t * TC, (t + 1) * TC)
                agt, agr = ag_rot.next()
                rz, rzr = rz_rot.next()
                yf, yfr = yf_rot.next()
                yb, ybr = yb_rot.next()
                sc.add("sp", lambda e: e.dma_start(out=agt[0:64, :], in_=AG[s_ * 64:(s_ + 1) * 64, tsl]), writes=[agr], dma=True)
                sc.add("dve", lambda e: e.tensor_copy(out=rz[0:64, :], in_=acc[64:128, tsl]), reads=["acc"], writes=[rzr])

                def rz_act(e):
                    e.activation(out=rz[0:64, :], in_=rz[0:64, :], func=AF.Ln)
                    return e.activation(out=rz[0:64, :], in_=rz[0:64, :], func=AF.Exp, scale=-1.0)
                sc.add("act", rz_act, reads=[rzr], writes=[rzr])
                ctx[t] = (tsl, agt, agr, rz, rzr, yf, yfr, yb, ybr)

            def st_mul(t):
                tsl, agt, agr, rz, rzr, yf, yfr, yb, ybr = ctx[t]

                def epi(e):
                    e.tensor_tensor(out=yf[0:64, :], in0=acc[0:64, tsl], in1=rz[0:64, :], op=ALU.mult)
                    return e.tensor_tensor(out=yb[0:64, :], in0=yf[0:64, :], in1=agt[0:64, :], op=ALU.mult)
                sc.add("dve", epi, reads=["acc", agr, rzr], writes=[yfr, ybr])
                sc.add("sp", lambda e: e.dma_start(out=Y[YA0 + s_ * 64:YA0 + (s_ + 1) * 64, tsl], in_=yb[0:64, :]), reads=[ybr], dma=True)

            for t in range(NT + 1):
                if t < NT:
                    st_copy(t)
                if t >= 1:
                    st_mul(t - 1)

        run_pipeline(items, [a_S, a_E, a_M, a_P, a_A], [0, 1, 2, 3, 4])
        sc.barrier()

        ar.reset()
        lw = ar.alloc([4 * NBC, 128], BF16)
        load_cast(lruw_d[l], 4 * NBC * 128, lw.rearrange("p a b -> p (a b)"), "lw")
        xp = ar.alloc([S + 4], F32)
        xc = ar.alloc([S], F32)
        Rb = ar.alloc([S], F32)
        Ib = ar.alloc([S], F32)
        Ab_ = ar.alloc([S], F32)
        Hf = ar.alloc([S], F32)
        Hb = ar.alloc([S], F32)
        xcb = ar.alloc([S], BF16)
        bgt = ar.alloc([S], BF16)
        sc.add("dve", lambda e: e.memset(xp[:, 0:1], 0.0), writes=["xp_pad0"])
        sc.add("dve", lambda e: e.memset(xp[:, S + 1:S + 4], 0.0), writes=["xp_pad1"])
        for c in range(NBC):
            sc.add("sp", lambda e, c=c: e.dma_start(out=xp[:, 1:S + 1], in_=BX[c * 128:(c + 1) * 128, :]), writes=["xp"], dma=True)
            sc.add("sp", lambda e, c=c: e.dma_start(out=bgt, in_=BG[c * 128:(c + 1) * 128, :]), writes=["bgt"], dma=True)

            def conv(e, c=c):
                e.tensor_scalar(out=xc, in0=xp[:, 0:S], scalar1=col(SM_CONVW + 0 * NBC + c), scalar2=col(SM_CONVB + c), op0=ALU.mult, op1=ALU.add)
                for j in range(1, 4):
                    r = e.scalar_tensor_tensor(out=xc, in0=xp[:, j:j + S], scalar=col(SM_CONVW + j * NBC + c), in1=xc, op0=ALU.mult, op1=ALU.add)
                return r
            sc.add("dve", conv, reads=["xp", "xp_pad0", "xp_pad1", "small"], writes=["xc"])
            sc.add("pool", lambda e: e.tensor_copy(out=xcb, in_=xc), reads=["xc"], writes=["xcb"])
            for dr in range(2):
                for t in range(NT):
                    tsl = slice(t * TC, (t + 1) * TC)
                    bR, bRr = bank((2 * t) % 8)
                    bI, bIr = bank((2 * t + 1) % 8)

                    def gmm(e, bR=bR, bI=bI, tsl=tsl, c=c, dr=dr):
                        e.matmul(bR, lhsT=lw[:, (0 * 2 + dr) * NBC + c, :], rhs=xcb[:, tsl], start=True, stop=True)
                        return e.matmul(bI, lhsT=lw[:, (1 * 2 + dr) * NBC + c, :], rhs=xcb[:, tsl], start=True, stop=True)
                    sc.add("pe", gmm, reads=["xcb", "lw"], writes=[bRr, bIr])
                    sc.add("dve", lambda e, bR=bR, tsl=tsl, c=c, dr=dr: e.tensor_scalar(out=Rb[:, tsl], in0=bR, scalar1=col(SM_BR + dr * NBC + c), scalar2=None, op0=ALU.add),
                           reads=[bRr, "small"], writes=[("Rb", t)])
                    sc.add("dve", lambda e, bI=bI, tsl=tsl, c=c, dr=dr: e.tensor_scalar(out=Ib[:, tsl], in0=bI, scalar1=col(SM_BI + dr * NBC + c), scalar2=None, op0=ALU.add),
                           reads=[bIr, "small"], writes=[("Ib", t)])
                Rres = [("Rb", t) for t in range(NT)]
                Ires = [("Ib", t) for t in range(NT)]

                def gates(e, c=c, dr=dr):
                    e.activation(out=Rb, in_=Rb, func=AF.Sigmoid)
                    e.activation(out=Ib, in_=Ib, func=AF.Sigmoid)
                    e.activation(out=Ab_, in_=Rb, func=AF.Exp, scale=spc[:, dr * NBC + c:dr * NBC + c + 1])
                    e.activation(out=Rb, in_=Ab_, func=AF.Square)
                    return e.activation(out=Rb, in_=Rb, func=AF.Sqrt, scale=-1.0, bias=1.0)
                sc.add("act", gates, reads=Rres + Ires + ["spc", "Hscan%d" % dr], writes=["Rw", "Ig", "Ab"])
                Hd = Hf if dr == 0 else Hb

                def premul(e):
                    e.tensor_tensor(out=Ib, in0=Ib, in1=xc, op=ALU.mult)
                    return e.tensor_tensor(out=Ib, in0=Ib, in1=Rb, op=ALU.mult)
                sc.add("dve", premul, reads=["Rw", "Ig", "Ab", "xc"], writes=["U", "Hscan%d" % (1 - dr)] + Rres + Ires)
                order = list(range(NT)) if dr == 0 else list(range(NT - 1, -1, -1))
                for oi, t in enumerate(order):
                    tsl = slice(t * TC, (t + 1) * TC)
                    if oi == 0:
                        init = 0.0
                    elif dr == 0:
                        init = Hd[:, t * TC - 1:t * TC]
                    else:
                        init = Hd[:, (t + 1) * TC:(t + 1) * TC + 1]
                    if dr == 0:
                        sc.add("dve", lambda e, Hd=Hd, tsl=tsl, init=init: e.tensor_tensor_scan(out=Hd[:, tsl], data0=Ab_[:, tsl], data1=Ib[:, tsl], initial=init, op0=ALU.mult, op1=ALU.add),
                               reads=["U", "Ab"] + ([("Hc", dr, order[oi - 1])] if oi else []), writes=[("Hc", dr, t)])
                    else:
                        sc.add("dve", lambda e, Hd=Hd, tsl=tsl, init=init: e.tensor_tensor_scan(out=Hd[:, tsl][:, ::-1], data0=Ab_[:, tsl][:, ::-1], data1=Ib[:, tsl][:, ::-1], initial=init, op0=ALU.mult, op1=ALU.add),
                               reads=["U", "Ab"] + ([("Hc", dr, order[oi - 1])] if oi else []), writes=[("Hc", dr, t)])

            def fin(e):
                e.tensor_tensor(out=Hf, in0=Hf, in1=Hb, op=ALU.add)
                return e.tensor_tensor(out=bgt, in0=Hf, in1=bgt, op=ALU.mult)
            sc.add("dve", fin, reads=[("Hc", 0, NT - 1), ("Hc", 1, 0), "bgt"], writes=["bgt"])
            sc.add("sp", lambda e, c=c: e.dma_start(out=Y[YB0 + c * 128:YB0 + (c + 1) * 128, :], in_=bgt), reads=["bgt"], dma=True)
        sc.barrier()

        ar.reset()
        cs = c_slopes() if not SPLIT else [min(c_slopes()[h], c_slopes()[h + 2]) for h in range(2)]
        ka_rot = Rot(ar, 4, [S], BF16, "ka")
        vc_rot = Rot(ar, 2, [32 * 128], BF16, "vc")
        qb_rot = Rot(ar, 4, [TC], BF16, "qbf")
        qa_rot = Rot(ar, 4, [TC], BF16, "qaf")
        cg_rot = Rot(ar, 2, [TC], BF16, "cg")
        pbc_rot = Rot(ar, 5, [TC], BF16, "pbc")
        pfc_rot = Rot(ar, 2, [128], F32, "pfc")
        eo_rot = Rot(ar, 4, [TC], F32, "eo")
        ez_rot = Rot(ar, 4, [TC], F32, "ez")
        e_o1 = ar.alloc([TC], F32)
        e_sq = ar.alloc([TC], F32)
        e_sd = ar.alloc([TC], F32)
        ybc_rot = Rot(ar, 2, [TC], BF16, "ybc")
        hctx, qctx = {}, {}

        def c_load_head(h):
            kas = []
            for c in range(2):
                ka, kar = ka_rot.next()
                kas.append((ka, kar))
                sc.add("sp", lambda e, ka=ka, c=c: e.dma_start(out=ka[0:64, :], in_=CK[(h * 2 + c) * 64:(h * 2 + c + 1) * 64, :]), writes=[kar], dma=True)
                sc.add("sp", lambda e, ka=ka: e.dma_start(out=ka[64:68, :], in_=caugk_d[h]), writes=[(kar, "aug")], dma=True)
            vc, vcr = vc_rot.next()
            sc.add("sp", lambda e: e.dma_start(out=vc, in_=CV[:, h * 4096:(h + 1) * 4096]), writes=[vcr], dma=True)
            hctx[h] = dict(kas=kas, vcv=vc.rearrange("p (b d) -> p b d", b=32), vcr=vcr)

        def c_load_chunk(h, qc):
            tsl = slice(qc * TC, (qc + 1) * TC)
            qs = []
            for c in range(2):
                qbf, qbr = qb_rot.next()
                qaf, qar = qa_rot.next()
                src = CQ[(h * 2 + c) * 64:(h * 2 + c + 1) * 64, tsl]
                sc.add("sp", lambda e, qbf=qbf, src=src: e.dma_start(out=qbf[0:64, :], in_=src), writes=[qbr], dma=True)
                sc.add("sp", lambda e, qbf=qbf: e.dma_start(out=qbf[64:68, :], in_=caugq_d[h, 0]), writes=[(qbr, "aug")], dma=True)
                sc.add("sp", lambda e, qaf=qaf, src=src: e.dma_start(out=qaf[0:64, :], in_=src), writes=[qar], dma=True)
                sc.add("sp", lambda e, qaf=qaf: e.dma_start(out=qaf[64:68, :], in_=caugq_d[h, 1]), writes=[(qar, "aug")], dma=True)
                qs.append((qbf, qbr, qaf, qar))
            cgt, cgr = cg_rot.next()
            sc.add("sp", lambda e: e.dma_start(out=cgt, in_=CG[h * 128:(h + 1) * 128, tsl]), writes=[cgr], dma=True)
            qctx[(h, qc)] = dict(qs=qs, cgt=cgt, cgr=cgr)

        items = []
        for h in range(C_HEADS):
            m = cs[h]
            for qc in range(NT):
                i0 = qc * TC
                for c in range(2):
                    kbs = []
                    for kb in range(32):
                        j0 = kb * 128
                        if j0 + 128 <= i0:
                            if m * (i0 - (j0 + 127)) > SKIP_T:
                                continue
                        elif j0 >= i0 + TC:
                            if m * (j0 - (i0 + TC - 1)) > SKIP_T:
                                continue
                        kbs.append(kb)
                    for ii, kb in enumerate(kbs):
                        items.append(dict(h=h, qc=qc, c=c, kb=kb, ii=ii, first=(ii == 0), last=(ii == len(kbs) - 1), idx=len(items),
                                          chunk_first=(c == 0 and ii == 0)))
        psOb = [bank(3), bank(5)]
        psZb = [bank(4), bank(6)]

        def c_S(it):
            h, qc, c, kb = it["h"], it["qc"], it["c"], it["kb"]
            if it["chunk_first"] and h == 0 and qc == 0:
                c_load_head(0)
                c_load_chunk(0, 0)
            if c == 0 and it["ii"] == 4:
                if qc == 0 and h + 1 < C_HEADS:
                    c_load_head(h + 1)
                nh, nq = (h, qc + 1) if qc + 1 < NT else (h + 1, 0)
                if nh < C_HEADS:
                    c_load_chunk(nh, nq)
            m = cs[h]
            i0 = qc * TC
            ka, kar = hctx[h]["kas"][c]
            qbf, qbr, qaf, qar = qctx[(h, qc)]["qs"][c]
            psS, psSr = bank(it["idx"] % 3)
            j0 = kb * 128
            rds = [kar, (kar, "aug"), qbr, (qbr, "aug"), qar, (qar, "aug")]
            pbt, pbr = pbc_rot.next()
            it.update(pbt=pbt, pbr=pbr)
            if j0 + 128 <= i0 or j0 >= i0 + TC:
                before = j0 + 128 <= i0
                qq = qbf if before else qaf
                bcol = h * CB_W + abs(i0 - j0) // 128
                sc.add("pe", lambda e: e.matmul(psS, lhsT=ka[0:68, j0:j0 + 128], rhs=qq[0:68, :], start=True, stop=True), reads=rds, writes=[psSr])
                sc.add("act", lambda e: e.activation(out=pbt, in_=psS, func=AF.Exp, bias=cbias[:, bcol:bcol + 1]), reads=[psSr, "cbias"], writes=[pbr])
            else:
                sb = (j0 - i0) // 128
                ca, cb_, cc_ = sb * 128, (sb + 1) * 128, TC

                def mmd(e):
                    r = None
                    if sb > 0:
                        r = e.matmul(psS[:, 0:ca], lhsT=ka[0:68, j0:j0 + 128], rhs=qaf[0:68, 0:ca], start=True, stop=True)
                    r = e.matmul(psS[:, ca:cb_], lhsT=ka[0:64, j0:j0 + 128], rhs=qbf[0:64, ca:cb_], start=True, stop=True)
                    if sb < 3:
                        r = e.matmul(psS[:, cb_:cc_], lhsT=ka[0:68, j0:j0 + 128], rhs=qbf[0:68, cb_:cc_], start=True, stop=True)
                    return r
                sc.add("pe", mmd, reads=rds, writes=[psSr])
                pfc, pfr = pfc_rot.next()

                def actd(e):
                    if sb > 0:
                        e.activation(out=pbt[:, 0:ca], in_=psS[:, 0:ca], func=AF.Exp, bias=cbias[:, h * CB_W + sb:h * CB_W + sb + 1])
                    if sb < 3:
                        e.activation(out=pbt[:, cb_:cc_], in_=psS[:, cb_:cc_], func=AF.Exp, bias=cbias[:, h * CB_W + 32 + sb:h * CB_W + 33 + sb])
                    return e.activation(out=pfc, in_=psS[:, ca:cb_], func=AF.Exp)
                sc.add("act", actd, reads=[psSr, "cbias"], writes=[pbr, (pbr, "a"), pfr])
                sc.add("dve", lambda e: e.tensor_tensor(out=pbt[:, ca:cb_], in0=pfc, in1=mdiag[:, h, :], op=ALU.mult), reads=[pfr, "mdiag", (pbr, "a")], writes=[pbr])

        def c_P(it):
            h, qc, c, kb = it["h"], it["qc"], it["c"], it["kb"]
            o_, or_ = psOb[c]
            z_, zr_ = psZb[c]
            vcv, vcr = hctx[h]["vcv"], hctx[h]["vcr"]
            pbt, first, last = it["pbt"], it["first"], it["last"]

            def f(e):
                e.matmul(o_, lhsT=vcv[:, kb, :], rhs=pbt, start=first, stop=last)
                return e.matmul(z_, lhsT=ones_bf, rhs=pbt, start=first, stop=last)
            sc.add("pe", f, reads=[it["pbr"], vcr, "ones_bf"], writes=[or_, zr_])
            if last:
                eo, eor = eo_rot.next()
                ez, ezr = ez_rot.next()
                sc.add("dve", lambda e: e.tensor_copy(out=eo, in_=o_), reads=[or_], writes=[eor])
                sc.add("dve", lambda e: e.tensor_copy(out=ez, in_=z_), reads=[zr_], writes=[ezr])
                qctx[(h, qc)]["ev%d" % c] = (eo, eor, ez, ezr)
                if c == 1:
                    c_epi(h, qc)

        def c_epi(h, qc):
            tsl = slice(qc * TC, (qc + 1) * TC)
            q = qctx[(h, qc)]
            eo0, eo0r, ez0, ez0r = q["ev0"]
            eo1, eo1r, ez1, ez1r = q["ev1"]
            cgt, cgr = q["cgt"], q["cgr"]

            def ep1(e):
                e.reciprocal(out=ez0, in_=ez0)
                e.reciprocal(out=ez1, in_=ez1)
                e.tensor_tensor(out=eo0, in0=eo0, in1=ez0, op=ALU.mult)
                e.tensor_tensor(out=eo1, in0=eo1, in1=ez1, op=ALU.mult)
                return e.scalar_tensor_tensor(out=eo0, in0=eo1, scalar=lamt[:, 4:5], in1=eo0, op0=ALU.mult, op1=ALU.add)
            sc.add("dve", ep1, reads=[eo0r, eo1r, ez0r, ez1r], writes=[eo0r, eo1r, ez0r, ez1r])
            sc.add("act", lambda e: e.activation(out=e_sq, in_=eo0, func=AF.Square), reads=[eo0r], writes=["e_sq"])
            bn, bnr = bank(7)
            sc.add("pe", lambda e: e.matmul(bn, lhsT=ones_f, rhs=e_sq, start=True, stop=True), reads=["e_sq", "ones_f"], writes=[bnr])
            def esd_op(e):
                e.activation(out=e_sd, in_=bn, func=AF.Ln, scale=1.0 / 128, bias=EPS)
                return e.activation(out=e_sd, in_=e_sd, func=AF.Exp, scale=-0.5)
            sc.add("act", esd_op, reads=[bnr], writes=["e_sd"])
            ybc, ybcr = ybc_rot.next()

            def ep2(e):
                e.scalar_tensor_tensor(out=eo1, in0=eo0, scalar=lamt[:, 5:6], in1=e_sd, op0=ALU.mult, op1=ALU.mult)
                return e.tensor_tensor(out=ybc, in0=eo1, in1=cgt, op=ALU.mult)
            sc.add("dve", ep2, reads=["e_sd", eo0r, eo1r, cgr], writes=[ybcr, "e_o1", eo1r])
            sc.add("sp", lambda e: e.dma_start(out=Y[YC0 + h * 128:YC0 + (h + 1) * 128, tsl], in_=ybc), reads=[ybcr], dma=True)

        run_pipeline(items, [c_S, c_P], [0, 2])
        sc.barrier()

        ar.reset()
        scale_d = 96.0 ** -0.5
        kd2_rot = Rot(ar, 2, [S], BF16, "kd2")
        vraw2_rot = Rot(ar, 2, [32 * 64], BF16, "vraw2")
        vd_tiles = [ar.alloc([32, 128], BF16) for _ in range(2)]
        for k in range(2):
            sc.add("pool", lambda e, k=k: e.memset(vd_tiles[k][:, :, 64:128], 1.0), writes=[("vd1", k)])
        qd2_rot = Rot(ar, 3, [TC], BF16, "qd2")
        dg_rot = Rot(ar, 3, [TC], BF16, "dg")
        pbd_rot = Rot(ar, 5, [TC], BF16, "pbd")
        rzd_rot = Rot(ar, 2, [TC], F32, "rzd")
        yfd_rot = Rot(ar, 2, [TC], F32, "yfd")
        ybd_rot = Rot(ar, 2, [TC], BF16, "ybd")
        dh, dq = {}, {}

        def d_load_head(h):
            kd, kdr = kd2_rot.next()
            sc.add("sp", lambda e: e.dma_start(out=kd[0:64, :], in_=DK[h * 64:(h + 1) * 64, :]), writes=[kdr], dma=True)
            sc.add("sp", lambda e: e.dma_start(out=kd[64:96, :], in_=DKR), writes=[(kdr, "r")], dma=True)
            vraw, vrr = vraw2_rot.next()
            vk = h % 2
            vt, vtr = vd_tiles[vk], ("vd", vk)
            sc.add("sp", lambda e: e.dma_start(out=vraw, in_=DV[:, h * 2048:(h + 1) * 2048]), writes=[vrr], dma=True)
            sc.add("pool", lambda e: e.tensor_copy(out=vt[:, :, 0:64], in_=vraw.rearrange("p (b d) -> p b d", b=32)),
                   reads=[vrr, ("vd1", vk)], writes=[vtr])
            dh[h] = dict(kd=kd, kdr=kdr, vt=vt, vtr=vtr)

        def d_load_chunk(h, qc):
            tsl = slice(qc * TC, (qc + 1) * TC)
            qd, qdr = qd2_rot.next()
            sc.add("sp", lambda e: e.dma_start(out=qd[0:96, :], in_=DQ[h * 96:(h + 1) * 96, tsl]), writes=[qdr], dma=True)
            dgt, dgr = dg_rot.next()
            sc.add("sp", lambda e: e.dma_start(out=dgt[0:64, :], in_=DG[h * 64:(h + 1) * 64, tsl]), writes=[dgr], dma=True)
            dq[(h, qc)] = dict(qd=qd, qdr=qdr, dgt=dgt, dgr=dgr)

        items = [dict(h=h, qc=qc, kb=kb, idx=(h * NT + qc) * 32 + kb) for h in range(D_HEADS) for qc in range(NT) for kb in range(32)]

        def d_S(it):
            h, qc, kb = it["h"], it["qc"], it["kb"]
            if kb == 0 and qc == 0 and h == 0:
                d_load_head(0)
                d_load_chunk(0, 0)
            if kb == 4:
                if qc == 0 and h + 1 < D_HEADS:
                    d_load_head(h + 1)
                nh, nq = (h, qc + 1) if qc + 1 < NT else (h + 1, 0)
                if nh < D_HEADS:
                    d_load_chunk(nh, nq)
            kd, kdr = dh[h]["kd"], dh[h]["kdr"]
            qd, qdr = dq[(h, qc)]["qd"], dq[(h, qc)]["qdr"]
            psS, psSr = bank(it["idx"] % 3)
            pbt, pbr = pbd_rot.next()
            sc.add("pe", lambda e: e.matmul(psS, lhsT=kd[0:96, kb * 128:(kb + 1) * 128], rhs=qd[0:96, :], start=True, stop=True),
                   reads=[kdr, (kdr, "r"), qdr], writes=[psSr])
            sc.add("act", lambda e: e.activation(out=pbt, in_=psS, func=AF.Exp, scale=scale_d), reads=[psSr], writes=[pbr])
            it.update(pbt=pbt, pbr=pbr)

        def d_P(it):
            h, qc, kb = it["h"], it["qc"], it["kb"]
            vt, vtr = dh[h]["vt"], dh[h]["vtr"]
            psO, psOr = bank(3 + (h * NT + qc) % 2)
            pbt = it["pbt"]
            sc.add("pe", lambda e: e.matmul(psO, lhsT=vt[:, kb, :], rhs=pbt, start=(kb == 0), stop=(kb == 31)), reads=[it["pbr"], vtr], writes=[psOr])
            if kb == 31:
                tsl = slice(qc * TC, (qc + 1) * TC)
                rz, rzr = rzd_rot.next()
                yf, yfr = yfd_rot.next()
                yb, ybr = ybd_rot.next()
                dgt, dgr = dq[(h, qc)]["dgt"], dq[(h, qc)]["dgr"]

                def epd(e):
                    e.reciprocal(out=rz[0:64, :], in_=psO[64:128, :])
                    e.tensor_tensor(out=yf[0:64, :], in0=psO[0:64, :], in1=rz[0:64, :], op=ALU.mult)
                    return e.tensor_tensor(out=yb[0:64, :], in0=yf[0:64, :], in1=dgt[0:64, :], op=ALU.mult)
                sc.add("dve", epd, reads=[psOr, dgr], writes=[rzr, yfr, ybr])
                sc.add("sp", lambda e: e.dma_start(out=Y[YD0 + h * 64:YD0 + (h + 1) * 64, tsl], in_=yb[0:64, :]), reads=[ybr], dma=True)

        run_pipeline(items, [d_S, d_P], [0, 2])
        sc.barrier()

        ar.reset()
        wbr = ar.alloc([NYC, 1024], BF16)
        wout = ar.alloc([8, 1024], BF16)
        wbr_f = wbr.rearrange("p a b -> p (a b)")
        wout_f = wout.rearrange("p a b -> p (a b)")
        nwb = (NYC * 1024 + 4095) // 4096
        for i in range(nwb):
            n = min(4096, NYC * 1024 - i * 4096)
            load_cast(wbr_d[l][:, i * 4096:i * 4096 + n], n, wbr_f[:, i * 4096:i * 4096 + n], ("wbr", i))
        for i in range(2):
            load_cast(wout_d[l][:, i * 4096:(i + 1) * 4096], 4096, wout_f[:, i * 4096:(i + 1) * 4096], ("wout", i))
        wbr_res = [("wbr", i) for i in range(nwb)]
        wout_res = [("wout", i) for i in range(2)]
        yt_rot = Rot(ar, 2, [NYC, TC], BF16, "yt")
        xt2 = ar.alloc([KC, TC], F32)
        out2 = ar.alloc([KC, TC], F32)
        merged = ar.alloc([KC, TC], BF16)
        mfull_rot = Rot(ar, 2, [KC, TC], BF16, "mfull") if SPLIT else None
        g_rot = Rot(ar, 4, [TC], BF16, "g")
        tm_rot = Rot(ar, 4, [TC], F32, "tm")
        macc_rot = Rot(ar, 3, [TC], F32, "macc")
        sqf_rot = Rot(ar, 2, [TC], F32, "sqf")
        rr2 = ar.alloc([TC], F32)
        Yv = Y.rearrange("(c p) s -> p c s", p=128)
        x_src_v2 = x_src.rearrange("(kc p) s -> p kc s", p=128)
        x_dst_v = x_dst.rearrange("(kc p) s -> p kc s", p=128)
        pbk = [0]
        mres = [("merged", oc) for oc in range(8)]

        def f_stage1(t):
            tsl = slice(t * TC, (t + 1) * TC)
            yt, ytr = yt_rot.next()
            sc.add("sp", lambda e: e.dma_start(out=yt, in_=Yv[:, :, tsl]), writes=[ytr], dma=True)
            for oc in range(8):
                macc = maccr = None
                for br in range(4):
                    gt, gr = g_rot.next()
                    sc.add("sp", lambda e, gt=gt, br=br, oc=oc: e.dma_start(out=gt, in_=GT[br * 1024 + oc * 128:br * 1024 + (oc + 1) * 128, tsl]), writes=[gr], dma=True)
                    bk, bkr = bank(pbk[0] % 4)
                    pbk[0] += 1
                    chs = BR_CHUNKS[br]

                    def bmm(e, bk=bk, chs=chs, oc=oc):
                        r = None
                        for ci, (cidx, nr) in enumerate(chs):
                            r = e.matmul(bk, lhsT=wbr[0:nr, cidx, oc * 128:(oc + 1) * 128], rhs=yt[0:nr, cidx, :], start=(ci == 0), stop=(ci == len(chs) - 1))
                        return r
                    sc.add("pe", bmm, reads=[ytr] + wbr_res, writes=[bkr])
                    if br == 0:
                        macc, maccr = macc_rot.next()
                        sc.add("dve", lambda e, bk=bk, gt=gt, macc=macc: e.tensor_tensor(out=macc, in0=bk, in1=gt, op=ALU.mult), reads=[bkr, gr], writes=[maccr])
                    else:
                        tm, tmr = tm_rot.next()
                        sc.add("dve", lambda e, bk=bk, gt=gt, tm=tm: e.tensor_tensor(out=tm, in0=bk, in1=gt, op=ALU.mult), reads=[bkr, gr], writes=[tmr])
                        if br < 3:
                            sc.add("pool", lambda e, tm=tm, macc=macc: e.tensor_tensor(out=macc, in0=macc, in1=tm, op=ALU.add), reads=[tmr, maccr], writes=[maccr])
                        else:
                            sc.add("pool", lambda e, tm=tm, oc=oc, macc=macc: e.tensor_tensor(out=merged[:, oc, :], in0=macc, in1=tm, op=ALU.add), reads=[tmr, maccr], writes=[("merged", oc)])
            if SPLIT:
                pr, q = t // 2, t % 2
                k = pr % 2
                sc.add("sp", lambda e: e.dma_start(out=ARI[k].rearrange("(kc p) s -> p kc s", p=128)[:, :, q * TC:(q + 1) * TC], in_=merged),
                       reads=mres, writes=[("ari", k, q)], dma=True)
                if q == 1:
                    sc.add("pool", lambda e: e.collective_compute("AllReduce", ALU.add, replica_groups=RG, ins=[ARI[k]], outs=[ARO[k]]),
                           reads=[("ari", k, 0), ("ari", k, 1)], writes=[("aro", k)], cc=True)

        def f_stage2(t):
            tsl = slice(t * TC, (t + 1) * TC)
            if SPLIT:
                pr, q = t // 2, t % 2
                k = pr % 2
                msrc, msr = mfull_rot.next()
                sc.add("sp", lambda e: e.dma_start(out=msrc, in_=ARO[k].rearrange("(kc p) s -> p kc s", p=128)[:, :, q * TC:(q + 1) * TC]),
                       reads=[("aro", k)], writes=[msr], dma=True)
                mrd = [msr]
            else:
                msrc, mrd = merged, mres
            sc.add("sp", lambda e: e.dma_start(out=xt2, in_=x_src_v2[:, :, tsl]), writes=["xt2"], dma=True)
            bn, bnr = bank(6)
            for oc2 in range(8):
                bo, bor = bank(4 + oc2 % 2)

                def omm(e, bo=bo, oc2=oc2):
                    r = None
                    for oc in range(8):
                        r = e.matmul(bo, lhsT=wout[:, oc, oc2 * 128:(oc2 + 1) * 128], rhs=msrc[:, oc, :], start=(oc == 0), stop=(oc == 7))
                    return r
                sc.add("pe", omm, reads=mrd + wout_res, writes=[bor])
                sqf, sqfr = sqf_rot.next()

                def oev(e, bo=bo, oc2=oc2, sqf=sqf):
                    e.activation(out=out2[:, oc2, :], in_=bo, func=AF.Copy)
                    return e.activation(out=sqf, in_=bo, func=AF.Square)
                sc.add("act", oev, reads=[bor], writes=[("out2", oc2), sqfr])
                sc.add("pe", lambda e, sqf=sqf, oc2=oc2: e.matmul(bn, lhsT=ones_f, rhs=sqf, start=(oc2 == 0), stop=(oc2 == 7)), reads=[sqfr, "ones_f"], writes=[bnr])
            def rr2_op(e):
                e.activation(out=rr2, in_=bn, func=AF.Ln, scale=1.0 / DM, bias=EPS)
                return e.activation(out=rr2, in_=rr2, func=AF.Exp, scale=-0.5)
            sc.add("act", rr2_op, reads=[bnr], writes=["rr2"])
            ores = [("out2", i) for i in range(8)]

            def resid(e):
                r = None
                for oc2 in range(8):
                    e.scalar_tensor_tensor(out=out2[:, oc2, :], in0=out2[:, oc2, :], scalar=col(SM_GPOST + oc2), in1=rr2, op0=ALU.mult, op1=ALU.mult)
                    r = e.tensor_tensor(out=xt2[:, oc2, :], in0=xt2[:, oc2, :], in1=out2[:, oc2, :], op=ALU.add)
                return r
            sc.add("dve", resid, reads=["rr2", "xt2", "small"] + ores, writes=["xt2", "rr2"] + ores)
            sc.add("sp", lambda e: e.dma_start(out=x_dst_v[:, :, tsl], in_=xt2), reads=["xt2"], dma=True)

        if SPLIT:
            for t in range(NT):
                f_stage1(t)
                if t % 2 == 1 and t >= 3:
                    f_stage2(t - 3)
                    f_stage2(t - 2)
            f_stage2(NT - 2)
            f_stage2(NT - 1)
        else:
            for t in range(NT):
                f_stage1(t)
                f_stage2(t)
        sc.barrier()

    for l in range(L):
        layer(l)
    sc.analyze()
    sc.emit(nc, es)
    es.close()
    return nc


def _bf(x):
    return np.asarray(x, dtype=np.float32).astype(ml_dtypes.bfloat16)


def _parts(par):
    if not SPLIT:
        return list(range(6)), list(range(6)), list(range(4)), list(range(6))
    return ([3 * par + i for i in range(3)], [3 * par + i for i in range(3)], [2 * par + i for i in range(2)],
            [3 * par + i for i in range(3)])


def build_consts(par=0):
    c = {}
    slots, _, cheads, _ = _parts(par)
    cs_all = c_slopes()
    cs = [cs_all[h] for h in cheads]
    caugq = np.zeros((C_HEADS, 2, 4, TC), np.float32)
    caugk = np.zeros((C_HEADS, 4, S), np.float32)
    cbias = np.zeros((128, C_HEADS, CB_W), np.float32)
    ii = np.arange(TC, dtype=np.float64)
    jj = (np.arange(S) % 128).astype(np.float64)
    for h, m in enumerate(cs):
        qb = (-m * ii).astype(np.float32)
        qb_hi = _bf(qb).astype(np.float32)
        qb_lo = _bf(qb - qb_hi).astype(np.float32)
        caugq[h, 0] = np.stack([qb_hi, qb_lo, np.ones(TC), np.ones(TC)])
        caugq[h, 1] = -caugq[h, 0]
        kb = (m * jj).astype(np.float32)
        kb_hi = _bf(kb).astype(np.float32)
        kb_lo = _bf(kb - kb_hi).astype(np.float32)
        caugk[h] = np.stack([np.ones(S), np.ones(S), kb_hi, kb_lo])
        cbias[:, h, 0:32] = (-m * 128.0 * np.arange(32))[None, :]
        cbias[:, h, 32:36] = (m * 128.0 * np.arange(4))[None, :]
    c["caugq"] = _bf(caugq)
    c["caugk"] = _bf(caugk)
    c["cbias"] = cbias.reshape(128, C_HEADS * CB_W)
    p = np.arange(128)[:, None].astype(np.float64)
    f = np.arange(128)[None, :].astype(np.float64)
    md = np.stack([np.exp(-m * np.abs(p - f)) for m in cs], axis=1)
    c["mdiag"] = md.reshape(128, C_HEADS * 128).astype(np.float32)
    sl_all = a_slopes()
    ma = np.zeros((128, 3 * A_SLOTS, 384), np.float64)
    k = np.arange(128)[:, None]
    for g, d in enumerate(A_DIL):
        for si, sg in enumerate(slots):
            for j in range(3):
                q = np.arange(128)[None, :]
                rel = np.abs((j - 1) * 128 + k - q)
                ma[:, g * A_SLOTS + si, j * 128:(j + 1) * 128] = np.where(rel <= 64, np.exp(-sl_all[sg] * d * rel), 0.0)
    c["ma"] = ma.reshape(128, 3 * A_SLOTS * 384).astype(np.float32)
    inv = (10000.0 ** (-np.arange(0, 32, 2, dtype=np.float32) / 32)).astype(np.float32)
    ang = np.arange(S, dtype=np.float32)[:, None] * inv[None, :]
    cos, sin = np.cos(ang).astype(np.float32).T, np.sin(ang).astype(np.float32).T
    c["ropec"] = np.ascontiguousarray(np.concatenate([cos, cos], 0))
    c["ropes"] = np.ascontiguousarray(np.concatenate([-sin, sin], 0))
    return c


def _bchan(par):
    _, blocks, _, _ = _parts(par)
    idx = -np.ones(BW, np.int64)
    for i, b in enumerate(blocks):
        idx[i * 64:(i + 1) * 64] = np.arange(b * 64, (b + 1) * 64)
    return idx


def pack_w_in(w, par):
    slots, _, cheads, dheads = _parts(par)
    L = w.shape[0]
    out = np.zeros((L, DM, IN_W), np.float32)
    O = OFF_ALL

    def put(name, off_in_fam, src_cols):
        n = len(src_cols)
        out[:, :, OFF[name] + off_in_fam:OFF[name] + off_in_fam + n] = w[:, :, src_cols]
    for fam in ("a_q", "a_k", "a_v"):
        for g in range(3):
            cols = np.concatenate([np.arange(O[fam] + g * 384 + s * 64, O[fam] + g * 384 + (s + 1) * 64) for s in slots])
            put(fam, g * AW, cols)
    put("a_g", 0, np.concatenate([np.arange(O["a_g"] + s * 64, O["a_g"] + (s + 1) * 64) for s in slots]))
    bidx = _bchan(par)
    nreal = int((bidx >= 0).sum())
    put("b_x", 0, O["b_x"] + bidx[:nreal])
    put("b_g", 0, O["b_g"] + bidx[:nreal])
    for fam in ("c_q", "c_k", "c_v", "c_g"):
        put(fam, 0, np.concatenate([np.arange(O[fam] + h * 128, O[fam] + (h + 1) * 128) for h in cheads]))
    put("d_cq", 0, np.arange(O["d_cq"], O["d_cq"] + 256))
    put("d_ckv", 0, np.arange(O["d_ckv"], O["d_ckv"] + 128))
    put("d_kr", 0, np.arange(O["d_kr"], O["d_kr"] + 32))
    put("d_g", 0, np.concatenate([np.arange(O["d_g"] + h * 64, O["d_g"] + (h + 1) * 64) for h in dheads]))
    put("gate", 0, np.arange(O["gate"], O["gate"] + 4096))
    return out


def pack_layers(inp, layers, par=0):
    f32 = np.float32
    L = len(layers)
    slots, blocks, cheads, dheads = _parts(par)
    bidx = _bchan(par)
    real = bidx >= 0
    small = np.zeros((L, 128, NSMALL), f32)
    lamv = np.zeros((L, 1, 256), f32)
    lruw = np.zeros((L, 128, 4 * NBC, 128), f32)
    wuqa = np.zeros((L, 128, 2, D_HEADS * 96), f32)
    wuqb = np.zeros((L, 128, 2, D_HEADS * 96), f32)
    wukvk = np.zeros((L, 128, DW), f32)
    wukvv = np.zeros((L, 128, DW), f32)
    wbr = np.zeros((L, 128, NYC, 1024), f32)
    wout = np.zeros((L, 128, 8, 1024), f32)

    def bvec(v):
        o = np.zeros(BW, f32)
        o[real] = v[bidx[real]]
        return o.reshape(NBC, 128).T
    for li, l in enumerate(layers):
        sm = small[li]
        sm[:, SM_GPRE:SM_GPRE + 8] = inp["norm_pre"][l].reshape(8, 128).T
        sm[:, SM_GPOST:SM_GPOST + 8] = inp["norm_post"][l].reshape(8, 128).T
        sm[:, SM_BGATE:SM_BGATE + 32] = inp["b_gate"][l].reshape(32, 128).T
        for j in range(4):
            sm[:, SM_CONVW + j * NBC:SM_CONVW + (j + 1) * NBC] = bvec(inp["conv_w"][l][j])
        sm[:, SM_CONVB:SM_CONVB + NBC] = bvec(inp["conv_b"][l])
        for dr in range(2):
            sm[:, SM_BR + dr * NBC:SM_BR + (dr + 1) * NBC] = bvec(inp["lru_br"][l][dr])
            sm[:, SM_BI + dr * NBC:SM_BI + (dr + 1) * NBC] = bvec(inp["lru_bi"][l][dr])
            sm[:, SM_LAM + dr * NBC:SM_LAM + (dr + 1) * NBC] = bvec(inp["lru_lambda"][l][dr])
        sm[:, SM_SUBLN] = inp["diff_subln"][l]
        sm[:, SM_QN:SM_QN + 2] = inp["mla_q_norm"][l].reshape(2, 128).T
        sm[:, SM_KVN] = inp["mla_kv_norm"][l]
        lam_init = 0.8 - 0.6 * math.exp(-0.3 * l)
        sm[:, SM_LAMINIT] = lam_init
        sm[:, SM_OML] = (1.0 - lam_init)
        lamv[li, 0, 0:64] = inp["diff_lam_q1"][l]
        lamv[li, 0, 64:128] = inp["diff_lam_k1"][l]
        lamv[li, 0, 128:192] = inp["diff_lam_q2"][l]
        lamv[li, 0, 192:256] = inp["diff_lam_k2"][l]
        for gi, w in enumerate((inp["lru_wr"][l], inp["lru_wi"][l])):
            for dr in range(2):
                for i, b in enumerate(blocks):
                    c, bb = i // 2, i % 2
                    lruw[li, bb * 64:(bb + 1) * 64, (gi * 2 + dr) * NBC + c, bb * 64:(bb + 1) * 64] = w[dr, b]
        uq = inp["mla_w_uq"][l].reshape(2, 128, 6, 96)[:, :, dheads, :]
        wuqa[li] = uq.transpose(1, 0, 2, 3).reshape(128, 2, D_HEADS * 96)
        uqb = uq.copy()
        uqb[..., 64:80] = uq[..., 80:96]
        uqb[..., 80:96] = uq[..., 64:80]
        wuqb[li] = uqb.transpose(1, 0, 2, 3).reshape(128, 2, D_HEADS * 96)
        ukv = inp["mla_w_ukv"][l].reshape(128, 6, 128)[:, dheads, :]
        wukvk[li] = ukv[:, :, 0:64].reshape(128, DW)
        wukvv[li] = ukv[:, :, 64:128].reshape(128, DW)
        wy = np.zeros((NYC * 128, 1024), f32)
        wa, wb_, wc, wd = inp["w_br_a"][l], inp["w_br_b"][l], inp["w_br_c"][l], inp["w_br_d"][l]
        for i, s_ in enumerate(slots):
            wy[YA0 + i * 64:YA0 + (i + 1) * 64] = wa[s_ * 64:(s_ + 1) * 64]
        wy[YB0:YB0 + BW][real] = wb_[bidx[real]]
        for i, h in enumerate(cheads):
            wy[YC0 + i * 128:YC0 + (i + 1) * 128] = wc[h * 128:(h + 1) * 128]
        for i, h in enumerate(dheads):
            wy[YD0 + i * 64:YD0 + (i + 1) * 64] = wd[h * 64:(h + 1) * 64]
        wbr[li] = wy.reshape(NYC, 128, 1024).transpose(1, 0, 2)
        wout[li] = inp["w_out"][l].reshape(8, 128, 1024).transpose(1, 0, 2)
    return dict(small=small, lamv=lamv, lruw=lruw.reshape(L, 128, 4 * NBC * 128), wuqa=wuqa.reshape(L, 128, 2 * D_HEADS * 96),
                wuqb=wuqb.reshape(L, 128, 2 * D_HEADS * 96), wukvk=wukvk, wukvv=wukvv, wbr=wbr.reshape(L, 128, NYC * 1024),
                wout=wout.reshape(L, 128, 8 * 1024))


_PROG = {}


def get_prog(nl, debug=False):
    key = (nl, tuple(debug) if debug else None)
    if key not in _PROG:
        _PROG[key] = build_program(nl, debug)
    return _PROG[key]


FUSED = True


def make_in_maps(inp, layers, xT_by_batch):
    npar = 2 if SPLIT else 1
    w_all = np.ascontiguousarray(inp["w_in"][layers[0]:layers[-1] + 1]).astype(np.float32)
    per = []
    for par in range(npar):
        m = dict(w_in=pack_w_in(w_all, par) if SPLIT else w_all)
        m.update(pack_layers(inp, layers, par))
        m.update(build_consts(par))
        per.append(m)
    in_maps = []
    for c in range(8):
        b, par = (c // 2, c % 2) if SPLIT else (c % 4, 0)
        m = dict(xT=xT_by_batch[b])
        m.update(per[par])
        in_maps.append(m)
    return in_maps


def kernel(**inputs):
    inp = {k: np.asarray(v) for k, v in inputs.items()}
    x = inp["x"].astype(np.float32)
    xT = [np.ascontiguousarray(x[b].T) for b in range(4)]
    groups = [list(range(DEPTH))] if FUSED else [[l] for l in range(DEPTH)]
    for layers in groups:
        nc = get_prog(len(layers))
        in_maps = make_in_maps(inp, layers, xT)
        res = run_bass_kernel_spmd(nc, in_maps, core_ids=list(range(8)))
        xT = [np.asarray(res.results[(2 * b) if SPLIT else b]["outT"]) for b in range(4)]
    out = np.stack([xT[b].T for b in range(4)], 0).astype(np.float32)
    return np.ascontiguousarray(out)
```
